# Optimizing a Trainium2 kernel written in Bass

```python
import jax
import jax.numpy as jnp
from jax import lax
import numpy as np

D_MODEL = 1024
BATCH = 8
SEQ = 2048
DEPTH = 4
DEC_BATCH = 2
DEC_SEQ = 8192
PAST_LEN = 128

MEM_LEN = 256
GRID_W = 64
HEAD_DIM = 64
Q_BLOCK = 128
ROPE_THETA = 10000.0
EPS = 1e-6
MLA_HEADS = 6
MLA_Q_RANK = 256
MLA_KV_RANK = 128
MLA_NOPE = 64
MLA_ROPE = 32
MLA_V = 64
MLA_QK = MLA_NOPE + MLA_ROPE
NA_HEADS = 6
NA_WIN_R = 8
NA_WIN_C = 16
DIL_WINDOWS = (128, 512, 2048)
DIL_RATES = (1, 4, 16)
DIL_GROUPS = 3
DIL_HEADS = 4
DIL_ALL = DIL_GROUPS * DIL_HEADS
X_HEADS = 4
X_HEAD_DIM = D_MODEL // X_HEADS
D_FF = 2816
CONV_W = 3
SPLIT_CQ = MLA_Q_RANK
SPLIT_CKV = SPLIT_CQ + MLA_KV_RANK
SPLIT_KR = SPLIT_CKV + MLA_ROPE
SPLIT_NA = SPLIT_KR + 3 * NA_HEADS * HEAD_DIM
D_IN = SPLIT_NA + 3 * DIL_ALL * HEAD_DIM
MIX_OUT = MLA_HEADS * MLA_V + NA_HEADS * HEAD_DIM + DIL_HEADS * HEAD_DIM

kernel_name = 'hybrid_mla_natten_dilated_encoder'


def rmsnorm(x, g):
    xf = x.astype(jnp.float32)
    y = xf * lax.rsqrt(jnp.mean(xf * xf, axis=-1, keepdims=True) + EPS)
    return (y * g.astype(jnp.float32)).astype(x.dtype)


def rope(x, pos):
    half = x.shape[-1] // 2
    inv = ROPE_THETA ** (-jnp.arange(half, dtype=jnp.float32) / half)
    ang = pos.astype(jnp.float32)[:, None] * inv[None, :]
    cos = jnp.cos(ang)[:, None, :].astype(x.dtype)
    sin = jnp.sin(ang)[:, None, :].astype(x.dtype)
    x1, x2 = x[..., :half], x[..., half:]
    return jnp.concatenate([x1 * cos - x2 * sin, x2 * cos + x1 * sin], axis=-1)


def block_dense_attention(q, k, v):
    B, S, H, dk = q.shape
    nb = S // Q_BLOCK
    scale = dk ** -0.5
    qb = q.reshape(B, nb, Q_BLOCK, H, dk).transpose(1, 0, 2, 3, 4)

    def one(qblk):
        s = jnp.einsum('bqhd,bkhd->bhqk', qblk, k).astype(jnp.float32) * scale
        p = jax.nn.softmax(s, axis=-1).astype(v.dtype)
        return jnp.einsum('bhqk,bkhd->bqhd', p, v)

    o = lax.map(one, qb)
    return o.transpose(1, 0, 2, 3, 4).reshape(B, S, H, v.shape[-1])


def mla_attention(c_q, c_kv, k_r, lp, pos):
    B, S, _ = c_q.shape
    c_q = rmsnorm(c_q, lp['mla_q_norm'])
    c_kv = rmsnorm(c_kv, lp['mla_kv_norm'])
    q = (c_q @ lp['w_uq']).reshape(B, S, MLA_HEADS, MLA_QK)
    k_nope = (c_kv @ lp['w_uk']).reshape(B, S, MLA_HEADS, MLA_NOPE)
    v = (c_kv @ lp['w_uv']).reshape(B, S, MLA_HEADS, MLA_V)
    k = jnp.concatenate([k_nope, jnp.broadcast_to(k_r[:, :, None, :], (B, S, MLA_HEADS, MLA_ROPE))], axis=-1)
    q = rmsnorm(q, lp['mla_qn'])
    k = rmsnorm(k, lp['mla_kn'])
    q = jnp.concatenate([q[..., :MLA_NOPE], rope(q[..., MLA_NOPE:], pos)], axis=-1)
    k = jnp.concatenate([k[..., :MLA_NOPE], rope(k[..., MLA_NOPE:], pos)], axis=-1)
    return block_dense_attention(q, k, v)


def neighbourhood_attention(q, k, v, rpb):
    B, S, H, d = q.shape
    rows = S // GRID_W
    kr_ = min(NA_WIN_R, rows)
    kc = NA_WIN_C
    qg = q.reshape(B, rows, GRID_W, H, d).transpose(1, 0, 2, 3, 4)
    kg = k.reshape(B, rows, GRID_W, H, d)
    vg = v.reshape(B, rows, GRID_W, H, d)
    r_idx = jnp.arange(rows)
    r_start = jnp.clip(r_idx - kr_ // 2, 0, rows - kr_)
    c_idx = np.arange(GRID_W)
    c_start = np.clip(c_idx - kc // 2, 0, GRID_W - kc)
    c_keys = c_start[:, None] + np.arange(kc)[None, :]
    c_off = c_keys - c_idx[:, None] + (NA_WIN_C - 1)
    bias_c = rpb[:, :, c_off]
    scale = d ** -0.5

    def one(args):
        q_row, r, rs = args
        k_rows = lax.dynamic_slice_in_dim(kg, rs, kr_, axis=1)
        v_rows = lax.dynamic_slice_in_dim(vg, rs, kr_, axis=1)
        k_nb = k_rows[:, :, c_keys]
        v_nb = v_rows[:, :, c_keys]
        s = jnp.einsum('bqhd,brqjhd->bhqrj', q_row, k_nb).astype(jnp.float32) * scale
        r_off = rs + jnp.arange(kr_) - r + (NA_WIN_R - 1)
        bias = jnp.take(bias_c, r_off, axis=1).transpose(0, 2, 1, 3)
        s = s + bias[None].astype(jnp.float32)
        p = jax.nn.softmax(s.reshape(B, H, GRID_W, kr_ * kc), axis=-1).reshape(s.shape).astype(v.dtype)
        return jnp.einsum('bhqrj,brqjhd->bqhd', p, v_nb)

    o = lax.map(one, (qg, r_idx, r_start))
    return o.transpose(1, 0, 2, 3, 4).reshape(B, S, H * d)


def dilated_attention(q, k, v):
    B, S, _, d = q.shape
    G, Hg = DIL_GROUPS, DIL_HEADS
    offs = jnp.asarray(np.stack([r * np.arange(-(w // (2 * r)), w // (2 * r) + 1)
                                 for w, r in zip(DIL_WINDOWS, DIL_RATES)]))
    nb = S // Q_BLOCK
    qb = q.reshape(B, nb, Q_BLOCK, G, Hg, d).transpose(1, 0, 2, 3, 4, 5)
    kG = k.reshape(B, S, G, Hg, d).transpose(2, 0, 1, 3, 4)
    vG = v.reshape(B, S, G, Hg, d).transpose(2, 0, 1, 3, 4)
    scale = d ** -0.5

    def one(args):
        qblk, b0 = args
        pos = b0 + jnp.arange(Q_BLOCK)
        idx = pos[None, :, None] + offs[:, None, :]
        valid = (idx >= 0) & (idx < S)
        idx = jnp.clip(idx, 0, S - 1)
        kn = jax.vmap(lambda t, i: jnp.take(t, i, axis=1))(kG, idx)
        vn = jax.vmap(lambda t, i: jnp.take(t, i, axis=1))(vG, idx)
        s = jnp.einsum('bqghd,gbqjhd->gbhqj', qblk, kn).astype(jnp.float32) * scale
        s = jnp.where(valid[:, None, None], s, -jnp.inf)
        lse = jax.nn.logsumexp(s, axis=-1, keepdims=True)
        p = jnp.exp(s - lse).astype(v.dtype)
        o = jnp.einsum('gbhqj,gbqjhd->gbhqd', p, vn)
        alpha = jax.nn.softmax(lse[..., 0], axis=0).astype(v.dtype)
        return jnp.einsum('gbhq,gbhqd->bqhd', alpha, o)

    o = lax.map(one, (qb, jnp.arange(nb) * Q_BLOCK))
    return o.transpose(1, 0, 2, 3, 4).reshape(B, S, Hg * d)


def token_mixer(h, lp, pos):
    B, S, _ = h.shape
    z = h @ lp['w_in']
    c_q = z[..., :SPLIT_CQ]
    c_kv = z[..., SPLIT_CQ:SPLIT_CKV]
    k_r = z[..., SPLIT_CKV:SPLIT_KR]
    na = z[..., SPLIT_KR:SPLIT_NA].reshape(B, S, 3, NA_HEADS, HEAD_DIM)
    dl = z[..., SPLIT_NA:].reshape(B, S, 3, DIL_ALL, HEAD_DIM)
    o_a = mla_attention(c_q, c_kv, k_r, lp, pos).reshape(B, S, MLA_HEADS * MLA_V)
    q_b = rmsnorm(na[:, :, 0], lp['na_qn'])
    k_b = rmsnorm(na[:, :, 1], lp['na_kn'])
    o_b = neighbourhood_attention(q_b, k_b, na[:, :, 2], lp['na_rpb'])
    q_c = rope(rmsnorm(dl[:, :, 0], lp['dil_qn']), pos)
    k_c = rope(rmsnorm(dl[:, :, 1], lp['dil_kn']), pos)
    o_c = dilated_attention(q_c, k_c, dl[:, :, 2])
    return jnp.concatenate([o_a, o_b, o_c], axis=-1) @ lp['w_o']


def memory_cross_attention(h, mem, lp):
    B, S, _ = h.shape
    M = mem.shape[1]
    q = (h @ lp['w_cq']).reshape(B, S, X_HEADS, X_HEAD_DIM)
    kv = (rmsnorm(mem, lp['norm_mem']) @ lp['w_ckv']).reshape(B, M, 2, X_HEADS, X_HEAD_DIM)
    q = rmsnorm(q, lp['x_qn'])
    k = rmsnorm(kv[:, :, 0], lp['x_kn'])
    v = kv[:, :, 1]
    s = jnp.einsum('bqhd,bkhd->bhqk', q, k).astype(jnp.float32) * (X_HEAD_DIM ** -0.5)
    p = jax.nn.softmax(s, axis=-1).astype(v.dtype)
    o = jnp.einsum('bhqk,bkhd->bqhd', p, v).reshape(B, S, D_MODEL)
    return o @ lp['w_co']


def conv_ffn(h, lp):
    S = h.shape[1]
    u = h @ lp['w_up']
    pad = CONV_W // 2
    up = jnp.pad(u, ((0, 0), (pad, pad), (0, 0)))
    w = lp['conv_w']
    u = sum(up[:, j:j + S] * w[j] for j in range(CONV_W)) + lp['conv_b']
    a, g = u[..., :D_FF], u[..., D_FF:]
    return (jax.nn.silu(g) * a) @ lp['w_down']


def encoder_trunk(x, mem, layers):
    S = x.shape[1]
    pos = jnp.arange(S)
    for l in range(DEPTH):
        lp = layers[l]
        x = x + token_mixer(rmsnorm(x, lp['norm_mix']), lp, pos)
        x = x + memory_cross_attention(rmsnorm(x, lp['norm_cross']), mem, lp)
        x = x + conv_ffn(rmsnorm(x, lp['norm_ffn']), lp)
    return x


def setup_inputs(seed: int = 0) -> dict:
    key = jax.random.key(seed)
    ks = jax.random.split(key, 32)
    f32 = jnp.float32
    L = DEPTH
    res = (3 * DEPTH) ** -0.5

    def nrm(k, shape, scale):
        return jax.random.normal(k, shape, f32) * scale

    def gain(k, n):
        return 1.0 + 0.02 * jax.random.normal(k, (L, n), f32)

    return {
        'x_prompt': nrm(ks[0], (BATCH, SEQ, D_MODEL), 1.0),
        'x_sample': nrm(ks[1], (DEC_BATCH, DEC_SEQ, D_MODEL), 1.0),
        'mem_prompt': nrm(ks[2], (BATCH, MEM_LEN, D_MODEL), 1.0),
        'mem_sample': nrm(ks[3], (DEC_BATCH, MEM_LEN, D_MODEL), 1.0),
        'norm_mix': gain(ks[4], D_MODEL),
        'w_in': nrm(ks[5], (L, D_MODEL, D_IN), D_MODEL ** -0.5),
        'mla_q_norm': gain(ks[6], MLA_Q_RANK),
        'mla_kv_norm': gain(ks[7], MLA_KV_RANK),
        'w_uq': nrm(ks[8], (L, MLA_Q_RANK, MLA_HEADS * MLA_QK), MLA_Q_RANK ** -0.5),
        'w_uk': nrm(ks[9], (L, MLA_KV_RANK, MLA_HEADS * MLA_NOPE), MLA_KV_RANK ** -0.5),
        'w_uv': nrm(ks[10], (L, MLA_KV_RANK, MLA_HEADS * MLA_V), MLA_KV_RANK ** -0.5),
        'mla_qn': gain(ks[11], MLA_QK),
        'mla_kn': gain(ks[12], MLA_QK),
        'na_qn': gain(ks[13], HEAD_DIM),
        'na_kn': gain(ks[14], HEAD_DIM),
        'na_rpb': nrm(ks[15], (L, NA_HEADS, 2 * NA_WIN_R - 1, 2 * NA_WIN_C - 1), 0.1),
        'dil_qn': gain(ks[16], HEAD_DIM),
        'dil_kn': gain(ks[17], HEAD_DIM),
        'w_o': nrm(ks[18], (L, MIX_OUT, D_MODEL), MIX_OUT ** -0.5 * res),
        'norm_cross': gain(ks[19], D_MODEL),
        'norm_mem': gain(ks[20], D_MODEL),
        'w_cq': nrm(ks[21], (L, D_MODEL, D_MODEL), D_MODEL ** -0.5),
        'w_ckv': nrm(ks[22], (L, D_MODEL, 2 * D_MODEL), D_MODEL ** -0.5),
        'x_qn': gain(ks[23], X_HEAD_DIM),
        'x_kn': gain(ks[24], X_HEAD_DIM),
        'w_co': nrm(ks[25], (L, D_MODEL, D_MODEL), D_MODEL ** -0.5 * res),
        'norm_ffn': gain(ks[26], D_MODEL),
        'w_up': nrm(ks[27], (L, D_MODEL, 2 * D_FF), D_MODEL ** -0.5),
        'conv_w': nrm(ks[28], (L, CONV_W, 2 * D_FF), CONV_W ** -0.5),
        'conv_b': nrm(ks[29], (L, 2 * D_FF), 0.01),
        'w_down': nrm(ks[30], (L, D_FF, D_MODEL), D_FF ** -0.5 * res),
    }


def reference(x_prompt, x_sample, mem_prompt, mem_sample, norm_mix, w_in, mla_q_norm, mla_kv_norm,
              w_uq, w_uk, w_uv, mla_qn, mla_kn, na_qn, na_kn, na_rpb, dil_qn, dil_kn, w_o,
              norm_cross, norm_mem, w_cq, w_ckv, x_qn, x_kn, w_co, norm_ffn, w_up, conv_w, conv_b, w_down):
    layers = []
    for l in range(DEPTH):
        layers.append({
            'norm_mix': norm_mix[l], 'w_in': w_in[l], 'mla_q_norm': mla_q_norm[l],
            'mla_kv_norm': mla_kv_norm[l], 'w_uq': w_uq[l], 'w_uk': w_uk[l], 'w_uv': w_uv[l],
            'mla_qn': mla_qn[l], 'mla_kn': mla_kn[l], 'na_qn': na_qn[l], 'na_kn': na_kn[l],
            'na_rpb': na_rpb[l], 'dil_qn': dil_qn[l], 'dil_kn': dil_kn[l], 'w_o': w_o[l],
            'norm_cross': norm_cross[l], 'norm_mem': norm_mem[l], 'w_cq': w_cq[l], 'w_ckv': w_ckv[l],
            'x_qn': x_qn[l], 'x_kn': x_kn[l], 'w_co': w_co[l], 'norm_ffn': norm_ffn[l],
            'w_up': w_up[l], 'conv_w': conv_w[l], 'conv_b': conv_b[l], 'w_down': w_down[l],
        })
    y_prompt = encoder_trunk(x_prompt, mem_prompt, layers)
    y_sample = encoder_trunk(x_sample, mem_sample, layers)
    return (y_prompt, y_sample)
```

```python
import numpy as np
import concourse.bass as bass
import concourse.mybir as mybir
from concourse.bass_utils import run_bass_kernel_spmd
from contextlib import ExitStack

F32 = mybir.dt.float32
BF16 = mybir.dt.bfloat16
AF = mybir.ActivationFunctionType
ALU = mybir.AluOpType
AX = mybir.AxisListType

D = 1024
L = 4
EPS = 1e-6
NEG = -30000.0
D_IN = 3872
DFF = 2816
SP_ = 2048
SS_ = 8192
MEM = 256
G_MIX, G_CQ, G_CKV, G_MQN, G_MKN, G_NQ, G_NK, G_DQ, G_DK, G_CROSS, G_XQ, G_FFN, G_MEM, G_XK = (
    0, 1024, 1280, 1408, 1504, 1600, 1664, 1728, 1792, 1856, 2880, 3136, 4160, 5184)
NG = 5440
DIL_R = (1, 4, 16)
DIL_DMIN = (-128, -256, -1024)
DIL_DMAX = (512, 640, 1408)
DIL_OFF = (0, 1152, 2560)
DIL_W = 5504
NA_MM = 22


class Ev:
    __slots__ = ("sem", "val", "eng", "ds")

    def __init__(self, sem, val, eng, ds=None):
        self.sem = sem
        self.val = val
        self.eng = eng
        self.ds = ds


class Buf:
    def __init__(self, name, t=None):
        self.name = name
        self.t = t
        self.w = []
        self.r = {}
        self.rd = []


class DS:
    def __init__(self, sem):
        self.sem = sem
        self.cnt = 0


class Eng:
    def __init__(self, name, h, pe=False):
        self.name = name
        self.h = h
        self.pe = pe
        self.sem = None
        self.count = 0
        self.waited = {}
        self.nsem = 0


class Tracker:
    EPOCH = 1 << 20

    def __init__(self, nc, es):
        self.nc = nc
        self.es = es
        self.uid = 0
        self.pe = Eng("pe", nc.tensor, True)
        self.act = Eng("act", nc.scalar)
        self.dve = Eng("dve", nc.vector)
        self.pool = Eng("pool", nc.gpsimd)
        self.sp = Eng("sp", nc.sync)
        self.engs = [self.pe, self.act, self.dve, self.pool, self.sp]
        for e in self.engs:
            e.sem = self.newsem(e.name + "_s0")
        self.free_ds = {}
        self.all_ds = []
        self.bar = DS(self.newsem("bar"))
        self.bar.kind = "sp"
        self.ninstr = 0

    def newsem(self, name):
        return self.es.enter_context(self.nc.semaphore(name))

    def get_ds(self, kind="sp"):
        fl = self.free_ds.setdefault(kind, [])
        if fl:
            return fl.pop()
        ds = DS(self.newsem("ds%s%d" % (kind, len(self.all_ds))))
        ds.kind = kind
        self.all_ds.append(ds)
        return ds

    def sb(self, ctx, name, shape, dt):
        self.uid += 1
        t = ctx.enter_context(self.nc.sbuf_tensor("%s_%d" % (name, self.uid), list(shape), dt))
        return Buf(name, t)

    def ps(self, ctx, name, shape, dt):
        self.uid += 1
        t = ctx.enter_context(self.nc.psum_tensor("%s_%d" % (name, self.uid), list(shape), dt))
        return Buf(name, t)

    def wait(self, eng, ev):
        if ev.eng is eng and eng.pe:
            return
        k = id(ev.sem)
        if eng.waited.get(k, 0) >= ev.val:
            return
        val = ev.ds.cnt if ev.ds is not None else ev.val
        eng.h.wait_ge(ev.sem, val)
        eng.waited[k] = val

    def deps(self, eng, reads, writes):
        for b in reads:
            for ev in b.w:
                self.wait(eng, ev)
        for b in writes:
            for ev in b.w:
                self.wait(eng, ev)
            for ev in b.r.values():
                self.wait(eng, ev)
            for ev in b.rd:
                self.wait(eng, ev)

    def done(self, ev, reads, writes):
        for b in reads:
            if ev.eng is None:
                b.rd.append(ev)
            else:
                b.r[ev.eng.name] = ev
        for b in writes:
            b.w = [ev]
            b.r = {}
            b.rd = []

    def op(self, eng, fn, reads=(), writes=()):
        self.deps(eng, reads, writes)
        ins = fn()
        eng.count += 1
        ins.then_inc(eng.sem, 1)
        self.ninstr += 1
        ev = Ev(eng.sem, eng.count, eng)
        self.done(ev, reads, writes)
        if eng.count >= self.EPOCH:
            eng.nsem += 1
            eng.sem = self.newsem("%s_s%d" % (eng.name, eng.nsem))
            eng.count = 0
        return ev

    def dma(self, q, out, in_, ds, reads=(), writes=()):
        assert ds.kind == ("pool" if q is self.pool else "sp"), (ds.kind, q.name)
        self.deps(q, reads, writes)
        ins = q.h.dma_start(out=out, in_=in_)
        ds.cnt += 16
        ins.then_inc(ds.sem, 16)
        self.ninstr += 1
        ev = Ev(ds.sem, ds.cnt, None, ds)
        self.done(ev, reads, writes)
        return ev

    def barrier(self, dummy_src, dummy_dst):
        sp = self.sp
        for e in self.engs:
            if e is not sp and e.count > 0:
                self.wait(sp, Ev(e.sem, e.count, e))
        for ds in self.all_ds:
            if ds.cnt > 0:
                self.wait(sp, Ev(ds.sem, ds.cnt, None, ds))
        ins = sp.h.dma_start(out=dummy_dst, in_=dummy_src)
        self.bar.cnt += 16
        ins.then_inc(self.bar.sem, 16)
        for e in self.engs:
            e.h.wait_ge(self.bar.sem, self.bar.cnt)


class Rot:
    def __init__(self, items):
        self.items = items
        self.i = 0

    def next(self):
        b = self.items[self.i % len(self.items)]
        self.i += 1
        return b


class Prog:
    def __init__(self, cfg):
        self.cfg = cfg
        self.nc = bass.Bass("TRN2", target_bir_lowering=False)
        self.es = ExitStack()

    def din(self, name, shape, dt=F32):
        return self.nc.dram_tensor(name, list(shape), dt, kind="ExternalInput").ap()

    def dscr(self, name, shape, dt, dbg=False):
        kind = "ExternalOutput" if (dbg and self.cfg.get("debug")) else "Internal"
        return self.nc.dram_tensor(name, list(shape), dt, kind=kind).ap()

    def build(self):
        nc = self.nc
        cfg = self.cfg
        parts = cfg["parts"]
        self.inp = {}
        I = self.inp
        I["xp"] = self.din("xp", [SP_, D])
        I["xs"] = self.din("xs", [SS_, D])
        I["memp"] = self.din("memp", [MEM, D])
        I["mems"] = self.din("mems", [MEM, D])
        I["w_in"] = self.din("w_in", [L, D, D_IN])
        I["w_uq"] = self.din("w_uq", [L, 256, 576])
        I["w_uk"] = self.din("w_uk", [L, 128, 384])
        I["w_uv"] = self.din("w_uv", [L, 128, 384])
        I["w_o"] = self.din("w_o", [L, D, D])
        I["w_cq"] = self.din("w_cq", [L, D, D])
        I["w_ckv"] = self.din("w_ckv", [L, D, 2 * D])
        I["w_co"] = self.din("w_co", [L, D, D])
        I["w_up"] = self.din("w_up", [L, D, 2 * DFF])
        I["w_down"] = self.din("w_down", [L, DFF, D])
        I["gvec"] = self.din("gvec", [L, NG])
        I["convp"] = self.din("convp", [L, 128, 4, 44])
        I["natab"] = self.din("natab", [L, 6, 128, NA_MM * 64])
        I["ident"] = self.din("ident", [128, 128])
        I["cosd"] = self.din("cosd", [128, 64, 32])
        I["sind"] = self.din("sind", [128, 64, 32])
        I["cosm"] = self.din("cosm", [128, 64, 16])
        I["sinm"] = self.din("sinm", [128, 64, 16])
        I["dstrip"] = self.din("dstrip", [128, DIL_W])
        I["rsmall"] = self.din("rsmall", [2, 3 * 8 * 8])
        I["ea"] = self.din("ea", [2, 128])
        self.yp = nc.dram_tensor("yp", [SP_, D], F32, kind="ExternalOutput").ap()
        self.ys = nc.dram_tensor("ys", [SS_, D], F32, kind="ExternalOutput").ap()
        SM = max(S for _, S in parts)
        self.SM = SM
        dbg = True
        self.X = self.dscr("X", [SM, D], F32, dbg)
        self.QTn = self.dscr("QTn", [384, SM], BF16, dbg)
        self.KTn = self.dscr("KTn", [384, SM], BF16, dbg)
        self.Vn = self.dscr("Vn", [SM, 384], BF16, dbg)
        self.QTd = self.dscr("QTd", [768, SM], BF16, dbg)
        self.KTd = self.dscr("KTd", [768, SM], BF16, dbg)
        self.Vd = self.dscr("Vd", [SM, 768], BF16, dbg)
        self.QTm = self.dscr("QTm", [6, 96, SM], BF16, dbg)
        self.KTm = self.dscr("KTm", [6, 96, SM], BF16, dbg)
        self.Vm = self.dscr("Vm", [SM, 384], BF16, dbg)
        self.OT = self.dscr("OT", [D, SM], BF16, dbg)
        self.H3T = self.dscr("H3T", [D, SM + 2], BF16, dbg)
        self.ACTT = self.dscr("ACTT", [DFF, SM], BF16, dbg)
        self.dum0 = self.dscr("dum0", [1, 16], F32)
        self.dum1 = self.dscr("dum1", [1, 16], F32)

        with self.es:
            self.T = Tracker(nc, self.es)
            T = self.T
            self.consts()
            for (pname, S) in parts:
                xin = I["xp"] if pname == "p" else I["xs"]
                mem = I["memp"] if pname == "p" else I["mems"]
                yout = self.yp if pname == "p" else self.ys
                nl = cfg.get("layers", L)
                for l in range(nl):
                    last = (l == nl - 1)
                    xsrc = xin if l == 0 else self.X
                    ph = cfg.get("phases", "1nmdxfg")
                    if "1" in ph:
                        self.phase1(l, S, xsrc)
                    if "n" in ph:
                        self.attn_na(l, S)
                    if "d" in ph:
                        self.attn_dil(l, S)
                    if "m" in ph:
                        self.attn_mla(l, S)
                    if "x" in ph:
                        self.phase_x(l, S, xsrc, mem)
                    if "f" in ph:
                        self.ffn_up(l, S)
                    if "g" in ph:
                        self.ffn_down(l, S, yout if (last and not cfg.get("debug")) else self.X)
            T.barrier(self.inp["ident"][0:1, 0:16], self.dum1)
        return nc

    def bar(self):
        self.T.barrier(self.inp["ident"][0:1, 0:16], self.dum1)

    def consts(self):
        T = self.T
        nc = self.nc
        I = self.inp
        es = self.es
        self.ident = T.sb(es, "ident", [128, 128], BF16)
        self.ones = T.sb(es, "ones", [128, 128], BF16)
        self.ea = T.sb(es, "ea", [2, 128], BF16)
        self.rsmall = T.sb(es, "rsmall", [2, 192], BF16)
        ds = T.get_ds("pool")
        T.dma(T.pool, self.ident.t[:], I["ident"][:, :], ds, writes=[self.ident])
        ds = T.get_ds("pool")
        T.dma(T.pool, self.ea.t[:], I["ea"][:, :], ds, writes=[self.ea])
        ds = T.get_ds("pool")
        T.dma(T.pool, self.rsmall.t[:], I["rsmall"][:, :], ds, writes=[self.rsmall])
        T.op(T.pool, lambda: nc.gpsimd.memset(self.ones.t[:], 1.0), writes=[self.ones])

    def load_w(self, ctx, name, src, K, N, nsplit=1):
        T = self.T
        kc = K // 128
        w = T.sb(ctx, name, [128, kc, N], BF16)
        bufs = []
        step = (N + nsplit - 1) // nsplit
        for c in range(kc):
            b = Buf("%s_c%d" % (name, c), w.t)
            ds = self.newds("pool")
            for n0 in range(0, N, step):
                n1 = min(N, n0 + step)
                T.dma(T.pool, w.t[:, c, n0:n1], src[c * 128:(c + 1) * 128, n0:n1], ds, writes=[b])
            bufs.append(b)
        return w.t, bufs

    def load_rep(self, ctx, name, src1d, n):
        T = self.T
        g = T.sb(ctx, name, [128, n], F32)
        ds = self.newds()
        T.dma(T.sp, g.t[:], src1d.partition_broadcast(128), ds, writes=[g])
        return g

    def begin_phase(self):
        self.phase_ds = []

    def end_phase(self):
        self.bar()
        for ds in self.phase_ds:
            self.T.free_ds.setdefault(ds.kind, []).append(ds)
        self.phase_ds = []

    def newds(self, kind="sp"):
        ds = self.T.get_ds(kind)
        self.phase_ds.append(ds)
        return ds

    def rms_rstd(self, ss, rstd, n, sc, bias):
        T = self.T
        nc = self.nc
        T.op(T.act, lambda: nc.scalar.activation(out=rstd.t[:, 0:n], in_=ss.t[:, 0:n], func=AF.Sqrt,
                                                 bias=self.cbias(bias), scale=sc), reads=[ss, self._cb[float(bias)]], writes=[rstd])
        T.op(T.dve, lambda: nc.vector.reciprocal(out=rstd.t[:, 0:n], in_=rstd.t[:, 0:n]), reads=[rstd], writes=[rstd])

    def cbias(self, v):
        key = float(v)
        if key not in self._cb:
            raise KeyError(key)
        return self._cb[key].t[:, 0:1]

    def make_cbias(self, ctx, vals):
        T = self.T
        nc = self.nc
        self._cb = {}
        for v in vals:
            b = T.sb(ctx, "cb", [128, 1], F32)
            T.op(T.pool, lambda: nc.gpsimd.memset(b.t[:], float(v)), writes=[b])
            self._cb[float(v)] = b
        self._cb_bufs = list(self._cb.values())

    def phase1(self, l, S, xsrc):
        T = self.T
        nc = self.nc
        I = self.inp
        NT = S // 128
        E2 = T.pool if self.cfg.get("use_pool", True) else T.dve
        H2 = nc.gpsimd if self.cfg.get("use_pool", True) else nc.vector
        self.begin_phase()
        with ExitStack() as ctx:
            self.make_cbias(ctx, [EPS, 64 * EPS, 96 * EPS])
            cb = self._cb_bufs
            win, winb = self.load_w(ctx, "win", I["w_in"][l], D, D_IN, nsplit=2)
            wuq, wuqb = self.load_w(ctx, "wuq", I["w_uq"][l], 256, 576)
            wuk, wukb = self.load_w(ctx, "wuk", I["w_uk"][l], 128, 384)
            wuv, wuvb = self.load_w(ctx, "wuv", I["w_uv"][l], 128, 384)
            gv = self.load_rep(ctx, "gv1", I["gvec"][l, 0:G_CROSS], G_CROSS)
            cosd = T.sb(ctx, "cosd", [128, NT, 32], F32)
            sind = T.sb(ctx, "sind", [128, NT, 32], F32)
            cosm = T.sb(ctx, "cosm", [128, NT, 16], F32)
            sinm = T.sb(ctx, "sinm", [128, NT, 16], F32)
            for tb, nm in ((cosd, "cosd"), (sind, "sind"), (cosm, "cosm"), (sinm, "sinm")):
                T.dma(T.sp, tb.t[:], I[nm][:, 0:NT, :], self.newds(), writes=[tb])
            xts = Rot([T.sb(ctx, "xt", [128, D], F32) for _ in range(2)])
            xds = [self.newds(), self.newds()]
            junk = T.sb(ctx, "junk", [128, D], BF16)
            ss1 = Rot([T.sb(ctx, "ss1", [128, 8], F32) for _ in range(2)])
            rs1 = Rot([T.sb(ctx, "rs1", [128, 8], F32) for _ in range(2)])
            hb = Rot([T.sb(ctx, "hb", [128, D], BF16) for _ in range(2)])
            hT = Rot([T.sb(ctx, "hT", [128, 8, 128], BF16) for _ in range(2)])
            pT = Rot([T.ps(ctx, "pT", [128, 1024], BF16) for _ in range(1)])
            pz = Rot([T.ps(ctx, "pz", [128, 512], F32) for _ in range(4)])
            pX = Rot([T.ps(ctx, "pX", [128, 1024], BF16) for _ in range(2)])
            sq = Rot([T.sb(ctx, "sq", [128, 576], F32) for _ in range(2)])
            yy = Rot([T.sb(ctx, "yy", [128, 576], F32) for _ in range(3)])
            y2 = Rot([T.sb(ctx, "y2", [128, 576], F32) for _ in range(2)])
            tt = Rot([T.sb(ctx, "tt", [128, 6, 32], F32) for _ in range(4)])
            yb = Rot([T.sb(ctx, "yb", [128, 576], BF16) for _ in range(3)])
            ssh = Rot([T.sb(ctx, "ssh", [128, 8], F32) for _ in range(3)])
            rsh = Rot([T.sb(ctx, "rsh", [128, 8], F32) for _ in range(3)])
            cqn = Rot([T.sb(ctx, "cqn", [128, 384], BF16) for _ in range(2)])
            cT = Rot([T.sb(ctx, "cT", [128, 3, 128], BF16) for _ in range(2)])
            krb = Rot([T.sb(ctx, "krb", [128, 32], F32) for _ in range(2)])
            sskr_r = Rot([T.sb(ctx, "sskr", [128, 1], F32) for _ in range(2)])
            stq = [T.sb(ctx, "stq", [128, 18, 128], BF16) for _ in range(2)]
            stm = [T.sb(ctx, "stm", [96, 12, 128], BF16) for _ in range(2)]
            stq_ds = [self.newds() for _ in range(2)]
            stm_ds = [self.newds() for _ in range(2)]
            vst = Rot([T.sb(ctx, "vst", [128, 1536], BF16) for _ in range(2)])
            vds = [self.newds(), self.newds()]

            dq = []
            DEPTH = self.cfg.get("p1_depth", 2)

            def defer(fn):
                dq.append(fn)
                while len(dq) > DEPTH:
                    dq.pop(0)()

            def load_x(t):
                b = xts.items[t % 2]
                T.dma(T.sp, b.t[:], xsrc[t * 128:(t + 1) * 128, :], xds[t % 2], writes=[b])

            def headnorm(pz_b, ncols, H, d, sc, bias, extra_ss=None):
                s_ = sq.next()
                T.op(T.act, lambda: nc.scalar.activation(out=s_.t[:, 0:ncols], in_=pz_b.t[:, 0:ncols], func=AF.Square),
                     reads=[pz_b], writes=[s_])
                ssb = ssh.next()
                T.op(T.dve, lambda: nc.vector.tensor_reduce(out=ssb.t[:, 0:H],
                                                            in_=s_.t[:, 0:ncols].rearrange("p (h d) -> p h d", d=d),
                                                            axis=AX.X, op=ALU.add), reads=[s_], writes=[ssb])
                if extra_ss is not None:
                    T.op(T.dve, lambda: nc.vector.tensor_scalar(out=ssb.t[:, 0:H], in0=ssb.t[:, 0:H],
                                                                scalar1=extra_ss.t[:, 0:1], scalar2=None, op0=ALU.add),
                         reads=[ssb, extra_ss], writes=[ssb])
                rb = rsh.next()
                self.rms_rstd(ssb, rb, H, sc, bias)
                return rb

            load_x(0)
            for t in range(NT):
                if t + 1 < NT:
                    load_x(t + 1)
                xt = xts.items[t % 2]
                stopk = self.cfg.get("p1_stop", 99)
                if stopk <= 0:
                    continue
                par = t % 2
                half = 0
                sQ = stq[par]
                sM = stm[par]
                ssb = ss1.next()
                T.op(T.act, lambda: nc.scalar.activation(out=junk.t[:], in_=xt.t[:], func=AF.Square,
                                                         accum_out=ssb.t[:, 0:1]), reads=[xt], writes=[junk, ssb])
                rb = rs1.next()
                self.rms_rstd(ssb, rb, 1, 1.0 / D, EPS)
                if stopk <= 1:
                    continue
                h = hb.next()
                T.op(T.dve, lambda: nc.vector.scalar_tensor_tensor(out=h.t[:], in0=xt.t[:], scalar=rb.t[:, 0:1],
                                                                   in1=gv.t[:, G_MIX:G_MIX + D], op0=ALU.mult, op1=ALU.mult),
                     reads=[xt, rb, gv], writes=[h])
                p = pT.next()

                def tr8():
                    ins = None
                    for c in range(8):
                        ins = nc.tensor.transpose(out=p.t[:, c * 128:(c + 1) * 128], in_=h.t[:, c * 128:(c + 1) * 128],
                                                  identity=self.ident.t[:])
                    return ins
                T.op(T.pe, tr8, reads=[h, self.ident], writes=[p])
                if stopk <= 2:
                    continue
                hTb = hT.next()
                T.op(T.act, lambda: nc.scalar.copy(out=hTb.t[:].rearrange("p c t -> p (c t)"), in_=p.t[:]),
                     reads=[p], writes=[hTb])

                def proj(col0, ncols):
                    z = pz.next()

                    def mm():
                        ins = None
                        for c in range(8):
                            ins = nc.tensor.matmul(z.t[:, 0:ncols], lhsT=hTb.t[:, c, :], rhs=win[:, c, col0:col0 + ncols],
                                                   start=(c == 0), stop=(c == 7))
                        return ins
                    T.op(T.pe, mm, reads=[hTb] + winb, writes=[z])
                    return z

                def transpose_out(src_b, slabs, width, dst_b, dst_idx0, rows, half=half):
                    defer(lambda: transpose_out_now(src_b, slabs, width, dst_b, dst_idx0, rows, half))

                def transpose_out_now(src_b, slabs, width, dst_b, dst_idx0, rows, half):
                    px = pX.next()

                    def trs():
                        ins = None
                        for s in range(slabs):
                            ins = nc.tensor.transpose(out=px.t[0:width, s * 128:(s + 1) * 128],
                                                      in_=src_b.t[:, s * width:(s + 1) * width], identity=self.ident.t[:])
                        return ins
                    T.op(T.pe, trs, reads=[src_b, self.ident], writes=[px])
                    T.op(T.act, lambda: nc.scalar.copy(
                        out=dst_b.t[0:rows, dst_idx0:dst_idx0 + slabs, half * 128:(half + 1) * 128],
                        in_=px.t[0:rows, 0:slabs * 128].rearrange("p (s t) -> p s t", t=128)),
                        reads=[px], writes=[dst_b])

                def rope_apply(y_b, H, d, hd, cos_b, sin_b, lo0, out_b):
                    yv = y_b.t[:, 0:H * d].rearrange("p (h d) -> p h d", d=d)
                    ov = out_b.t[:, 0:H * d].rearrange("p (h d) -> p h d", d=d)
                    lo = yv[:, :, lo0:lo0 + hd]
                    hi = yv[:, :, lo0 + hd:lo0 + 2 * hd]
                    cs = cos_b.t[:, t, :].unsqueeze(1).to_broadcast([128, H, hd])
                    sn = sin_b.t[:, t, :].unsqueeze(1).to_broadcast([128, H, hd])
                    t1, t2, t3, t4 = tt.next(), tt.next(), tt.next(), tt.next()
                    T.op(T.dve, lambda: nc.vector.tensor_tensor(out=t1.t[:, 0:H, 0:hd], in0=lo, in1=cs, op=ALU.mult),
                         reads=[y_b, cos_b], writes=[t1])
                    T.op(E2, lambda: H2.tensor_tensor(out=t2.t[:, 0:H, 0:hd], in0=hi, in1=sn, op=ALU.mult),
                         reads=[y_b, sin_b], writes=[t2])
                    T.op(T.dve, lambda: nc.vector.tensor_tensor(out=t3.t[:, 0:H, 0:hd], in0=hi, in1=cs, op=ALU.mult),
                         reads=[y_b, cos_b], writes=[t3])
                    T.op(E2, lambda: H2.tensor_tensor(out=t4.t[:, 0:H, 0:hd], in0=lo, in1=sn, op=ALU.mult),
                         reads=[y_b, sin_b], writes=[t4])
                    T.op(T.dve, lambda: nc.vector.tensor_tensor(out=ov[:, :, lo0:lo0 + hd], in0=t1.t[:, 0:H, 0:hd],
                                                                in1=t2.t[:, 0:H, 0:hd], op=ALU.subtract),
                         reads=[t1, t2], writes=[out_b])
                    T.op(E2, lambda: H2.tensor_tensor(out=ov[:, :, lo0 + hd:lo0 + 2 * hd], in0=t3.t[:, 0:H, 0:hd],
                                                                 in1=t4.t[:, 0:H, 0:hd], op=ALU.add),
                         reads=[t3, t4], writes=[out_b])
                    if lo0 > 0:
                        T.op(E2, lambda: H2.tensor_copy(out=ov[:, :, 0:lo0], in_=yv[:, :, 0:lo0]),
                             reads=[y_b], writes=[out_b])

                def qk_block(col0, H, goff, qscale, rope, dst_idx0):
                    d = 64
                    ncols = H * d
                    z = proj(col0, ncols)
                    rb_ = headnorm(z, ncols, H, d, 1.0 if qscale else 1.0 / d, d * EPS if qscale else EPS)
                    y = yy.next()
                    T.op(T.dve, lambda: nc.vector.tensor_tensor(
                        out=y.t[:, 0:ncols].rearrange("p (h d) -> p h d", d=d),
                        in0=z.t[:, 0:ncols].rearrange("p (h d) -> p h d", d=d),
                        in1=rb_.t[:, 0:H].unsqueeze(2).to_broadcast([128, H, d]), op=ALU.mult),
                        reads=[z, rb_], writes=[y])
                    ob = yb.next()
                    gb = gv.t[:, goff:goff + d].unsqueeze(1).to_broadcast([128, H, d])
                    if rope:
                        yg = y2.next()
                        T.op(E2, lambda: H2.tensor_tensor(
                            out=yg.t[:, 0:ncols].rearrange("p (h d) -> p h d", d=d),
                            in0=y.t[:, 0:ncols].rearrange("p (h d) -> p h d", d=d), in1=gb, op=ALU.mult),
                            reads=[y, gv], writes=[yg])
                        rope_apply(yg, H, d, 32, cosd, sind, 0, ob)
                    else:
                        T.op(E2, lambda: H2.tensor_tensor(
                            out=ob.t[:, 0:ncols].rearrange("p (h d) -> p h d", d=d),
                            in0=y.t[:, 0:ncols].rearrange("p (h d) -> p h d", d=d), in1=gb, op=ALU.mult),
                            reads=[y, gv], writes=[ob])
                    transpose_out(ob, H // 2, 128, sQ, dst_idx0, 128)

                def v_block(col0, ncols, vb, voff):
                    z = proj(col0, ncols)
                    T.op(T.act, lambda: nc.scalar.copy(out=vb.t[:, voff:voff + ncols], in_=z.t[:, 0:ncols]),
                         reads=[z], writes=[vb])

                if stopk <= 3:
                    continue
                vb = vst.next()
                z0 = proj(0, 416)
                if stopk <= 3.02:
                    continue
                cq = cqn.next()
                rq = headnorm(Buf_view(z0), 256, 1, 256, 1.0 / 256, EPS)
                if stopk <= 3.06:
                    continue
                import os
                skip = os.environ.get("P1SKIP", "")
                yq = yy.next()
                if "A" not in skip:
                  T.op(T.dve, lambda: nc.vector.scalar_tensor_tensor(out=cq.t[:, 0:256], in0=z0.t[:, 0:256], scalar=rq.t[:, 0:1],
                                                                   in1=gv.t[:, G_CQ:G_CQ + 256], op0=ALU.mult, op1=ALU.mult),
                     reads=[z0, rq, gv], writes=[cq])
                s_ = sq.next()
                ssk = ssh.next()
                if "B" not in skip:
                  T.op(T.act, lambda: nc.scalar.activation(out=s_.t[:, 0:128], in_=z0.t[:, 256:384], func=AF.Square,
                                                         accum_out=ssk.t[:, 0:1]), reads=[z0], writes=[s_, ssk])
                rk = rsh.next()
                if "B" not in skip:
                  self.rms_rstd(ssk, rk, 1, 1.0 / 128, EPS)
                if "C" not in skip:
                  T.op(T.dve, lambda: nc.vector.scalar_tensor_tensor(out=cq.t[:, 256:384], in0=z0.t[:, 256:384], scalar=rk.t[:, 0:1],
                                                                   in1=gv.t[:, G_CKV:G_CKV + 128], op0=ALU.mult, op1=ALU.mult),
                     reads=[z0, rk, gv], writes=[cq])
                kr = krb.next()
                s2_ = sq.next()
                sskr = sskr_r.next()
                if "D" not in skip:
                  T.op(T.dve, lambda: nc.vector.tensor_copy(out=kr.t[:], in_=z0.t[:, 384:416]), reads=[z0], writes=[kr])
                T.op(T.dve, lambda: nc.vector.tensor_tensor(out=s2_.t[:, 0:32], in0=kr.t[:], in1=kr.t[:], op=ALU.mult),
                     reads=[kr], writes=[s2_])
                T.op(T.dve, lambda: nc.vector.tensor_reduce(out=sskr.t[:, 0:1], in_=s2_.t[:, 0:32], axis=AX.X, op=ALU.add),
                     reads=[s2_], writes=[sskr])
                if stopk <= 3.2:
                    continue
                p2 = pT.next()

                def tr3():
                    ins = None
                    for c in range(3):
                        ins = nc.tensor.transpose(out=p2.t[:, c * 128:(c + 1) * 128], in_=cq.t[:, c * 128:(c + 1) * 128],
                                                  identity=self.ident.t[:])
                    return ins
                T.op(T.pe, tr3, reads=[cq, self.ident], writes=[p2])
                cTb = cT.next()
                T.op(T.act, lambda: nc.scalar.copy(out=cTb.t[:].rearrange("p c t -> p (c t)"), in_=p2.t[:, 0:384]),
                     reads=[p2], writes=[cTb])
                if stopk <= 3.4:
                    continue
                for hq in range(2):
                    zq = pz.next()

                    def mmq():
                        ins = None
                        for c in range(2):
                            ins = nc.tensor.matmul(zq.t[:, 0:288], lhsT=cTb.t[:, c, :], rhs=wuq[:, c, hq * 288:(hq + 1) * 288],
                                                   start=(c == 0), stop=(c == 1))
                        return ins
                    T.op(T.pe, mmq, reads=[cTb] + wuqb, writes=[zq])
                    rb_ = headnorm(zq, 288, 3, 96, 1.0, 96 * EPS)
                    y = yy.next()
                    T.op(T.dve, lambda: nc.vector.tensor_tensor(
                        out=y.t[:, 0:288].rearrange("p (h d) -> p h d", d=96),
                        in0=zq.t[:, 0:288].rearrange("p (h d) -> p h d", d=96),
                        in1=rb_.t[:, 0:3].unsqueeze(2).to_broadcast([128, 3, 96]), op=ALU.mult),
                        reads=[zq, rb_], writes=[y])
                    yg = y2.next()
                    T.op(E2, lambda: H2.tensor_tensor(
                        out=yg.t[:, 0:288].rearrange("p (h d) -> p h d", d=96),
                        in0=y.t[:, 0:288].rearrange("p (h d) -> p h d", d=96),
                        in1=gv.t[:, G_MQN:G_MQN + 96].unsqueeze(1).to_broadcast([128, 3, 96]), op=ALU.mult),
                        reads=[y, gv], writes=[yg])
                    ob = yb.next()
                    rope_apply(yg, 3, 96, 16, cosm, sinm, 64, ob)
                    transpose_out(ob, 3, 96, sM, hq * 3, 96)
                if stopk <= 3.6:
                    continue
                zk = pz.next()
                T.op(T.pe, lambda: nc.tensor.matmul(zk.t[:, 0:384], lhsT=cTb.t[:, 2, :], rhs=wuk[:, 0, :], start=True, stop=True),
                     reads=[cTb] + wukb, writes=[zk])
                rbk = headnorm(zk, 384, 6, 64, 1.0 / 96, EPS, extra_ss=sskr)
                yk = yy.next()
                ykv = yk.t[:, 0:576].rearrange("p (h d) -> p h d", d=96)
                T.op(T.dve, lambda: nc.vector.tensor_tensor(
                    out=ykv[:, :, 0:64], in0=zk.t[:, 0:384].rearrange("p (h d) -> p h d", d=64),
                    in1=rbk.t[:, 0:6].unsqueeze(2).to_broadcast([128, 6, 64]), op=ALU.mult),
                    reads=[zk, rbk], writes=[yk])
                T.op(T.dve, lambda: nc.vector.tensor_tensor(
                    out=ykv[:, :, 64:96], in0=kr.t[:, :].unsqueeze(1).to_broadcast([128, 6, 32]),
                    in1=rbk.t[:, 0:6].unsqueeze(2).to_broadcast([128, 6, 32]), op=ALU.mult),
                    reads=[kr, rbk], writes=[yk])
                ykg = y2.next()
                T.op(E2, lambda: H2.tensor_tensor(
                    out=ykg.t[:, 0:576].rearrange("p (h d) -> p h d", d=96), in0=ykv,
                    in1=gv.t[:, G_MKN:G_MKN + 96].unsqueeze(1).to_broadcast([128, 6, 96]), op=ALU.mult),
                    reads=[yk, gv], writes=[ykg])
                obk = yb.next()
                rope_apply(ykg, 6, 96, 16, cosm, sinm, 64, obk)
                transpose_out(obk, 6, 96, sM, 6, 96)
                zv = pz.next()
                T.op(T.pe, lambda: nc.tensor.matmul(zv.t[:, 0:384], lhsT=cTb.t[:, 2, :], rhs=wuv[:, 0, :], start=True, stop=True),
                     reads=[cTb] + wuvb, writes=[zv])
                T.op(T.act, lambda: nc.scalar.copy(out=vb.t[:, 0:384], in_=zv.t[:, 0:384]), reads=[zv], writes=[vb])
                if stopk <= 4:
                    continue
                qk_block(416, 6, G_NQ, True, False, 0)
                qk_block(800, 6, G_NK, False, False, 3)
                v_block(1184, 384, vb, 384)
                if stopk <= 5:
                    continue
                qk_block(1568, 6, G_DQ, True, True, 6)
                qk_block(1952, 6, G_DQ, True, True, 9)
                qk_block(2336, 6, G_DK, False, True, 12)
                qk_block(2720, 6, G_DK, False, True, 15)
                v_block(3104, 384, vb, 768)
                v_block(3488, 384, vb, 1152)
                if stopk <= 6:
                    continue
                def stores(t=t, vb=vb, sQ=sQ, sM=sM, par=par):
                    r0 = t * 128
                    T.dma(T.sp, self.Vm[r0:r0 + 128, :], vb.t[:, 0:384], vds[t % 2], reads=[vb])
                    T.dma(T.sp, self.Vn[r0:r0 + 128, :], vb.t[:, 384:768], vds[t % 2], reads=[vb])
                    T.dma(T.sp, self.Vd[r0:r0 + 128, :], vb.t[:, 768:1536], vds[t % 2], reads=[vb])
                    t0 = t * 128
                    w = 128
                    for (dst, i0, ns) in ((self.QTn, 0, 3), (self.KTn, 3, 3), (self.QTd, 6, 6), (self.KTd, 12, 6)):
                        T.dma(T.sp, dst.rearrange("(s p) t -> p s t", p=128)[:, :, t0:t0 + w], sQ.t[:, i0:i0 + ns, 0:w],
                              stq_ds[par], reads=[sQ])
                    T.dma(T.sp, self.QTm[:, :, t0:t0 + w].rearrange("h d t -> d h t"), sM.t[:, 0:6, 0:w], stm_ds[par], reads=[sM])
                    T.dma(T.sp, self.KTm[:, :, t0:t0 + w].rearrange("h d t -> d h t"), sM.t[:, 6:12, 0:w], stm_ds[par], reads=[sM])
                defer(stores)
            while dq:
                dq.pop(0)()
            self.end_phase()

    def attention(self, S, slots, l):
        T = self.T
        nc = self.nc
        I = self.inp
        NT = S // 128
        NQB = S // 512
        nstream = len(slots[0]["streams"])
        nset = 2 if nstream == 1 else 1
        use_dil = any(m[0] == "dil" for sl in slots for (_, _, ms) in sl["chunks"](0) for m in ms)
        self.begin_phase()
        with ExitStack() as ctx:
            sets = []
            for si in range(nset):
                st = []
                for k in range(nstream):
                    kt = T.sb(ctx, "kt", [96, S], BF16)
                    va = T.sb(ctx, "va", [128, NT, 128], BF16)
                    T.op(T.pool, lambda: nc.gpsimd.memset(va.t[:, :, 64:128], 1.0), writes=[va])
                    st.append((kt, va, self.newds(), self.newds()))
                nat = T.sb(ctx, "nat", [128, NA_MM * 64], BF16)
                sets.append((st, nat, self.newds("pool")))
            dstrip = None
            if use_dil:
                dstrip = T.sb(ctx, "dstrip", [128, DIL_W], BF16)
                dsd = self.newds("pool")
                for c0 in range(0, DIL_W, 2048):
                    c1 = min(DIL_W, c0 + 2048)
                    T.dma(T.pool, dstrip.t[:, c0:c1], I["dstrip"][:, c0:c1], dsd, writes=[dstrip])
            qtb = [Rot([T.sb(ctx, "qtb", [96, 512], BF16) for _ in range(3)]) for _ in range(nstream)]
            qds = [[self.newds(), self.newds(), self.newds()] for _ in range(nstream)]
            ptr = Rot([T.sb(ctx, "pt", [128, 2, 512], BF16) for _ in range(5)])
            pS = Rot([T.ps(ctx, "pS", [128, 1024], F32) for _ in range(3)])
            pO = Rot([T.ps(ctx, "pO", [128, 512], F32) for _ in range(2)])
            rzr = Rot([T.sb(ctx, "rz", [64, 512], F32) for _ in range(2)])
            ost = [T.sb(ctx, "ost", [64, 512], BF16) for _ in range(3)]
            ods = [self.newds() for _ in range(3)]
            oi = 0

            def load_slot(i):
                sl = slots[i]
                st, nat, nds = sets[i % nset]
                for k, sm in enumerate(sl["streams"]):
                    kt, va, kds, vds_ = st[k]
                    d = sm["d"]
                    T.dma(T.sp, kt.t[0:d, :], sm["kt"], kds, writes=[kt])
                    for c0 in range(0, NT, 16):
                        c1 = min(NT, c0 + 16)
                        T.dma(T.sp, va.t[:, c0:c1, 0:64],
                              sm["v"][c0 * 128:c1 * 128, :].rearrange("(c p) d -> p c d", p=128), vds_, writes=[va])
                if sl.get("natab") is not None:
                    T.dma(T.pool, nat.t[:], sl["natab"], nds, writes=[nat])

            pending = []

            def emit_pv(G):
                grp, st_, pt, po, g0, total, fin = G

                def pv():
                    ins = None
                    for gi, (k, c, masks) in enumerate(grp):
                        va = st_[k][1]
                        idx = g0 + gi
                        ins = nc.tensor.matmul(po.t[:, :], lhsT=va.t[:, c, :], rhs=pt.t[:, gi, :],
                                               start=(idx == 0), stop=(idx == total - 1))
                    return ins
                T.op(T.pe, pv, reads=[pt] + [st_[k][1] for (k, _, _) in grp], writes=[po])
                if fin is not None:
                    row0, jq = fin
                    rz = rzr.next()
                    T.op(T.dve, lambda: nc.vector.reciprocal(out=rz.t[:, :], in_=po.t[64:128, :]), reads=[po], writes=[rz])
                    ob = ost[self._oi % 3]
                    T.op(T.dve, lambda: nc.vector.tensor_tensor(out=ob.t[:, :], in0=po.t[0:64, :], in1=rz.t[:, :], op=ALU.mult),
                         reads=[po, rz], writes=[ob])
                    T.dma(T.sp, self.OT[row0:row0 + 64, jq * 512:(jq + 1) * 512], ob.t[:, :], ods[self._oi % 3], reads=[ob])
                    self._oi += 1

            self._oi = 0
            SKEW = 2
            load_slot(0)
            for i, sl in enumerate(slots):
                if nset == 2 and i + 1 < len(slots):
                    while pending:
                        emit_pv(pending.pop(0))
                    load_slot(i + 1)
                st, nat, _ = sets[i % nset]
                for jq in range(NQB):
                    qs = []
                    for k, sm in enumerate(sl["streams"]):
                        qb = qtb[k].next()
                        d = sm["d"]
                        T.dma(T.sp, qb.t[0:d, :], sm["qt"][:, jq * 512:(jq + 1) * 512], qds[k][(qtb[k].i - 1) % 3], writes=[qb])
                        qs.append(qb)
                    items = sl["chunks"](jq)
                    total = len(items)
                    po = pO.next()
                    for g0 in range(0, total, 2):
                        grp = items[g0:g0 + 2]
                        ps = pS.next()

                        def qk():
                            ins = None
                            for gi, (k, c, masks) in enumerate(grp):
                                kt = st[k][0]
                                d = sl["streams"][k]["d"]
                                o_ = ps.t[:, gi * 512:(gi + 1) * 512]
                                mm_masks = [m for m in masks if m[0] != "dil"]
                                ins = nc.tensor.matmul(o_, lhsT=kt.t[0:d, c * 128:(c + 1) * 128], rhs=qs[k].t[0:d, :],
                                                       start=True, stop=(len(mm_masks) == 0))
                                masks = mm_masks
                                for mi, m in enumerate(masks):
                                    last = (mi == len(masks) - 1)
                                    if m[0] == "nat":
                                        ins = nc.tensor.matmul(o_, lhsT=self.ident.t[:], rhs=nat.t[:, m[1]:m[1] + 512],
                                                               start=False, stop=last)
                                    elif m[0] == "dil":
                                        pass
                                    else:
                                        ins = nc.tensor.matmul(o_.rearrange("p (j c) -> p j c", c=64), lhsT=self.ea.t[0:2, :],
                                                               rhs=self.rsmall.t[0:2, m[1] * 8:(m[1] + 1) * 8].unsqueeze(2).to_broadcast([2, 8, 64]),
                                                               start=False, stop=last)
                            return ins
                        rd = [st[k][0] for (k, _, _) in grp] + [qs[k] for (k, _, _) in grp] + [self.ident, self.ea, self.rsmall, nat]
                        if dstrip is not None:
                            rd.append(dstrip)
                        T.op(T.pe, qk, reads=rd, writes=[ps])
                        pt = ptr.next()
                        n = len(grp)
                        T.op(T.act, lambda: nc.scalar.activation(out=pt.t[:, 0:n, :].rearrange("p g q -> p (g q)"),
                                                                 in_=ps.t[:, 0:n * 512], func=AF.Exp), reads=[ps], writes=[pt])
                        for gi, (k_, c_, masks_) in enumerate(grp):
                            for m in masks_:
                                if m[0] == "dil":
                                    self._mi = getattr(self, "_mi", 0) + 1
                                    if self._mi % 3 == 0:
                                        T.op(T.pool, lambda: nc.gpsimd.tensor_tensor(out=pt.t[:, gi, :], in0=pt.t[:, gi, :],
                                                                                     in1=dstrip.t[:, m[1]:m[1] + 512], op=ALU.mult),
                                             reads=[pt, dstrip], writes=[pt])
                                    else:
                                        T.op(T.dve, lambda: nc.vector.tensor_tensor(out=pt.t[:, gi, :], in0=pt.t[:, gi, :],
                                                                                    in1=dstrip.t[:, m[1]:m[1] + 512], op=ALU.mult),
                                             reads=[pt, dstrip], writes=[pt])
                        fin = (sl["row0"], jq) if g0 + 2 >= total else None
                        pending.append((grp, st, pt, po, g0, total, fin))
                        if len(pending) > SKEW:
                            emit_pv(pending.pop(0))
                if nset == 1:
                    while pending:
                        emit_pv(pending.pop(0))
                    if i + 1 < len(slots):
                        load_slot(i + 1)
            while pending:
                emit_pv(pending.pop(0))
            self.end_phase()

    def attn_mla(self, l, S):
        NT = S // 128
        slots = []
        for h in range(6):
            slots.append(dict(row0=h * 64, natab=None,
                              streams=[dict(qt=self.QTm[h, :, 0:S], kt=self.KTm[h, :, 0:S], v=self.Vm[0:S, h * 64:(h + 1) * 64], d=96)],
                              chunks=(lambda jq: [(0, c, []) for c in range(NT)])))
        self.attention(S, slots, l)

    def attn_na(self, l, S):
        NT = S // 128
        NQB = S // 512
        I = self.inp

        def chunks(jq):
            ty = 0 if jq == 0 else (2 if jq == NQB - 1 else 1)
            out = []
            for ci in range(8):
                c = 4 * jq - 2 + ci
                if 0 <= c < NT:
                    m0 = 11 - 2 * ci
                    out.append((0, c, [("nat", (m0 + 3) * 64), ("row", ty * 8 + ci)]))
            return out
        slots = []
        for h in range(6):
            slots.append(dict(row0=384 + h * 64, natab=I["natab"][l, h],
                              streams=[dict(qt=self.QTn[h * 64:(h + 1) * 64, 0:S], kt=self.KTn[h * 64:(h + 1) * 64, 0:S],
                                            v=self.Vn[0:S, h * 64:(h + 1) * 64], d=64)], chunks=chunks))
        self.attention(S, slots, l)

    def attn_dil(self, l, S):
        NT = S // 128

        def chunks(jq):
            out = []
            for g in range(3):
                for Dd in range(DIL_DMIN[g], DIL_DMAX[g] + 1, 128):
                    c = (jq * 512 + Dd) // 128
                    if 0 <= c < NT:
                        out.append((g, c, [("dil", DIL_OFF[g] + DIL_DMAX[g] - Dd)]))
            return out
        slots = []
        for h in range(4):
            sts = []
            for g in range(3):
                hh = g * 4 + h
                sts.append(dict(qt=self.QTd[hh * 64:(hh + 1) * 64, 0:S], kt=self.KTd[hh * 64:(hh + 1) * 64, 0:S],
                                v=self.Vd[0:S, hh * 64:(hh + 1) * 64], d=64))
            slots.append(dict(row0=768 + h * 64, natab=None, streams=sts, chunks=chunks))
        self.attention(S, slots, l)

    def phase_x(self, l, S, xsrc, mem):
        T = self.T
        nc = self.nc
        I = self.inp
        NT = S // 128
        GO = G_CROSS
        self.begin_phase()
        with ExitStack() as ctx:
            self.make_cbias(ctx, [EPS, 256 * EPS])
            wo, wob = self.load_w(ctx, "wo", I["w_o"][l], D, D)
            wcq, wcqb = self.load_w(ctx, "wcq", I["w_cq"][l], D, D)
            wco, wcob = self.load_w(ctx, "wco", I["w_co"][l], D, D)
            wkv, wkvb = self.load_w(ctx, "wkv", I["w_ckv"][l], D, 2 * D)
            gv = self.load_rep(ctx, "gv2", I["gvec"][l, GO:NG], NG - GO)
            g_cross, g_xq, g_ffn, g_mem, g_xk = 0, G_XQ - GO, G_FFN - GO, G_MEM - GO, G_XK - GO
            pA = T.ps(ctx, "pA", [128, 1024], F32)
            pB = T.ps(ctx, "pB", [128, 1024], F32)
            pC = T.ps(ctx, "pC", [128, 1024], F32)
            pT = T.ps(ctx, "pT", [128, 1024], BF16)
            pZ = T.ps(ctx, "pZ", [128, 512], F32)
            xts = [T.sb(ctx, "xt", [128, D], F32) for _ in range(3)]
            xds = [self.newds() for _ in range(3)]
            ots = [T.sb(ctx, "oT", [128, 8, 128], BF16) for _ in range(2)]
            otds = [self.newds() for _ in range(2)]
            junk = T.sb(ctx, "junk", [128, D], BF16)
            ss = Rot([T.sb(ctx, "ss", [128, 8], F32) for _ in range(3)])
            rs = Rot([T.sb(ctx, "rs", [128, 8], F32) for _ in range(3)])
            hb = Rot([T.sb(ctx, "hb", [128, D], BF16) for _ in range(2)])
            hT = Rot([T.sb(ctx, "hT", [128, 8, 128], BF16) for _ in range(2)])
            sqb = T.sb(ctx, "sqb", [128, D], F32)
            yf = T.sb(ctx, "yf", [128, D], F32)
            kmT = T.sb(ctx, "kmT", [128, 8, 256], BF16)
            vms = T.sb(ctx, "vms", [128, 2, D], BF16)
            ptb = Rot([T.sb(ctx, "ptb", [128, 8, 128], BF16) for _ in range(2)])
            rzb = T.sb(ctx, "rzb", [128, 512], F32)
            ocT = Rot([T.sb(ctx, "ocT", [128, 8, 128], BF16) for _ in range(2)])
            h3s = [T.sb(ctx, "h3s", [128, 8, 128], BF16) for _ in range(2)]
            h3ds = [self.newds() for _ in range(2)]
            zt = T.sb(ctx, "zt", [128, 8, 1], BF16)
            zds = self.newds()
            H3v = self.H3T.rearrange("(c p) t -> p c t", p=128)
            OTv = self.OT.rearrange("(c p) t -> p c t", p=128)

            def rmsnorm_T(xb, goff, hT_dst):
                ssb = ss.next()
                T.op(T.act, lambda: nc.scalar.activation(out=junk.t[:], in_=xb.t[:], func=AF.Square, accum_out=ssb.t[:, 0:1]),
                     reads=[xb], writes=[junk, ssb])
                rb = rs.next()
                self.rms_rstd(ssb, rb, 1, 1.0 / D, EPS)
                h = hb.next()
                T.op(T.dve, lambda: nc.vector.scalar_tensor_tensor(out=h.t[:], in0=xb.t[:], scalar=rb.t[:, 0:1],
                                                                   in1=gv.t[:, goff:goff + D], op0=ALU.mult, op1=ALU.mult),
                     reads=[xb, rb, gv], writes=[h])
                self.tr8(h, pT, hT_dst)

            def proj2(dst_ps, lhs_b, w, wb, coff=0):
                def mm():
                    ins = None
                    for nh in range(2):
                        for c in range(8):
                            ins = nc.tensor.matmul(dst_ps.t[:, nh * 512:(nh + 1) * 512], lhsT=lhs_b.t[:, c, :],
                                                   rhs=w[:, c, coff + nh * 512:coff + (nh + 1) * 512], start=(c == 0), stop=(c == 7))
                    return ins
                T.op(T.pe, mm, reads=[lhs_b] + wb, writes=[dst_ps])

            def headnorm4(src_ps, goff, sc, bias, out_b):
                T.op(T.act, lambda: nc.scalar.activation(out=sqb.t[:], in_=src_ps.t[:], func=AF.Square), reads=[src_ps], writes=[sqb])
                ssb = ss.next()
                T.op(T.dve, lambda: nc.vector.tensor_reduce(out=ssb.t[:, 0:4], in_=sqb.t[:].rearrange("p (h d) -> p h d", d=256),
                                                            axis=AX.X, op=ALU.add), reads=[sqb], writes=[ssb])
                rb = rs.next()
                self.rms_rstd(ssb, rb, 4, sc, bias)
                T.op(T.dve, lambda: nc.vector.tensor_tensor(out=yf.t[:].rearrange("p (h d) -> p h d", d=256),
                                                            in0=src_ps.t[:].rearrange("p (h d) -> p h d", d=256),
                                                            in1=rb.t[:, 0:4].unsqueeze(2).to_broadcast([128, 4, 256]), op=ALU.mult),
                     reads=[src_ps, rb], writes=[yf])
                T.op(T.dve, lambda: nc.vector.tensor_tensor(out=out_b.t[:].rearrange("p (h d) -> p h d", d=256),
                                                            in0=yf.t[:].rearrange("p (h d) -> p h d", d=256),
                                                            in1=gv.t[:, goff:goff + 256].unsqueeze(1).to_broadcast([128, 4, 256]), op=ALU.mult),
                     reads=[yf, gv], writes=[out_b])

            for mt in range(2):
                xb = xts[mt]
                T.dma(T.sp, xb.t[:], mem[mt * 128:(mt + 1) * 128, :], xds[mt], writes=[xb])
                mT = hT.next()
                rmsnorm_T(xb, g_mem, mT)
                proj2(pA, mT, wkv, wkvb, 0)
                kn = hb.next()
                headnorm4(pA, g_xk, 1.0 / 256, EPS, kn)

                def trk():
                    ins = None
                    for c in range(8):
                        ins = nc.tensor.transpose(out=pT.t[:, c * 128:(c + 1) * 128], in_=kn.t[:, c * 128:(c + 1) * 128],
                                                  identity=self.ident.t[:])
                    return ins
                T.op(T.pe, trk, reads=[kn, self.ident], writes=[pT])
                T.op(T.act, lambda: nc.scalar.copy(out=kmT.t[:, :, mt * 128:(mt + 1) * 128],
                                                   in_=pT.t[:].rearrange("p (c t) -> p c t", t=128)), reads=[pT], writes=[kmT])
                proj2(pB, mT, wkv, wkvb, D)
                T.op(T.act, lambda: nc.scalar.copy(out=vms.t[:, mt, :], in_=pB.t[:]), reads=[pB], writes=[vms])

            def loads(t):
                xb = xts[t % 3]
                T.dma(T.sp, xb.t[:], xsrc[t * 128:(t + 1) * 128, :], xds[t % 3], writes=[xb])
                ob = ots[t % 2]
                T.dma(T.sp, ob.t[:], OTv[:, :, t * 128:(t + 1) * 128], otds[t % 2], writes=[ob])

            loads(0)
            for t in range(NT):
                if t + 1 < NT:
                    loads(t + 1)
                xb = xts[t % 3]
                ob = ots[t % 2]
                proj2(pA, ob, wo, wob)
                T.op(T.dve, lambda: nc.vector.tensor_tensor(out=xb.t[:], in0=pA.t[:], in1=xb.t[:], op=ALU.add), reads=[pA, xb], writes=[xb])
                h2T = hT.next()
                rmsnorm_T(xb, g_cross, h2T)
                proj2(pB, h2T, wcq, wcqb)
                qn = hb.next()
                headnorm4(pB, g_xq, 1.0, 256 * EPS, qn)
                qcT = hT.next()
                self.tr8(qn, pT, qcT)

                def sc_mm():
                    ins = None
                    for hh in range(4):
                        for kc in range(2):
                            o_ = pC.t[:, (hh * 2 + kc) * 128:(hh * 2 + kc + 1) * 128]
                            for dc in range(2):
                                ins = nc.tensor.matmul(o_, lhsT=kmT.t[:, hh * 2 + dc, kc * 128:(kc + 1) * 128], rhs=qcT.t[:, hh * 2 + dc, :],
                                                       start=(dc == 0), stop=(dc == 1))
                    return ins
                T.op(T.pe, sc_mm, reads=[kmT, qcT], writes=[pC])
                pt = ptb.next()
                T.op(T.act, lambda: nc.scalar.activation(out=pt.t[:].rearrange("p c q -> p (c q)"), in_=pC.t[:], func=AF.Exp),
                     reads=[pC], writes=[pt])

                def pv_mm():
                    ins = None
                    for hh in range(4):
                        for kc in range(2):
                            ins = nc.tensor.matmul(pZ.t[:, hh * 128:(hh + 1) * 128], lhsT=self.ones.t[:], rhs=pt.t[:, hh * 2 + kc, :],
                                                   start=(kc == 0), stop=(kc == 1))
                        for dvc in range(2):
                            for kc in range(2):
                                ins = nc.tensor.matmul(pA.t[:, (hh * 2 + dvc) * 128:(hh * 2 + dvc + 1) * 128],
                                                       lhsT=vms.t[:, kc, hh * 256 + dvc * 128:hh * 256 + (dvc + 1) * 128],
                                                       rhs=pt.t[:, hh * 2 + kc, :], start=(kc == 0), stop=(kc == 1))
                    return ins
                T.op(T.pe, pv_mm, reads=[pt, vms, self.ones], writes=[pZ, pA])
                T.op(T.dve, lambda: nc.vector.reciprocal(out=rzb.t[:], in_=pZ.t[:]), reads=[pZ], writes=[rzb])
                oc = ocT.next()
                T.op(T.dve, lambda: nc.vector.tensor_tensor(
                    out=oc.t[:].rearrange("p (h e) q -> p h e q", e=2),
                    in0=pA.t[:].rearrange("p (h e q) -> p h e q", e=2, q=128),
                    in1=rzb.t[:].rearrange("p (h q) -> p h q", q=128).unsqueeze(2).to_broadcast([128, 4, 2, 128]), op=ALU.mult),
                    reads=[pA, rzb], writes=[oc])
                proj2(pB, oc, wco, wcob)
                T.op(T.dve, lambda: nc.vector.tensor_tensor(out=xb.t[:], in0=pB.t[:], in1=xb.t[:], op=ALU.add), reads=[pB, xb], writes=[xb])
                T.dma(T.sp, self.X[t * 128:(t + 1) * 128, :], xb.t[:], xds[t % 3], reads=[xb])
                h3 = h3s[t % 2]
                rmsnorm_T(xb, g_ffn, h3)
                T.dma(T.sp, H3v[:, :, 1 + t * 128:1 + (t + 1) * 128], h3.t[:], h3ds[t % 2], reads=[h3])
            self.end_phase()

    def tr8(self, src_b, pT, dst_b):
        T = self.T
        nc = self.nc

        def tr():
            ins = None
            for c in range(8):
                ins = nc.tensor.transpose(out=pT.t[:, c * 128:(c + 1) * 128], in_=src_b.t[:, c * 128:(c + 1) * 128],
                                          identity=self.ident.t[:])
            return ins
        T.op(T.pe, tr, reads=[src_b, self.ident], writes=[pT])
        T.op(T.act, lambda: nc.scalar.copy(out=dst_b.t[:].rearrange("p c t -> p (c t)"), in_=pT.t[:]), reads=[pT], writes=[dst_b])

    def ffn_up(self, l, S):
        T = self.T
        nc = self.nc
        I = self.inp
        nblk = (S + 509) // 510
        blocks = []
        for b in range(nblk):
            c0 = 510 * b
            w = min(512, S + 2 - c0)
            blocks.append((c0, w))
        passes = [blocks[i:i + 9] for i in range(0, nblk, 9)]
        H3v = self.H3T.rearrange("(c p) t -> p c t", p=128)
        WUv = I["w_up"][l].rearrange("(c p) n -> p c n", p=128)
        self.begin_phase()
        with ExitStack() as ctx:
            cp = T.sb(ctx, "convp", [128, 4, 44], F32)
            T.dma(T.sp, cp.t[:], I["convp"][l], self.newds(), writes=[cp])
            wab = [T.sb(ctx, "wab", [128, 8, 256], BF16) for _ in range(2)]
            wds = [self.newds("pool") for _ in range(2)]
            maxc = max(pb[-1][0] + pb[-1][1] - pb[0][0] for pb in passes)
            hres = T.sb(ctx, "hres", [128, 8, maxc], BF16)
            hds = self.newds()
            pU = Rot([T.ps(ctx, "pU", [128, 512], F32) for _ in range(6)])
            ca = Rot([T.sb(ctx, "ca", [128, 512], F32) for _ in range(2)])
            cg = Rot([T.sb(ctx, "cg", [128, 512], F32) for _ in range(2)])
            sg = Rot([T.sb(ctx, "sg", [128, 512], F32) for _ in range(2)])
            ast = [T.sb(ctx, "ast", [128, 512], BF16) for _ in range(3)]
            ads = [self.newds() for _ in range(3)]
            ai = 0

            def load_wab(i):
                b = wab[i % 2]
                T.dma(T.pool, b.t[:, :, 0:128], WUv[:, :, i * 128:(i + 1) * 128], wds[i % 2], writes=[b])
                T.dma(T.pool, b.t[:, :, 128:256], WUv[:, :, DFF + i * 128:DFF + (i + 1) * 128], wds[i % 2], writes=[b])

            for pb in passes:
                col_lo = pb[0][0]
                col_hi = pb[-1][0] + pb[-1][1]
                v_lo = max(col_lo, 1)
                v_hi = min(col_hi, S + 1)
                T.dma(T.sp, hres.t[:, :, v_lo - col_lo:v_hi - col_lo], H3v[:, :, v_lo:v_hi], hds, writes=[hres])
                if col_lo == 0:
                    T.op(T.pool, lambda: nc.gpsimd.memset(hres.t[:, :, 0:1], 0.0), writes=[hres])
                if col_hi == S + 2:
                    T.op(T.pool, lambda: nc.gpsimd.memset(hres.t[:, :, col_hi - col_lo - 1:col_hi - col_lo], 0.0), writes=[hres])
                load_wab(0)
                for i in range(22):
                    if i + 1 < 22:
                        load_wab(i + 1)
                    wb = wab[i % 2]
                    for (c0, w) in pb:
                        off = c0 - col_lo
                        ua = pU.next()
                        ug = pU.next()

                        def mm():
                            ins = None
                            for (dst, wo_) in ((ua, 0), (ug, 128)):
                                for c in range(8):
                                    ins = nc.tensor.matmul(dst.t[:, 0:w], lhsT=wb.t[:, c, wo_:wo_ + 128], rhs=hres.t[:, c, off:off + w],
                                                           start=(c == 0), stop=(c == 7))
                            return ins
                        T.op(T.pe, mm, reads=[wb, hres], writes=[ua, ug])
                        n = w - 2
                        outs = []
                        for (u, col, dstr) in ((ua, i, ca), (ug, 22 + i, cg)):
                            cb_ = dstr.next()
                            if n >= 256:
                                T.op(T.act, lambda: nc.scalar.activation(out=cb_.t[:, 0:n], in_=u.t[:, 0:n], func=AF.Identity,
                                                                         scale=cp.t[:, 0, col:col + 1], bias=cp.t[:, 3, col:col + 1]),
                                     reads=[u, cp], writes=[cb_])
                            else:
                                T.op(T.dve, lambda: nc.vector.tensor_scalar(out=cb_.t[:, 0:n], in0=u.t[:, 0:n], scalar1=cp.t[:, 0, col:col + 1],
                                                                            scalar2=cp.t[:, 3, col:col + 1], op0=ALU.mult, op1=ALU.add),
                                     reads=[u, cp], writes=[cb_])
                            T.op(T.dve, lambda: nc.vector.scalar_tensor_tensor(out=cb_.t[:, 0:n], in0=u.t[:, 1:n + 1], scalar=cp.t[:, 1, col:col + 1],
                                                                               in1=cb_.t[:, 0:n], op0=ALU.mult, op1=ALU.add),
                                 reads=[u, cp, cb_], writes=[cb_])
                            T.op(T.dve, lambda: nc.vector.scalar_tensor_tensor(out=cb_.t[:, 0:n], in0=u.t[:, 2:n + 2], scalar=cp.t[:, 2, col:col + 1],
                                                                               in1=cb_.t[:, 0:n], op0=ALU.mult, op1=ALU.add),
                                 reads=[u, cp, cb_], writes=[cb_])
                            outs.append(cb_)
                        sgb = sg.next()
                        T.op(T.act, lambda: nc.scalar.activation(out=sgb.t[:, 0:n], in_=outs[1].t[:, 0:n], func=AF.Silu), reads=[outs[1]], writes=[sgb])
                        ab = ast[ai % 3]
                        T.op(T.pool, lambda: nc.gpsimd.tensor_tensor(out=ab.t[:, 0:n], in0=sgb.t[:, 0:n], in1=outs[0].t[:, 0:n], op=ALU.mult),
                             reads=[sgb, outs[0]], writes=[ab])
                        T.dma(T.sp, self.ACTT[i * 128:(i + 1) * 128, c0:c0 + n], ab.t[:, 0:n], ads[ai % 3], reads=[ab])
                        ai += 1
            self.end_phase()

    def ffn_down(self, l, S, dst):
        T = self.T
        nc = self.nc
        I = self.inp
        NT = S // 128
        AV = self.ACTT.rearrange("(c p) t -> p c t", p=128)
        self.begin_phase()
        with ExitStack() as ctx:
            wd, wdb = self.load_w(ctx, "wd", I["w_down"][l], DFF, D)
            pY = Rot([T.ps(ctx, "pY", [128, 1024], F32) for _ in range(2)])
            xts = [T.sb(ctx, "xt", [128, D], F32) for _ in range(3)]
            xds = [self.newds() for _ in range(3)]
            ats = [T.sb(ctx, "aT", [128, 22, 128], BF16) for _ in range(2)]
            atds = [self.newds() for _ in range(2)]

            def loads(t):
                T.dma(T.sp, xts[t % 3].t[:], self.X[t * 128:(t + 1) * 128, :], xds[t % 3], writes=[xts[t % 3]])
                T.dma(T.sp, ats[t % 2].t[:], AV[:, :, t * 128:(t + 1) * 128], atds[t % 2], writes=[ats[t % 2]])
            loads(0)
            for t in range(NT):
                if t + 1 < NT:
                    loads(t + 1)
                xb = xts[t % 3]
                ab = ats[t % 2]
                py = pY.next()

                def mm():
                    ins = None
                    for nh in range(2):
                        for c in range(22):
                            ins = nc.tensor.matmul(py.t[:, nh * 512:(nh + 1) * 512], lhsT=ab.t[:, c, :], rhs=wd[:, c, nh * 512:(nh + 1) * 512],
                                                   start=(c == 0), stop=(c == 21))
                    return ins
                T.op(T.pe, mm, reads=[ab] + wdb, writes=[py])
                T.op(T.dve, lambda: nc.vector.tensor_tensor(out=xb.t[:], in0=py.t[:], in1=xb.t[:], op=ALU.add), reads=[py, xb], writes=[xb])
                T.dma(T.sp, dst[t * 128:(t + 1) * 128, :], xb.t[:], xds[t % 3], reads=[xb])
            self.end_phase()


def Buf_view(b):
    return b


def _rope_tab(half):
    pos = np.arange(SS_, dtype=np.float32)
    inv = (np.float32(10000.0) ** (-np.arange(half, dtype=np.float32) / np.float32(half))).astype(np.float32)
    ang = (pos[:, None] * inv[None, :]).astype(np.float32)
    c = np.cos(ang).astype(np.float32).reshape(64, 128, half).transpose(1, 0, 2)
    s = np.sin(ang).astype(np.float32).reshape(64, 128, half).transpose(1, 0, 2)
    return np.ascontiguousarray(c), np.ascontiguousarray(s)


def _dil_strips():
    out = np.zeros((128, DIL_W), np.float32)
    p = np.arange(128)[:, None]
    for g in range(3):
        r = DIL_R[g]
        w = DIL_DMAX[g] - DIL_DMIN[g] + 512
        x = np.arange(w)[None, :]
        delta = p - x + DIL_DMAX[g]
        ok = (delta % r == 0) & (np.abs(delta) <= 64 * r)
        out[:, DIL_OFF[g]:DIL_OFF[g] + w] = np.where(ok, 1.0, 0.0)
    return out


def _na_rowmask():
    R = 1000
    out = np.full((2, 3 * 8 * 8), NEG, np.float32)
    for ty, r0 in ((0, 0), (1, 496), (2, R - 8)):
        for ci in range(8):
            kr0 = r0 - 4 + 2 * ci
            for a in range(2):
                kr = kr0 + a
                for j in range(8):
                    r = r0 + j
                    rs = min(max(r - 4, 0), R - 8)
                    if rs <= kr < rs + 8:
                        out[a, (ty * 8 + ci) * 8 + j] = 0.0
    return out


def _na_table(rpb):
    Lh = rpb.shape[0]
    kc = np.arange(64)[:, None]
    c = np.arange(64)[None, :]
    cs = np.clip(c - 8, 0, 48)
    mcol = (kc >= cs) & (kc < cs + 16)
    dcidx = np.clip(kc - c + 15, 0, 30)
    out = np.full((Lh, 6, 128, NA_MM, 64), NEG, np.float32)
    for a in range(2):
        for mm in range(-3, NA_MM - 3):
            m = mm - a
            if 0 <= m <= 14:
                vals = rpb[:, :, 14 - m, :][:, :, dcidx]
                out[:, :, a * 64:(a + 1) * 64, mm + 3, :] = np.where(mcol[None, None], vals, np.float32(NEG))
    return np.ascontiguousarray(out.reshape(Lh, 6, 128, NA_MM * 64))


def prep_inputs(inputs, n_cores=8):
    f = lambda a: np.ascontiguousarray(np.asarray(a, dtype=np.float32))
    gv = np.concatenate([f(inputs[k]) for k in ("norm_mix", "mla_q_norm", "mla_kv_norm", "mla_qn", "mla_kn", "na_qn", "na_kn",
                                                "dil_qn", "dil_kn", "norm_cross", "x_qn", "norm_ffn", "norm_mem", "x_kn")], axis=1)
    assert gv.shape == (L, NG)
    cw = f(inputs["conv_w"])
    cbv = f(inputs["conv_b"])
    convp = np.concatenate([cw, cbv[:, None, :]], axis=1).reshape(L, 4, 44, 128).transpose(0, 3, 1, 2)
    cosd, sind = _rope_tab(32)
    cosm, sinm = _rope_tab(16)
    ea = np.zeros((2, 128), np.float32)
    ea[0, :64] = 1.0
    ea[1, 64:] = 1.0
    shared = {
        "gvec": np.ascontiguousarray(gv), "convp": np.ascontiguousarray(convp), "natab": _na_table(f(inputs["na_rpb"])),
        "ident": np.eye(128, dtype=np.float32), "cosd": cosd, "sind": sind, "cosm": cosm, "sinm": sinm,
        "dstrip": _dil_strips(), "rsmall": _na_rowmask(), "ea": ea,
    }
    for k in ("w_in", "w_uq", "w_uk", "w_uv", "w_o", "w_cq", "w_ckv", "w_co", "w_up", "w_down"):
        shared[k] = f(inputs[k])
    xp = f(inputs["x_prompt"])
    xs = f(inputs["x_sample"])
    mp = f(inputs["mem_prompt"])
    ms = f(inputs["mem_sample"])
    zs = np.zeros((SS_, D), np.float32)
    zm = np.zeros((MEM, D), np.float32)
    maps = []
    for c in range(n_cores):
        m = dict(shared)
        m["xp"] = xp[c]
        m["memp"] = mp[c]
        if c == 0:
            m["xs"], m["mems"] = xs[0], ms[0]
        elif c == 4:
            m["xs"], m["mems"] = xs[1], ms[1]
        else:
            m["xs"], m["mems"] = zs, zm
        maps.append(m)
    return maps


def kernel(**inputs):
    cfg = {"parts": [("p", SP_), ("s", SS_)]}
    nc = Prog(cfg).build()
    maps = prep_inputs(inputs)
    res = run_bass_kernel_spmd(nc, maps, core_ids=list(range(8)))
    yp = np.stack([np.asarray(res.results[c]["yp"], dtype=np.float32) for c in range(8)], axis=0)
    ys = np.stack([np.asarray(res.results[c]["ys"], dtype=np.float32) for c in (0, 4)], axis=0)
    return (yp, ys)
```

```python
import numpy as np
import concourse.bass as bass
import concourse.mybir as mybir
from concourse.bass_utils import run_bass_kernel_spmd
from contextlib import ExitStack

F32 = mybir.dt.float32
BF16 = mybir.dt.bfloat16
AF = mybir.ActivationFunctionType
ALU = mybir.AluOpType
AX = mybir.AxisListType

D = 1024
L = 4
EPS = 1e-6
NEG = -30000.0
D_IN = 3872
DFF = 2816
SP_ = 2048
SS_ = 8192
MEM = 256
G_MIX, G_CQ, G_CKV, G_MQN, G_MKN, G_NQ, G_NK, G_DQ, G_DK, G_CROSS, G_XQ, G_FFN, G_MEM, G_XK = (
    0, 1024, 1280, 1408, 1504, 1600, 1664, 1728, 1792, 1856, 2880, 3136, 4160, 5184)
NG = 5440
DIL_R = (1, 4, 16)
DIL_DMIN = (-128, -256, -1024)
DIL_DMAX = (512, 640, 1408)
DIL_OFF = (0, 1152, 2560)
DIL_W = 5504
NA_MM = 22


class Ev:
    __slots__ = ("sem", "val", "eng", "ds")

    def __init__(self, sem, val, eng, ds=None):
        self.sem = sem
        self.val = val
        self.eng = eng
        self.ds = ds


class Buf:
    def __init__(self, name, t=None):
        self.name = name
        self.t = t
        self.w = []
        self.r = {}
        self.rd = []


class DS:
    def __init__(self, sem):
        self.sem = sem
        self.cnt = 0


class Eng:
    def __init__(self, name, h, pe=False):
        self.name = name
        self.h = h
        self.pe = pe
        self.sem = None
        self.count = 0
        self.waited = {}
        self.nsem = 0


class Tracker:
    EPOCH = 1 << 20

    def __init__(self, nc, es):
        self.nc = nc
        self.es = es
        self.uid = 0
        self.pe = Eng("pe", nc.tensor, True)
        self.act = Eng("act", nc.scalar)
        self.dve = Eng("dve", nc.vector)
        self.pool = Eng("pool", nc.gpsimd)
        self.sp = Eng("sp", nc.sync)
        self.engs = [self.pe, self.act, self.dve, self.pool, self.sp]
        for e in self.engs:
            e.sem = self.newsem(e.name + "_s0")
        self.free_ds = {}
        self.all_ds = []
        self.bar = DS(self.newsem("bar"))
        self.bar.kind = "sp"
        self.ninstr = 0

    def newsem(self, name):
        return self.es.enter_context(self.nc.semaphore(name))

    def get_ds(self, kind="sp"):
        fl = self.free_ds.setdefault(kind, [])
        if fl:
            return fl.pop()
        ds = DS(self.newsem("ds%s%d" % (kind, len(self.all_ds))))
        ds.kind = kind
        self.all_ds.append(ds)
        return ds

    def sb(self, ctx, name, shape, dt):
        self.uid += 1
        t = ctx.enter_context(self.nc.sbuf_tensor("%s_%d" % (name, self.uid), list(shape), dt))
        return Buf(name, t)

    def ps(self, ctx, name, shape, dt):
        self.uid += 1
        t = ctx.enter_context(self.nc.psum_tensor("%s_%d" % (name, self.uid), list(shape), dt))
        return Buf(name, t)

    def wait(self, eng, ev):
        if ev.eng is eng and eng.pe:
            return
        k = id(ev.sem)
        if eng.waited.get(k, 0) >= ev.val:
            return
        val = ev.ds.cnt if ev.ds is not None else ev.val
        eng.h.wait_ge(ev.sem, val)
        eng.waited[k] = val

    def deps(self, eng, reads, writes):
        for b in reads:
            for ev in b.w:
                self.wait(eng, ev)
        for b in writes:
            for ev in b.w:
                self.wait(eng, ev)
            for ev in b.r.values():
                self.wait(eng, ev)
            for ev in b.rd:
                self.wait(eng, ev)

    def done(self, ev, reads, writes):
        for b in reads:
            if ev.eng is None:
                b.rd.append(ev)
            else:
                b.r[ev.eng.name] = ev
        for b in writes:
            b.w = [ev]
            b.r = {}
            b.rd = []

    def op(self, eng, fn, reads=(), writes=()):
        self.deps(eng, reads, writes)
        ins = fn()
        eng.count += 1
        ins.then_inc(eng.sem, 1)
        self.ninstr += 1
        ev = Ev(eng.sem, eng.count, eng)
        self.done(ev, reads, writes)
        if eng.count >= self.EPOCH:
            eng.nsem += 1
            eng.sem = self.newsem("%s_s%d" % (eng.name, eng.nsem))
            eng.count = 0
        return ev

    def dma(self, q, out, in_, ds, reads=(), writes=()):
        assert ds.kind == ("pool" if q is self.pool else "sp"), (ds.kind, q.name)
        self.deps(q, reads, writes)
        ins = q.h.dma_start(out=out, in_=in_)
        ds.cnt += 16
        ins.then_inc(ds.sem, 16)
        self.ninstr += 1
        ev = Ev(ds.sem, ds.cnt, None, ds)
        self.done(ev, reads, writes)
        return ev

    def barrier(self, dummy_src, dummy_dst):
        sp = self.sp
        for e in self.engs:
            if e is not sp and e.count > 0:
                self.wait(sp, Ev(e.sem, e.count, e))
        for ds in self.all_ds:
            if ds.cnt > 0:
                self.wait(sp, Ev(ds.sem, ds.cnt, None, ds))
        ins = sp.h.dma_start(out=dummy_dst, in_=dummy_src)
        self.bar.cnt += 16
        ins.then_inc(self.bar.sem, 16)
        for e in self.engs:
            e.h.wait_ge(self.bar.sem, self.bar.cnt)


class Rot:
    def __init__(self, items):
        self.items = items
        self.i = 0

    def next(self):
        b = self.items[self.i % len(self.items)]
        self.i += 1
        return b


class Prog:
    def __init__(self, cfg):
        self.cfg = cfg
        self.nc = bass.Bass("TRN2", target_bir_lowering=False)
        self.es = ExitStack()

    def din(self, name, shape, dt=F32):
        return self.nc.dram_tensor(name, list(shape), dt, kind="ExternalInput").ap()

    def dscr(self, name, shape, dt, dbg=False):
        kind = "ExternalOutput" if (dbg and self.cfg.get("debug")) else "Internal"
        return self.nc.dram_tensor(name, list(shape), dt, kind=kind).ap()

    def build(self):
        nc = self.nc
        cfg = self.cfg
        parts = cfg["parts"]
        self.inp = {}
        I = self.inp
        I["xp"] = self.din("xp", [SP_, D])
        I["xs"] = self.din("xs", [SS_, D])
        I["memp"] = self.din("memp", [MEM, D])
        I["mems"] = self.din("mems", [MEM, D])
        I["w_in"] = self.din("w_in", [L, D, D_IN])
        I["w_uq"] = self.din("w_uq", [L, 256, 576])
        I["w_uk"] = self.din("w_uk", [L, 128, 384])
        I["w_uv"] = self.din("w_uv", [L, 128, 384])
        I["w_o"] = self.din("w_o", [L, D, D])
        I["w_cq"] = self.din("w_cq", [L, D, D])
        I["w_ckv"] = self.din("w_ckv", [L, D, 2 * D])
        I["w_co"] = self.din("w_co", [L, D, D])
        I["w_up"] = self.din("w_up", [L, D, 2 * DFF])
        I["w_down"] = self.din("w_down", [L, DFF, D])
        I["gvec"] = self.din("gvec", [L, NG])
        I["convp"] = self.din("convp", [L, 128, 4, 44])
        I["natab"] = self.din("natab", [L, 6, 128, NA_MM * 64])
        I["ident"] = self.din("ident", [128, 128])
        I["cosd"] = self.din("cosd", [128, 64, 32])
        I["sind"] = self.din("sind", [128, 64, 32])
        I["cosm"] = self.din("cosm", [128, 64, 16])
        I["sinm"] = self.din("sinm", [128, 64, 16])
        I["dstrip"] = self.din("dstrip", [128, DIL_W])
        I["rsmall"] = self.din("rsmall", [2, 3 * 8 * 8])
        I["ea"] = self.din("ea", [2, 128])
        self.yp = nc.dram_tensor("yp", [SP_, D], F32, kind="ExternalOutput").ap()
        self.ys = nc.dram_tensor("ys", [SS_, D], F32, kind="ExternalOutput").ap()
        SM = max(S for _, S in parts)
        self.SM = SM
        dbg = True
        self.X = self.dscr("X", [SM, D], F32, dbg)
        self.QTn = self.dscr("QTn", [384, SM], BF16, dbg)
        self.KTn = self.dscr("KTn", [384, SM], BF16, dbg)
        self.Vn = self.dscr("Vn", [SM, 384], BF16, dbg)
        self.QTd = self.dscr("QTd", [768, SM], BF16, dbg)
        self.KTd = self.dscr("KTd", [768, SM], BF16, dbg)
        self.Vd = self.dscr("Vd", [SM, 768], BF16, dbg)
        self.QTm = self.dscr("QTm", [6, 96, SM], BF16, dbg)
        self.KTm = self.dscr("KTm", [6, 96, SM], BF16, dbg)
        self.Vm = self.dscr("Vm", [SM, 384], BF16, dbg)
        self.OT = self.dscr("OT", [D, SM], BF16, dbg)
        self.H3T = self.dscr("H3T", [D, SM + 2], BF16, dbg)
        self.ACTT = self.dscr("ACTT", [DFF, SM], BF16, dbg)
        self.dum0 = self.dscr("dum0", [1, 16], F32)
        self.dum1 = self.dscr("dum1", [1, 16], F32)

        with self.es:
            self.T = Tracker(nc, self.es)
            T = self.T
            self.consts()
            for (pname, S) in parts:
                xin = I["xp"] if pname == "p" else I["xs"]
                mem = I["memp"] if pname == "p" else I["mems"]
                yout = self.yp if pname == "p" else self.ys
                nl = cfg.get("layers", L)
                for l in range(nl):
                    last = (l == nl - 1)
                    xsrc = xin if l == 0 else self.X
                    ph = cfg.get("phases", "1nmdxfg")
                    if "1" in ph:
                        self.phase1(l, S, xsrc)
                    if "n" in ph:
                        self.attn_na(l, S)
                    if "d" in ph:
                        self.attn_dil(l, S)
                    if "m" in ph:
                        self.attn_mla(l, S)
                    if "x" in ph:
                        self.phase_x(l, S, xsrc, mem)
                    if "f" in ph:
                        self.ffn_up(l, S)
                    if "g" in ph:
                        self.ffn_down(l, S, yout if (last and not cfg.get("debug")) else self.X)
            T.barrier(self.inp["ident"][0:1, 0:16], self.dum1)
        return nc

    def bar(self):
        self.T.barrier(self.inp["ident"][0:1, 0:16], self.dum1)

    def consts(self):
        T = self.T
        nc = self.nc
        I = self.inp
        es = self.es
        self.ident = T.sb(es, "ident", [128, 128], BF16)
        self.ones = T.sb(es, "ones", [128, 128], BF16)
        self.ea = T.sb(es, "ea", [2, 128], BF16)
        self.rsmall = T.sb(es, "rsmall", [2, 192], BF16)
        ds = T.get_ds("pool")
        T.dma(T.pool, self.ident.t[:], I["ident"][:, :], ds, writes=[self.ident])
        ds = T.get_ds("pool")
        T.dma(T.pool, self.ea.t[:], I["ea"][:, :], ds, writes=[self.ea])
        ds = T.get_ds("pool")
        T.dma(T.pool, self.rsmall.t[:], I["rsmall"][:, :], ds, writes=[self.rsmall])
        T.op(T.pool, lambda: nc.gpsimd.memset(self.ones.t[:], 1.0), writes=[self.ones])

    def load_w(self, ctx, name, src, K, N, nsplit=1):
        T = self.T
        kc = K // 128
        w = T.sb(ctx, name, [128, kc, N], BF16)
        bufs = []
        step = (N + nsplit - 1) // nsplit
        for c in range(kc):
            b = Buf("%s_c%d" % (name, c), w.t)
            ds = self.newds("pool")
            for n0 in range(0, N, step):
                n1 = min(N, n0 + step)
                T.dma(T.pool, w.t[:, c, n0:n1], src[c * 128:(c + 1) * 128, n0:n1], ds, writes=[b])
            bufs.append(b)
        return w.t, bufs

    def load_rep(self, ctx, name, src1d, n):
        T = self.T
        g = T.sb(ctx, name, [128, n], F32)
        ds = self.newds()
        T.dma(T.sp, g.t[:], src1d.partition_broadcast(128), ds, writes=[g])
        return g

    def begin_phase(self):
        self.phase_ds = []

    def end_phase(self):
        self.bar()
        for ds in self.phase_ds:
            self.T.free_ds.setdefault(ds.kind, []).append(ds)
        self.phase_ds = []

    def newds(self, kind="sp"):
        ds = self.T.get_ds(kind)
        self.phase_ds.append(ds)
        return ds

    def rms_rstd(self, ss, rstd, n, sc, bias):
        T = self.T
        nc = self.nc
        T.op(T.act, lambda: nc.scalar.activation(out=rstd.t[:, 0:n], in_=ss.t[:, 0:n], func=AF.Sqrt,
                                                 bias=self.cbias(bias), scale=sc), reads=[ss, self._cb[float(bias)]], writes=[rstd])
        T.op(T.dve, lambda: nc.vector.reciprocal(out=rstd.t[:, 0:n], in_=rstd.t[:, 0:n]), reads=[rstd], writes=[rstd])

    def cbias(self, v):
        key = float(v)
        if key not in self._cb:
            raise KeyError(key)
        return self._cb[key].t[:, 0:1]

    def make_cbias(self, ctx, vals):
        T = self.T
        nc = self.nc
        self._cb = {}
        for v in vals:
            b = T.sb(ctx, "cb", [128, 1], F32)
            T.op(T.pool, lambda: nc.gpsimd.memset(b.t[:], float(v)), writes=[b])
            self._cb[float(v)] = b
        self._cb_bufs = list(self._cb.values())

    def phase1(self, l, S, xsrc):
        T = self.T
        nc = self.nc
        I = self.inp
        NT = S // 128
        E2 = T.pool
        H2 = nc.gpsimd
        self.begin_phase()
        with ExitStack() as ctx:
            self.make_cbias(ctx, [EPS, 64 * EPS, 96 * EPS])
            win, winb = self.load_w(ctx, "win", I["w_in"][l], D, D_IN, nsplit=2)
            wuq, wuqb = self.load_w(ctx, "wuq", I["w_uq"][l], 256, 576)
            wuk, wukb = self.load_w(ctx, "wuk", I["w_uk"][l], 128, 384)
            wuv, wuvb = self.load_w(ctx, "wuv", I["w_uv"][l], 128, 384)
            gv = self.load_rep(ctx, "gv1", I["gvec"][l, 0:G_CROSS], G_CROSS)
            ropes = [T.sb(ctx, "rope", [128, 96], F32) for _ in range(3)]
            rope_ds = [self.newds() for _ in range(3)]
            xts = [T.sb(ctx, "xt", [128, D], F32) for _ in range(2)]
            xds = [self.newds(), self.newds()]
            junk = T.sb(ctx, "junk", [128, D], BF16)
            ss1 = Rot([T.sb(ctx, "ss1", [128, 8], F32) for _ in range(2)])
            rs1 = Rot([T.sb(ctx, "rs1", [128, 8], F32) for _ in range(2)])
            hb = Rot([T.sb(ctx, "hb", [128, D], BF16) for _ in range(2)])
            hT = Rot([T.sb(ctx, "hT", [128, 8, 128], BF16) for _ in range(2)])
            pT = Rot([T.ps(ctx, "pT", [128, 1024], BF16) for _ in range(1)])
            pz = Rot([T.ps(ctx, "pz", [128, 512], F32) for _ in range(5)])
            pX = Rot([T.ps(ctx, "pX", [128, 1024], BF16) for _ in range(2)])
            sq = Rot([T.sb(ctx, "sq", [128, 576], F32) for _ in range(4)])
            yy = Rot([T.sb(ctx, "yy", [128, 576], F32) for _ in range(4)])
            y2 = Rot([T.sb(ctx, "y2", [128, 576], F32) for _ in range(4)])
            tt = Rot([T.sb(ctx, "tt", [128, 6, 32], F32) for _ in range(12)])
            yb = Rot([T.sb(ctx, "yb", [128, 576], BF16) for _ in range(8)])
            ssh = Rot([T.sb(ctx, "ssh", [128, 8], F32) for _ in range(8)])
            rsh = Rot([T.sb(ctx, "rsh", [128, 8], F32) for _ in range(8)])
            cqn = Rot([T.sb(ctx, "cqn", [128, 384], BF16) for _ in range(2)])
            cT = Rot([T.sb(ctx, "cT", [128, 3, 128], BF16) for _ in range(2)])
            krb = Rot([T.sb(ctx, "krb", [128, 32], F32) for _ in range(2)])
            sskr_r = Rot([T.sb(ctx, "sskr", [128, 1], F32) for _ in range(2)])
            stq = [T.sb(ctx, "stq", [128, 18, 128], BF16) for _ in range(2)]
            stm = [T.sb(ctx, "stm", [96, 12, 128], BF16) for _ in range(2)]
            stq_ds = [self.newds() for _ in range(2)]
            stm_ds = [self.newds() for _ in range(2)]
            vst = [T.sb(ctx, "vst", [128, 1536], BF16) for _ in range(2)]
            vds = [self.newds(), self.newds()]

            dq = []
            DEPTH = 2

            def defer(fn):
                dq.append(fn)
                while len(dq) > DEPTH:
                    dq.pop(0)()

            def load_x(t):
                b = xts[t % 2]
                T.dma(T.sp, b.t[:], xsrc[t * 128:(t + 1) * 128, :], xds[t % 2], writes=[b])
                rp = ropes[t % 3]
                T.dma(T.sp, rp.t[:, 0:32], I["cosd"][:, t, :], rope_ds[t % 3], writes=[rp])
                T.dma(T.sp, rp.t[:, 32:64], I["sind"][:, t, :], rope_ds[t % 3], writes=[rp])
                T.dma(T.sp, rp.t[:, 64:80], I["cosm"][:, t, :], rope_ds[t % 3], writes=[rp])
                T.dma(T.sp, rp.t[:, 80:96], I["sinm"][:, t, :], rope_ds[t % 3], writes=[rp])

            def headnorm_g(pz_b, ncols, H, d, sc, bias, res, extra_ss=None):
                s_ = sq.next()
                T.op(T.act, lambda: nc.scalar.activation(out=s_.t[:, 0:ncols], in_=pz_b.t[:, 0:ncols], func=AF.Square),
                     reads=[pz_b], writes=[s_])
                ssb = ssh.next()
                T.op(T.dve, lambda: nc.vector.tensor_reduce(out=ssb.t[:, 0:H],
                                                            in_=s_.t[:, 0:ncols].rearrange("p (h d) -> p h d", d=d),
                                                            axis=AX.X, op=ALU.add), reads=[s_], writes=[ssb])
                if extra_ss is not None:
                    T.op(T.dve, lambda: nc.vector.tensor_scalar(out=ssb.t[:, 0:H], in0=ssb.t[:, 0:H],
                                                                scalar1=extra_ss.t[:, 0:1], scalar2=None, op0=ALU.add),
                         reads=[ssb, extra_ss], writes=[ssb])
                yield
                rb = rsh.next()
                self.rms_rstd(ssb, rb, H, sc, bias)
                res.append(rb)
                yield

            def tile_jobs(t):
                xt = xts[t % 2]
                rp = ropes[t % 3]
                par = t % 2
                sQ = stq[par]
                sM = stm[par]
                vb = vst[par]
                st = {}

                def head():
                    ssb = ss1.next()
                    T.op(T.act, lambda: nc.scalar.activation(out=junk.t[:], in_=xt.t[:], func=AF.Square,
                                                             accum_out=ssb.t[:, 0:1]), reads=[xt], writes=[junk, ssb])
                    rb = rs1.next()
                    self.rms_rstd(ssb, rb, 1, 1.0 / D, EPS)
                    yield
                    h = hb.next()
                    T.op(T.dve, lambda: nc.vector.scalar_tensor_tensor(out=h.t[:], in0=xt.t[:], scalar=rb.t[:, 0:1],
                                                                       in1=gv.t[:, G_MIX:G_MIX + D], op0=ALU.mult, op1=ALU.mult),
                         reads=[xt, rb, gv], writes=[h])
                    yield
                    hTb = hT.next()
                    self.tr8(h, pT.items[0], hTb)
                    st["hT"] = hTb

                def proj(col0, ncols):
                    z = pz.next()
                    hTb = st["hT"]

                    def mm():
                        ins = None
                        for c in range(8):
                            ins = nc.tensor.matmul(z.t[:, 0:ncols], lhsT=hTb.t[:, c, :], rhs=win[:, c, col0:col0 + ncols],
                                                   start=(c == 0), stop=(c == 7))
                        return ins
                    T.op(T.pe, mm, reads=[hTb] + winb, writes=[z])
                    return z

                def transpose_out(src_b, slabs, width, dst_b, dst_idx0, rows):
                    def now():
                        px = pX.next()

                        def trs():
                            ins = None
                            for s_ in range(slabs):
                                ins = nc.tensor.transpose(out=px.t[0:width, s_ * 128:(s_ + 1) * 128],
                                                          in_=src_b.t[:, s_ * width:(s_ + 1) * width], identity=self.ident.t[:])
                            return ins
                        T.op(T.pe, trs, reads=[src_b, self.ident], writes=[px])
                        T.op(T.act, lambda: nc.scalar.copy(
                            out=dst_b.t[0:rows, dst_idx0:dst_idx0 + slabs, 0:128],
                            in_=px.t[0:rows, 0:slabs * 128].rearrange("p (s t) -> p s t", t=128)),
                            reads=[px], writes=[dst_b])
                    defer(now)

                def rope_g(y_b, H, d, hd, c0, s0, lo0, out_b):
                    yv = y_b.t[:, 0:H * d].rearrange("p (h d) -> p h d", d=d)
                    ov = out_b.t[:, 0:H * d].rearrange("p (h d) -> p h d", d=d)
                    lo = yv[:, :, lo0:lo0 + hd]
                    hi = yv[:, :, lo0 + hd:lo0 + 2 * hd]
                    cs = rp.t[:, c0:c0 + hd].unsqueeze(1).to_broadcast([128, H, hd])
                    sn = rp.t[:, s0:s0 + hd].unsqueeze(1).to_broadcast([128, H, hd])
                    t1, t2, t3, t4 = tt.next(), tt.next(), tt.next(), tt.next()
                    T.op(T.dve, lambda: nc.vector.tensor_tensor(out=t1.t[:, 0:H, 0:hd], in0=lo, in1=cs, op=ALU.mult),
                         reads=[y_b, rp], writes=[t1])
                    T.op(E2, lambda: H2.tensor_tensor(out=t2.t[:, 0:H, 0:hd], in0=hi, in1=sn, op=ALU.mult),
                         reads=[y_b, rp], writes=[t2])
                    T.op(T.dve, lambda: nc.vector.tensor_tensor(out=t3.t[:, 0:H, 0:hd], in0=hi, in1=cs, op=ALU.mult),
                         reads=[y_b, rp], writes=[t3])
                    T.op(E2, lambda: H2.tensor_tensor(out=t4.t[:, 0:H, 0:hd], in0=lo, in1=sn, op=ALU.mult),
                         reads=[y_b, rp], writes=[t4])
                    yield
                    T.op(T.dve, lambda: nc.vector.tensor_tensor(out=ov[:, :, lo0:lo0 + hd], in0=t1.t[:, 0:H, 0:hd],
                                                                in1=t2.t[:, 0:H, 0:hd], op=ALU.subtract),
                         reads=[t1, t2], writes=[out_b])
                    T.op(E2, lambda: H2.tensor_tensor(out=ov[:, :, lo0 + hd:lo0 + 2 * hd], in0=t3.t[:, 0:H, 0:hd],
                                                      in1=t4.t[:, 0:H, 0:hd], op=ALU.add),
                         reads=[t3, t4], writes=[out_b])
                    if lo0 > 0:
                        T.op(E2, lambda: H2.tensor_copy(out=ov[:, :, 0:lo0], in_=yv[:, :, 0:lo0]),
                             reads=[y_b], writes=[out_b])
                    yield

                def qk_block(col0, H, goff, qscale, rope, dst_idx0):
                    d = 64
                    ncols = H * d
                    z = proj(col0, ncols)
                    yield
                    res = []
                    yield from headnorm_g(z, ncols, H, d, 1.0 if qscale else 1.0 / d, d * EPS if qscale else EPS, res)
                    rb_ = res[0]
                    y = yy.next()
                    T.op(T.dve, lambda: nc.vector.tensor_tensor(
                        out=y.t[:, 0:ncols].rearrange("p (h d) -> p h d", d=d),
                        in0=z.t[:, 0:ncols].rearrange("p (h d) -> p h d", d=d),
                        in1=rb_.t[:, 0:H].unsqueeze(2).to_broadcast([128, H, d]), op=ALU.mult),
                        reads=[z, rb_], writes=[y])
                    yield
                    ob = yb.next()
                    gb = gv.t[:, goff:goff + d].unsqueeze(1).to_broadcast([128, H, d])
                    if rope:
                        yg = y2.next()
                        T.op(E2, lambda: H2.tensor_tensor(
                            out=yg.t[:, 0:ncols].rearrange("p (h d) -> p h d", d=d),
                            in0=y.t[:, 0:ncols].rearrange("p (h d) -> p h d", d=d), in1=gb, op=ALU.mult),
                            reads=[y, gv], writes=[yg])
                        yield
                        yield from rope_g(yg, H, d, 32, 0, 32, 0, ob)
                    else:
                        T.op(E2, lambda: H2.tensor_tensor(
                            out=ob.t[:, 0:ncols].rearrange("p (h d) -> p h d", d=d),
                            in0=y.t[:, 0:ncols].rearrange("p (h d) -> p h d", d=d), in1=gb, op=ALU.mult),
                            reads=[y, gv], writes=[ob])
                        yield
                    transpose_out(ob, H // 2, 128, sQ, dst_idx0, 128)

                def v_block(col0, ncols, voff):
                    z = proj(col0, ncols)
                    yield
                    T.op(T.act, lambda: nc.scalar.copy(out=vb.t[:, voff:voff + ncols], in_=z.t[:, 0:ncols]),
                         reads=[z], writes=[vb])

                def mla():
                    z0 = proj(0, 416)
                    yield
                    cq = cqn.next()
                    res = []
                    yield from headnorm_g(z0, 256, 1, 256, 1.0 / 256, EPS, res)
                    rq = res[0]
                    T.op(T.dve, lambda: nc.vector.scalar_tensor_tensor(out=cq.t[:, 0:256], in0=z0.t[:, 0:256], scalar=rq.t[:, 0:1],
                                                                       in1=gv.t[:, G_CQ:G_CQ + 256], op0=ALU.mult, op1=ALU.mult),
                         reads=[z0, rq, gv], writes=[cq])
                    s_ = sq.next()
                    ssk = ssh.next()
                    T.op(T.act, lambda: nc.scalar.activation(out=s_.t[:, 0:128], in_=z0.t[:, 256:384], func=AF.Square,
                                                             accum_out=ssk.t[:, 0:1]), reads=[z0], writes=[s_, ssk])
                    yield
                    rk = rsh.next()
                    self.rms_rstd(ssk, rk, 1, 1.0 / 128, EPS)
                    yield
                    T.op(T.dve, lambda: nc.vector.scalar_tensor_tensor(out=cq.t[:, 256:384], in0=z0.t[:, 256:384], scalar=rk.t[:, 0:1],
                                                                       in1=gv.t[:, G_CKV:G_CKV + 128], op0=ALU.mult, op1=ALU.mult),
                         reads=[z0, rk, gv], writes=[cq])
                    kr = krb.next()
                    s2_ = sq.next()
                    sskr = sskr_r.next()
                    T.op(T.dve, lambda: nc.vector.tensor_copy(out=kr.t[:], in_=z0.t[:, 384:416]), reads=[z0], writes=[kr])
                    T.op(T.dve, lambda: nc.vector.tensor_tensor(out=s2_.t[:, 0:32], in0=kr.t[:], in1=kr.t[:], op=ALU.mult),
                         reads=[kr], writes=[s2_])
                    T.op(T.dve, lambda: nc.vector.tensor_reduce(out=sskr.t[:, 0:1], in_=s2_.t[:, 0:32], axis=AX.X, op=ALU.add),
                         reads=[s2_], writes=[sskr])
                    yield
                    p2 = pT.items[0]

                    def tr3():
                        ins = None
                        for c in range(3):
                            ins = nc.tensor.transpose(out=p2.t[:, c * 128:(c + 1) * 128], in_=cq.t[:, c * 128:(c + 1) * 128],
                                                      identity=self.ident.t[:])
                        return ins
                    T.op(T.pe, tr3, reads=[cq, self.ident], writes=[p2])
                    cTb = cT.next()
                    T.op(T.act, lambda: nc.scalar.copy(out=cTb.t[:].rearrange("p c t -> p (c t)"), in_=p2.t[:, 0:384]),
                         reads=[p2], writes=[cTb])
                    yield
                    for hq in range(2):
                        zq = pz.next()

                        def mmq():
                            ins = None
                            for c in range(2):
                                ins = nc.tensor.matmul(zq.t[:, 0:288], lhsT=cTb.t[:, c, :], rhs=wuq[:, c, hq * 288:(hq + 1) * 288],
                                                       start=(c == 0), stop=(c == 1))
                            return ins
                        T.op(T.pe, mmq, reads=[cTb] + wuqb, writes=[zq])
                        yield
                        res = []
                        yield from headnorm_g(zq, 288, 3, 96, 1.0, 96 * EPS, res)
                        rb_ = res[0]
                        y = yy.next()
                        T.op(T.dve, lambda: nc.vector.tensor_tensor(
                            out=y.t[:, 0:288].rearrange("p (h d) -> p h d", d=96),
                            in0=zq.t[:, 0:288].rearrange("p (h d) -> p h d", d=96),
                            in1=rb_.t[:, 0:3].unsqueeze(2).to_broadcast([128, 3, 96]), op=ALU.mult),
                            reads=[zq, rb_], writes=[y])
                        yield
                        yg = y2.next()
                        T.op(E2, lambda: H2.tensor_tensor(
                            out=yg.t[:, 0:288].rearrange("p (h d) -> p h d", d=96),
                            in0=y.t[:, 0:288].rearrange("p (h d) -> p h d", d=96),
                            in1=gv.t[:, G_MQN:G_MQN + 96].unsqueeze(1).to_broadcast([128, 3, 96]), op=ALU.mult),
                            reads=[y, gv], writes=[yg])
                        yield
                        ob = yb.next()
                        yield from rope_g(yg, 3, 96, 16, 64, 80, 64, ob)
                        transpose_out(ob, 3, 96, sM, hq * 3, 96)
                    zk = pz.next()
                    T.op(T.pe, lambda: nc.tensor.matmul(zk.t[:, 0:384], lhsT=cTb.t[:, 2, :], rhs=wuk[:, 0, :], start=True, stop=True),
                         reads=[cTb] + wukb, writes=[zk])
                    zv = pz.next()
                    T.op(T.pe, lambda: nc.tensor.matmul(zv.t[:, 0:384], lhsT=cTb.t[:, 2, :], rhs=wuv[:, 0, :], start=True, stop=True),
                         reads=[cTb] + wuvb, writes=[zv])
                    yield
                    T.op(T.act, lambda: nc.scalar.copy(out=vb.t[:, 0:384], in_=zv.t[:, 0:384]), reads=[zv], writes=[vb])
                    res = []
                    yield from headnorm_g(zk, 384, 6, 64, 1.0 / 96, EPS, res, extra_ss=sskr)
                    rbk = res[0]
                    yk = yy.next()
                    ykv = yk.t[:, 0:576].rearrange("p (h d) -> p h d", d=96)
                    T.op(T.dve, lambda: nc.vector.tensor_tensor(
                        out=ykv[:, :, 0:64], in0=zk.t[:, 0:384].rearrange("p (h d) -> p h d", d=64),
                        in1=rbk.t[:, 0:6].unsqueeze(2).to_broadcast([128, 6, 64]), op=ALU.mult),
                        reads=[zk, rbk], writes=[yk])
                    T.op(T.dve, lambda: nc.vector.tensor_tensor(
                        out=ykv[:, :, 64:96], in0=kr.t[:, :].unsqueeze(1).to_broadcast([128, 6, 32]),
                        in1=rbk.t[:, 0:6].unsqueeze(2).to_broadcast([128, 6, 32]), op=ALU.mult),
                        reads=[kr, rbk], writes=[yk])
                    yield
                    ykg = y2.next()
                    T.op(E2, lambda: H2.tensor_tensor(
                        out=ykg.t[:, 0:576].rearrange("p (h d) -> p h d", d=96), in0=ykv,
                        in1=gv.t[:, G_MKN:G_MKN + 96].unsqueeze(1).to_broadcast([128, 6, 96]), op=ALU.mult),
                        reads=[yk, gv], writes=[ykg])
                    yield
                    obk = yb.next()
                    yield from rope_g(ykg, 6, 96, 16, 64, 80, 64, obk)
                    transpose_out(obk, 6, 96, sM, 6, 96)

                def stores():
                    r0 = t * 128
                    T.dma(T.sp, self.Vm[r0:r0 + 128, :], vb.t[:, 0:384], vds[par], reads=[vb])
                    T.dma(T.sp, self.Vn[r0:r0 + 128, :], vb.t[:, 384:768], vds[par], reads=[vb])
                    T.dma(T.sp, self.Vd[r0:r0 + 128, :], vb.t[:, 768:1536], vds[par], reads=[vb])
                    for (dst, i0, ns) in ((self.QTn, 0, 3), (self.KTn, 3, 3), (self.QTd, 6, 6), (self.KTd, 12, 6)):
                        T.dma(T.sp, dst.rearrange("(s p) t -> p s t", p=128)[:, :, r0:r0 + 128], sQ.t[:, i0:i0 + ns, 0:128],
                              stq_ds[par], reads=[sQ])
                    T.dma(T.sp, self.QTm[:, :, r0:r0 + 128].rearrange("h d t -> d h t"), sM.t[:, 0:6, 0:128], stm_ds[par], reads=[sM])
                    T.dma(T.sp, self.KTm[:, :, r0:r0 + 128].rearrange("h d t -> d h t"), sM.t[:, 6:12, 0:128], stm_ds[par], reads=[sM])

                blocks = [
                    mla,
                    lambda: qk_block(416, 6, G_NQ, True, False, 0),
                    lambda: qk_block(800, 6, G_NK, False, False, 3),
                    lambda: qk_block(1568, 6, G_DQ, True, True, 6),
                    lambda: v_block(1184, 384, 384),
                    lambda: qk_block(1952, 6, G_DQ, True, True, 9),
                    lambda: qk_block(2336, 6, G_DK, False, True, 12),
                    lambda: v_block(3104, 384, 768),
                    lambda: qk_block(2720, 6, G_DK, False, True, 15),
                    lambda: v_block(3488, 384, 1152),
                ]
                return head, blocks, stores

            W = 3
            active = []

            def pump():
                for g in list(active):
                    try:
                        next(g)
                    except StopIteration:
                        active.remove(g)

            load_x(0)
            head0, blocks0, stores0 = tile_jobs(0)
            for _ in head0():
                pass
            cur = (blocks0, stores0)
            for t in range(NT):
                if t + 1 < NT:
                    load_x(t + 1)
                blocks, stores = cur
                for bf in blocks:
                    active.append(bf())
                    while len(active) >= W:
                        pump()
                if t + 1 < NT:
                    hd, nb, ns = tile_jobs(t + 1)
                    active.append(hd())
                    cur = (nb, ns)
                while active:
                    pump()
                defer(stores)
            while dq:
                dq.pop(0)()
            self.end_phase()

    def attention(self, S, slots, l):
        T = self.T
        nc = self.nc
        I = self.inp
        NT = S // 128
        NQB = S // 512
        nstream = len(slots[0]["streams"])
        nset = 2 if nstream == 1 else 1
        use_dil = any(m[0] == "dil" for sl in slots for (_, _, ms) in sl["chunks"](0) for m in ms)
        self.begin_phase()
        with ExitStack() as ctx:
            sets = []
            for si in range(nset):
                st = []
                for k in range(nstream):
                    kt = T.sb(ctx, "kt", [96, S], BF16)
                    va = T.sb(ctx, "va", [128, NT, 128], BF16)
                    T.op(T.pool, lambda: nc.gpsimd.memset(va.t[:, :, 64:128], 1.0), writes=[va])
                    st.append((kt, va, self.newds(), self.newds()))
                nat = T.sb(ctx, "nat", [128, NA_MM * 64], BF16)
                sets.append((st, nat, self.newds("pool")))
            dstrip = None
            if use_dil:
                dstrip = T.sb(ctx, "dstrip", [128, DIL_W], BF16)
                dsd = self.newds("pool")
                for c0 in range(0, DIL_W, 2048):
                    c1 = min(DIL_W, c0 + 2048)
                    T.dma(T.pool, dstrip.t[:, c0:c1], I["dstrip"][:, c0:c1], dsd, writes=[dstrip])
            qtb = [Rot([T.sb(ctx, "qtb", [96, 512], BF16) for _ in range(3)]) for _ in range(nstream)]
            qds = [[self.newds(), self.newds(), self.newds()] for _ in range(nstream)]
            ptr = Rot([T.sb(ctx, "pt", [128, 2, 512], BF16) for _ in range(5)])
            pS = Rot([T.ps(ctx, "pS", [128, 1024], F32) for _ in range(3)])
            pO = Rot([T.ps(ctx, "pO", [128, 512], F32) for _ in range(2)])
            rzr = Rot([T.sb(ctx, "rz", [64, 512], F32) for _ in range(2)])
            ost = [T.sb(ctx, "ost", [64, 512], BF16) for _ in range(3)]
            ods = [self.newds() for _ in range(3)]
            oi = 0

            def load_slot(i):
                sl = slots[i]
                st, nat, nds = sets[i % nset]
                for k, sm in enumerate(sl["streams"]):
                    kt, va, kds, vds_ = st[k]
                    d = sm["d"]
                    T.dma(T.sp, kt.t[0:d, :], sm["kt"], kds, writes=[kt])
                    for c0 in range(0, NT, 16):
                        c1 = min(NT, c0 + 16)
                        T.dma(T.sp, va.t[:, c0:c1, 0:64],
                              sm["v"][c0 * 128:c1 * 128, :].rearrange("(c p) d -> p c d", p=128), vds_, writes=[va])
                if sl.get("natab") is not None:
                    T.dma(T.pool, nat.t[:], sl["natab"], nds, writes=[nat])

            pending = []

            def emit_pv(G):
                grp, st_, pt, po, g0, total, fin = G

                def pv():
                    ins = None
                    for gi, (k, c, masks) in enumerate(grp):
                        va = st_[k][1]
                        idx = g0 + gi
                        ins = nc.tensor.matmul(po.t[:, :], lhsT=va.t[:, c, :], rhs=pt.t[:, gi, :],
                                               start=(idx == 0), stop=(idx == total - 1))
                    return ins
                T.op(T.pe, pv, reads=[pt] + [st_[k][1] for (k, _, _) in grp], writes=[po])
                if fin is not None:
                    row0, jq = fin
                    rz = rzr.next()
                    T.op(T.dve, lambda: nc.vector.reciprocal(out=rz.t[:, :], in_=po.t[64:128, :]), reads=[po], writes=[rz])
                    ob = ost[self._oi % 3]
                    T.op(T.dve, lambda: nc.vector.tensor_tensor(out=ob.t[:, :], in0=po.t[0:64, :], in1=rz.t[:, :], op=ALU.mult),
                         reads=[po, rz], writes=[ob])
                    T.dma(T.sp, self.OT[row0:row0 + 64, jq * 512:(jq + 1) * 512], ob.t[:, :], ods[self._oi % 3], reads=[ob])
                    self._oi += 1

            self._oi = 0
            SKEW = 2
            load_slot(0)
            for i, sl in enumerate(slots):
                if nset == 2 and i + 1 < len(slots):
                    while pending:
                        emit_pv(pending.pop(0))
                    load_slot(i + 1)
                st, nat, _ = sets[i % nset]
                for jq in range(NQB):
                    qs = []
                    for k, sm in enumerate(sl["streams"]):
                        qb = qtb[k].next()
                        d = sm["d"]
                        T.dma(T.sp, qb.t[0:d, :], sm["qt"][:, jq * 512:(jq + 1) * 512], qds[k][(qtb[k].i - 1) % 3], writes=[qb])
                        qs.append(qb)
                    items = sl["chunks"](jq)
                    total = len(items)
                    po = pO.next()
                    for g0 in range(0, total, 2):
                        grp = items[g0:g0 + 2]
                        ps = pS.next()

                        def qk():
                            ins = None
                            for gi, (k, c, masks) in enumerate(grp):
                                kt = st[k][0]
                                d = sl["streams"][k]["d"]
                                o_ = ps.t[:, gi * 512:(gi + 1) * 512]
                                mm_masks = [m for m in masks if m[0] != "dil"]
                                ins = nc.tensor.matmul(o_, lhsT=kt.t[0:d, c * 128:(c + 1) * 128], rhs=qs[k].t[0:d, :],
                                                       start=True, stop=(len(mm_masks) == 0))
                                masks = mm_masks
                                for mi, m in enumerate(masks):
                                    last = (mi == len(masks) - 1)
                                    if m[0] == "nat":
                                        ins = nc.tensor.matmul(o_, lhsT=self.ident.t[:], rhs=nat.t[:, m[1]:m[1] + 512],
                                                               start=False, stop=last)
                                    elif m[0] == "dil":
                                        pass
                                    else:
                                        ins = nc.tensor.matmul(o_.rearrange("p (j c) -> p j c", c=64), lhsT=self.ea.t[0:2, :],
                                                               rhs=self.rsmall.t[0:2, m[1] * 8:(m[1] + 1) * 8].unsqueeze(2).to_broadcast([2, 8, 64]),
                                                               start=False, stop=last)
                            return ins
                        rd = [st[k][0] for (k, _, _) in grp] + [qs[k] for (k, _, _) in grp] + [self.ident, self.ea, self.rsmall, nat]
                        if dstrip is not None:
                            rd.append(dstrip)
                        T.op(T.pe, qk, reads=rd, writes=[ps])
                        pt = ptr.next()
                        n = len(grp)
                        T.op(T.act, lambda: nc.scalar.activation(out=pt.t[:, 0:n, :].rearrange("p g q -> p (g q)"),
                                                                 in_=ps.t[:, 0:n * 512], func=AF.Exp), reads=[ps], writes=[pt])
                        for gi, (k_, c_, masks_) in enumerate(grp):
                            for m in masks_:
                                if m[0] == "dil":
                                    self._mi = getattr(self, "_mi", 0) + 1
                                    if self._mi % 3 == 0:
                                        T.op(T.pool, lambda: nc.gpsimd.tensor_tensor(out=pt.t[:, gi, :], in0=pt.t[:, gi, :],
                                                                                     in1=dstrip.t[:, m[1]:m[1] + 512], op=ALU.mult),
                                             reads=[pt, dstrip], writes=[pt])
                                    else:
                                        T.op(T.dve, lambda: nc.vector.tensor_tensor(out=pt.t[:, gi, :], in0=pt.t[:, gi, :],
                                                                                    in1=dstrip.t[:, m[1]:m[1] + 512], op=ALU.mult),
                                             reads=[pt, dstrip], writes=[pt])
                        fin = (sl["row0"], jq) if g0 + 2 >= total else None
                        pending.append((grp, st, pt, po, g0, total, fin))
                        if len(pending) > SKEW:
                            emit_pv(pending.pop(0))
                if nset == 1:
                    while pending:
                        emit_pv(pending.pop(0))
                    if i + 1 < len(slots):
                        load_slot(i + 1)
            while pending:
                emit_pv(pending.pop(0))
            self.end_phase()

    def attn_mla(self, l, S):
        NT = S // 128
        slots = []
        for h in range(6):
            slots.append(dict(row0=h * 64, natab=None,
                              streams=[dict(qt=self.QTm[h, :, 0:S], kt=self.KTm[h, :, 0:S], v=self.Vm[0:S, h * 64:(h + 1) * 64], d=96)],
                              chunks=(lambda jq: [(0, c, []) for c in range(NT)])))
        self.attention(S, slots, l)

    def attn_na(self, l, S):
        NT = S // 128
        NQB = S // 512
        I = self.inp

        def chunks(jq):
            ty = 0 if jq == 0 else (2 if jq == NQB - 1 else 1)
            out = []
            for ci in range(8):
                c = 4 * jq - 2 + ci
                if 0 <= c < NT:
                    m0 = 11 - 2 * ci
                    out.append((0, c, [("nat", (m0 + 3) * 64), ("row", ty * 8 + ci)]))
            return out
        slots = []
        for h in range(6):
            slots.append(dict(row0=384 + h * 64, natab=I["natab"][l, h],
                              streams=[dict(qt=self.QTn[h * 64:(h + 1) * 64, 0:S], kt=self.KTn[h * 64:(h + 1) * 64, 0:S],
                                            v=self.Vn[0:S, h * 64:(h + 1) * 64], d=64)], chunks=chunks))
        self.attention(S, slots, l)

    def attn_dil(self, l, S):
        NT = S // 128

        def chunks(jq):
            out = []
            for g in range(3):
                for Dd in range(DIL_DMIN[g], DIL_DMAX[g] + 1, 128):
                    c = (jq * 512 + Dd) // 128
                    if 0 <= c < NT:
                        out.append((g, c, [("dil", DIL_OFF[g] + DIL_DMAX[g] - Dd)]))
            return out
        slots = []
        for h in range(4):
            sts = []
            for g in range(3):
                hh = g * 4 + h
                sts.append(dict(qt=self.QTd[hh * 64:(hh + 1) * 64, 0:S], kt=self.KTd[hh * 64:(hh + 1) * 64, 0:S],
                                v=self.Vd[0:S, hh * 64:(hh + 1) * 64], d=64))
            slots.append(dict(row0=768 + h * 64, natab=None, streams=sts, chunks=chunks))
        self.attention(S, slots, l)

    def phase_x(self, l, S, xsrc, mem):
        T = self.T
        nc = self.nc
        I = self.inp
        NT = S // 128
        GO = G_CROSS
        self.begin_phase()
        with ExitStack() as ctx:
            self.make_cbias(ctx, [EPS, 256 * EPS])
            wo, wob = self.load_w(ctx, "wo", I["w_o"][l], D, D)
            wcq, wcqb = self.load_w(ctx, "wcq", I["w_cq"][l], D, D)
            wco, wcob = self.load_w(ctx, "wco", I["w_co"][l], D, D)
            wkv, wkvb = self.load_w(ctx, "wkv", I["w_ckv"][l], D, 2 * D)
            gv = self.load_rep(ctx, "gv2", I["gvec"][l, GO:NG], NG - GO)
            g_cross, g_xq, g_ffn, g_mem, g_xk = 0, G_XQ - GO, G_FFN - GO, G_MEM - GO, G_XK - GO
            pA = T.ps(ctx, "pA", [128, 1024], F32)
            pC = T.ps(ctx, "pC", [128, 1024], F32)
            pD = T.ps(ctx, "pD", [128, 1024], F32)
            pT = T.ps(ctx, "pT", [128, 1024], BF16)
            pZ = T.ps(ctx, "pZ", [128, 512], F32)
            xts = [T.sb(ctx, "xt", [128, D], F32) for _ in range(3)]
            xds = [self.newds() for _ in range(3)]
            ots = [T.sb(ctx, "oT", [128, 8, 128], BF16) for _ in range(2)]
            otds = [self.newds() for _ in range(2)]
            junk = T.sb(ctx, "junk", [128, D], BF16)
            ss = Rot([T.sb(ctx, "ss", [128, 8], F32) for _ in range(6)])
            rs = Rot([T.sb(ctx, "rs", [128, 8], F32) for _ in range(6)])
            hb = Rot([T.sb(ctx, "hb", [128, D], BF16) for _ in range(4)])
            hT = Rot([T.sb(ctx, "hT", [128, 8, 128], BF16) for _ in range(3)])
            sqb = T.sb(ctx, "sqb", [128, D], F32)
            yf = T.sb(ctx, "yf", [128, D], F32)
            kmT = T.sb(ctx, "kmT", [128, 8, 256], BF16)
            vms = T.sb(ctx, "vms", [128, 2, D], BF16)
            ptb = Rot([T.sb(ctx, "ptb", [128, 8, 128], BF16) for _ in range(2)])
            rzb = T.sb(ctx, "rzb", [128, 512], F32)
            ocT = Rot([T.sb(ctx, "ocT", [128, 8, 128], BF16) for _ in range(2)])
            h3s = [T.sb(ctx, "h3s", [128, 8, 128], BF16) for _ in range(2)]
            h3ds = [self.newds() for _ in range(2)]
            zt = T.sb(ctx, "zt", [128, 8, 1], BF16)
            zds = self.newds()
            H3v = self.H3T.rearrange("(c p) t -> p c t", p=128)
            OTv = self.OT.rearrange("(c p) t -> p c t", p=128)

            def rmsnorm_T(xb, goff, hT_dst):
                ssb = ss.next()
                T.op(T.act, lambda: nc.scalar.activation(out=junk.t[:], in_=xb.t[:], func=AF.Square, accum_out=ssb.t[:, 0:1]),
                     reads=[xb], writes=[junk, ssb])
                rb = rs.next()
                self.rms_rstd(ssb, rb, 1, 1.0 / D, EPS)
                h = hb.next()
                T.op(T.dve, lambda: nc.vector.scalar_tensor_tensor(out=h.t[:], in0=xb.t[:], scalar=rb.t[:, 0:1],
                                                                   in1=gv.t[:, goff:goff + D], op0=ALU.mult, op1=ALU.mult),
                     reads=[xb, rb, gv], writes=[h])
                self.tr8(h, pT, hT_dst)

            def proj2(dst_ps, lhs_b, w, wb, coff=0):
                def mm():
                    ins = None
                    for nh in range(2):
                        for c in range(8):
                            ins = nc.tensor.matmul(dst_ps.t[:, nh * 512:(nh + 1) * 512], lhsT=lhs_b.t[:, c, :],
                                                   rhs=w[:, c, coff + nh * 512:coff + (nh + 1) * 512], start=(c == 0), stop=(c == 7))
                    return ins
                T.op(T.pe, mm, reads=[lhs_b] + wb, writes=[dst_ps])

            def headnorm4(src_ps, goff, sc, bias, out_b):
                T.op(T.act, lambda: nc.scalar.activation(out=sqb.t[:], in_=src_ps.t[:], func=AF.Square), reads=[src_ps], writes=[sqb])
                ssb = ss.next()
                T.op(T.dve, lambda: nc.vector.tensor_reduce(out=ssb.t[:, 0:4], in_=sqb.t[:].rearrange("p (h d) -> p h d", d=256),
                                                            axis=AX.X, op=ALU.add), reads=[sqb], writes=[ssb])
                rb = rs.next()
                self.rms_rstd(ssb, rb, 4, sc, bias)
                T.op(T.dve, lambda: nc.vector.tensor_tensor(out=yf.t[:].rearrange("p (h d) -> p h d", d=256),
                                                            in0=src_ps.t[:].rearrange("p (h d) -> p h d", d=256),
                                                            in1=rb.t[:, 0:4].unsqueeze(2).to_broadcast([128, 4, 256]), op=ALU.mult),
                     reads=[src_ps, rb], writes=[yf])
                T.op(T.dve, lambda: nc.vector.tensor_tensor(out=out_b.t[:].rearrange("p (h d) -> p h d", d=256),
                                                            in0=yf.t[:].rearrange("p (h d) -> p h d", d=256),
                                                            in1=gv.t[:, goff:goff + 256].unsqueeze(1).to_broadcast([128, 4, 256]), op=ALU.mult),
                     reads=[yf, gv], writes=[out_b])

            for mt in range(2):
                xb = xts[mt]
                T.dma(T.sp, xb.t[:], mem[mt * 128:(mt + 1) * 128, :], xds[mt], writes=[xb])
                mT = hT.next()
                rmsnorm_T(xb, g_mem, mT)
                proj2(pA, mT, wkv, wkvb, 0)
                kn = hb.next()
                headnorm4(pA, g_xk, 1.0 / 256, EPS, kn)

                def trk():
                    ins = None
                    for c in range(8):
                        ins = nc.tensor.transpose(out=pT.t[:, c * 128:(c + 1) * 128], in_=kn.t[:, c * 128:(c + 1) * 128],
                                                  identity=self.ident.t[:])
                    return ins
                T.op(T.pe, trk, reads=[kn, self.ident], writes=[pT])
                T.op(T.act, lambda: nc.scalar.copy(out=kmT.t[:, :, mt * 128:(mt + 1) * 128],
                                                   in_=pT.t[:].rearrange("p (c t) -> p c t", t=128)), reads=[pT], writes=[kmT])
                proj2(pC, mT, wkv, wkvb, D)
                T.op(T.act, lambda: nc.scalar.copy(out=vms.t[:, mt, :], in_=pC.t[:]), reads=[pC], writes=[vms])

            def loads(t):
                xb = xts[t % 3]
                T.dma(T.sp, xb.t[:], xsrc[t * 128:(t + 1) * 128, :], xds[t % 3], writes=[xb])
                ob = ots[t % 2]
                T.dma(T.sp, ob.t[:], OTv[:, :, t * 128:(t + 1) * 128], otds[t % 2], writes=[ob])

            qcTs = [T.sb(ctx, "qcT", [128, 8, 128], BF16) for _ in range(2)]

            def genA(t):
                xb = xts[t % 3]
                ob = ots[t % 2]
                proj2(pA, ob, wo, wob)
                T.op(T.dve, lambda: nc.vector.tensor_tensor(out=xb.t[:], in0=pA.t[:], in1=xb.t[:], op=ALU.add), reads=[pA, xb], writes=[xb])
                yield
                h2T = hT.next()
                rmsnorm_T(xb, g_cross, h2T)
                yield
                proj2(pA, h2T, wcq, wcqb)
                qn = hb.next()
                headnorm4(pA, g_xq, 1.0, 256 * EPS, qn)
                yield
                self.tr8(qn, pT, qcTs[t % 2])

            def genB(t):
                xb = xts[t % 3]
                qcT = qcTs[t % 2]

                def sc_mm():
                    ins = None
                    for hh in range(4):
                        for kc in range(2):
                            o_ = pC.t[:, (hh * 2 + kc) * 128:(hh * 2 + kc + 1) * 128]
                            for dc in range(2):
                                ins = nc.tensor.matmul(o_, lhsT=kmT.t[:, hh * 2 + dc, kc * 128:(kc + 1) * 128], rhs=qcT.t[:, hh * 2 + dc, :],
                                                       start=(dc == 0), stop=(dc == 1))
                    return ins
                T.op(T.pe, sc_mm, reads=[kmT, qcT], writes=[pC])
                pt = ptb.next()
                T.op(T.act, lambda: nc.scalar.activation(out=pt.t[:].rearrange("p c q -> p (c q)"), in_=pC.t[:], func=AF.Exp),
                     reads=[pC], writes=[pt])
                yield

                def pv_mm():
                    ins = None
                    for hh in range(4):
                        for kc in range(2):
                            ins = nc.tensor.matmul(pZ.t[:, hh * 128:(hh + 1) * 128], lhsT=self.ones.t[:], rhs=pt.t[:, hh * 2 + kc, :],
                                                   start=(kc == 0), stop=(kc == 1))
                        for dvc in range(2):
                            for kc in range(2):
                                ins = nc.tensor.matmul(pD.t[:, (hh * 2 + dvc) * 128:(hh * 2 + dvc + 1) * 128],
                                                       lhsT=vms.t[:, kc, hh * 256 + dvc * 128:hh * 256 + (dvc + 1) * 128],
                                                       rhs=pt.t[:, hh * 2 + kc, :], start=(kc == 0), stop=(kc == 1))
                    return ins
                T.op(T.pe, pv_mm, reads=[pt, vms, self.ones], writes=[pZ, pD])
                T.op(T.dve, lambda: nc.vector.reciprocal(out=rzb.t[:], in_=pZ.t[:]), reads=[pZ], writes=[rzb])
                oc = ocT.next()
                T.op(T.dve, lambda: nc.vector.tensor_tensor(
                    out=oc.t[:].rearrange("p (h e) q -> p h e q", e=2),
                    in0=pD.t[:].rearrange("p (h e q) -> p h e q", e=2, q=128),
                    in1=rzb.t[:].rearrange("p (h q) -> p h q", q=128).unsqueeze(2).to_broadcast([128, 4, 2, 128]), op=ALU.mult),
                    reads=[pD, rzb], writes=[oc])
                yield
                proj2(pC, oc, wco, wcob)
                T.op(T.dve, lambda: nc.vector.tensor_tensor(out=xb.t[:], in0=pC.t[:], in1=xb.t[:], op=ALU.add), reads=[pC, xb], writes=[xb])
                T.dma(T.sp, self.X[t * 128:(t + 1) * 128, :], xb.t[:], xds[t % 3], reads=[xb])
                yield
                h3 = h3s[t % 2]
                rmsnorm_T(xb, g_ffn, h3)
                T.dma(T.sp, H3v[:, :, 1 + t * 128:1 + (t + 1) * 128], h3.t[:], h3ds[t % 2], reads=[h3])

            loads(0)
            if NT > 1:
                loads(1)
            for _ in genA(0):
                pass
            for t in range(NT):
                if t + 2 < NT:
                    loads(t + 2)
                gens = [genB(t)]
                if t + 1 < NT:
                    gens.insert(0, genA(t + 1))
                while gens:
                    for g in list(gens):
                        try:
                            next(g)
                        except StopIteration:
                            gens.remove(g)
            self.end_phase()

    def tr8(self, src_b, pT, dst_b):
        T = self.T
        nc = self.nc

        def tr():
            ins = None
            for c in range(8):
                ins = nc.tensor.transpose(out=pT.t[:, c * 128:(c + 1) * 128], in_=src_b.t[:, c * 128:(c + 1) * 128],
                                          identity=self.ident.t[:])
            return ins
        T.op(T.pe, tr, reads=[src_b, self.ident], writes=[pT])
        T.op(T.act, lambda: nc.scalar.copy(out=dst_b.t[:].rearrange("p c t -> p (c t)"), in_=pT.t[:]), reads=[pT], writes=[dst_b])

    def ffn_up(self, l, S):
        T = self.T
        nc = self.nc
        I = self.inp
        nblk = (S + 509) // 510
        blocks = []
        for b in range(nblk):
            c0 = 510 * b
            w = min(512, S + 2 - c0)
            blocks.append((c0, w))
        passes = [blocks[i:i + 9] for i in range(0, nblk, 9)]
        H3v = self.H3T.rearrange("(c p) t -> p c t", p=128)
        WUv = I["w_up"][l].rearrange("(c p) n -> p c n", p=128)
        self.begin_phase()
        with ExitStack() as ctx:
            cp = T.sb(ctx, "convp", [128, 4, 44], F32)
            T.dma(T.sp, cp.t[:], I["convp"][l], self.newds(), writes=[cp])
            wab = [T.sb(ctx, "wab", [128, 8, 256], BF16) for _ in range(2)]
            wds = [self.newds("pool") for _ in range(2)]
            maxc = max(pb[-1][0] + pb[-1][1] - pb[0][0] for pb in passes)
            hres = T.sb(ctx, "hres", [128, 8, maxc], BF16)
            hds = self.newds()
            pU = Rot([T.ps(ctx, "pU", [128, 512], F32) for _ in range(6)])
            ca = Rot([T.sb(ctx, "ca", [128, 512], F32) for _ in range(2)])
            cg = Rot([T.sb(ctx, "cg", [128, 512], F32) for _ in range(2)])
            sg = Rot([T.sb(ctx, "sg", [128, 512], F32) for _ in range(2)])
            ast = [T.sb(ctx, "ast", [128, 512], BF16) for _ in range(3)]
            ads = [self.newds() for _ in range(3)]
            ai = 0

            def load_wab(i):
                b = wab[i % 2]
                T.dma(T.pool, b.t[:, :, 0:128], WUv[:, :, i * 128:(i + 1) * 128], wds[i % 2], writes=[b])
                T.dma(T.pool, b.t[:, :, 128:256], WUv[:, :, DFF + i * 128:DFF + (i + 1) * 128], wds[i % 2], writes=[b])

            for pb in passes:
                col_lo = pb[0][0]
                col_hi = pb[-1][0] + pb[-1][1]
                v_lo = max(col_lo, 1)
                v_hi = min(col_hi, S + 1)
                T.dma(T.sp, hres.t[:, :, v_lo - col_lo:v_hi - col_lo], H3v[:, :, v_lo:v_hi], hds, writes=[hres])
                if col_lo == 0:
                    T.op(T.pool, lambda: nc.gpsimd.memset(hres.t[:, :, 0:1], 0.0), writes=[hres])
                if col_hi == S + 2:
                    T.op(T.pool, lambda: nc.gpsimd.memset(hres.t[:, :, col_hi - col_lo - 1:col_hi - col_lo], 0.0), writes=[hres])
                load_wab(0)
                for i in range(22):
                    if i + 1 < 22:
                        load_wab(i + 1)
                    wb = wab[i % 2]
                    for (c0, w) in pb:
                        off = c0 - col_lo
                        ua = pU.next()
                        ug = pU.next()

                        def mm():
                            ins = None
                            for (dst, wo_) in ((ua, 0), (ug, 128)):
                                for c in range(8):
                                    ins = nc.tensor.matmul(dst.t[:, 0:w], lhsT=wb.t[:, c, wo_:wo_ + 128], rhs=hres.t[:, c, off:off + w],
                                                           start=(c == 0), stop=(c == 7))
                            return ins
                        T.op(T.pe, mm, reads=[wb, hres], writes=[ua, ug])
                        n = w - 2
                        outs = []
                        for (u, col, dstr) in ((ua, i, ca), (ug, 22 + i, cg)):
                            cb_ = dstr.next()
                            if n >= 256:
                                T.op(T.act, lambda: nc.scalar.activation(out=cb_.t[:, 0:n], in_=u.t[:, 0:n], func=AF.Identity,
                                                                         scale=cp.t[:, 0, col:col + 1], bias=cp.t[:, 3, col:col + 1]),
                                     reads=[u, cp], writes=[cb_])
                            else:
                                T.op(T.dve, lambda: nc.vector.tensor_scalar(out=cb_.t[:, 0:n], in0=u.t[:, 0:n], scalar1=cp.t[:, 0, col:col + 1],
                                                                            scalar2=cp.t[:, 3, col:col + 1], op0=ALU.mult, op1=ALU.add),
                                     reads=[u, cp], writes=[cb_])
                            T.op(T.dve, lambda: nc.vector.scalar_tensor_tensor(out=cb_.t[:, 0:n], in0=u.t[:, 1:n + 1], scalar=cp.t[:, 1, col:col + 1],
                                                                               in1=cb_.t[:, 0:n], op0=ALU.mult, op1=ALU.add),
                                 reads=[u, cp, cb_], writes=[cb_])
                            T.op(T.dve, lambda: nc.vector.scalar_tensor_tensor(out=cb_.t[:, 0:n], in0=u.t[:, 2:n + 2], scalar=cp.t[:, 2, col:col + 1],
                                                                               in1=cb_.t[:, 0:n], op0=ALU.mult, op1=ALU.add),
                                 reads=[u, cp, cb_], writes=[cb_])
                            outs.append(cb_)
                        sgb = sg.next()
                        T.op(T.act, lambda: nc.scalar.activation(out=sgb.t[:, 0:n], in_=outs[1].t[:, 0:n], func=AF.Silu), reads=[outs[1]], writes=[sgb])
                        ab = ast[ai % 3]
                        T.op(T.pool, lambda: nc.gpsimd.tensor_tensor(out=ab.t[:, 0:n], in0=sgb.t[:, 0:n], in1=outs[0].t[:, 0:n], op=ALU.mult),
                             reads=[sgb, outs[0]], writes=[ab])
                        T.dma(T.sp, self.ACTT[i * 128:(i + 1) * 128, c0:c0 + n], ab.t[:, 0:n], ads[ai % 3], reads=[ab])
                        ai += 1
            self.end_phase()

    def ffn_down(self, l, S, dst):
        T = self.T
        nc = self.nc
        I = self.inp
        NT = S // 128
        AV = self.ACTT.rearrange("(c p) t -> p c t", p=128)
        self.begin_phase()
        with ExitStack() as ctx:
            wd, wdb = self.load_w(ctx, "wd", I["w_down"][l], DFF, D)
            pY = Rot([T.ps(ctx, "pY", [128, 1024], F32) for _ in range(2)])
            xts = [T.sb(ctx, "xt", [128, D], F32) for _ in range(3)]
            xds = [self.newds() for _ in range(3)]
            ats = [T.sb(ctx, "aT", [128, 22, 128], BF16) for _ in range(2)]
            atds = [self.newds() for _ in range(2)]

            def loads(t):
                T.dma(T.sp, xts[t % 3].t[:], self.X[t * 128:(t + 1) * 128, :], xds[t % 3], writes=[xts[t % 3]])
                T.dma(T.sp, ats[t % 2].t[:], AV[:, :, t * 128:(t + 1) * 128], atds[t % 2], writes=[ats[t % 2]])
            loads(0)
            for t in range(NT):
                if t + 1 < NT:
                    loads(t + 1)
                xb = xts[t % 3]
                ab = ats[t % 2]
                py = pY.next()

                def mm():
                    ins = None
                    for nh in range(2):
                        for c in range(22):
                            ins = nc.tensor.matmul(py.t[:, nh * 512:(nh + 1) * 512], lhsT=ab.t[:, c, :], rhs=wd[:, c, nh * 512:(nh + 1) * 512],
                                                   start=(c == 0), stop=(c == 21))
                    return ins
                T.op(T.pe, mm, reads=[ab] + wdb, writes=[py])
                T.op(T.dve, lambda: nc.vector.tensor_tensor(out=xb.t[:], in0=py.t[:], in1=xb.t[:], op=ALU.add), reads=[py, xb], writes=[xb])
                T.dma(T.sp, dst[t * 128:(t + 1) * 128, :], xb.t[:], xds[t % 3], reads=[xb])
            self.end_phase()


def Buf_view(b):
    return b


def _rope_tab(half):
    pos = np.arange(SS_, dtype=np.float32)
    inv = (np.float32(10000.0) ** (-np.arange(half, dtype=np.float32) / np.float32(half))).astype(np.float32)
    ang = (pos[:, None] * inv[None, :]).astype(np.float32)
    c = np.cos(ang).astype(np.float32).reshape(64, 128, half).transpose(1, 0, 2)
    s = np.sin(ang).astype(np.float32).reshape(64, 128, half).transpose(1, 0, 2)
    return np.ascontiguousarray(c), np.ascontiguousarray(s)


def _dil_strips():
    out = np.zeros((128, DIL_W), np.float32)
    p = np.arange(128)[:, None]
    for g in range(3):
        r = DIL_R[g]
        w = DIL_DMAX[g] - DIL_DMIN[g] + 512
        x = np.arange(w)[None, :]
        delta = p - x + DIL_DMAX[g]
        ok = (delta % r == 0) & (np.abs(delta) <= 64 * r)
        out[:, DIL_OFF[g]:DIL_OFF[g] + w] = np.where(ok, 1.0, 0.0)
    return out


def _na_rowmask():
    R = 1000
    out = np.full((2, 3 * 8 * 8), NEG, np.float32)
    for ty, r0 in ((0, 0), (1, 496), (2, R - 8)):
        for ci in range(8):
            kr0 = r0 - 4 + 2 * ci
            for a in range(2):
                kr = kr0 + a
                for j in range(8):
                    r = r0 + j
                    rs = min(max(r - 4, 0), R - 8)
                    if rs <= kr < rs + 8:
                        out[a, (ty * 8 + ci) * 8 + j] = 0.0
    return out


def _na_table(rpb):
    Lh = rpb.shape[0]
    kc = np.arange(64)[:, None]
    c = np.arange(64)[None, :]
    cs = np.clip(c - 8, 0, 48)
    mcol = (kc >= cs) & (kc < cs + 16)
    dcidx = np.clip(kc - c + 15, 0, 30)
    out = np.full((Lh, 6, 128, NA_MM, 64), NEG, np.float32)
    for a in range(2):
        for mm in range(-3, NA_MM - 3):
            m = mm - a
            if 0 <= m <= 14:
                vals = rpb[:, :, 14 - m, :][:, :, dcidx]
                out[:, :, a * 64:(a + 1) * 64, mm + 3, :] = np.where(mcol[None, None], vals, np.float32(NEG))
    return np.ascontiguousarray(out.reshape(Lh, 6, 128, NA_MM * 64))


def prep_inputs(inputs, n_cores=8):
    f = lambda a: np.ascontiguousarray(np.asarray(a, dtype=np.float32))
    gv = np.concatenate([f(inputs[k]) for k in ("norm_mix", "mla_q_norm", "mla_kv_norm", "mla_qn", "mla_kn", "na_qn", "na_kn",
                                                "dil_qn", "dil_kn", "norm_cross", "x_qn", "norm_ffn", "norm_mem", "x_kn")], axis=1)
    assert gv.shape == (L, NG)
    cw = f(inputs["conv_w"])
    cbv = f(inputs["conv_b"])
    convp = np.concatenate([cw, cbv[:, None, :]], axis=1).reshape(L, 4, 44, 128).transpose(0, 3, 1, 2)
    cosd, sind = _rope_tab(32)
    cosm, sinm = _rope_tab(16)
    ea = np.zeros((2, 128), np.float32)
    ea[0, :64] = 1.0
    ea[1, 64:] = 1.0
    shared = {
        "gvec": np.ascontiguousarray(gv), "convp": np.ascontiguousarray(convp), "natab": _na_table(f(inputs["na_rpb"])),
        "ident": np.eye(128, dtype=np.float32), "cosd": cosd, "sind": sind, "cosm": cosm, "sinm": sinm,
        "dstrip": _dil_strips(), "rsmall": _na_rowmask(), "ea": ea,
    }
    for k in ("w_in", "w_uq", "w_uk", "w_uv", "w_o", "w_cq", "w_ckv", "w_co", "w_up", "w_down"):
        shared[k] = f(inputs[k])
    xp = f(inputs["x_prompt"])
    xs = f(inputs["x_sample"])
    mp = f(inputs["mem_prompt"])
    ms = f(inputs["mem_sample"])
    zs = np.zeros((SS_, D), np.float32)
    zm = np.zeros((MEM, D), np.float32)
    maps = []
    for c in range(n_cores):
        m = dict(shared)
        m["xp"] = xp[c]
        m["memp"] = mp[c]
        if c == 0:
            m["xs"], m["mems"] = xs[0], ms[0]
        elif c == 4:
            m["xs"], m["mems"] = xs[1], ms[1]
        else:
            m["xs"], m["mems"] = zs, zm
        maps.append(m)
    return maps


def kernel(**inputs):
    cfg = {"parts": [("p", SP_), ("s", SS_)]}
    nc = Prog(cfg).build()
    maps = prep_inputs(inputs)
    res = run_bass_kernel_spmd(nc, maps, core_ids=list(range(8)))
    yp = np.stack([np.asarray(res.results[c]["yp"], dtype=np.float32) for c in range(8)], axis=0)
    ys = np.stack([np.asarray(res.results[c]["ys"], dtype=np.float32) for c in (0, 4)], axis=0)
    return (yp, ys)
```

```python
import numpy as np
import concourse.bass as bass
import concourse.mybir as mybir
from concourse.bass_utils import run_bass_kernel_spmd
from contextlib import ExitStack

F32 = mybir.dt.float32
BF16 = mybir.dt.bfloat16
AF = mybir.ActivationFunctionType
ALU = mybir.AluOpType
AX = mybir.AxisListType

D = 1024
L = 4
EPS = 1e-6
NEG = -30000.0
D_IN = 3872
DFF = 2816
SP_ = 2048
SS_ = 8192
MEM = 256
G_MIX, G_CQ, G_CKV, G_MQN, G_MKN, G_NQ, G_NK, G_DQ, G_DK, G_CROSS, G_XQ, G_FFN, G_MEM, G_XK = (
    0, 1024, 1280, 1408, 1504, 1600, 1664, 1728, 1792, 1856, 2880, 3136, 4160, 5184)
NG = 5440
DIL_R = (1, 4, 16)
DIL_DMIN = (-128, -256, -1024)
DIL_DMAX = (512, 640, 1408)
DIL_OFF = (0, 1152, 2560)
DIL_W = 5504
NA_MM = 22


class Ev:
    __slots__ = ("sem", "val", "eng", "ds")

    def __init__(self, sem, val, eng, ds=None):
        self.sem = sem
        self.val = val
        self.eng = eng
        self.ds = ds


class Buf:
    def __init__(self, name, t=None):
        self.name = name
        self.t = t
        self.w = []
        self.r = {}
        self.rd = []


class DS:
    def __init__(self, sem):
        self.sem = sem
        self.cnt = 0


class Eng:
    def __init__(self, name, h, pe=False):
        self.name = name
        self.h = h
        self.pe = pe
        self.sem = None
        self.count = 0
        self.waited = {}
        self.nsem = 0


class Tracker:
    EPOCH = 1 << 20

    def __init__(self, nc, es):
        self.nc = nc
        self.es = es
        self.uid = 0
        self.pe = Eng("pe", nc.tensor, True)
        self.act = Eng("act", nc.scalar)
        self.dve = Eng("dve", nc.vector)
        self.pool = Eng("pool", nc.gpsimd)
        self.sp = Eng("sp", nc.sync)
        self.engs = [self.pe, self.act, self.dve, self.pool, self.sp]
        for e in self.engs:
            e.sem = self.newsem(e.name + "_s0")
        self.free_ds = {}
        self.all_ds = []
        self.bar = DS(self.newsem("bar"))
        self.bar.kind = "sp"
        self.ninstr = 0

    def newsem(self, name):
        return self.es.enter_context(self.nc.semaphore(name))

    def get_ds(self, kind="sp"):
        fl = self.free_ds.setdefault(kind, [])
        if fl:
            return fl.pop()
        ds = DS(self.newsem("ds%s%d" % (kind, len(self.all_ds))))
        ds.kind = kind
        self.all_ds.append(ds)
        return ds

    def sb(self, ctx, name, shape, dt):
        self.uid += 1
        t = ctx.enter_context(self.nc.sbuf_tensor("%s_%d" % (name, self.uid), list(shape), dt))
        return Buf(name, t)

    def ps(self, ctx, name, shape, dt):
        self.uid += 1
        t = ctx.enter_context(self.nc.psum_tensor("%s_%d" % (name, self.uid), list(shape), dt))
        return Buf(name, t)

    def wait(self, eng, ev):
        if ev.eng is eng and eng.pe:
            return
        k = id(ev.sem)
        if eng.waited.get(k, 0) >= ev.val:
            return
        val = ev.ds.cnt if ev.ds is not None else ev.val
        eng.h.wait_ge(ev.sem, val)
        eng.waited[k] = val

    def deps(self, eng, reads, writes):
        for b in reads:
            for ev in b.w:
                self.wait(eng, ev)
        for b in writes:
            for ev in b.w:
                self.wait(eng, ev)
            for ev in b.r.values():
                self.wait(eng, ev)
            for ev in b.rd:
                self.wait(eng, ev)

    def done(self, ev, reads, writes):
        for b in reads:
            if ev.eng is None:
                b.rd.append(ev)
            else:
                b.r[ev.eng.name] = ev
        for b in writes:
            b.w = [ev]
            b.r = {}
            b.rd = []

    def op(self, eng, fn, reads=(), writes=()):
        self.deps(eng, reads, writes)
        ins = fn()
        eng.count += 1
        ins.then_inc(eng.sem, 1)
        self.ninstr += 1
        ev = Ev(eng.sem, eng.count, eng)
        self.done(ev, reads, writes)
        if eng.count >= self.EPOCH:
            eng.nsem += 1
            eng.sem = self.newsem("%s_s%d" % (eng.name, eng.nsem))
            eng.count = 0
        return ev

    def dma(self, q, out, in_, ds, reads=(), writes=()):
        assert ds.kind == ("pool" if q is self.pool else "sp"), (ds.kind, q.name)
        self.deps(q, reads, writes)
        ins = q.h.dma_start(out=out, in_=in_)
        ds.cnt += 16
        ins.then_inc(ds.sem, 16)
        self.ninstr += 1
        ev = Ev(ds.sem, ds.cnt, None, ds)
        self.done(ev, reads, writes)
        return ev

    def barrier(self, dummy_src, dummy_dst):
        sp = self.sp
        for e in self.engs:
            if e is not sp and e.count > 0:
                self.wait(sp, Ev(e.sem, e.count, e))
        for ds in self.all_ds:
            if ds.cnt > 0:
                self.wait(sp, Ev(ds.sem, ds.cnt, None, ds))
        ins = sp.h.dma_start(out=dummy_dst, in_=dummy_src)
        self.bar.cnt += 16
        ins.then_inc(self.bar.sem, 16)
        for e in self.engs:
            e.h.wait_ge(self.bar.sem, self.bar.cnt)


class Rot:
    def __init__(self, items):
        self.items = items
        self.i = 0

    def next(self):
        b = self.items[self.i % len(self.items)]
        self.i += 1
        return b


class Prog:
    def __init__(self, cfg):
        self.cfg = cfg
        self.nc = bass.Bass("TRN2", target_bir_lowering=False)
        self.es = ExitStack()

    def din(self, name, shape, dt=F32):
        return self.nc.dram_tensor(name, list(shape), dt, kind="ExternalInput").ap()

    def dscr(self, name, shape, dt, dbg=False):
        kind = "ExternalOutput" if (dbg and self.cfg.get("debug")) else "Internal"
        return self.nc.dram_tensor(name, list(shape), dt, kind=kind).ap()

    def build(self):
        nc = self.nc
        cfg = self.cfg
        parts = cfg["parts"]
        self.inp = {}
        I = self.inp
        I["xp"] = self.din("xp", [SP_, D])
        I["xs"] = self.din("xs", [SS_, D])
        I["memp"] = self.din("memp", [MEM, D])
        I["mems"] = self.din("mems", [MEM, D])
        I["w_in"] = self.din("w_in", [L, D, D_IN])
        I["w_uq"] = self.din("w_uq", [L, 256, 576])
        I["w_uk"] = self.din("w_uk", [L, 128, 384])
        I["w_uv"] = self.din("w_uv", [L, 128, 384])
        I["w_o"] = self.din("w_o", [L, D, D])
        I["w_cq"] = self.din("w_cq", [L, D, D])
        I["w_ckv"] = self.din("w_ckv", [L, D, 2 * D])
        I["w_co"] = self.din("w_co", [L, D, D])
        I["w_up"] = self.din("w_up", [L, D, 2 * DFF])
        I["w_down"] = self.din("w_down", [L, DFF, D])
        I["gvec"] = self.din("gvec", [L, NG])
        I["convp"] = self.din("convp", [L, 128, 4, 44])
        I["natab"] = self.din("natab", [L, 6, 128, NA_MM * 64])
        I["ident"] = self.din("ident", [128, 128])
        I["cosd"] = self.din("cosd", [128, 64, 32])
        I["sind"] = self.din("sind", [128, 64, 32])
        I["cosm"] = self.din("cosm", [128, 64, 16])
        I["sinm"] = self.din("sinm", [128, 64, 16])
        I["dstrip"] = self.din("dstrip", [128, DIL_W])
        I["rsmall"] = self.din("rsmall", [2, 3 * 8 * 8])
        I["ea"] = self.din("ea", [2, 128])
        self.yp = nc.dram_tensor("yp", [SP_, D], F32, kind="ExternalOutput").ap()
        self.ys = nc.dram_tensor("ys", [SS_, D], F32, kind="ExternalOutput").ap()
        SM = max(S for _, S in parts)
        self.SM = SM
        dbg = True
        self.X = self.dscr("X", [SM, D], F32, dbg)
        self.QTn = self.dscr("QTn", [384, SM], BF16, dbg)
        self.KTn = self.dscr("KTn", [384, SM], BF16, dbg)
        self.Vn = self.dscr("Vn", [SM, 384], BF16, dbg)
        self.QTd = self.dscr("QTd", [768, SM], BF16, dbg)
        self.KTd = self.dscr("KTd", [768, SM], BF16, dbg)
        self.Vd = self.dscr("Vd", [SM, 768], BF16, dbg)
        self.QTm = self.dscr("QTm", [6, 96, SM], BF16, dbg)
        self.KTm = self.dscr("KTm", [6, 96, SM], BF16, dbg)
        self.Vm = self.dscr("Vm", [SM, 384], BF16, dbg)
        self.OT = self.dscr("OT", [D, SM], BF16, dbg)
        self.H3T = self.dscr("H3T", [D, SM + 2], BF16, dbg)
        self.ACTT = self.dscr("ACTT", [DFF, SM], BF16, dbg)
        self.dum0 = self.dscr("dum0", [1, 16], F32)
        self.dum1 = self.dscr("dum1", [1, 16], F32)

        with self.es:
            self.T = Tracker(nc, self.es)
            T = self.T
            self.consts()
            for (pname, S) in parts:
                xin = I["xp"] if pname == "p" else I["xs"]
                mem = I["memp"] if pname == "p" else I["mems"]
                yout = self.yp if pname == "p" else self.ys
                nl = cfg.get("layers", L)
                for l in range(nl):
                    last = (l == nl - 1)
                    xsrc = xin if l == 0 else self.X
                    ph = cfg.get("phases", "1nmdxfg")
                    if "1" in ph:
                        self.phase1(l, S, xsrc)
                    if "n" in ph:
                        self.attn_na(l, S)
                    if "d" in ph:
                        self.attn_dil(l, S)
                    if "m" in ph:
                        self.attn_mla(l, S)
                    if "x" in ph:
                        self.phase_x(l, S, xsrc, mem)
                    if "f" in ph:
                        self.ffn_up(l, S)
                    if "g" in ph:
                        self.ffn_down(l, S, yout if (last and not cfg.get("debug")) else self.X)
            T.barrier(self.inp["ident"][0:1, 0:16], self.dum1)
        return nc

    def bar(self):
        self.T.barrier(self.inp["ident"][0:1, 0:16], self.dum1)

    def consts(self):
        T = self.T
        nc = self.nc
        I = self.inp
        es = self.es
        self.ident = T.sb(es, "ident", [128, 128], BF16)
        self.ones = T.sb(es, "ones", [128, 128], BF16)
        self.ea = T.sb(es, "ea", [128, 128], BF16)
        self.rsmall = T.sb(es, "rsmall", [128, 192], BF16)
        T.op(T.pool, lambda: nc.gpsimd.memset(self.ea.t[:], 0.0), writes=[self.ea])
        T.op(T.pool, lambda: nc.gpsimd.memset(self.rsmall.t[:], 0.0), writes=[self.rsmall])
        ds = T.get_ds("pool")
        T.dma(T.pool, self.ident.t[:], I["ident"][:, :], ds, writes=[self.ident])
        ds = T.get_ds("pool")
        T.dma(T.pool, self.ea.t[0:2, :], I["ea"][:, :], ds, writes=[self.ea])
        ds = T.get_ds("pool")
        T.dma(T.pool, self.rsmall.t[0:2, :], I["rsmall"][:, :], ds, writes=[self.rsmall])
        T.op(T.pool, lambda: nc.gpsimd.memset(self.ones.t[:], 1.0), writes=[self.ones])

    def load_w(self, ctx, name, src, K, N, nsplit=1):
        T = self.T
        kc = K // 128
        w = T.sb(ctx, name, [128, kc, N], BF16)
        bufs = []
        step = (N + nsplit - 1) // nsplit
        for c in range(kc):
            b = Buf("%s_c%d" % (name, c), w.t)
            ds = self.newds("pool")
            for n0 in range(0, N, step):
                n1 = min(N, n0 + step)
                T.dma(T.pool, w.t[:, c, n0:n1], src[c * 128:(c + 1) * 128, n0:n1], ds, writes=[b])
            bufs.append(b)
        return w.t, bufs

    def load_rep(self, ctx, name, src1d, n):
        T = self.T
        g = T.sb(ctx, name, [128, n], F32)
        ds = self.newds()
        T.dma(T.sp, g.t[:], src1d.partition_broadcast(128), ds, writes=[g])
        return g

    def begin_phase(self):
        self.phase_ds = []

    def end_phase(self):
        self.bar()
        for ds in self.phase_ds:
            self.T.free_ds.setdefault(ds.kind, []).append(ds)
        self.phase_ds = []

    def newds(self, kind="sp"):
        ds = self.T.get_ds(kind)
        self.phase_ds.append(ds)
        return ds

    def rms_rstd(self, ss, rstd, n, sc, bias):
        T = self.T
        nc = self.nc
        T.op(T.act, lambda: nc.scalar.activation(out=rstd.t[:, 0:n], in_=ss.t[:, 0:n], func=AF.Sqrt,
                                                 bias=self.cbias(bias), scale=sc), reads=[ss, self._cb[float(bias)]], writes=[rstd])
        T.op(T.dve, lambda: nc.vector.reciprocal(out=rstd.t[:, 0:n], in_=rstd.t[:, 0:n]), reads=[rstd], writes=[rstd])

    def cbias(self, v):
        key = float(v)
        if key not in self._cb:
            raise KeyError(key)
        return self._cb[key].t[:, 0:1]

    def make_cbias(self, ctx, vals):
        T = self.T
        nc = self.nc
        self._cb = {}
        for v in vals:
            b = T.sb(ctx, "cb", [128, 1], F32)
            T.op(T.pool, lambda: nc.gpsimd.memset(b.t[:], float(v)), writes=[b])
            self._cb[float(v)] = b
        self._cb_bufs = list(self._cb.values())

    def phase1(self, l, S, xsrc):
        T = self.T
        nc = self.nc
        I = self.inp
        NT = S // 128
        E2 = T.pool
        H2 = nc.gpsimd
        self.begin_phase()
        with ExitStack() as ctx:
            self.make_cbias(ctx, [EPS, 64 * EPS, 96 * EPS])
            win, winb = self.load_w(ctx, "win", I["w_in"][l], D, D_IN, nsplit=2)
            wuq, wuqb = self.load_w(ctx, "wuq", I["w_uq"][l], 256, 576)
            wuk, wukb = self.load_w(ctx, "wuk", I["w_uk"][l], 128, 384)
            wuv, wuvb = self.load_w(ctx, "wuv", I["w_uv"][l], 128, 384)
            gv = self.load_rep(ctx, "gv1", I["gvec"][l, 0:G_CROSS], G_CROSS)
            ropes = [T.sb(ctx, "rope", [128, 96], F32) for _ in range(3)]
            rope_ds = [self.newds() for _ in range(3)]
            xts = [T.sb(ctx, "xt", [128, D], F32) for _ in range(2)]
            xds = [self.newds(), self.newds()]
            junk = T.sb(ctx, "junk", [128, D], BF16)
            ss1 = Rot([T.sb(ctx, "ss1", [128, 8], F32) for _ in range(2)])
            rs1 = Rot([T.sb(ctx, "rs1", [128, 8], F32) for _ in range(2)])
            hb = Rot([T.sb(ctx, "hb", [128, D], BF16) for _ in range(2)])
            hT = Rot([T.sb(ctx, "hT", [128, 8, 128], BF16) for _ in range(2)])
            pT = Rot([T.ps(ctx, "pT", [128, 1024], BF16) for _ in range(1)])
            pz = Rot([T.ps(ctx, "pz", [128, 512], F32) for _ in range(5)])
            pX = Rot([T.ps(ctx, "pX", [128, 1024], BF16) for _ in range(2)])
            sq = Rot([T.sb(ctx, "sq", [128, 576], F32) for _ in range(4)])
            yy = Rot([T.sb(ctx, "yy", [128, 576], F32) for _ in range(4)])
            y2 = Rot([T.sb(ctx, "y2", [128, 576], F32) for _ in range(4)])
            tt = Rot([T.sb(ctx, "tt", [128, 6, 32], F32) for _ in range(12)])
            yb = Rot([T.sb(ctx, "yb", [128, 576], BF16) for _ in range(8)])
            ssh = Rot([T.sb(ctx, "ssh", [128, 8], F32) for _ in range(8)])
            rsh = Rot([T.sb(ctx, "rsh", [128, 8], F32) for _ in range(8)])
            cqn = Rot([T.sb(ctx, "cqn", [128, 384], BF16) for _ in range(2)])
            cT = Rot([T.sb(ctx, "cT", [128, 3, 128], BF16) for _ in range(2)])
            krb = Rot([T.sb(ctx, "krb", [128, 32], F32) for _ in range(2)])
            sskr_r = Rot([T.sb(ctx, "sskr", [128, 1], F32) for _ in range(2)])
            stq = [T.sb(ctx, "stq", [128, 18, 128], BF16) for _ in range(2)]
            stm = [T.sb(ctx, "stm", [96, 12, 128], BF16) for _ in range(2)]
            stq_ds = [self.newds() for _ in range(2)]
            stm_ds = [self.newds() for _ in range(2)]
            vst = [T.sb(ctx, "vst", [128, 1536], BF16) for _ in range(2)]
            vds = [self.newds(), self.newds()]

            dq = []
            DEPTH = 2

            def defer(fn):
                dq.append(fn)
                while len(dq) > DEPTH:
                    dq.pop(0)()

            def load_x(t):
                b = xts[t % 2]
                T.dma(T.sp, b.t[:], xsrc[t * 128:(t + 1) * 128, :], xds[t % 2], writes=[b])
                rp = ropes[t % 3]
                T.dma(T.sp, rp.t[:, 0:32], I["cosd"][:, t, :], rope_ds[t % 3], writes=[rp])
                T.dma(T.sp, rp.t[:, 32:64], I["sind"][:, t, :], rope_ds[t % 3], writes=[rp])
                T.dma(T.sp, rp.t[:, 64:80], I["cosm"][:, t, :], rope_ds[t % 3], writes=[rp])
                T.dma(T.sp, rp.t[:, 80:96], I["sinm"][:, t, :], rope_ds[t % 3], writes=[rp])

            def headnorm_g(pz_b, ncols, H, d, sc, bias, res, extra_ss=None):
                s_ = sq.next()
                T.op(T.act, lambda: nc.scalar.activation(out=s_.t[:, 0:ncols], in_=pz_b.t[:, 0:ncols], func=AF.Square),
                     reads=[pz_b], writes=[s_])
                ssb = ssh.next()
                T.op(T.dve, lambda: nc.vector.tensor_reduce(out=ssb.t[:, 0:H],
                                                            in_=s_.t[:, 0:ncols].rearrange("p (h d) -> p h d", d=d),
                                                            axis=AX.X, op=ALU.add), reads=[s_], writes=[ssb])
                if extra_ss is not None:
                    T.op(T.dve, lambda: nc.vector.tensor_scalar(out=ssb.t[:, 0:H], in0=ssb.t[:, 0:H],
                                                                scalar1=extra_ss.t[:, 0:1], scalar2=None, op0=ALU.add),
                         reads=[ssb, extra_ss], writes=[ssb])
                yield
                rb = rsh.next()
                self.rms_rstd(ssb, rb, H, sc, bias)
                res.append(rb)
                yield

            def tile_jobs(t):
                xt = xts[t % 2]
                rp = ropes[t % 3]
                par = t % 2
                sQ = stq[par]
                sM = stm[par]
                vb = vst[par]
                st = {}

                def head():
                    ssb = ss1.next()
                    T.op(T.act, lambda: nc.scalar.activation(out=junk.t[:], in_=xt.t[:], func=AF.Square,
                                                             accum_out=ssb.t[:, 0:1]), reads=[xt], writes=[junk, ssb])
                    rb = rs1.next()
                    self.rms_rstd(ssb, rb, 1, 1.0 / D, EPS)
                    yield
                    h = hb.next()
                    T.op(T.dve, lambda: nc.vector.scalar_tensor_tensor(out=h.t[:], in0=xt.t[:], scalar=rb.t[:, 0:1],
                                                                       in1=gv.t[:, G_MIX:G_MIX + D], op0=ALU.mult, op1=ALU.mult),
                         reads=[xt, rb, gv], writes=[h])
                    yield
                    hTb = hT.next()
                    self.tr8(h, pT.items[0], hTb)
                    st["hT"] = hTb

                def proj(col0, ncols):
                    z = pz.next()
                    hTb = st["hT"]

                    def mm():
                        ins = None
                        for c in range(8):
                            ins = nc.tensor.matmul(z.t[:, 0:ncols], lhsT=hTb.t[:, c, :], rhs=win[:, c, col0:col0 + ncols],
                                                   start=(c == 0), stop=(c == 7))
                        return ins
                    T.op(T.pe, mm, reads=[hTb] + winb, writes=[z])
                    return z

                def transpose_out(src_b, slabs, width, dst_b, dst_idx0, rows):
                    def now():
                        px = pX.next()

                        def trs():
                            ins = None
                            for s_ in range(slabs):
                                ins = nc.tensor.transpose(out=px.t[0:width, s_ * 128:(s_ + 1) * 128],
                                                          in_=src_b.t[:, s_ * width:(s_ + 1) * width], identity=self.ident.t[:])
                            return ins
                        T.op(T.pe, trs, reads=[src_b, self.ident], writes=[px])
                        T.op(T.act, lambda: nc.scalar.copy(
                            out=dst_b.t[0:rows, dst_idx0:dst_idx0 + slabs, 0:128],
                            in_=px.t[0:rows, 0:slabs * 128].rearrange("p (s t) -> p s t", t=128)),
                            reads=[px], writes=[dst_b])
                    defer(now)

                def rope_g(y_b, H, d, hd, c0, s0, lo0, out_b):
                    yv = y_b.t[:, 0:H * d].rearrange("p (h d) -> p h d", d=d)
                    ov = out_b.t[:, 0:H * d].rearrange("p (h d) -> p h d", d=d)
                    lo = yv[:, :, lo0:lo0 + hd]
                    hi = yv[:, :, lo0 + hd:lo0 + 2 * hd]
                    cs = rp.t[:, c0:c0 + hd].unsqueeze(1).to_broadcast([128, H, hd])
                    sn = rp.t[:, s0:s0 + hd].unsqueeze(1).to_broadcast([128, H, hd])
                    t1, t2, t3, t4 = tt.next(), tt.next(), tt.next(), tt.next()
                    T.op(T.dve, lambda: nc.vector.tensor_tensor(out=t1.t[:, 0:H, 0:hd], in0=lo, in1=cs, op=ALU.mult),
                         reads=[y_b, rp], writes=[t1])
                    T.op(E2, lambda: H2.tensor_tensor(out=t2.t[:, 0:H, 0:hd], in0=hi, in1=sn, op=ALU.mult),
                         reads=[y_b, rp], writes=[t2])
                    T.op(T.dve, lambda: nc.vector.tensor_tensor(out=t3.t[:, 0:H, 0:hd], in0=hi, in1=cs, op=ALU.mult),
                         reads=[y_b, rp], writes=[t3])
                    T.op(E2, lambda: H2.tensor_tensor(out=t4.t[:, 0:H, 0:hd], in0=lo, in1=sn, op=ALU.mult),
                         reads=[y_b, rp], writes=[t4])
                    yield
                    T.op(T.dve, lambda: nc.vector.tensor_tensor(out=ov[:, :, lo0:lo0 + hd], in0=t1.t[:, 0:H, 0:hd],
                                                                in1=t2.t[:, 0:H, 0:hd], op=ALU.subtract),
                         reads=[t1, t2], writes=[out_b])
                    T.op(E2, lambda: H2.tensor_tensor(out=ov[:, :, lo0 + hd:lo0 + 2 * hd], in0=t3.t[:, 0:H, 0:hd],
                                                      in1=t4.t[:, 0:H, 0:hd], op=ALU.add),
                         reads=[t3, t4], writes=[out_b])
                    if lo0 > 0:
                        T.op(E2, lambda: H2.tensor_copy(out=ov[:, :, 0:lo0], in_=yv[:, :, 0:lo0]),
                             reads=[y_b], writes=[out_b])
                    yield

                def qk_block(col0, H, goff, qscale, rope, dst_idx0):
                    d = 64
                    ncols = H * d
                    z = proj(col0, ncols)
                    yield
                    res = []
                    yield from headnorm_g(z, ncols, H, d, 1.0 if qscale else 1.0 / d, d * EPS if qscale else EPS, res)
                    rb_ = res[0]
                    y = yy.next()
                    T.op(T.dve, lambda: nc.vector.tensor_tensor(
                        out=y.t[:, 0:ncols].rearrange("p (h d) -> p h d", d=d),
                        in0=z.t[:, 0:ncols].rearrange("p (h d) -> p h d", d=d),
                        in1=rb_.t[:, 0:H].unsqueeze(2).to_broadcast([128, H, d]), op=ALU.mult),
                        reads=[z, rb_], writes=[y])
                    yield
                    ob = yb.next()
                    gb = gv.t[:, goff:goff + d].unsqueeze(1).to_broadcast([128, H, d])
                    if rope:
                        yg = y2.next()
                        T.op(E2, lambda: H2.tensor_tensor(
                            out=yg.t[:, 0:ncols].rearrange("p (h d) -> p h d", d=d),
                            in0=y.t[:, 0:ncols].rearrange("p (h d) -> p h d", d=d), in1=gb, op=ALU.mult),
                            reads=[y, gv], writes=[yg])
                        yield
                        yield from rope_g(yg, H, d, 32, 0, 32, 0, ob)
                    else:
                        T.op(E2, lambda: H2.tensor_tensor(
                            out=ob.t[:, 0:ncols].rearrange("p (h d) -> p h d", d=d),
                            in0=y.t[:, 0:ncols].rearrange("p (h d) -> p h d", d=d), in1=gb, op=ALU.mult),
                            reads=[y, gv], writes=[ob])
                        yield
                    transpose_out(ob, H // 2, 128, sQ, dst_idx0, 128)

                def v_block(col0, ncols, voff):
                    z = proj(col0, ncols)
                    yield
                    T.op(T.act, lambda: nc.scalar.copy(out=vb.t[:, voff:voff + ncols], in_=z.t[:, 0:ncols]),
                         reads=[z], writes=[vb])

                def mla():
                    z0 = proj(0, 416)
                    yield
                    cq = cqn.next()
                    res = []
                    yield from headnorm_g(z0, 256, 1, 256, 1.0 / 256, EPS, res)
                    rq = res[0]
                    T.op(T.dve, lambda: nc.vector.scalar_tensor_tensor(out=cq.t[:, 0:256], in0=z0.t[:, 0:256], scalar=rq.t[:, 0:1],
                                                                       in1=gv.t[:, G_CQ:G_CQ + 256], op0=ALU.mult, op1=ALU.mult),
                         reads=[z0, rq, gv], writes=[cq])
                    s_ = sq.next()
                    ssk = ssh.next()
                    T.op(T.act, lambda: nc.scalar.activation(out=s_.t[:, 0:128], in_=z0.t[:, 256:384], func=AF.Square,
                                                             accum_out=ssk.t[:, 0:1]), reads=[z0], writes=[s_, ssk])
                    yield
                    rk = rsh.next()
                    self.rms_rstd(ssk, rk, 1, 1.0 / 128, EPS)
                    yield
                    T.op(T.dve, lambda: nc.vector.scalar_tensor_tensor(out=cq.t[:, 256:384], in0=z0.t[:, 256:384], scalar=rk.t[:, 0:1],
                                                                       in1=gv.t[:, G_CKV:G_CKV + 128], op0=ALU.mult, op1=ALU.mult),
                         reads=[z0, rk, gv], writes=[cq])
                    kr = krb.next()
                    s2_ = sq.next()
                    sskr = sskr_r.next()
                    T.op(T.dve, lambda: nc.vector.tensor_copy(out=kr.t[:], in_=z0.t[:, 384:416]), reads=[z0], writes=[kr])
                    T.op(T.dve, lambda: nc.vector.tensor_tensor(out=s2_.t[:, 0:32], in0=kr.t[:], in1=kr.t[:], op=ALU.mult),
                         reads=[kr], writes=[s2_])
                    T.op(T.dve, lambda: nc.vector.tensor_reduce(out=sskr.t[:, 0:1], in_=s2_.t[:, 0:32], axis=AX.X, op=ALU.add),
                         reads=[s2_], writes=[sskr])
                    yield
                    p2 = pT.items[0]

                    def tr3():
                        ins = None
                        for c in range(3):
                            ins = nc.tensor.transpose(out=p2.t[:, c * 128:(c + 1) * 128], in_=cq.t[:, c * 128:(c + 1) * 128],
                                                      identity=self.ident.t[:])
                        return ins
                    T.op(T.pe, tr3, reads=[cq, self.ident], writes=[p2])
                    cTb = cT.next()
                    T.op(T.act, lambda: nc.scalar.copy(out=cTb.t[:].rearrange("p c t -> p (c t)"), in_=p2.t[:, 0:384]),
                         reads=[p2], writes=[cTb])
                    yield
                    for hq in range(2):
                        zq = pz.next()

                        def mmq():
                            ins = None
                            for c in range(2):
                                ins = nc.tensor.matmul(zq.t[:, 0:288], lhsT=cTb.t[:, c, :], rhs=wuq[:, c, hq * 288:(hq + 1) * 288],
                                                       start=(c == 0), stop=(c == 1))
                            return ins
                        T.op(T.pe, mmq, reads=[cTb] + wuqb, writes=[zq])
                        yield
                        res = []
                        yield from headnorm_g(zq, 288, 3, 96, 1.0, 96 * EPS, res)
                        rb_ = res[0]
                        y = yy.next()
                        T.op(T.dve, lambda: nc.vector.tensor_tensor(
                            out=y.t[:, 0:288].rearrange("p (h d) -> p h d", d=96),
                            in0=zq.t[:, 0:288].rearrange("p (h d) -> p h d", d=96),
                            in1=rb_.t[:, 0:3].unsqueeze(2).to_broadcast([128, 3, 96]), op=ALU.mult),
                            reads=[zq, rb_], writes=[y])
                        yield
                        yg = y2.next()
                        T.op(E2, lambda: H2.tensor_tensor(
                            out=yg.t[:, 0:288].rearrange("p (h d) -> p h d", d=96),
                            in0=y.t[:, 0:288].rearrange("p (h d) -> p h d", d=96),
                            in1=gv.t[:, G_MQN:G_MQN + 96].unsqueeze(1).to_broadcast([128, 3, 96]), op=ALU.mult),
                            reads=[y, gv], writes=[yg])
                        yield
                        ob = yb.next()
                        yield from rope_g(yg, 3, 96, 16, 64, 80, 64, ob)
                        transpose_out(ob, 3, 96, sM, hq * 3, 96)
                    zk = pz.next()
                    T.op(T.pe, lambda: nc.tensor.matmul(zk.t[:, 0:384], lhsT=cTb.t[:, 2, :], rhs=wuk[:, 0, :], start=True, stop=True),
                         reads=[cTb] + wukb, writes=[zk])
                    zv = pz.next()
                    T.op(T.pe, lambda: nc.tensor.matmul(zv.t[:, 0:384], lhsT=cTb.t[:, 2, :], rhs=wuv[:, 0, :], start=True, stop=True),
                         reads=[cTb] + wuvb, writes=[zv])
                    yield
                    T.op(T.act, lambda: nc.scalar.copy(out=vb.t[:, 0:384], in_=zv.t[:, 0:384]), reads=[zv], writes=[vb])
                    res = []
                    yield from headnorm_g(zk, 384, 6, 64, 1.0 / 96, EPS, res, extra_ss=sskr)
                    rbk = res[0]
                    yk = yy.next()
                    ykv = yk.t[:, 0:576].rearrange("p (h d) -> p h d", d=96)
                    T.op(T.dve, lambda: nc.vector.tensor_tensor(
                        out=ykv[:, :, 0:64], in0=zk.t[:, 0:384].rearrange("p (h d) -> p h d", d=64),
                        in1=rbk.t[:, 0:6].unsqueeze(2).to_broadcast([128, 6, 64]), op=ALU.mult),
                        reads=[zk, rbk], writes=[yk])
                    T.op(T.dve, lambda: nc.vector.tensor_tensor(
                        out=ykv[:, :, 64:96], in0=kr.t[:, :].unsqueeze(1).to_broadcast([128, 6, 32]),
                        in1=rbk.t[:, 0:6].unsqueeze(2).to_broadcast([128, 6, 32]), op=ALU.mult),
                        reads=[kr, rbk], writes=[yk])
                    yield
                    ykg = y2.next()
                    T.op(E2, lambda: H2.tensor_tensor(
                        out=ykg.t[:, 0:576].rearrange("p (h d) -> p h d", d=96), in0=ykv,
                        in1=gv.t[:, G_MKN:G_MKN + 96].unsqueeze(1).to_broadcast([128, 6, 96]), op=ALU.mult),
                        reads=[yk, gv], writes=[ykg])
                    yield
                    obk = yb.next()
                    yield from rope_g(ykg, 6, 96, 16, 64, 80, 64, obk)
                    transpose_out(obk, 6, 96, sM, 6, 96)

                def stores():
                    r0 = t * 128
                    T.dma(T.sp, self.Vm[r0:r0 + 128, :], vb.t[:, 0:384], vds[par], reads=[vb])
                    T.dma(T.sp, self.Vn[r0:r0 + 128, :], vb.t[:, 384:768], vds[par], reads=[vb])
                    T.dma(T.sp, self.Vd[r0:r0 + 128, :], vb.t[:, 768:1536], vds[par], reads=[vb])
                    for (dst, i0, ns) in ((self.QTn, 0, 3), (self.KTn, 3, 3), (self.QTd, 6, 6), (self.KTd, 12, 6)):
                        T.dma(T.sp, dst.rearrange("(s p) t -> p s t", p=128)[:, :, r0:r0 + 128], sQ.t[:, i0:i0 + ns, 0:128],
                              stq_ds[par], reads=[sQ])
                    T.dma(T.sp, self.QTm[:, :, r0:r0 + 128].rearrange("h d t -> d h t"), sM.t[:, 0:6, 0:128], stm_ds[par], reads=[sM])
                    T.dma(T.sp, self.KTm[:, :, r0:r0 + 128].rearrange("h d t -> d h t"), sM.t[:, 6:12, 0:128], stm_ds[par], reads=[sM])

                blocks = [
                    mla,
                    lambda: qk_block(416, 6, G_NQ, True, False, 0),
                    lambda: qk_block(800, 6, G_NK, False, False, 3),
                    lambda: qk_block(1568, 6, G_DQ, True, True, 6),
                    lambda: v_block(1184, 384, 384),
                    lambda: qk_block(1952, 6, G_DQ, True, True, 9),
                    lambda: qk_block(2336, 6, G_DK, False, True, 12),
                    lambda: v_block(3104, 384, 768),
                    lambda: qk_block(2720, 6, G_DK, False, True, 15),
                    lambda: v_block(3488, 384, 1152),
                ]
                return head, blocks, stores

            W = 3
            active = []

            def pump():
                for g in list(active):
                    try:
                        next(g)
                    except StopIteration:
                        active.remove(g)

            load_x(0)
            head0, blocks0, stores0 = tile_jobs(0)
            for _ in head0():
                pass
            cur = (blocks0, stores0)
            for t in range(NT):
                if t + 1 < NT:
                    load_x(t + 1)
                blocks, stores = cur
                for bf in blocks:
                    active.append(bf())
                    while len(active) >= W:
                        pump()
                if t + 1 < NT:
                    hd, nb, ns = tile_jobs(t + 1)
                    active.append(hd())
                    cur = (nb, ns)
                while active:
                    pump()
                defer(stores)
            while dq:
                dq.pop(0)()
            self.end_phase()

    def attention(self, S, slots, l):
        T = self.T
        nc = self.nc
        I = self.inp
        NT = S // 128
        NQB = S // 512
        nstream = len(slots[0]["streams"])
        nset = 2 if nstream == 1 else 1
        use_dil = any(m[0] == "dil" for sl in slots for (_, _, ms) in sl["chunks"](0) for m in ms)
        dpad = slots[0]["streams"][0]["d"]
        self.begin_phase()
        with ExitStack() as ctx:
            sets = []
            for si in range(nset):
                st = []
                for k in range(nstream):
                    kt = T.sb(ctx, "kt", [128, S], BF16)
                    va = T.sb(ctx, "va", [128, NT, 128], BF16)
                    T.op(T.pool, lambda: nc.gpsimd.memset(va.t[:, :, 64:128], 1.0), writes=[va])
                    T.op(T.pool, lambda: nc.gpsimd.memset(kt.t[dpad:128, :], 0.0), writes=[kt])
                    st.append((kt, va, self.newds(), self.newds()))
                nat = T.sb(ctx, "nat", [128, NA_MM * 64], BF16)
                sets.append((st, nat, self.newds("pool")))
            dstrip = None
            if use_dil:
                dstrip = T.sb(ctx, "dstrip", [128, DIL_W], BF16)
                dsd = self.newds("pool")
                for c0 in range(0, DIL_W, 2048):
                    c1 = min(DIL_W, c0 + 2048)
                    T.dma(T.pool, dstrip.t[:, c0:c1], I["dstrip"][:, c0:c1], dsd, writes=[dstrip])
            qtb = [Rot([T.sb(ctx, "qtb", [128, 512], BF16) for _ in range(3)]) for _ in range(nstream)]
            for r_ in qtb:
                for b_ in r_.items:
                    T.op(T.pool, lambda: nc.gpsimd.memset(b_.t[dpad:128, :], 0.0), writes=[b_])
            qds = [[self.newds(), self.newds(), self.newds()] for _ in range(nstream)]
            ptr = Rot([T.sb(ctx, "pt", [128, 2, 512], BF16) for _ in range(5)])
            pS = Rot([T.ps(ctx, "pS", [128, 1024], F32) for _ in range(3)])
            pO = Rot([T.ps(ctx, "pO", [128, 512], F32) for _ in range(2)])
            rzr = Rot([T.sb(ctx, "rz", [64, 512], F32) for _ in range(2)])
            ost = [T.sb(ctx, "ost", [64, 512], BF16) for _ in range(3)]
            ods = [self.newds() for _ in range(3)]
            oi = 0

            def load_slot(i):
                sl = slots[i]
                st, nat, nds = sets[i % nset]
                for k, sm in enumerate(sl["streams"]):
                    kt, va, kds, vds_ = st[k]
                    d = sm["d"]
                    T.dma(T.sp, kt.t[0:d, :], sm["kt"], kds, writes=[kt])
                    for c0 in range(0, NT, 16):
                        c1 = min(NT, c0 + 16)
                        T.dma(T.sp, va.t[:, c0:c1, 0:64],
                              sm["v"][c0 * 128:c1 * 128, :].rearrange("(c p) d -> p c d", p=128), vds_, writes=[va])
                if sl.get("natab") is not None:
                    T.dma(T.pool, nat.t[:], sl["natab"], nds, writes=[nat])

            pending = []

            def emit_pv(G):
                grp, st_, pt, po, g0, total, fin = G

                def pv():
                    ins = None
                    for gi, (k, c, masks) in enumerate(grp):
                        va = st_[k][1]
                        idx = g0 + gi
                        ins = nc.tensor.matmul(po.t[:, :], lhsT=va.t[:, c, :], rhs=pt.t[:, gi, :],
                                               start=(idx == 0), stop=(idx == total - 1))
                    return ins
                T.op(T.pe, pv, reads=[pt] + [st_[k][1] for (k, _, _) in grp], writes=[po])
                if fin is not None:
                    row0, jq = fin
                    rz = rzr.next()
                    T.op(T.dve, lambda: nc.vector.reciprocal(out=rz.t[:, :], in_=po.t[64:128, :]), reads=[po], writes=[rz])
                    ob = ost[self._oi % 3]
                    T.op(T.dve, lambda: nc.vector.tensor_tensor(out=ob.t[:, :], in0=po.t[0:64, :], in1=rz.t[:, :], op=ALU.mult),
                         reads=[po, rz], writes=[ob])
                    T.dma(T.sp, self.OT[row0:row0 + 64, jq * 512:(jq + 1) * 512], ob.t[:, :], ods[self._oi % 3], reads=[ob])
                    self._oi += 1

            self._oi = 0
            SKEW = 2
            load_slot(0)
            for i, sl in enumerate(slots):
                if nset == 2 and i + 1 < len(slots):
                    while pending:
                        emit_pv(pending.pop(0))
                    load_slot(i + 1)
                st, nat, _ = sets[i % nset]
                for jq in range(NQB):
                    qs = []
                    for k, sm in enumerate(sl["streams"]):
                        qb = qtb[k].next()
                        d = sm["d"]
                        T.dma(T.sp, qb.t[0:d, :], sm["qt"][:, jq * 512:(jq + 1) * 512], qds[k][(qtb[k].i - 1) % 3], writes=[qb])
                        qs.append(qb)
                    items = sl["chunks"](jq)
                    total = len(items)
                    po = pO.next()
                    for g0 in range(0, total, 2):
                        grp = items[g0:g0 + 2]
                        ps = pS.next()

                        def qk():
                            ins = None
                            for gi, (k, c, masks) in enumerate(grp):
                                kt = st[k][0]
                                d = sl["streams"][k]["d"]
                                o_ = ps.t[:, gi * 512:(gi + 1) * 512]
                                mm_masks = [m for m in masks if m[0] != "dil"]
                                ins = nc.tensor.matmul(o_, lhsT=kt.t[:, c * 128:(c + 1) * 128], rhs=qs[k].t[:, :],
                                                       start=True, stop=(len(mm_masks) == 0))
                                masks = mm_masks
                                for mi, m in enumerate(masks):
                                    last = (mi == len(masks) - 1)
                                    if m[0] == "nat":
                                        ins = nc.tensor.matmul(o_, lhsT=self.ident.t[:], rhs=nat.t[:, m[1]:m[1] + 512],
                                                               start=False, stop=last)
                                    elif m[0] == "dil":
                                        pass
                                    else:
                                        ins = nc.tensor.matmul(o_.rearrange("p (j c) -> p j c", c=64), lhsT=self.ea.t[:, :],
                                                               rhs=self.rsmall.t[:, m[1] * 8:(m[1] + 1) * 8].unsqueeze(2).to_broadcast([128, 8, 64]),
                                                               start=False, stop=last)
                            return ins
                        rd = [st[k][0] for (k, _, _) in grp] + [qs[k] for (k, _, _) in grp] + [self.ident, self.ea, self.rsmall, nat]
                        if dstrip is not None:
                            rd.append(dstrip)
                        T.op(T.pe, qk, reads=rd, writes=[ps])
                        pt = ptr.next()
                        n = len(grp)
                        T.op(T.act, lambda: nc.scalar.activation(out=pt.t[:, 0:n, :].rearrange("p g q -> p (g q)"),
                                                                 in_=ps.t[:, 0:n * 512], func=AF.Exp), reads=[ps], writes=[pt])
                        for gi, (k_, c_, masks_) in enumerate(grp):
                            for m in masks_:
                                if m[0] == "dil":
                                    self._mi = getattr(self, "_mi", 0) + 1
                                    if self._mi % 3 == 0:
                                        T.op(T.pool, lambda: nc.gpsimd.tensor_tensor(out=pt.t[:, gi, :], in0=pt.t[:, gi, :],
                                                                                     in1=dstrip.t[:, m[1]:m[1] + 512], op=ALU.mult),
                                             reads=[pt, dstrip], writes=[pt])
                                    else:
                                        T.op(T.dve, lambda: nc.vector.tensor_tensor(out=pt.t[:, gi, :], in0=pt.t[:, gi, :],
                                                                                    in1=dstrip.t[:, m[1]:m[1] + 512], op=ALU.mult),
                                             reads=[pt, dstrip], writes=[pt])
                        fin = (sl["row0"], jq) if g0 + 2 >= total else None
                        pending.append((grp, st, pt, po, g0, total, fin))
                        if len(pending) > SKEW:
                            emit_pv(pending.pop(0))
                if nset == 1:
                    while pending:
                        emit_pv(pending.pop(0))
                    if i + 1 < len(slots):
                        load_slot(i + 1)
            while pending:
                emit_pv(pending.pop(0))
            self.end_phase()

    def attn_mla(self, l, S):
        NT = S // 128
        slots = []
        for h in range(6):
            slots.append(dict(row0=h * 64, natab=None,
                              streams=[dict(qt=self.QTm[h, :, 0:S], kt=self.KTm[h, :, 0:S], v=self.Vm[0:S, h * 64:(h + 1) * 64], d=96)],
                              chunks=(lambda jq: [(0, c, []) for c in range(NT)])))
        self.attention(S, slots, l)

    def attn_na(self, l, S):
        NT = S // 128
        NQB = S // 512
        I = self.inp

        def chunks(jq):
            ty = 0 if jq == 0 else (2 if jq == NQB - 1 else 1)
            out = []
            for ci in range(8):
                c = 4 * jq - 2 + ci
                if 0 <= c < NT:
                    m0 = 11 - 2 * ci
                    out.append((0, c, [("nat", (m0 + 3) * 64), ("row", ty * 8 + ci)]))
            return out
        slots = []
        for h in range(6):
            slots.append(dict(row0=384 + h * 64, natab=I["natab"][l, h],
                              streams=[dict(qt=self.QTn[h * 64:(h + 1) * 64, 0:S], kt=self.KTn[h * 64:(h + 1) * 64, 0:S],
                                            v=self.Vn[0:S, h * 64:(h + 1) * 64], d=64)], chunks=chunks))
        self.attention(S, slots, l)

    def attn_dil(self, l, S):
        NT = S // 128

        def chunks(jq):
            out = []
            for g in range(3):
                for Dd in range(DIL_DMIN[g], DIL_DMAX[g] + 1, 128):
                    c = (jq * 512 + Dd) // 128
                    if 0 <= c < NT:
                        out.append((g, c, [("dil", DIL_OFF[g] + DIL_DMAX[g] - Dd)]))
            return out
        slots = []
        for h in range(4):
            sts = []
            for g in range(3):
                hh = g * 4 + h
                sts.append(dict(qt=self.QTd[hh * 64:(hh + 1) * 64, 0:S], kt=self.KTd[hh * 64:(hh + 1) * 64, 0:S],
                                v=self.Vd[0:S, hh * 64:(hh + 1) * 64], d=64))
            slots.append(dict(row0=768 + h * 64, natab=None, streams=sts, chunks=chunks))
        self.attention(S, slots, l)

    def phase_x(self, l, S, xsrc, mem):
        T = self.T
        nc = self.nc
        I = self.inp
        NT = S // 128
        GO = G_CROSS
        self.begin_phase()
        with ExitStack() as ctx:
            self.make_cbias(ctx, [EPS, 256 * EPS])
            wo, wob = self.load_w(ctx, "wo", I["w_o"][l], D, D)
            wcq, wcqb = self.load_w(ctx, "wcq", I["w_cq"][l], D, D)
            wco, wcob = self.load_w(ctx, "wco", I["w_co"][l], D, D)
            wkv, wkvb = self.load_w(ctx, "wkv", I["w_ckv"][l], D, 2 * D)
            gv = self.load_rep(ctx, "gv2", I["gvec"][l, GO:NG], NG - GO)
            g_cross, g_xq, g_ffn, g_mem, g_xk = 0, G_XQ - GO, G_FFN - GO, G_MEM - GO, G_XK - GO
            pA = T.ps(ctx, "pA", [128, 1024], F32)
            pC = T.ps(ctx, "pC", [128, 1024], F32)
            pD = T.ps(ctx, "pD", [128, 1024], F32)
            pT = T.ps(ctx, "pT", [128, 1024], BF16)
            pZ = T.ps(ctx, "pZ", [128, 512], F32)
            xts = [T.sb(ctx, "xt", [128, D], F32) for _ in range(3)]
            xds = [self.newds() for _ in range(3)]
            ots = [T.sb(ctx, "oT", [128, 8, 128], BF16) for _ in range(2)]
            otds = [self.newds() for _ in range(2)]
            junk = T.sb(ctx, "junk", [128, D], BF16)
            ss = Rot([T.sb(ctx, "ss", [128, 8], F32) for _ in range(6)])
            rs = Rot([T.sb(ctx, "rs", [128, 8], F32) for _ in range(6)])
            hb = Rot([T.sb(ctx, "hb", [128, D], BF16) for _ in range(4)])
            hT = Rot([T.sb(ctx, "hT", [128, 8, 128], BF16) for _ in range(3)])
            sqb = T.sb(ctx, "sqb", [128, D], F32)
            yf = T.sb(ctx, "yf", [128, D], F32)
            kmT = T.sb(ctx, "kmT", [128, 8, 256], BF16)
            vms = T.sb(ctx, "vms", [128, 2, D], BF16)
            ptb = Rot([T.sb(ctx, "ptb", [128, 8, 128], BF16) for _ in range(2)])
            rzb = T.sb(ctx, "rzb", [128, 512], F32)
            ocT = Rot([T.sb(ctx, "ocT", [128, 8, 128], BF16) for _ in range(2)])
            h3s = [T.sb(ctx, "h3s", [128, 8, 128], BF16) for _ in range(2)]
            h3ds = [self.newds() for _ in range(2)]
            zt = T.sb(ctx, "zt", [128, 8, 1], BF16)
            zds = self.newds()
            H3v = self.H3T.rearrange("(c p) t -> p c t", p=128)
            OTv = self.OT.rearrange("(c p) t -> p c t", p=128)

            def rmsnorm_T(xb, goff, hT_dst):
                ssb = ss.next()
                T.op(T.act, lambda: nc.scalar.activation(out=junk.t[:], in_=xb.t[:], func=AF.Square, accum_out=ssb.t[:, 0:1]),
                     reads=[xb], writes=[junk, ssb])
                rb = rs.next()
                self.rms_rstd(ssb, rb, 1, 1.0 / D, EPS)
                h = hb.next()
                T.op(T.dve, lambda: nc.vector.scalar_tensor_tensor(out=h.t[:], in0=xb.t[:], scalar=rb.t[:, 0:1],
                                                                   in1=gv.t[:, goff:goff + D], op0=ALU.mult, op1=ALU.mult),
                     reads=[xb, rb, gv], writes=[h])
                self.tr8(h, pT, hT_dst)

            def proj2(dst_ps, lhs_b, w, wb, coff=0):
                def mm():
                    ins = None
                    for nh in range(2):
                        for c in range(8):
                            ins = nc.tensor.matmul(dst_ps.t[:, nh * 512:(nh + 1) * 512], lhsT=lhs_b.t[:, c, :],
                                                   rhs=w[:, c, coff + nh * 512:coff + (nh + 1) * 512], start=(c == 0), stop=(c == 7))
                    return ins
                T.op(T.pe, mm, reads=[lhs_b] + wb, writes=[dst_ps])

            def headnorm4(src_ps, goff, sc, bias, out_b):
                T.op(T.act, lambda: nc.scalar.activation(out=sqb.t[:], in_=src_ps.t[:], func=AF.Square), reads=[src_ps], writes=[sqb])
                ssb = ss.next()
                T.op(T.dve, lambda: nc.vector.tensor_reduce(out=ssb.t[:, 0:4], in_=sqb.t[:].rearrange("p (h d) -> p h d", d=256),
                                                            axis=AX.X, op=ALU.add), reads=[sqb], writes=[ssb])
                rb = rs.next()
                self.rms_rstd(ssb, rb, 4, sc, bias)
                T.op(T.dve, lambda: nc.vector.tensor_tensor(out=yf.t[:].rearrange("p (h d) -> p h d", d=256),
                                                            in0=src_ps.t[:].rearrange("p (h d) -> p h d", d=256),
                                                            in1=rb.t[:, 0:4].unsqueeze(2).to_broadcast([128, 4, 256]), op=ALU.mult),
                     reads=[src_ps, rb], writes=[yf])
                T.op(T.dve, lambda: nc.vector.tensor_tensor(out=out_b.t[:].rearrange("p (h d) -> p h d", d=256),
                                                            in0=yf.t[:].rearrange("p (h d) -> p h d", d=256),
                                                            in1=gv.t[:, goff:goff + 256].unsqueeze(1).to_broadcast([128, 4, 256]), op=ALU.mult),
                     reads=[yf, gv], writes=[out_b])

            for mt in range(2):
                xb = xts[mt]
                T.dma(T.sp, xb.t[:], mem[mt * 128:(mt + 1) * 128, :], xds[mt], writes=[xb])
                mT = hT.next()
                rmsnorm_T(xb, g_mem, mT)
                proj2(pA, mT, wkv, wkvb, 0)
                kn = hb.next()
                headnorm4(pA, g_xk, 1.0 / 256, EPS, kn)

                def trk():
                    ins = None
                    for c in range(8):
                        ins = nc.tensor.transpose(out=pT.t[:, c * 128:(c + 1) * 128], in_=kn.t[:, c * 128:(c + 1) * 128],
                                                  identity=self.ident.t[:])
                    return ins
                T.op(T.pe, trk, reads=[kn, self.ident], writes=[pT])
                T.op(T.act, lambda: nc.scalar.copy(out=kmT.t[:, :, mt * 128:(mt + 1) * 128],
                                                   in_=pT.t[:].rearrange("p (c t) -> p c t", t=128)), reads=[pT], writes=[kmT])
                proj2(pC, mT, wkv, wkvb, D)
                T.op(T.act, lambda: nc.scalar.copy(out=vms.t[:, mt, :], in_=pC.t[:]), reads=[pC], writes=[vms])

            def loads(t):
                xb = xts[t % 3]
                T.dma(T.sp, xb.t[:], xsrc[t * 128:(t + 1) * 128, :], xds[t % 3], writes=[xb])
                ob = ots[t % 2]
                T.dma(T.sp, ob.t[:], OTv[:, :, t * 128:(t + 1) * 128], otds[t % 2], writes=[ob])

            qcTs = [T.sb(ctx, "qcT", [128, 8, 128], BF16) for _ in range(2)]

            def genA(t):
                xb = xts[t % 3]
                ob = ots[t % 2]
                proj2(pA, ob, wo, wob)
                T.op(T.dve, lambda: nc.vector.tensor_tensor(out=xb.t[:], in0=pA.t[:], in1=xb.t[:], op=ALU.add), reads=[pA, xb], writes=[xb])
                yield
                h2T = hT.next()
                rmsnorm_T(xb, g_cross, h2T)
                yield
                proj2(pA, h2T, wcq, wcqb)
                qn = hb.next()
                headnorm4(pA, g_xq, 1.0, 256 * EPS, qn)
                yield
                self.tr8(qn, pT, qcTs[t % 2])

            def genB(t):
                xb = xts[t % 3]
                qcT = qcTs[t % 2]

                def sc_mm():
                    ins = None
                    for hh in range(4):
                        for kc in range(2):
                            o_ = pC.t[:, (hh * 2 + kc) * 128:(hh * 2 + kc + 1) * 128]
                            for dc in range(2):
                                ins = nc.tensor.matmul(o_, lhsT=kmT.t[:, hh * 2 + dc, kc * 128:(kc + 1) * 128], rhs=qcT.t[:, hh * 2 + dc, :],
                                                       start=(dc == 0), stop=(dc == 1))
                    return ins
                T.op(T.pe, sc_mm, reads=[kmT, qcT], writes=[pC])
                pt = ptb.next()
                T.op(T.act, lambda: nc.scalar.activation(out=pt.t[:].rearrange("p c q -> p (c q)"), in_=pC.t[:], func=AF.Exp),
                     reads=[pC], writes=[pt])
                yield

                def pv_mm():
                    ins = None
                    for hh in range(4):
                        for kc in range(2):
                            ins = nc.tensor.matmul(pZ.t[:, hh * 128:(hh + 1) * 128], lhsT=self.ones.t[:], rhs=pt.t[:, hh * 2 + kc, :],
                                                   start=(kc == 0), stop=(kc == 1))
                        for dvc in range(2):
                            for kc in range(2):
                                ins = nc.tensor.matmul(pD.t[:, (hh * 2 + dvc) * 128:(hh * 2 + dvc + 1) * 128],
                                                       lhsT=vms.t[:, kc, hh * 256 + dvc * 128:hh * 256 + (dvc + 1) * 128],
                                                       rhs=pt.t[:, hh * 2 + kc, :], start=(kc == 0), stop=(kc == 1))
                    return ins
                T.op(T.pe, pv_mm, reads=[pt, vms, self.ones], writes=[pZ, pD])
                T.op(T.dve, lambda: nc.vector.reciprocal(out=rzb.t[:], in_=pZ.t[:]), reads=[pZ], writes=[rzb])
                oc = ocT.next()
                T.op(T.dve, lambda: nc.vector.tensor_tensor(
                    out=oc.t[:].rearrange("p (h e) q -> p h e q", e=2),
                    in0=pD.t[:].rearrange("p (h e q) -> p h e q", e=2, q=128),
                    in1=rzb.t[:].rearrange("p (h q) -> p h q", q=128).unsqueeze(2).to_broadcast([128, 4, 2, 128]), op=ALU.mult),
                    reads=[pD, rzb], writes=[oc])
                yield
                proj2(pC, oc, wco, wcob)
                T.op(T.dve, lambda: nc.vector.tensor_tensor(out=xb.t[:], in0=pC.t[:], in1=xb.t[:], op=ALU.add), reads=[pC, xb], writes=[xb])
                T.dma(T.sp, self.X[t * 128:(t + 1) * 128, :], xb.t[:], xds[t % 3], reads=[xb])
                yield
                h3 = h3s[t % 2]
                rmsnorm_T(xb, g_ffn, h3)
                T.dma(T.sp, H3v[:, :, 1 + t * 128:1 + (t + 1) * 128], h3.t[:], h3ds[t % 2], reads=[h3])

            loads(0)
            if NT > 1:
                loads(1)
            for _ in genA(0):
                pass
            for t in range(NT):
                if t + 2 < NT:
                    loads(t + 2)
                gens = [genB(t)]
                if t + 1 < NT:
                    gens.insert(0, genA(t + 1))
                while gens:
                    for g in list(gens):
                        try:
                            next(g)
                        except StopIteration:
                            gens.remove(g)
            self.end_phase()

    def tr8(self, src_b, pT, dst_b):
        T = self.T
        nc = self.nc

        def tr():
            ins = None
            for c in range(8):
                ins = nc.tensor.transpose(out=pT.t[:, c * 128:(c + 1) * 128], in_=src_b.t[:, c * 128:(c + 1) * 128],
                                          identity=self.ident.t[:])
            return ins
        T.op(T.pe, tr, reads=[src_b, self.ident], writes=[pT])
        T.op(T.act, lambda: nc.scalar.copy(out=dst_b.t[:].rearrange("p c t -> p (c t)"), in_=pT.t[:]), reads=[pT], writes=[dst_b])

    def ffn_up(self, l, S):
        T = self.T
        nc = self.nc
        I = self.inp
        nblk = (S + 509) // 510
        blocks = []
        for b in range(nblk):
            c0 = 510 * b
            w = min(512, S + 2 - c0)
            blocks.append((c0, w))
        passes = [blocks[i:i + 9] for i in range(0, nblk, 9)]
        H3v = self.H3T.rearrange("(c p) t -> p c t", p=128)
        WUv = I["w_up"][l].rearrange("(c p) n -> p c n", p=128)
        self.begin_phase()
        with ExitStack() as ctx:
            cp = T.sb(ctx, "convp", [128, 4, 44], F32)
            T.dma(T.sp, cp.t[:], I["convp"][l], self.newds(), writes=[cp])
            wab = [T.sb(ctx, "wab", [128, 8, 256], BF16) for _ in range(2)]
            wds = [self.newds("pool") for _ in range(2)]
            maxc = max(pb[-1][0] + pb[-1][1] - pb[0][0] for pb in passes)
            hres = T.sb(ctx, "hres", [128, 8, maxc], BF16)
            hds = self.newds()
            pU = Rot([T.ps(ctx, "pU", [128, 512], F32) for _ in range(6)])
            ca = Rot([T.sb(ctx, "ca", [128, 512], F32) for _ in range(2)])
            cg = Rot([T.sb(ctx, "cg", [128, 512], F32) for _ in range(2)])
            sg = Rot([T.sb(ctx, "sg", [128, 512], F32) for _ in range(2)])
            ast = [T.sb(ctx, "ast", [128, 512], BF16) for _ in range(3)]
            ads = [self.newds() for _ in range(3)]
            ai = 0

            def load_wab(i):
                b = wab[i % 2]
                T.dma(T.pool, b.t[:, :, 0:128], WUv[:, :, i * 128:(i + 1) * 128], wds[i % 2], writes=[b])
                T.dma(T.pool, b.t[:, :, 128:256], WUv[:, :, DFF + i * 128:DFF + (i + 1) * 128], wds[i % 2], writes=[b])

            for pb in passes:
                col_lo = pb[0][0]
                col_hi = pb[-1][0] + pb[-1][1]
                v_lo = max(col_lo, 1)
                v_hi = min(col_hi, S + 1)
                T.dma(T.sp, hres.t[:, :, v_lo - col_lo:v_hi - col_lo], H3v[:, :, v_lo:v_hi], hds, writes=[hres])
                if col_lo == 0:
                    T.op(T.pool, lambda: nc.gpsimd.memset(hres.t[:, :, 0:1], 0.0), writes=[hres])
                if col_hi == S + 2:
                    T.op(T.pool, lambda: nc.gpsimd.memset(hres.t[:, :, col_hi - col_lo - 1:col_hi - col_lo], 0.0), writes=[hres])
                load_wab(0)
                for i in range(22):
                    if i + 1 < 22:
                        load_wab(i + 1)
                    wb = wab[i % 2]
                    for (c0, w) in pb:
                        off = c0 - col_lo
                        ua = pU.next()
                        ug = pU.next()

                        def mm():
                            ins = None
                            for (dst, wo_) in ((ua, 0), (ug, 128)):
                                for c in range(8):
                                    ins = nc.tensor.matmul(dst.t[:, 0:w], lhsT=wb.t[:, c, wo_:wo_ + 128], rhs=hres.t[:, c, off:off + w],
                                                           start=(c == 0), stop=(c == 7))
                            return ins
                        T.op(T.pe, mm, reads=[wb, hres], writes=[ua, ug])
                        n = w - 2
                        outs = []
                        for (u, col, dstr) in ((ua, i, ca), (ug, 22 + i, cg)):
                            cb_ = dstr.next()
                            if n >= 256:
                                T.op(T.act, lambda: nc.scalar.activation(out=cb_.t[:, 0:n], in_=u.t[:, 0:n], func=AF.Identity,
                                                                         scale=cp.t[:, 0, col:col + 1], bias=cp.t[:, 3, col:col + 1]),
                                     reads=[u, cp], writes=[cb_])
                            else:
                                T.op(T.dve, lambda: nc.vector.tensor_scalar(out=cb_.t[:, 0:n], in0=u.t[:, 0:n], scalar1=cp.t[:, 0, col:col + 1],
                                                                            scalar2=cp.t[:, 3, col:col + 1], op0=ALU.mult, op1=ALU.add),
                                     reads=[u, cp], writes=[cb_])
                            T.op(T.dve, lambda: nc.vector.scalar_tensor_tensor(out=cb_.t[:, 0:n], in0=u.t[:, 1:n + 1], scalar=cp.t[:, 1, col:col + 1],
                                                                               in1=cb_.t[:, 0:n], op0=ALU.mult, op1=ALU.add),
                                 reads=[u, cp, cb_], writes=[cb_])
                            T.op(T.dve, lambda: nc.vector.scalar_tensor_tensor(out=cb_.t[:, 0:n], in0=u.t[:, 2:n + 2], scalar=cp.t[:, 2, col:col + 1],
                                                                               in1=cb_.t[:, 0:n], op0=ALU.mult, op1=ALU.add),
                                 reads=[u, cp, cb_], writes=[cb_])
                            outs.append(cb_)
                        sgb = sg.next()
                        T.op(T.act, lambda: nc.scalar.activation(out=sgb.t[:, 0:n], in_=outs[1].t[:, 0:n], func=AF.Silu), reads=[outs[1]], writes=[sgb])
                        ab = ast[ai % 3]
                        T.op(T.pool, lambda: nc.gpsimd.tensor_tensor(out=ab.t[:, 0:n], in0=sgb.t[:, 0:n], in1=outs[0].t[:, 0:n], op=ALU.mult),
                             reads=[sgb, outs[0]], writes=[ab])
                        T.dma(T.sp, self.ACTT[i * 128:(i + 1) * 128, c0:c0 + n], ab.t[:, 0:n], ads[ai % 3], reads=[ab])
                        ai += 1
            self.end_phase()

    def ffn_down(self, l, S, dst):
        T = self.T
        nc = self.nc
        I = self.inp
        NT = S // 128
        AV = self.ACTT.rearrange("(c p) t -> p c t", p=128)
        self.begin_phase()
        with ExitStack() as ctx:
            wd, wdb = self.load_w(ctx, "wd", I["w_down"][l], DFF, D)
            pY = Rot([T.ps(ctx, "pY", [128, 1024], F32) for _ in range(2)])
            xts = [T.sb(ctx, "xt", [128, D], F32) for _ in range(3)]
            xds = [self.newds() for _ in range(3)]
            ats = [T.sb(ctx, "aT", [128, 22, 128], BF16) for _ in range(2)]
            atds = [self.newds() for _ in range(2)]

            def loads(t):
                T.dma(T.sp, xts[t % 3].t[:], self.X[t * 128:(t + 1) * 128, :], xds[t % 3], writes=[xts[t % 3]])
                T.dma(T.sp, ats[t % 2].t[:], AV[:, :, t * 128:(t + 1) * 128], atds[t % 2], writes=[ats[t % 2]])
            loads(0)
            for t in range(NT):
                if t + 1 < NT:
                    loads(t + 1)
                xb = xts[t % 3]
                ab = ats[t % 2]
                py = pY.next()

                def mm():
                    ins = None
                    for nh in range(2):
                        for c in range(22):
                            ins = nc.tensor.matmul(py.t[:, nh * 512:(nh + 1) * 512], lhsT=ab.t[:, c, :], rhs=wd[:, c, nh * 512:(nh + 1) * 512],
                                                   start=(c == 0), stop=(c == 21))
                    return ins
                T.op(T.pe, mm, reads=[ab] + wdb, writes=[py])
                T.op(T.dve, lambda: nc.vector.tensor_tensor(out=xb.t[:], in0=py.t[:], in1=xb.t[:], op=ALU.add), reads=[py, xb], writes=[xb])
                T.dma(T.sp, dst[t * 128:(t + 1) * 128, :], xb.t[:], xds[t % 3], reads=[xb])
            self.end_phase()


def Buf_view(b):
    return b


def _rope_tab(half):
    pos = np.arange(SS_, dtype=np.float32)
    inv = (np.float32(10000.0) ** (-np.arange(half, dtype=np.float32) / np.float32(half))).astype(np.float32)
    ang = (pos[:, None] * inv[None, :]).astype(np.float32)
    c = np.cos(ang).astype(np.float32).reshape(64, 128, half).transpose(1, 0, 2)
    s = np.sin(ang).astype(np.float32).reshape(64, 128, half).transpose(1, 0, 2)
    return np.ascontiguousarray(c), np.ascontiguousarray(s)


def _dil_strips():
    out = np.zeros((128, DIL_W), np.float32)
    p = np.arange(128)[:, None]
    for g in range(3):
        r = DIL_R[g]
        w = DIL_DMAX[g] - DIL_DMIN[g] + 512
        x = np.arange(w)[None, :]
        delta = p - x + DIL_DMAX[g]
        ok = (delta % r == 0) & (np.abs(delta) <= 64 * r)
        out[:, DIL_OFF[g]:DIL_OFF[g] + w] = np.where(ok, 1.0, 0.0)
    return out


def _na_rowmask():
    R = 1000
    out = np.full((2, 3 * 8 * 8), NEG, np.float32)
    for ty, r0 in ((0, 0), (1, 496), (2, R - 8)):
        for ci in range(8):
            kr0 = r0 - 4 + 2 * ci
            for a in range(2):
                kr = kr0 + a
                for j in range(8):
                    r = r0 + j
                    rs = min(max(r - 4, 0), R - 8)
                    if rs <= kr < rs + 8:
                        out[a, (ty * 8 + ci) * 8 + j] = 0.0
    return out


def _na_table(rpb):
    Lh = rpb.shape[0]
    kc = np.arange(64)[:, None]
    c = np.arange(64)[None, :]
    cs = np.clip(c - 8, 0, 48)
    mcol = (kc >= cs) & (kc < cs + 16)
    dcidx = np.clip(kc - c + 15, 0, 30)
    out = np.full((Lh, 6, 128, NA_MM, 64), NEG, np.float32)
    for a in range(2):
        for mm in range(-3, NA_MM - 3):
            m = mm - a
            if 0 <= m <= 14:
                vals = rpb[:, :, 14 - m, :][:, :, dcidx]
                out[:, :, a * 64:(a + 1) * 64, mm + 3, :] = np.where(mcol[None, None], vals, np.float32(NEG))
    return np.ascontiguousarray(out.reshape(Lh, 6, 128, NA_MM * 64))


def prep_inputs(inputs, n_cores=8):
    f = lambda a: np.ascontiguousarray(np.asarray(a, dtype=np.float32))
    gv = np.concatenate([f(inputs[k]) for k in ("norm_mix", "mla_q_norm", "mla_kv_norm", "mla_qn", "mla_kn", "na_qn", "na_kn",
                                                "dil_qn", "dil_kn", "norm_cross", "x_qn", "norm_ffn", "norm_mem", "x_kn")], axis=1)
    assert gv.shape == (L, NG)
    cw = f(inputs["conv_w"])
    cbv = f(inputs["conv_b"])
    convp = np.concatenate([cw, cbv[:, None, :]], axis=1).reshape(L, 4, 44, 128).transpose(0, 3, 1, 2)
    cosd, sind = _rope_tab(32)
    cosm, sinm = _rope_tab(16)
    ea = np.zeros((2, 128), np.float32)
    ea[0, :64] = 1.0
    ea[1, 64:] = 1.0
    shared = {
        "gvec": np.ascontiguousarray(gv), "convp": np.ascontiguousarray(convp), "natab": _na_table(f(inputs["na_rpb"])),
        "ident": np.eye(128, dtype=np.float32), "cosd": cosd, "sind": sind, "cosm": cosm, "sinm": sinm,
        "dstrip": _dil_strips(), "rsmall": _na_rowmask(), "ea": ea,
    }
    for k in ("w_in", "w_uq", "w_uk", "w_uv", "w_o", "w_cq", "w_ckv", "w_co", "w_up", "w_down"):
        shared[k] = f(inputs[k])
    xp = f(inputs["x_prompt"])
    xs = f(inputs["x_sample"])
    mp = f(inputs["mem_prompt"])
    ms = f(inputs["mem_sample"])
    zs = np.zeros((SS_, D), np.float32)
    zm = np.zeros((MEM, D), np.float32)
    maps = []
    for c in range(n_cores):
        m = dict(shared)
        m["xp"] = xp[c]
        m["memp"] = mp[c]
        if c == 0:
            m["xs"], m["mems"] = xs[0], ms[0]
        elif c == 4:
            m["xs"], m["mems"] = xs[1], ms[1]
        else:
            m["xs"], m["mems"] = zs, zm
        maps.append(m)
    return maps


def kernel(**inputs):
    cfg = {"parts": [("p", SP_), ("s", SS_)]}
    nc = Prog(cfg).build()
    maps = prep_inputs(inputs)
    res = run_bass_kernel_spmd(nc, maps, core_ids=list(range(8)))
    yp = np.stack([np.asarray(res.results[c]["yp"], dtype=np.float32) for c in range(8)], axis=0)
    ys = np.stack([np.asarray(res.results[c]["ys"], dtype=np.float32) for c in (0, 4)], axis=0)
    return (yp, ys)
```

```python
import numpy as np
import concourse.bass as bass
import concourse.mybir as mybir
from concourse.bass_utils import run_bass_kernel_spmd
from contextlib import ExitStack

F32 = mybir.dt.float32
BF16 = mybir.dt.bfloat16
AF = mybir.ActivationFunctionType
ALU = mybir.AluOpType
AX = mybir.AxisListType

D = 1024
L = 4
EPS = 1e-6
NEG = -30000.0
D_IN = 3872
DFF = 2816
SP_ = 2048
SS_ = 8192
MEM = 256
G_MIX, G_CQ, G_CKV, G_MQN, G_MKN, G_NQ, G_NK, G_DQ, G_DK, G_CROSS, G_XQ, G_FFN, G_MEM, G_XK = (
    0, 1024, 1280, 1408, 1504, 1600, 1664, 1728, 1792, 1856, 2880, 3136, 4160, 5184)
NG = 5440
DIL_R = (1, 4, 16)
DIL_DMIN = (-128, -256, -1024)
DIL_DMAX = (512, 640, 1408)
DIL_OFF = (0, 1152, 2560)
DIL_W = 5504
NA_MM = 22


class Ev:
    __slots__ = ("sem", "val", "eng", "ds")

    def __init__(self, sem, val, eng, ds=None):
        self.sem = sem
        self.val = val
        self.eng = eng
        self.ds = ds


class Buf:
    def __init__(self, name, t=None):
        self.name = name
        self.t = t
        self.w = []
        self.r = {}
        self.rd = []


class DS:
    def __init__(self, sem):
        self.sem = sem
        self.cnt = 0


class Eng:
    def __init__(self, name, h, pe=False):
        self.name = name
        self.h = h
        self.pe = pe
        self.sem = None
        self.count = 0
        self.waited = {}
        self.nsem = 0


class Tracker:
    EPOCH = 1 << 20

    def __init__(self, nc, es):
        self.nc = nc
        self.es = es
        self.uid = 0
        self.pe = Eng("pe", nc.tensor, True)
        self.act = Eng("act", nc.scalar)
        self.dve = Eng("dve", nc.vector)
        self.pool = Eng("pool", nc.gpsimd)
        self.sp = Eng("sp", nc.sync)
        self.engs = [self.pe, self.act, self.dve, self.pool, self.sp]
        for e in self.engs:
            e.sem = self.newsem(e.name + "_s0")
        self.free_ds = {}
        self.all_ds = []
        self.bar = DS(self.newsem("bar"))
        self.bar.kind = "sp"
        self.ninstr = 0

    def newsem(self, name):
        return self.es.enter_context(self.nc.semaphore(name))

    def get_ds(self, kind="sp"):
        fl = self.free_ds.setdefault(kind, [])
        if fl:
            return fl.pop()
        ds = DS(self.newsem("ds%s%d" % (kind, len(self.all_ds))))
        ds.kind = kind
        self.all_ds.append(ds)
        return ds

    def sb(self, ctx, name, shape, dt):
        self.uid += 1
        t = ctx.enter_context(self.nc.sbuf_tensor("%s_%d" % (name, self.uid), list(shape), dt))
        return Buf(name, t)

    def ps(self, ctx, name, shape, dt):
        self.uid += 1
        t = ctx.enter_context(self.nc.psum_tensor("%s_%d" % (name, self.uid), list(shape), dt))
        return Buf(name, t)

    def wait(self, eng, ev):
        if ev.eng is eng and eng.pe:
            return
        k = id(ev.sem)
        if eng.waited.get(k, 0) >= ev.val:
            return
        val = ev.ds.cnt if ev.ds is not None else ev.val
        eng.h.wait_ge(ev.sem, val)
        eng.waited[k] = val

    def deps(self, eng, reads, writes):
        for b in reads:
            for ev in b.w:
                self.wait(eng, ev)
        for b in writes:
            for ev in b.w:
                self.wait(eng, ev)
            for ev in b.r.values():
                self.wait(eng, ev)
            for ev in b.rd:
                self.wait(eng, ev)

    def done(self, ev, reads, writes):
        for b in reads:
            if ev.eng is None:
                b.rd.append(ev)
            else:
                b.r[ev.eng.name] = ev
        for b in writes:
            b.w = [ev]
            b.r = {}
            b.rd = []

    def op(self, eng, fn, reads=(), writes=()):
        self.deps(eng, reads, writes)
        ins = fn()
        eng.count += 1
        ins.then_inc(eng.sem, 1)
        self.ninstr += 1
        ev = Ev(eng.sem, eng.count, eng)
        self.done(ev, reads, writes)
        if eng.count >= self.EPOCH:
            eng.nsem += 1
            eng.sem = self.newsem("%s_s%d" % (eng.name, eng.nsem))
            eng.count = 0
        return ev

    def dma(self, q, out, in_, ds, reads=(), writes=()):
        assert ds.kind == ("pool" if q is self.pool else "sp"), (ds.kind, q.name)
        self.deps(q, reads, writes)
        ins = q.h.dma_start(out=out, in_=in_)
        ds.cnt += 16
        ins.then_inc(ds.sem, 16)
        self.ninstr += 1
        ev = Ev(ds.sem, ds.cnt, None, ds)
        self.done(ev, reads, writes)
        return ev

    def barrier(self, dummy_src, dummy_dst):
        sp = self.sp
        for e in self.engs:
            if e is not sp and e.count > 0:
                self.wait(sp, Ev(e.sem, e.count, e))
        for ds in self.all_ds:
            if ds.cnt > 0:
                self.wait(sp, Ev(ds.sem, ds.cnt, None, ds))
        ins = sp.h.dma_start(out=dummy_dst, in_=dummy_src)
        self.bar.cnt += 16
        ins.then_inc(self.bar.sem, 16)
        for e in self.engs:
            e.h.wait_ge(self.bar.sem, self.bar.cnt)


class Rot:
    def __init__(self, items):
        self.items = items
        self.i = 0

    def next(self):
        b = self.items[self.i % len(self.items)]
        self.i += 1
        return b


class Prog:
    def __init__(self, cfg):
        self.cfg = cfg
        self.nc = bass.Bass("TRN2", target_bir_lowering=False)
        self.es = ExitStack()

    def din(self, name, shape, dt=F32):
        return self.nc.dram_tensor(name, list(shape), dt, kind="ExternalInput").ap()

    def dscr(self, name, shape, dt, dbg=False):
        kind = "ExternalOutput" if (dbg and self.cfg.get("debug")) else "Internal"
        return self.nc.dram_tensor(name, list(shape), dt, kind=kind).ap()

    def build(self):
        nc = self.nc
        cfg = self.cfg
        parts = cfg["parts"]
        self.inp = {}
        I = self.inp
        I["xp"] = self.din("xp", [SP_, D])
        I["xs"] = self.din("xs", [SS_, D])
        I["memp"] = self.din("memp", [MEM, D])
        I["mems"] = self.din("mems", [MEM, D])
        I["w_in"] = self.din("w_in", [L, D, D_IN])
        I["w_uq"] = self.din("w_uq", [L, 256, 576])
        I["w_uk"] = self.din("w_uk", [L, 128, 384])
        I["w_uv"] = self.din("w_uv", [L, 128, 384])
        I["w_o"] = self.din("w_o", [L, D, D])
        I["w_cq"] = self.din("w_cq", [L, D, D])
        I["w_ckv"] = self.din("w_ckv", [L, D, 2 * D])
        I["w_co"] = self.din("w_co", [L, D, D])
        I["w_up"] = self.din("w_up", [L, D, 2 * DFF])
        I["w_down"] = self.din("w_down", [L, DFF, D])
        I["gvec"] = self.din("gvec", [L, NG])
        I["convp"] = self.din("convp", [L, 128, 4, 44])
        I["natab"] = self.din("natab", [L, 6, 128, NA_MM * 64])
        I["ident"] = self.din("ident", [128, 128])
        I["cosd"] = self.din("cosd", [128, 64, 32])
        I["sind"] = self.din("sind", [128, 64, 32])
        I["cosm"] = self.din("cosm", [128, 64, 16])
        I["sinm"] = self.din("sinm", [128, 64, 16])
        I["dstrip"] = self.din("dstrip", [128, DIL_W])
        I["rsmall"] = self.din("rsmall", [2, 3 * 8 * 8])
        I["ea"] = self.din("ea", [2, 128])
        self.yp = nc.dram_tensor("yp", [SP_, D], F32, kind="ExternalOutput").ap()
        self.ys = nc.dram_tensor("ys", [SS_, D], F32, kind="ExternalOutput").ap()
        SM = max(S for _, S in parts)
        self.SM = SM
        dbg = True
        self.X = self.dscr("X", [SM, D], F32, dbg)
        self.QTn = self.dscr("QTn", [384, SM], BF16, dbg)
        self.KTn = self.dscr("KTn", [384, SM], BF16, dbg)
        self.Vn = self.dscr("Vn", [SM, 384], BF16, dbg)
        self.QTd = self.dscr("QTd", [768, SM], BF16, dbg)
        self.KTd = self.dscr("KTd", [768, SM], BF16, dbg)
        self.Vd = self.dscr("Vd", [SM, 768], BF16, dbg)
        self.QTm = self.dscr("QTm", [6, 96, SM], BF16, dbg)
        self.KTm = self.dscr("KTm", [6, 96, SM], BF16, dbg)
        self.Vm = self.dscr("Vm", [SM, 384], BF16, dbg)
        self.OT = self.dscr("OT", [D, SM], BF16, dbg)
        self.H3T = self.dscr("H3T", [D, SM + 2], BF16, dbg)
        self.ACTT = self.dscr("ACTT", [DFF, SM], BF16, dbg)
        self.dum0 = self.dscr("dum0", [1, 16], F32)
        self.dum1 = self.dscr("dum1", [1, 16], F32)

        with self.es:
            self.T = Tracker(nc, self.es)
            T = self.T
            self.consts()
            for (pname, S) in parts:
                xin = I["xp"] if pname == "p" else I["xs"]
                mem = I["memp"] if pname == "p" else I["mems"]
                yout = self.yp if pname == "p" else self.ys
                nl = cfg.get("layers", L)
                for l in range(nl):
                    last = (l == nl - 1)
                    xsrc = xin if l == 0 else self.X
                    ph = cfg.get("phases", "1nmdxfg")
                    if "1" in ph:
                        self.phase1(l, S, xsrc)
                    if "n" in ph:
                        self.attn_na(l, S)
                    if "d" in ph:
                        self.attn_dil(l, S)
                    if "m" in ph:
                        self.attn_mla(l, S)
                    if "x" in ph:
                        self.phase_x(l, S, xsrc, mem)
                    if "f" in ph:
                        self.ffn_up(l, S)
                    if "g" in ph:
                        self.ffn_down(l, S, yout if (last and not cfg.get("debug")) else self.X)
            T.barrier(self.inp["ident"][0:1, 0:16], self.dum1)
        return nc

    def bar(self):
        self.T.barrier(self.inp["ident"][0:1, 0:16], self.dum1)

    def consts(self):
        T = self.T
        nc = self.nc
        I = self.inp
        es = self.es
        self.ident = T.sb(es, "ident", [128, 128], BF16)
        self.ones = T.sb(es, "ones", [128, 128], BF16)
        self.ea = T.sb(es, "ea", [128, 128], BF16)
        self.rsmall = T.sb(es, "rsmall", [128, 192], BF16)
        T.op(T.pool, lambda: nc.gpsimd.memset(self.ea.t[:], 0.0), writes=[self.ea])
        T.op(T.pool, lambda: nc.gpsimd.memset(self.rsmall.t[:], 0.0), writes=[self.rsmall])
        ds = T.get_ds("pool")
        T.dma(T.pool, self.ident.t[:], I["ident"][:, :], ds, writes=[self.ident])
        ds = T.get_ds("pool")
        T.dma(T.pool, self.ea.t[0:2, :], I["ea"][:, :], ds, writes=[self.ea])
        ds = T.get_ds("pool")
        T.dma(T.pool, self.rsmall.t[0:2, :], I["rsmall"][:, :], ds, writes=[self.rsmall])
        T.op(T.pool, lambda: nc.gpsimd.memset(self.ones.t[:], 1.0), writes=[self.ones])

    def load_w(self, ctx, name, src, K, N, nsplit=1):
        T = self.T
        kc = K // 128
        w = T.sb(ctx, name, [128, kc, N], BF16)
        bufs = []
        step = (N + nsplit - 1) // nsplit
        for c in range(kc):
            b = Buf("%s_c%d" % (name, c), w.t)
            ds = self.newds("pool")
            for n0 in range(0, N, step):
                n1 = min(N, n0 + step)
                T.dma(T.pool, w.t[:, c, n0:n1], src[c * 128:(c + 1) * 128, n0:n1], ds, writes=[b])
            bufs.append(b)
        return w.t, bufs

    def load_rep(self, ctx, name, src1d, n):
        T = self.T
        g = T.sb(ctx, name, [128, n], F32)
        ds = self.newds()
        T.dma(T.sp, g.t[:], src1d.partition_broadcast(128), ds, writes=[g])
        return g

    def begin_phase(self):
        self.phase_ds = []

    def end_phase(self):
        self.bar()
        for ds in self.phase_ds:
            self.T.free_ds.setdefault(ds.kind, []).append(ds)
        self.phase_ds = []

    def newds(self, kind="sp"):
        ds = self.T.get_ds(kind)
        self.phase_ds.append(ds)
        return ds

    def rms_rstd(self, ss, rstd, n, sc, bias):
        T = self.T
        nc = self.nc
        T.op(T.act, lambda: nc.scalar.activation(out=rstd.t[:, 0:n], in_=ss.t[:, 0:n], func=AF.Sqrt,
                                                 bias=self.cbias(bias), scale=sc), reads=[ss, self._cb[float(bias)]], writes=[rstd])
        T.op(T.dve, lambda: nc.vector.reciprocal(out=rstd.t[:, 0:n], in_=rstd.t[:, 0:n]), reads=[rstd], writes=[rstd])

    def cbias(self, v):
        key = float(v)
        if key not in self._cb:
            raise KeyError(key)
        return self._cb[key].t[:, 0:1]

    def make_cbias(self, ctx, vals):
        T = self.T
        nc = self.nc
        self._cb = {}
        for v in vals:
            b = T.sb(ctx, "cb", [128, 1], F32)
            T.op(T.pool, lambda: nc.gpsimd.memset(b.t[:], float(v)), writes=[b])
            self._cb[float(v)] = b
        self._cb_bufs = list(self._cb.values())

    def phase1(self, l, S, xsrc):
        T = self.T
        nc = self.nc
        I = self.inp
        NT = S // 128
        E2 = T.pool
        H2 = nc.gpsimd
        self.begin_phase()
        with ExitStack() as ctx:
            self.make_cbias(ctx, [EPS, 64 * EPS, 96 * EPS])
            win, winb = self.load_w(ctx, "win", I["w_in"][l], D, D_IN, nsplit=2)
            wuq, wuqb = self.load_w(ctx, "wuq", I["w_uq"][l], 256, 576)
            wuk, wukb = self.load_w(ctx, "wuk", I["w_uk"][l], 128, 384)
            wuv, wuvb = self.load_w(ctx, "wuv", I["w_uv"][l], 128, 384)
            gv = self.load_rep(ctx, "gv1", I["gvec"][l, 0:G_CROSS], G_CROSS)
            ropes = [T.sb(ctx, "rope", [128, 96], F32) for _ in range(3)]
            rope_ds = [self.newds() for _ in range(3)]
            xts = [T.sb(ctx, "xt", [128, D], F32) for _ in range(2)]
            xds = [self.newds(), self.newds()]
            junk = T.sb(ctx, "junk", [128, D], BF16)
            ss1 = Rot([T.sb(ctx, "ss1", [128, 8], F32) for _ in range(2)])
            rs1 = Rot([T.sb(ctx, "rs1", [128, 8], F32) for _ in range(2)])
            hb = Rot([T.sb(ctx, "hb", [128, D], BF16) for _ in range(2)])
            hT = Rot([T.sb(ctx, "hT", [128, 8, 128], BF16) for _ in range(2)])
            pT = Rot([T.ps(ctx, "pT", [128, 1024], BF16) for _ in range(1)])
            pz = Rot([T.ps(ctx, "pz", [128, 512], F32) for _ in range(5)])
            pX = Rot([T.ps(ctx, "pX", [128, 1024], BF16) for _ in range(2)])
            sq = Rot([T.sb(ctx, "sq", [128, 576], F32) for _ in range(4)])
            yy = Rot([T.sb(ctx, "yy", [128, 576], F32) for _ in range(4)])
            y2 = Rot([T.sb(ctx, "y2", [128, 576], F32) for _ in range(4)])
            tt = Rot([T.sb(ctx, "tt", [128, 6, 32], F32) for _ in range(12)])
            yb = Rot([T.sb(ctx, "yb", [128, 576], BF16) for _ in range(8)])
            ssh = Rot([T.sb(ctx, "ssh", [128, 8], F32) for _ in range(8)])
            rsh = Rot([T.sb(ctx, "rsh", [128, 8], F32) for _ in range(8)])
            cqn = Rot([T.sb(ctx, "cqn", [128, 384], BF16) for _ in range(2)])
            cT = Rot([T.sb(ctx, "cT", [128, 3, 128], BF16) for _ in range(2)])
            krb = Rot([T.sb(ctx, "krb", [128, 32], F32) for _ in range(2)])
            sskr_r = Rot([T.sb(ctx, "sskr", [128, 1], F32) for _ in range(2)])
            stq = [T.sb(ctx, "stq", [128, 18, 128], BF16) for _ in range(2)]
            stm = [T.sb(ctx, "stm", [96, 12, 128], BF16) for _ in range(2)]
            stq_ds = [self.newds() for _ in range(2)]
            stm_ds = [self.newds() for _ in range(2)]
            vst = [T.sb(ctx, "vst", [128, 1536], BF16) for _ in range(2)]
            vds = [self.newds(), self.newds()]

            dq = []
            DEPTH = 2

            def defer(fn):
                dq.append(fn)
                while len(dq) > DEPTH:
                    dq.pop(0)()

            def load_x(t):
                b = xts[t % 2]
                T.dma(T.sp, b.t[:], xsrc[t * 128:(t + 1) * 128, :], xds[t % 2], writes=[b])
                rp = ropes[t % 3]
                T.dma(T.sp, rp.t[:, 0:32], I["cosd"][:, t, :], rope_ds[t % 3], writes=[rp])
                T.dma(T.sp, rp.t[:, 32:64], I["sind"][:, t, :], rope_ds[t % 3], writes=[rp])
                T.dma(T.sp, rp.t[:, 64:80], I["cosm"][:, t, :], rope_ds[t % 3], writes=[rp])
                T.dma(T.sp, rp.t[:, 80:96], I["sinm"][:, t, :], rope_ds[t % 3], writes=[rp])

            def headnorm_g(pz_b, ncols, H, d, sc, bias, res, extra_ss=None):
                s_ = sq.next()
                T.op(T.act, lambda: nc.scalar.activation(out=s_.t[:, 0:ncols], in_=pz_b.t[:, 0:ncols], func=AF.Square),
                     reads=[pz_b], writes=[s_])
                ssb = ssh.next()
                T.op(T.dve, lambda: nc.vector.tensor_reduce(out=ssb.t[:, 0:H],
                                                            in_=s_.t[:, 0:ncols].rearrange("p (h d) -> p h d", d=d),
                                                            axis=AX.X, op=ALU.add), reads=[s_], writes=[ssb])
                if extra_ss is not None:
                    T.op(T.dve, lambda: nc.vector.tensor_scalar(out=ssb.t[:, 0:H], in0=ssb.t[:, 0:H],
                                                                scalar1=extra_ss.t[:, 0:1], scalar2=None, op0=ALU.add),
                         reads=[ssb, extra_ss], writes=[ssb])
                yield
                rb = rsh.next()
                self.rms_rstd(ssb, rb, H, sc, bias)
                res.append(rb)
                yield

            def tile_jobs(t):
                xt = xts[t % 2]
                rp = ropes[t % 3]
                par = t % 2
                sQ = stq[par]
                sM = stm[par]
                vb = vst[par]
                st = {}

                def head():
                    ssb = ss1.next()
                    T.op(T.act, lambda: nc.scalar.activation(out=junk.t[:], in_=xt.t[:], func=AF.Square,
                                                             accum_out=ssb.t[:, 0:1]), reads=[xt], writes=[junk, ssb])
                    rb = rs1.next()
                    self.rms_rstd(ssb, rb, 1, 1.0 / D, EPS)
                    yield
                    h = hb.next()
                    T.op(T.dve, lambda: nc.vector.scalar_tensor_tensor(out=h.t[:], in0=xt.t[:], scalar=rb.t[:, 0:1],
                                                                       in1=gv.t[:, G_MIX:G_MIX + D], op0=ALU.mult, op1=ALU.mult),
                         reads=[xt, rb, gv], writes=[h])
                    yield
                    hTb = hT.next()
                    self.tr8(h, pT.items[0], hTb)
                    st["hT"] = hTb

                def proj(col0, ncols):
                    z = pz.next()
                    hTb = st["hT"]

                    def mm():
                        ins = None
                        for c in range(8):
                            ins = nc.tensor.matmul(z.t[:, 0:ncols], lhsT=hTb.t[:, c, :], rhs=win[:, c, col0:col0 + ncols],
                                                   start=(c == 0), stop=(c == 7))
                        return ins
                    T.op(T.pe, mm, reads=[hTb] + winb, writes=[z])
                    return z

                def transpose_out(src_b, slabs, width, dst_b, dst_idx0, rows):
                    def now():
                        px = pX.next()

                        def trs():
                            ins = None
                            for s_ in range(slabs):
                                ins = nc.tensor.transpose(out=px.t[0:width, s_ * 128:(s_ + 1) * 128],
                                                          in_=src_b.t[:, s_ * width:(s_ + 1) * width], identity=self.ident.t[:])
                            return ins
                        T.op(T.pe, trs, reads=[src_b, self.ident], writes=[px])
                        T.op(T.act, lambda: nc.scalar.copy(
                            out=dst_b.t[0:rows, dst_idx0:dst_idx0 + slabs, 0:128],
                            in_=px.t[0:rows, 0:slabs * 128].rearrange("p (s t) -> p s t", t=128)),
                            reads=[px], writes=[dst_b])
                    defer(now)

                def rope_g(y_b, H, d, hd, c0, s0, lo0, out_b):
                    yv = y_b.t[:, 0:H * d].rearrange("p (h d) -> p h d", d=d)
                    ov = out_b.t[:, 0:H * d].rearrange("p (h d) -> p h d", d=d)
                    lo = yv[:, :, lo0:lo0 + hd]
                    hi = yv[:, :, lo0 + hd:lo0 + 2 * hd]
                    cs = rp.t[:, c0:c0 + hd].unsqueeze(1).to_broadcast([128, H, hd])
                    sn = rp.t[:, s0:s0 + hd].unsqueeze(1).to_broadcast([128, H, hd])
                    t1, t2, t3, t4 = tt.next(), tt.next(), tt.next(), tt.next()
                    T.op(T.dve, lambda: nc.vector.tensor_tensor(out=t1.t[:, 0:H, 0:hd], in0=lo, in1=cs, op=ALU.mult),
                         reads=[y_b, rp], writes=[t1])
                    T.op(E2, lambda: H2.tensor_tensor(out=t2.t[:, 0:H, 0:hd], in0=hi, in1=sn, op=ALU.mult),
                         reads=[y_b, rp], writes=[t2])
                    T.op(T.dve, lambda: nc.vector.tensor_tensor(out=t3.t[:, 0:H, 0:hd], in0=hi, in1=cs, op=ALU.mult),
                         reads=[y_b, rp], writes=[t3])
                    T.op(E2, lambda: H2.tensor_tensor(out=t4.t[:, 0:H, 0:hd], in0=lo, in1=sn, op=ALU.mult),
                         reads=[y_b, rp], writes=[t4])
                    yield
                    T.op(T.dve, lambda: nc.vector.tensor_tensor(out=ov[:, :, lo0:lo0 + hd], in0=t1.t[:, 0:H, 0:hd],
                                                                in1=t2.t[:, 0:H, 0:hd], op=ALU.subtract),
                         reads=[t1, t2], writes=[out_b])
                    T.op(E2, lambda: H2.tensor_tensor(out=ov[:, :, lo0 + hd:lo0 + 2 * hd], in0=t3.t[:, 0:H, 0:hd],
                                                      in1=t4.t[:, 0:H, 0:hd], op=ALU.add),
                         reads=[t3, t4], writes=[out_b])
                    if lo0 > 0:
                        T.op(E2, lambda: H2.tensor_copy(out=ov[:, :, 0:lo0], in_=yv[:, :, 0:lo0]),
                             reads=[y_b], writes=[out_b])
                    yield

                def qk_block(col0, H, goff, qscale, rope, dst_idx0):
                    d = 64
                    ncols = H * d
                    z = proj(col0, ncols)
                    yield
                    res = []
                    yield from headnorm_g(z, ncols, H, d, 1.0 if qscale else 1.0 / d, d * EPS if qscale else EPS, res)
                    rb_ = res[0]
                    y = yy.next()
                    T.op(T.dve, lambda: nc.vector.tensor_tensor(
                        out=y.t[:, 0:ncols].rearrange("p (h d) -> p h d", d=d),
                        in0=z.t[:, 0:ncols].rearrange("p (h d) -> p h d", d=d),
                        in1=rb_.t[:, 0:H].unsqueeze(2).to_broadcast([128, H, d]), op=ALU.mult),
                        reads=[z, rb_], writes=[y])
                    yield
                    ob = yb.next()
                    gb = gv.t[:, goff:goff + d].unsqueeze(1).to_broadcast([128, H, d])
                    if rope:
                        yg = y2.next()
                        T.op(E2, lambda: H2.tensor_tensor(
                            out=yg.t[:, 0:ncols].rearrange("p (h d) -> p h d", d=d),
                            in0=y.t[:, 0:ncols].rearrange("p (h d) -> p h d", d=d), in1=gb, op=ALU.mult),
                            reads=[y, gv], writes=[yg])
                        yield
                        yield from rope_g(yg, H, d, 32, 0, 32, 0, ob)
                    else:
                        T.op(E2, lambda: H2.tensor_tensor(
                            out=ob.t[:, 0:ncols].rearrange("p (h d) -> p h d", d=d),
                            in0=y.t[:, 0:ncols].rearrange("p (h d) -> p h d", d=d), in1=gb, op=ALU.mult),
                            reads=[y, gv], writes=[ob])
                        yield
                    transpose_out(ob, H // 2, 128, sQ, dst_idx0, 128)

                def v_block(col0, ncols, voff):
                    z = proj(col0, ncols)
                    yield
                    T.op(T.act, lambda: nc.scalar.copy(out=vb.t[:, voff:voff + ncols], in_=z.t[:, 0:ncols]),
                         reads=[z], writes=[vb])

                def mla():
                    z0 = proj(0, 416)
                    yield
                    cq = cqn.next()
                    res = []
                    yield from headnorm_g(z0, 256, 1, 256, 1.0 / 256, EPS, res)
                    rq = res[0]
                    T.op(T.dve, lambda: nc.vector.scalar_tensor_tensor(out=cq.t[:, 0:256], in0=z0.t[:, 0:256], scalar=rq.t[:, 0:1],
                                                                       in1=gv.t[:, G_CQ:G_CQ + 256], op0=ALU.mult, op1=ALU.mult),
                         reads=[z0, rq, gv], writes=[cq])
                    s_ = sq.next()
                    ssk = ssh.next()
                    T.op(T.act, lambda: nc.scalar.activation(out=s_.t[:, 0:128], in_=z0.t[:, 256:384], func=AF.Square,
                                                             accum_out=ssk.t[:, 0:1]), reads=[z0], writes=[s_, ssk])
                    yield
                    rk = rsh.next()
                    self.rms_rstd(ssk, rk, 1, 1.0 / 128, EPS)
                    yield
                    T.op(T.dve, lambda: nc.vector.scalar_tensor_tensor(out=cq.t[:, 256:384], in0=z0.t[:, 256:384], scalar=rk.t[:, 0:1],
                                                                       in1=gv.t[:, G_CKV:G_CKV + 128], op0=ALU.mult, op1=ALU.mult),
                         reads=[z0, rk, gv], writes=[cq])
                    kr = krb.next()
                    s2_ = sq.next()
                    sskr = sskr_r.next()
                    T.op(T.dve, lambda: nc.vector.tensor_copy(out=kr.t[:], in_=z0.t[:, 384:416]), reads=[z0], writes=[kr])
                    T.op(T.dve, lambda: nc.vector.tensor_tensor(out=s2_.t[:, 0:32], in0=kr.t[:], in1=kr.t[:], op=ALU.mult),
                         reads=[kr], writes=[s2_])
                    T.op(T.dve, lambda: nc.vector.tensor_reduce(out=sskr.t[:, 0:1], in_=s2_.t[:, 0:32], axis=AX.X, op=ALU.add),
                         reads=[s2_], writes=[sskr])
                    yield
                    p2 = pT.items[0]

                    def tr3():
                        ins = None
                        for c in range(3):
                            ins = nc.tensor.transpose(out=p2.t[:, c * 128:(c + 1) * 128], in_=cq.t[:, c * 128:(c + 1) * 128],
                                                      identity=self.ident.t[:])
                        return ins
                    T.op(T.pe, tr3, reads=[cq, self.ident], writes=[p2])
                    cTb = cT.next()
                    T.op(T.act, lambda: nc.scalar.copy(out=cTb.t[:].rearrange("p c t -> p (c t)"), in_=p2.t[:, 0:384]),
                         reads=[p2], writes=[cTb])
                    yield
                    for hq in range(2):
                        zq = pz.next()

                        def mmq():
                            ins = None
                            for c in range(2):
                                ins = nc.tensor.matmul(zq.t[:, 0:288], lhsT=cTb.t[:, c, :], rhs=wuq[:, c, hq * 288:(hq + 1) * 288],
                                                       start=(c == 0), stop=(c == 1))
                            return ins
                        T.op(T.pe, mmq, reads=[cTb] + wuqb, writes=[zq])
                        yield
                        res = []
                        yield from headnorm_g(zq, 288, 3, 96, 1.0, 96 * EPS, res)
                        rb_ = res[0]
                        y = yy.next()
                        T.op(T.dve, lambda: nc.vector.tensor_tensor(
                            out=y.t[:, 0:288].rearrange("p (h d) -> p h d", d=96),
                            in0=zq.t[:, 0:288].rearrange("p (h d) -> p h d", d=96),
                            in1=rb_.t[:, 0:3].unsqueeze(2).to_broadcast([128, 3, 96]), op=ALU.mult),
                            reads=[zq, rb_], writes=[y])
                        yield
                        yg = y2.next()
                        T.op(E2, lambda: H2.tensor_tensor(
                            out=yg.t[:, 0:288].rearrange("p (h d) -> p h d", d=96),
                            in0=y.t[:, 0:288].rearrange("p (h d) -> p h d", d=96),
                            in1=gv.t[:, G_MQN:G_MQN + 96].unsqueeze(1).to_broadcast([128, 3, 96]), op=ALU.mult),
                            reads=[y, gv], writes=[yg])
                        yield
                        ob = yb.next()
                        yield from rope_g(yg, 3, 96, 16, 64, 80, 64, ob)
                        transpose_out(ob, 3, 96, sM, hq * 3, 96)
                    zk = pz.next()
                    T.op(T.pe, lambda: nc.tensor.matmul(zk.t[:, 0:384], lhsT=cTb.t[:, 2, :], rhs=wuk[:, 0, :], start=True, stop=True),
                         reads=[cTb] + wukb, writes=[zk])
                    zv = pz.next()
                    T.op(T.pe, lambda: nc.tensor.matmul(zv.t[:, 0:384], lhsT=cTb.t[:, 2, :], rhs=wuv[:, 0, :], start=True, stop=True),
                         reads=[cTb] + wuvb, writes=[zv])
                    yield
                    T.op(T.act, lambda: nc.scalar.copy(out=vb.t[:, 0:384], in_=zv.t[:, 0:384]), reads=[zv], writes=[vb])
                    res = []
                    yield from headnorm_g(zk, 384, 6, 64, 1.0 / 96, EPS, res, extra_ss=sskr)
                    rbk = res[0]
                    yk = yy.next()
                    ykv = yk.t[:, 0:576].rearrange("p (h d) -> p h d", d=96)
                    T.op(T.dve, lambda: nc.vector.tensor_tensor(
                        out=ykv[:, :, 0:64], in0=zk.t[:, 0:384].rearrange("p (h d) -> p h d", d=64),
                        in1=rbk.t[:, 0:6].unsqueeze(2).to_broadcast([128, 6, 64]), op=ALU.mult),
                        reads=[zk, rbk], writes=[yk])
                    T.op(T.dve, lambda: nc.vector.tensor_tensor(
                        out=ykv[:, :, 64:96], in0=kr.t[:, :].unsqueeze(1).to_broadcast([128, 6, 32]),
                        in1=rbk.t[:, 0:6].unsqueeze(2).to_broadcast([128, 6, 32]), op=ALU.mult),
                        reads=[kr, rbk], writes=[yk])
                    yield
                    ykg = y2.next()
                    T.op(E2, lambda: H2.tensor_tensor(
                        out=ykg.t[:, 0:576].rearrange("p (h d) -> p h d", d=96), in0=ykv,
                        in1=gv.t[:, G_MKN:G_MKN + 96].unsqueeze(1).to_broadcast([128, 6, 96]), op=ALU.mult),
                        reads=[yk, gv], writes=[ykg])
                    yield
                    obk = yb.next()
                    yield from rope_g(ykg, 6, 96, 16, 64, 80, 64, obk)
                    transpose_out(obk, 6, 96, sM, 6, 96)

                def stores():
                    r0 = t * 128
                    T.dma(T.sp, self.Vm[r0:r0 + 128, :], vb.t[:, 0:384], vds[par], reads=[vb])
                    T.dma(T.sp, self.Vn[r0:r0 + 128, :], vb.t[:, 384:768], vds[par], reads=[vb])
                    T.dma(T.sp, self.Vd[r0:r0 + 128, :], vb.t[:, 768:1536], vds[par], reads=[vb])
                    for (dst, i0, ns) in ((self.QTn, 0, 3), (self.KTn, 3, 3), (self.QTd, 6, 6), (self.KTd, 12, 6)):
                        T.dma(T.sp, dst.rearrange("(s p) t -> p s t", p=128)[:, :, r0:r0 + 128], sQ.t[:, i0:i0 + ns, 0:128],
                              stq_ds[par], reads=[sQ])
                    T.dma(T.sp, self.QTm[:, :, r0:r0 + 128].rearrange("h d t -> d h t"), sM.t[:, 0:6, 0:128], stm_ds[par], reads=[sM])
                    T.dma(T.sp, self.KTm[:, :, r0:r0 + 128].rearrange("h d t -> d h t"), sM.t[:, 6:12, 0:128], stm_ds[par], reads=[sM])

                blocks = [
                    mla,
                    lambda: qk_block(416, 6, G_NQ, True, False, 0),
                    lambda: qk_block(800, 6, G_NK, False, False, 3),
                    lambda: qk_block(1568, 6, G_DQ, True, True, 6),
                    lambda: v_block(1184, 384, 384),
                    lambda: qk_block(1952, 6, G_DQ, True, True, 9),
                    lambda: qk_block(2336, 6, G_DK, False, True, 12),
                    lambda: v_block(3104, 384, 768),
                    lambda: qk_block(2720, 6, G_DK, False, True, 15),
                    lambda: v_block(3488, 384, 1152),
                ]
                return head, blocks, stores

            W = 3
            active = []

            def pump():
                for g in list(active):
                    try:
                        next(g)
                    except StopIteration:
                        active.remove(g)

            load_x(0)
            head0, blocks0, stores0 = tile_jobs(0)
            for _ in head0():
                pass
            cur = (blocks0, stores0)
            for t in range(NT):
                if t + 1 < NT:
                    load_x(t + 1)
                blocks, stores = cur
                for bf in blocks:
                    active.append(bf())
                    while len(active) >= W:
                        pump()
                if t + 1 < NT:
                    hd, nb, ns = tile_jobs(t + 1)
                    active.append(hd())
                    cur = (nb, ns)
                while active:
                    pump()
                defer(stores)
            while dq:
                dq.pop(0)()
            self.end_phase()

    def attention(self, S, slots, l, G=2, SKEW=2):
        T = self.T
        nc = self.nc
        I = self.inp
        NT = S // 128
        NQB = S // 512
        nstream = len(slots[0]["streams"])
        nset = 2 if nstream == 1 else 1
        use_dil = any(m[0] == "dil" for sl in slots for (_, _, ms) in sl["chunks"](0) for m in ms)
        dpad = slots[0]["streams"][0]["d"]
        self.begin_phase()
        with ExitStack() as ctx:
            sets = []
            for si in range(nset):
                st = []
                for k in range(nstream):
                    kt = T.sb(ctx, "kt", [128, S], BF16)
                    va = T.sb(ctx, "va", [128, NT, 128], BF16)
                    T.op(T.pool, lambda: nc.gpsimd.memset(va.t[:, :, 64:128], 1.0), writes=[va])
                    T.op(T.pool, lambda: nc.gpsimd.memset(kt.t[dpad:128, :], 0.0), writes=[kt])
                    st.append((kt, va, self.newds(), self.newds()))
                nat = T.sb(ctx, "nat", [128, NA_MM * 64], BF16)
                sets.append((st, nat, self.newds("pool")))
            dstrip = None
            if use_dil:
                dstrip = T.sb(ctx, "dstrip", [128, DIL_W], BF16)
                dsd = self.newds("pool")
                for c0 in range(0, DIL_W, 2048):
                    c1 = min(DIL_W, c0 + 2048)
                    T.dma(T.pool, dstrip.t[:, c0:c1], I["dstrip"][:, c0:c1], dsd, writes=[dstrip])
            qtb = [Rot([T.sb(ctx, "qtb", [128, 512], BF16) for _ in range(3)]) for _ in range(nstream)]
            for r_ in qtb:
                for b_ in r_.items:
                    T.op(T.pool, lambda: nc.gpsimd.memset(b_.t[dpad:128, :], 0.0), writes=[b_])
            qds = [[self.newds(), self.newds(), self.newds()] for _ in range(nstream)]
            ptr = Rot([T.sb(ctx, "pt", [128, 2, 512], BF16) for _ in range(5 if G == 2 else 8)])
            pS_t = [T.ps(ctx, "pS", [128, 1024], F32) for _ in range(3)]
            if G == 2:
                pS = Rot([(b_, 0) for b_ in pS_t])
            else:
                pS = Rot([(Buf("pSh", b_.t), off_) for b_ in pS_t for off_ in (0, 512)])
            pO = Rot([T.ps(ctx, "pO", [128, 512], F32) for _ in range(2)])
            rzr = Rot([T.sb(ctx, "rz", [64, 512], F32) for _ in range(2)])
            ost = [T.sb(ctx, "ost", [64, 512], BF16) for _ in range(3)]
            ods = [self.newds() for _ in range(3)]
            oi = 0

            def load_slot(i):
                sl = slots[i]
                st, nat, nds = sets[i % nset]
                for k, sm in enumerate(sl["streams"]):
                    kt, va, kds, vds_ = st[k]
                    d = sm["d"]
                    T.dma(T.sp, kt.t[0:d, :], sm["kt"], kds, writes=[kt])
                    for c0 in range(0, NT, 16):
                        c1 = min(NT, c0 + 16)
                        T.dma(T.sp, va.t[:, c0:c1, 0:64],
                              sm["v"][c0 * 128:c1 * 128, :].rearrange("(c p) d -> p c d", p=128), vds_, writes=[va])
                if sl.get("natab") is not None:
                    T.dma(T.pool, nat.t[:], sl["natab"], nds, writes=[nat])

            pending = []

            def emit_pv(G):
                grp, st_, pt, po, g0, total, fin = G

                def pv():
                    ins = None
                    for gi, (k, c, masks) in enumerate(grp):
                        va = st_[k][1]
                        idx = g0 + gi
                        ins = nc.tensor.matmul(po.t[:, :], lhsT=va.t[:, c, :], rhs=pt.t[:, gi, :],
                                               start=(idx == 0), stop=(idx == total - 1))
                    return ins
                T.op(T.pe, pv, reads=[pt] + [st_[k][1] for (k, _, _) in grp], writes=[po])
                if fin is not None:
                    row0, jq = fin
                    rz = rzr.next()
                    T.op(T.dve, lambda: nc.vector.reciprocal(out=rz.t[:, :], in_=po.t[64:128, :]), reads=[po], writes=[rz])
                    ob = ost[self._oi % 3]
                    T.op(T.dve, lambda: nc.vector.tensor_tensor(out=ob.t[:, :], in0=po.t[0:64, :], in1=rz.t[:, :], op=ALU.mult),
                         reads=[po, rz], writes=[ob])
                    T.dma(T.sp, self.OT[row0:row0 + 64, jq * 512:(jq + 1) * 512], ob.t[:, :], ods[self._oi % 3], reads=[ob])
                    self._oi += 1

            self._oi = 0
            load_slot(0)
            for i, sl in enumerate(slots):
                if nset == 2 and i + 1 < len(slots):
                    while pending:
                        emit_pv(pending.pop(0))
                    load_slot(i + 1)
                st, nat, _ = sets[i % nset]
                for jq in range(NQB):
                    qs = []
                    for k, sm in enumerate(sl["streams"]):
                        qb = qtb[k].next()
                        d = sm["d"]
                        T.dma(T.sp, qb.t[0:d, :], sm["qt"][:, jq * 512:(jq + 1) * 512], qds[k][(qtb[k].i - 1) % 3], writes=[qb])
                        qs.append(qb)
                    items = sl["chunks"](jq)
                    total = len(items)
                    po = pO.next()
                    for g0 in range(0, total, G):
                        grp = items[g0:g0 + G]
                        ps, poff = pS.next()

                        def qk():
                            ins = None
                            for gi, (k, c, masks) in enumerate(grp):
                                kt = st[k][0]
                                d = sl["streams"][k]["d"]
                                o_ = ps.t[:, poff + gi * 512:poff + (gi + 1) * 512]
                                mm_masks = [m for m in masks if m[0] != "dil"]
                                ins = nc.tensor.matmul(o_, lhsT=kt.t[:, c * 128:(c + 1) * 128], rhs=qs[k].t[:, :],
                                                       start=True, stop=(len(mm_masks) == 0))
                                masks = mm_masks
                                for mi, m in enumerate(masks):
                                    last = (mi == len(masks) - 1)
                                    if m[0] == "nat":
                                        ins = nc.tensor.matmul(o_, lhsT=self.ident.t[:], rhs=nat.t[:, m[1]:m[1] + 512],
                                                               start=False, stop=last)
                                    elif m[0] == "dil":
                                        pass
                                    else:
                                        ins = nc.tensor.matmul(o_.rearrange("p (j c) -> p j c", c=64), lhsT=self.ea.t[:, :],
                                                               rhs=self.rsmall.t[:, m[1] * 8:(m[1] + 1) * 8].unsqueeze(2).to_broadcast([128, 8, 64]),
                                                               start=False, stop=last)
                            return ins
                        rd = [st[k][0] for (k, _, _) in grp] + [qs[k] for (k, _, _) in grp] + [self.ident, self.ea, self.rsmall, nat]
                        if dstrip is not None:
                            rd.append(dstrip)
                        T.op(T.pe, qk, reads=rd, writes=[ps])
                        pt = ptr.next()
                        n = len(grp)
                        T.op(T.act, lambda: nc.scalar.activation(out=pt.t[:, 0:n, :].rearrange("p g q -> p (g q)"),
                                                                 in_=ps.t[:, poff:poff + n * 512], func=AF.Exp), reads=[ps], writes=[pt])
                        for gi, (k_, c_, masks_) in enumerate(grp):
                            for m in masks_:
                                if m[0] == "dil":
                                    self._mi = getattr(self, "_mi", 0) + 1
                                    if self._mi % 3 == 0:
                                        T.op(T.pool, lambda: nc.gpsimd.tensor_tensor(out=pt.t[:, gi, :], in0=pt.t[:, gi, :],
                                                                                     in1=dstrip.t[:, m[1]:m[1] + 512], op=ALU.mult),
                                             reads=[pt, dstrip], writes=[pt])
                                    else:
                                        T.op(T.dve, lambda: nc.vector.tensor_tensor(out=pt.t[:, gi, :], in0=pt.t[:, gi, :],
                                                                                    in1=dstrip.t[:, m[1]:m[1] + 512], op=ALU.mult),
                                             reads=[pt, dstrip], writes=[pt])
                        fin = (sl["row0"], jq) if g0 + G >= total else None
                        pending.append((grp, st, pt, po, g0, total, fin))
                        if len(pending) > SKEW:
                            emit_pv(pending.pop(0))
                if nset == 1:
                    while pending:
                        emit_pv(pending.pop(0))
                    if i + 1 < len(slots):
                        load_slot(i + 1)
            while pending:
                emit_pv(pending.pop(0))
            self.end_phase()

    def attn_mla(self, l, S):
        NT = S // 128
        slots = []
        for h in range(6):
            slots.append(dict(row0=h * 64, natab=None,
                              streams=[dict(qt=self.QTm[h, :, 0:S], kt=self.KTm[h, :, 0:S], v=self.Vm[0:S, h * 64:(h + 1) * 64], d=96)],
                              chunks=(lambda jq: [(0, c, []) for c in range(NT)])))
        self.attention(S, slots, l)

    def attn_na(self, l, S):
        NT = S // 128
        NQB = S // 512
        I = self.inp

        def chunks(jq):
            ty = 0 if jq == 0 else (2 if jq == NQB - 1 else 1)
            out = []
            for ci in range(8):
                c = 4 * jq - 2 + ci
                if 0 <= c < NT:
                    m0 = 11 - 2 * ci
                    out.append((0, c, [("nat", (m0 + 3) * 64), ("row", ty * 8 + ci)]))
            return out
        slots = []
        for h in range(6):
            slots.append(dict(row0=384 + h * 64, natab=I["natab"][l, h],
                              streams=[dict(qt=self.QTn[h * 64:(h + 1) * 64, 0:S], kt=self.KTn[h * 64:(h + 1) * 64, 0:S],
                                            v=self.Vn[0:S, h * 64:(h + 1) * 64], d=64)], chunks=chunks))
        self.attention(S, slots, l, G=1, SKEW=4)

    def attn_dil(self, l, S):
        NT = S // 128

        def chunks(jq):
            out = []
            for g in range(3):
                for Dd in range(DIL_DMIN[g], DIL_DMAX[g] + 1, 128):
                    c = (jq * 512 + Dd) // 128
                    if 0 <= c < NT:
                        out.append((g, c, [("dil", DIL_OFF[g] + DIL_DMAX[g] - Dd)]))
            return out
        slots = []
        for h in range(4):
            sts = []
            for g in range(3):
                hh = g * 4 + h
                sts.append(dict(qt=self.QTd[hh * 64:(hh + 1) * 64, 0:S], kt=self.KTd[hh * 64:(hh + 1) * 64, 0:S],
                                v=self.Vd[0:S, hh * 64:(hh + 1) * 64], d=64))
            slots.append(dict(row0=768 + h * 64, natab=None, streams=sts, chunks=chunks))
        self.attention(S, slots, l, G=1, SKEW=4)

    def phase_x(self, l, S, xsrc, mem):
        T = self.T
        nc = self.nc
        I = self.inp
        NT = S // 128
        GO = G_CROSS
        self.begin_phase()
        with ExitStack() as ctx:
            self.make_cbias(ctx, [EPS, 256 * EPS])
            wo, wob = self.load_w(ctx, "wo", I["w_o"][l], D, D)
            wcq, wcqb = self.load_w(ctx, "wcq", I["w_cq"][l], D, D)
            wco, wcob = self.load_w(ctx, "wco", I["w_co"][l], D, D)
            wkv, wkvb = self.load_w(ctx, "wkv", I["w_ckv"][l], D, 2 * D)
            gv = self.load_rep(ctx, "gv2", I["gvec"][l, GO:NG], NG - GO)
            g_cross, g_xq, g_ffn, g_mem, g_xk = 0, G_XQ - GO, G_FFN - GO, G_MEM - GO, G_XK - GO
            pA = T.ps(ctx, "pA", [128, 1024], F32)
            pC = T.ps(ctx, "pC", [128, 1024], F32)
            pD = T.ps(ctx, "pD", [128, 1024], F32)
            pT = T.ps(ctx, "pT", [128, 1024], BF16)
            pZ = T.ps(ctx, "pZ", [128, 512], F32)
            xts = [T.sb(ctx, "xt", [128, D], F32) for _ in range(3)]
            xds = [self.newds() for _ in range(3)]
            ots = [T.sb(ctx, "oT", [128, 8, 128], BF16) for _ in range(2)]
            otds = [self.newds() for _ in range(2)]
            junk = T.sb(ctx, "junk", [128, D], BF16)
            ss = Rot([T.sb(ctx, "ss", [128, 8], F32) for _ in range(6)])
            rs = Rot([T.sb(ctx, "rs", [128, 8], F32) for _ in range(6)])
            hb = Rot([T.sb(ctx, "hb", [128, D], BF16) for _ in range(4)])
            hT = Rot([T.sb(ctx, "hT", [128, 8, 128], BF16) for _ in range(3)])
            sqb = T.sb(ctx, "sqb", [128, D], F32)
            yf = T.sb(ctx, "yf", [128, D], F32)
            kmT = T.sb(ctx, "kmT", [128, 8, 256], BF16)
            vms = T.sb(ctx, "vms", [128, 2, D], BF16)
            ptb = Rot([T.sb(ctx, "ptb", [128, 8, 128], BF16) for _ in range(2)])
            rzb = T.sb(ctx, "rzb", [128, 512], F32)
            ocT = Rot([T.sb(ctx, "ocT", [128, 8, 128], BF16) for _ in range(2)])
            h3s = [T.sb(ctx, "h3s", [128, 8, 128], BF16) for _ in range(2)]
            h3ds = [self.newds() for _ in range(2)]
            zt = T.sb(ctx, "zt", [128, 8, 1], BF16)
            zds = self.newds()
            H3v = self.H3T.rearrange("(c p) t -> p c t", p=128)
            OTv = self.OT.rearrange("(c p) t -> p c t", p=128)

            def rmsnorm_T(xb, goff, hT_dst):
                ssb = ss.next()
                T.op(T.act, lambda: nc.scalar.activation(out=junk.t[:], in_=xb.t[:], func=AF.Square, accum_out=ssb.t[:, 0:1]),
                     reads=[xb], writes=[junk, ssb])
                rb = rs.next()
                self.rms_rstd(ssb, rb, 1, 1.0 / D, EPS)
                h = hb.next()
                T.op(T.dve, lambda: nc.vector.scalar_tensor_tensor(out=h.t[:], in0=xb.t[:], scalar=rb.t[:, 0:1],
                                                                   in1=gv.t[:, goff:goff + D], op0=ALU.mult, op1=ALU.mult),
                     reads=[xb, rb, gv], writes=[h])
                self.tr8(h, pT, hT_dst)

            def proj2(dst_ps, lhs_b, w, wb, coff=0):
                def mm():
                    ins = None
                    for nh in range(2):
                        for c in range(8):
                            ins = nc.tensor.matmul(dst_ps.t[:, nh * 512:(nh + 1) * 512], lhsT=lhs_b.t[:, c, :],
                                                   rhs=w[:, c, coff + nh * 512:coff + (nh + 1) * 512], start=(c == 0), stop=(c == 7))
                    return ins
                T.op(T.pe, mm, reads=[lhs_b] + wb, writes=[dst_ps])

            def headnorm4(src_ps, goff, sc, bias, out_b):
                T.op(T.act, lambda: nc.scalar.activation(out=sqb.t[:], in_=src_ps.t[:], func=AF.Square), reads=[src_ps], writes=[sqb])
                ssb = ss.next()
                T.op(T.dve, lambda: nc.vector.tensor_reduce(out=ssb.t[:, 0:4], in_=sqb.t[:].rearrange("p (h d) -> p h d", d=256),
                                                            axis=AX.X, op=ALU.add), reads=[sqb], writes=[ssb])
                rb = rs.next()
                self.rms_rstd(ssb, rb, 4, sc, bias)
                T.op(T.dve, lambda: nc.vector.tensor_tensor(out=yf.t[:].rearrange("p (h d) -> p h d", d=256),
                                                            in0=src_ps.t[:].rearrange("p (h d) -> p h d", d=256),
                                                            in1=rb.t[:, 0:4].unsqueeze(2).to_broadcast([128, 4, 256]), op=ALU.mult),
                     reads=[src_ps, rb], writes=[yf])
                T.op(T.dve, lambda: nc.vector.tensor_tensor(out=out_b.t[:].rearrange("p (h d) -> p h d", d=256),
                                                            in0=yf.t[:].rearrange("p (h d) -> p h d", d=256),
                                                            in1=gv.t[:, goff:goff + 256].unsqueeze(1).to_broadcast([128, 4, 256]), op=ALU.mult),
                     reads=[yf, gv], writes=[out_b])

            for mt in range(2):
                xb = xts[mt]
                T.dma(T.sp, xb.t[:], mem[mt * 128:(mt + 1) * 128, :], xds[mt], writes=[xb])
                mT = hT.next()
                rmsnorm_T(xb, g_mem, mT)
                proj2(pA, mT, wkv, wkvb, 0)
                kn = hb.next()
                headnorm4(pA, g_xk, 1.0 / 256, EPS, kn)

                def trk():
                    ins = None
                    for c in range(8):
                        ins = nc.tensor.transpose(out=pT.t[:, c * 128:(c + 1) * 128], in_=kn.t[:, c * 128:(c + 1) * 128],
                                                  identity=self.ident.t[:])
                    return ins
                T.op(T.pe, trk, reads=[kn, self.ident], writes=[pT])
                T.op(T.act, lambda: nc.scalar.copy(out=kmT.t[:, :, mt * 128:(mt + 1) * 128],
                                                   in_=pT.t[:].rearrange("p (c t) -> p c t", t=128)), reads=[pT], writes=[kmT])
                proj2(pC, mT, wkv, wkvb, D)
                T.op(T.act, lambda: nc.scalar.copy(out=vms.t[:, mt, :], in_=pC.t[:]), reads=[pC], writes=[vms])

            def loads(t):
                xb = xts[t % 3]
                T.dma(T.sp, xb.t[:], xsrc[t * 128:(t + 1) * 128, :], xds[t % 3], writes=[xb])
                ob = ots[t % 2]
                T.dma(T.sp, ob.t[:], OTv[:, :, t * 128:(t + 1) * 128], otds[t % 2], writes=[ob])

            qcTs = [T.sb(ctx, "qcT", [128, 8, 128], BF16) for _ in range(2)]

            def genA(t):
                xb = xts[t % 3]
                ob = ots[t % 2]
                proj2(pA, ob, wo, wob)
                T.op(T.dve, lambda: nc.vector.tensor_tensor(out=xb.t[:], in0=pA.t[:], in1=xb.t[:], op=ALU.add), reads=[pA, xb], writes=[xb])
                yield
                h2T = hT.next()
                rmsnorm_T(xb, g_cross, h2T)
                yield
                proj2(pA, h2T, wcq, wcqb)
                qn = hb.next()
                headnorm4(pA, g_xq, 1.0, 256 * EPS, qn)
                yield
                self.tr8(qn, pT, qcTs[t % 2])

            def genB(t):
                xb = xts[t % 3]
                qcT = qcTs[t % 2]

                def sc_mm():
                    ins = None
                    for hh in range(4):
                        for kc in range(2):
                            o_ = pC.t[:, (hh * 2 + kc) * 128:(hh * 2 + kc + 1) * 128]
                            for dc in range(2):
                                ins = nc.tensor.matmul(o_, lhsT=kmT.t[:, hh * 2 + dc, kc * 128:(kc + 1) * 128], rhs=qcT.t[:, hh * 2 + dc, :],
                                                       start=(dc == 0), stop=(dc == 1))
                    return ins
                T.op(T.pe, sc_mm, reads=[kmT, qcT], writes=[pC])
                pt = ptb.next()
                T.op(T.act, lambda: nc.scalar.activation(out=pt.t[:].rearrange("p c q -> p (c q)"), in_=pC.t[:], func=AF.Exp),
                     reads=[pC], writes=[pt])
                yield

                def pv_mm():
                    ins = None
                    for hh in range(4):
                        for kc in range(2):
                            ins = nc.tensor.matmul(pZ.t[:, hh * 128:(hh + 1) * 128], lhsT=self.ones.t[:], rhs=pt.t[:, hh * 2 + kc, :],
                                                   start=(kc == 0), stop=(kc == 1))
                        for dvc in range(2):
                            for kc in range(2):
                                ins = nc.tensor.matmul(pD.t[:, (hh * 2 + dvc) * 128:(hh * 2 + dvc + 1) * 128],
                                                       lhsT=vms.t[:, kc, hh * 256 + dvc * 128:hh * 256 + (dvc + 1) * 128],
                                                       rhs=pt.t[:, hh * 2 + kc, :], start=(kc == 0), stop=(kc == 1))
                    return ins
                T.op(T.pe, pv_mm, reads=[pt, vms, self.ones], writes=[pZ, pD])
                T.op(T.dve, lambda: nc.vector.reciprocal(out=rzb.t[:], in_=pZ.t[:]), reads=[pZ], writes=[rzb])
                oc = ocT.next()
                T.op(T.dve, lambda: nc.vector.tensor_tensor(
                    out=oc.t[:].rearrange("p (h e) q -> p h e q", e=2),
                    in0=pD.t[:].rearrange("p (h e q) -> p h e q", e=2, q=128),
                    in1=rzb.t[:].rearrange("p (h q) -> p h q", q=128).unsqueeze(2).to_broadcast([128, 4, 2, 128]), op=ALU.mult),
                    reads=[pD, rzb], writes=[oc])
                yield
                proj2(pC, oc, wco, wcob)
                T.op(T.dve, lambda: nc.vector.tensor_tensor(out=xb.t[:], in0=pC.t[:], in1=xb.t[:], op=ALU.add), reads=[pC, xb], writes=[xb])
                T.dma(T.sp, self.X[t * 128:(t + 1) * 128, :], xb.t[:], xds[t % 3], reads=[xb])
                yield
                h3 = h3s[t % 2]
                rmsnorm_T(xb, g_ffn, h3)
                T.dma(T.sp, H3v[:, :, 1 + t * 128:1 + (t + 1) * 128], h3.t[:], h3ds[t % 2], reads=[h3])

            loads(0)
            if NT > 1:
                loads(1)
            for _ in genA(0):
                pass
            for t in range(NT):
                if t + 2 < NT:
                    loads(t + 2)
                gens = [genB(t)]
                if t + 1 < NT:
                    gens.insert(0, genA(t + 1))
                while gens:
                    for g in list(gens):
                        try:
                            next(g)
                        except StopIteration:
                            gens.remove(g)
            self.end_phase()

    def tr8(self, src_b, pT, dst_b):
        T = self.T
        nc = self.nc

        def tr():
            ins = None
            for c in range(8):
                ins = nc.tensor.transpose(out=pT.t[:, c * 128:(c + 1) * 128], in_=src_b.t[:, c * 128:(c + 1) * 128],
                                          identity=self.ident.t[:])
            return ins
        T.op(T.pe, tr, reads=[src_b, self.ident], writes=[pT])
        T.op(T.act, lambda: nc.scalar.copy(out=dst_b.t[:].rearrange("p c t -> p (c t)"), in_=pT.t[:]), reads=[pT], writes=[dst_b])

    def ffn_up(self, l, S):
        T = self.T
        nc = self.nc
        I = self.inp
        nblk = (S + 509) // 510
        blocks = []
        for b in range(nblk):
            c0 = 510 * b
            w = min(512, S + 2 - c0)
            blocks.append((c0, w))
        passes = [blocks[i:i + 9] for i in range(0, nblk, 9)]
        H3v = self.H3T.rearrange("(c p) t -> p c t", p=128)
        WUv = I["w_up"][l].rearrange("(c p) n -> p c n", p=128)
        self.begin_phase()
        with ExitStack() as ctx:
            cp = T.sb(ctx, "convp", [128, 4, 44], F32)
            T.dma(T.sp, cp.t[:], I["convp"][l], self.newds(), writes=[cp])
            wab = [T.sb(ctx, "wab", [128, 8, 256], BF16) for _ in range(2)]
            wds = [self.newds("pool") for _ in range(2)]
            maxc = max(pb[-1][0] + pb[-1][1] - pb[0][0] for pb in passes)
            hres = T.sb(ctx, "hres", [128, 8, maxc], BF16)
            hds = self.newds()
            pU = Rot([T.ps(ctx, "pU", [128, 512], F32) for _ in range(6)])
            ca = Rot([T.sb(ctx, "ca", [128, 512], F32) for _ in range(4)])
            cg = Rot([T.sb(ctx, "cg", [128, 512], F32) for _ in range(4)])
            sg = Rot([T.sb(ctx, "sg", [128, 512], F32) for _ in range(3)])
            ast = [T.sb(ctx, "ast", [128, 512], BF16) for _ in range(4)]
            ads = [self.newds() for _ in range(4)]
            ai = 0

            def load_wab(i):
                b = wab[i % 2]
                T.dma(T.pool, b.t[:, :, 0:128], WUv[:, :, i * 128:(i + 1) * 128], wds[i % 2], writes=[b])
                T.dma(T.pool, b.t[:, :, 128:256], WUv[:, :, DFF + i * 128:DFF + (i + 1) * 128], wds[i % 2], writes=[b])

            for pb in passes:
                col_lo = pb[0][0]
                col_hi = pb[-1][0] + pb[-1][1]
                v_lo = max(col_lo, 1)
                v_hi = min(col_hi, S + 1)
                T.dma(T.sp, hres.t[:, :, v_lo - col_lo:v_hi - col_lo], H3v[:, :, v_lo:v_hi], hds, writes=[hres])
                if col_lo == 0:
                    T.op(T.pool, lambda: nc.gpsimd.memset(hres.t[:, :, 0:1], 0.0), writes=[hres])
                if col_hi == S + 2:
                    T.op(T.pool, lambda: nc.gpsimd.memset(hres.t[:, :, col_hi - col_lo - 1:col_hi - col_lo], 0.0), writes=[hres])
                load_wab(0)
                for i in range(22):
                    if i + 1 < 22:
                        load_wab(i + 1)
                    wb = wab[i % 2]
                    for (c0, w) in pb:
                        off = c0 - col_lo
                        ua = pU.next()
                        ug = pU.next()

                        def mm():
                            ins = None
                            for (dst, wo_) in ((ua, 0), (ug, 128)):
                                for c in range(8):
                                    ins = nc.tensor.matmul(dst.t[:, 0:w], lhsT=wb.t[:, c, wo_:wo_ + 128], rhs=hres.t[:, c, off:off + w],
                                                           start=(c == 0), stop=(c == 7))
                            return ins
                        T.op(T.pe, mm, reads=[wb, hres], writes=[ua, ug])
                        n = w - 2
                        outs = []
                        for (u, col, dstr) in ((ua, i, ca), (ug, 22 + i, cg)):
                            cb_ = dstr.next()
                            if n >= 256:
                                T.op(T.act, lambda: nc.scalar.activation(out=cb_.t[:, 0:n], in_=u.t[:, 0:n], func=AF.Identity,
                                                                         scale=cp.t[:, 0, col:col + 1], bias=cp.t[:, 3, col:col + 1]),
                                     reads=[u, cp], writes=[cb_])
                            else:
                                T.op(T.dve, lambda: nc.vector.tensor_scalar(out=cb_.t[:, 0:n], in0=u.t[:, 0:n], scalar1=cp.t[:, 0, col:col + 1],
                                                                            scalar2=cp.t[:, 3, col:col + 1], op0=ALU.mult, op1=ALU.add),
                                     reads=[u, cp], writes=[cb_])
                            T.op(T.dve, lambda: nc.vector.scalar_tensor_tensor(out=cb_.t[:, 0:n], in0=u.t[:, 1:n + 1], scalar=cp.t[:, 1, col:col + 1],
                                                                               in1=cb_.t[:, 0:n], op0=ALU.mult, op1=ALU.add),
                                 reads=[u, cp, cb_], writes=[cb_])
                            T.op(T.dve, lambda: nc.vector.scalar_tensor_tensor(out=cb_.t[:, 0:n], in0=u.t[:, 2:n + 2], scalar=cp.t[:, 2, col:col + 1],
                                                                               in1=cb_.t[:, 0:n], op0=ALU.mult, op1=ALU.add),
                                 reads=[u, cp, cb_], writes=[cb_])
                            outs.append(cb_)
                        sgb = sg.next()
                        T.op(T.act, lambda: nc.scalar.activation(out=sgb.t[:, 0:n], in_=outs[1].t[:, 0:n], func=AF.Silu), reads=[outs[1]], writes=[sgb])
                        ab = ast[ai % 4]
                        T.op(T.pool, lambda: nc.gpsimd.tensor_tensor(out=ab.t[:, 0:n], in0=sgb.t[:, 0:n], in1=outs[0].t[:, 0:n], op=ALU.mult),
                             reads=[sgb, outs[0]], writes=[ab])
                        T.dma(T.sp, self.ACTT[i * 128:(i + 1) * 128, c0:c0 + n], ab.t[:, 0:n], ads[ai % 4], reads=[ab])
                        ai += 1
            self.end_phase()

    def ffn_down(self, l, S, dst):
        T = self.T
        nc = self.nc
        I = self.inp
        NT = S // 128
        AV = self.ACTT.rearrange("(c p) t -> p c t", p=128)
        self.begin_phase()
        with ExitStack() as ctx:
            wd, wdb = self.load_w(ctx, "wd", I["w_down"][l], DFF, D)
            pY = Rot([T.ps(ctx, "pY", [128, 1024], F32) for _ in range(2)])
            xts = [T.sb(ctx, "xt", [128, D], F32) for _ in range(3)]
            xds = [self.newds() for _ in range(3)]
            ats = [T.sb(ctx, "aT", [128, 22, 128], BF16) for _ in range(2)]
            atds = [self.newds() for _ in range(2)]

            def loads(t):
                T.dma(T.sp, xts[t % 3].t[:], self.X[t * 128:(t + 1) * 128, :], xds[t % 3], writes=[xts[t % 3]])
                T.dma(T.sp, ats[t % 2].t[:], AV[:, :, t * 128:(t + 1) * 128], atds[t % 2], writes=[ats[t % 2]])
            loads(0)
            for t in range(NT):
                if t + 1 < NT:
                    loads(t + 1)
                xb = xts[t % 3]
                ab = ats[t % 2]
                py = pY.next()

                def mm():
                    ins = None
                    for nh in range(2):
                        for c in range(22):
                            ins = nc.tensor.matmul(py.t[:, nh * 512:(nh + 1) * 512], lhsT=ab.t[:, c, :], rhs=wd[:, c, nh * 512:(nh + 1) * 512],
                                                   start=(c == 0), stop=(c == 21))
                    return ins
                T.op(T.pe, mm, reads=[ab] + wdb, writes=[py])
                T.op(T.dve, lambda: nc.vector.tensor_tensor(out=xb.t[:], in0=py.t[:], in1=xb.t[:], op=ALU.add), reads=[py, xb], writes=[xb])
                T.dma(T.sp, dst[t * 128:(t + 1) * 128, :], xb.t[:], xds[t % 3], reads=[xb])
            self.end_phase()


def Buf_view(b):
    return b


def _rope_tab(half):
    pos = np.arange(SS_, dtype=np.float32)
    inv = (np.float32(10000.0) ** (-np.arange(half, dtype=np.float32) / np.float32(half))).astype(np.float32)
    ang = (pos[:, None] * inv[None, :]).astype(np.float32)
    c = np.cos(ang).astype(np.float32).reshape(64, 128, half).transpose(1, 0, 2)
    s = np.sin(ang).astype(np.float32).reshape(64, 128, half).transpose(1, 0, 2)
    return np.ascontiguousarray(c), np.ascontiguousarray(s)


def _dil_strips():
    out = np.zeros((128, DIL_W), np.float32)
    p = np.arange(128)[:, None]
    for g in range(3):
        r = DIL_R[g]
        w = DIL_DMAX[g] - DIL_DMIN[g] + 512
        x = np.arange(w)[None, :]
        delta = p - x + DIL_DMAX[g]
        ok = (delta % r == 0) & (np.abs(delta) <= 64 * r)
        out[:, DIL_OFF[g]:DIL_OFF[g] + w] = np.where(ok, 1.0, 0.0)
    return out


def _na_rowmask():
    R = 1000
    out = np.full((2, 3 * 8 * 8), NEG, np.float32)
    for ty, r0 in ((0, 0), (1, 496), (2, R - 8)):
        for ci in range(8):
            kr0 = r0 - 4 + 2 * ci
            for a in range(2):
                kr = kr0 + a
                for j in range(8):
                    r = r0 + j
                    rs = min(max(r - 4, 0), R - 8)
                    if rs <= kr < rs + 8:
                        out[a, (ty * 8 + ci) * 8 + j] = 0.0
    return out


def _na_table(rpb):
    Lh = rpb.shape[0]
    kc = np.arange(64)[:, None]
    c = np.arange(64)[None, :]
    cs = np.clip(c - 8, 0, 48)
    mcol = (kc >= cs) & (kc < cs + 16)
    dcidx = np.clip(kc - c + 15, 0, 30)
    out = np.full((Lh, 6, 128, NA_MM, 64), NEG, np.float32)
    for a in range(2):
        for mm in range(-3, NA_MM - 3):
            m = mm - a
            if 0 <= m <= 14:
                vals = rpb[:, :, 14 - m, :][:, :, dcidx]
                out[:, :, a * 64:(a + 1) * 64, mm + 3, :] = np.where(mcol[None, None], vals, np.float32(NEG))
    return np.ascontiguousarray(out.reshape(Lh, 6, 128, NA_MM * 64))


def prep_inputs(inputs, n_cores=8):
    f = lambda a: np.ascontiguousarray(np.asarray(a, dtype=np.float32))
    gv = np.concatenate([f(inputs[k]) for k in ("norm_mix", "mla_q_norm", "mla_kv_norm", "mla_qn", "mla_kn", "na_qn", "na_kn",
                                                "dil_qn", "dil_kn", "norm_cross", "x_qn", "norm_ffn", "norm_mem", "x_kn")], axis=1)
    assert gv.shape == (L, NG)
    cw = f(inputs["conv_w"])
    cbv = f(inputs["conv_b"])
    convp = np.concatenate([cw, cbv[:, None, :]], axis=1).reshape(L, 4, 44, 128).transpose(0, 3, 1, 2)
    cosd, sind = _rope_tab(32)
    cosm, sinm = _rope_tab(16)
    ea = np.zeros((2, 128), np.float32)
    ea[0, :64] = 1.0
    ea[1, 64:] = 1.0
    shared = {
        "gvec": np.ascontiguousarray(gv), "convp": np.ascontiguousarray(convp), "natab": _na_table(f(inputs["na_rpb"])),
        "ident": np.eye(128, dtype=np.float32), "cosd": cosd, "sind": sind, "cosm": cosm, "sinm": sinm,
        "dstrip": _dil_strips(), "rsmall": _na_rowmask(), "ea": ea,
    }
    for k in ("w_in", "w_uq", "w_uk", "w_uv", "w_o", "w_cq", "w_ckv", "w_co", "w_up", "w_down"):
        shared[k] = f(inputs[k])
    xp = f(inputs["x_prompt"])
    xs = f(inputs["x_sample"])
    mp = f(inputs["mem_prompt"])
    ms = f(inputs["mem_sample"])
    zs = np.zeros((SS_, D), np.float32)
    zm = np.zeros((MEM, D), np.float32)
    maps = []
    for c in range(n_cores):
        m = dict(shared)
        m["xp"] = xp[c]
        m["memp"] = mp[c]
        if c == 0:
            m["xs"], m["mems"] = xs[0], ms[0]
        elif c == 4:
            m["xs"], m["mems"] = xs[1], ms[1]
        else:
            m["xs"], m["mems"] = zs, zm
        maps.append(m)
    return maps


def kernel(**inputs):
    cfg = {"parts": [("p", SP_), ("s", SS_)]}
    nc = Prog(cfg).build()
    maps = prep_inputs(inputs)
    res = run_bass_kernel_spmd(nc, maps, core_ids=list(range(8)))
    yp = np.stack([np.asarray(res.results[c]["yp"], dtype=np.float32) for c in range(8)], axis=0)
    ys = np.stack([np.asarray(res.results[c]["ys"], dtype=np.float32) for c in (0, 4)], axis=0)
    return (yp, ys)
```

```python
import numpy as np
import concourse.bass as bass
import concourse.mybir as mybir
from concourse.bass_utils import run_bass_kernel_spmd
from contextlib import ExitStack

F32 = mybir.dt.float32
BF16 = mybir.dt.bfloat16
AF = mybir.ActivationFunctionType
ALU = mybir.AluOpType
AX = mybir.AxisListType

D = 1024
L = 4
EPS = 1e-6
NEG = -30000.0
D_IN = 3872
DFF = 2816
SP_ = 2048
SS_ = 8192
MEM = 256
G_MIX, G_CQ, G_CKV, G_MQN, G_MKN, G_NQ, G_NK, G_DQ, G_DK, G_CROSS, G_XQ, G_FFN, G_MEM, G_XK = (
    0, 1024, 1280, 1408, 1504, 1600, 1664, 1728, 1792, 1856, 2880, 3136, 4160, 5184)
NG = 5440
DIL_R = (1, 4, 16)
DIL_DMIN = (-128, -256, -1024)
DIL_DMAX = (512, 640, 1408)
DIL_OFF = (0, 1152, 2560)
DIL_W = 5504
NA_MM = 22


class Ev:
    __slots__ = ("sem", "val", "eng", "ds")

    def __init__(self, sem, val, eng, ds=None):
        self.sem = sem
        self.val = val
        self.eng = eng
        self.ds = ds


class Buf:
    def __init__(self, name, t=None):
        self.name = name
        self.t = t
        self.w = []
        self.r = {}
        self.rd = []


class DS:
    def __init__(self, sem):
        self.sem = sem
        self.cnt = 0


class Eng:
    def __init__(self, name, h, pe=False):
        self.name = name
        self.h = h
        self.pe = pe
        self.sem = None
        self.count = 0
        self.waited = {}
        self.nsem = 0


class Tracker:
    EPOCH = 1 << 20

    def __init__(self, nc, es):
        self.nc = nc
        self.es = es
        self.uid = 0
        self.pe = Eng("pe", nc.tensor, True)
        self.act = Eng("act", nc.scalar)
        self.dve = Eng("dve", nc.vector)
        self.pool = Eng("pool", nc.gpsimd)
        self.sp = Eng("sp", nc.sync)
        self.engs = [self.pe, self.act, self.dve, self.pool, self.sp]
        for e in self.engs:
            e.sem = self.newsem(e.name + "_s0")
        self.free_ds = {}
        self.all_ds = []
        self.bar = DS(self.newsem("bar"))
        self.bar.kind = "sp"
        self.ninstr = 0

    def newsem(self, name):
        return self.es.enter_context(self.nc.semaphore(name))

    def get_ds(self, kind="sp"):
        fl = self.free_ds.setdefault(kind, [])
        if fl:
            return fl.pop()
        ds = DS(self.newsem("ds%s%d" % (kind, len(self.all_ds))))
        ds.kind = kind
        self.all_ds.append(ds)
        return ds

    def sb(self, ctx, name, shape, dt):
        self.uid += 1
        t = ctx.enter_context(self.nc.sbuf_tensor("%s_%d" % (name, self.uid), list(shape), dt))
        return Buf(name, t)

    def ps(self, ctx, name, shape, dt):
        self.uid += 1
        t = ctx.enter_context(self.nc.psum_tensor("%s_%d" % (name, self.uid), list(shape), dt))
        return Buf(name, t)

    def wait(self, eng, ev):
        if ev.eng is eng and eng.pe:
            return
        k = id(ev.sem)
        if eng.waited.get(k, 0) >= ev.val:
            return
        val = ev.ds.cnt if ev.ds is not None else ev.val
        eng.h.wait_ge(ev.sem, val)
        eng.waited[k] = val

    def deps(self, eng, reads, writes):
        for b in reads:
            for ev in b.w:
                self.wait(eng, ev)
        for b in writes:
            for ev in b.w:
                self.wait(eng, ev)
            for ev in b.r.values():
                self.wait(eng, ev)
            for ev in b.rd:
                self.wait(eng, ev)

    def done(self, ev, reads, writes):
        for b in reads:
            if ev.eng is None:
                b.rd.append(ev)
            else:
                b.r[ev.eng.name] = ev
        for b in writes:
            b.w = [ev]
            b.r = {}
            b.rd = []

    def op(self, eng, fn, reads=(), writes=()):
        self.deps(eng, reads, writes)
        ins = fn()
        eng.count += 1
        ins.then_inc(eng.sem, 1)
        self.ninstr += 1
        ev = Ev(eng.sem, eng.count, eng)
        self.done(ev, reads, writes)
        if eng.count >= self.EPOCH:
            eng.nsem += 1
            eng.sem = self.newsem("%s_s%d" % (eng.name, eng.nsem))
            eng.count = 0
        return ev

    def dma(self, q, out, in_, ds, reads=(), writes=()):
        assert ds.kind == ("pool" if q is self.pool else "sp"), (ds.kind, q.name)
        self.deps(q, reads, writes)
        ins = q.h.dma_start(out=out, in_=in_)
        ds.cnt += 16
        ins.then_inc(ds.sem, 16)
        self.ninstr += 1
        ev = Ev(ds.sem, ds.cnt, None, ds)
        self.done(ev, reads, writes)
        return ev

    def barrier(self, dummy_src, dummy_dst):
        sp = self.sp
        for e in self.engs:
            if e is not sp and e.count > 0:
                self.wait(sp, Ev(e.sem, e.count, e))
        for ds in self.all_ds:
            if ds.cnt > 0:
                self.wait(sp, Ev(ds.sem, ds.cnt, None, ds))
        ins = sp.h.dma_start(out=dummy_dst, in_=dummy_src)
        self.bar.cnt += 16
        ins.then_inc(self.bar.sem, 16)
        for e in self.engs:
            e.h.wait_ge(self.bar.sem, self.bar.cnt)


class Rot:
    def __init__(self, items):
        self.items = items
        self.i = 0

    def next(self):
        b = self.items[self.i % len(self.items)]
        self.i += 1
        return b


class Prog:
    def __init__(self, cfg):
        self.cfg = cfg
        self.nc = bass.Bass("TRN2", target_bir_lowering=False)
        self.es = ExitStack()

    def din(self, name, shape, dt=F32):
        return self.nc.dram_tensor(name, list(shape), dt, kind="ExternalInput").ap()

    def dscr(self, name, shape, dt, dbg=False):
        kind = "ExternalOutput" if (dbg and self.cfg.get("debug")) else "Internal"
        return self.nc.dram_tensor(name, list(shape), dt, kind=kind).ap()

    def build(self):
        nc = self.nc
        cfg = self.cfg
        parts = cfg["parts"]
        self.inp = {}
        I = self.inp
        I["xp"] = self.din("xp", [SP_, D])
        I["xs"] = self.din("xs", [SS_, D])
        I["memp"] = self.din("memp", [MEM, D])
        I["mems"] = self.din("mems", [MEM, D])
        I["w_in"] = self.din("w_in", [L, D, D_IN])
        I["w_uq"] = self.din("w_uq", [L, 256, 576])
        I["w_uk"] = self.din("w_uk", [L, 128, 384])
        I["w_uv"] = self.din("w_uv", [L, 128, 384])
        I["w_o"] = self.din("w_o", [L, D, D])
        I["w_cq"] = self.din("w_cq", [L, D, D])
        I["w_ckv"] = self.din("w_ckv", [L, D, 2 * D])
        I["w_co"] = self.din("w_co", [L, D, D])
        I["w_up"] = self.din("w_up", [L, D, 2 * DFF])
        I["w_down"] = self.din("w_down", [L, DFF, D])
        I["gvec"] = self.din("gvec", [L, NG])
        I["convp"] = self.din("convp", [L, 128, 4, 44])
        I["natab"] = self.din("natab", [L, 6, 128, NA_MM * 64])
        I["ident"] = self.din("ident", [128, 128])
        I["cosd"] = self.din("cosd", [128, 64, 32])
        I["sind"] = self.din("sind", [128, 64, 32])
        I["cosm"] = self.din("cosm", [128, 64, 16])
        I["sinm"] = self.din("sinm", [128, 64, 16])
        I["dstrip"] = self.din("dstrip", [128, DIL_W])
        I["rsmall"] = self.din("rsmall", [2, 3 * 8 * 8])
        I["ea"] = self.din("ea", [2, 128])
        self.yp = nc.dram_tensor("yp", [SP_, D], F32, kind="ExternalOutput").ap()
        self.ys = nc.dram_tensor("ys", [SS_, D], F32, kind="ExternalOutput").ap()
        SM = max(S for _, S in parts)
        self.SM = SM
        dbg = True
        self.X = self.dscr("X", [SM, D], F32, dbg)
        self.QTn = self.dscr("QTn", [384, SM], BF16, dbg)
        self.KTn = self.dscr("KTn", [384, SM], BF16, dbg)
        self.Vn = self.dscr("Vn", [SM, 384], BF16, dbg)
        self.QTd = self.dscr("QTd", [768, SM], BF16, dbg)
        self.KTd = self.dscr("KTd", [768, SM], BF16, dbg)
        self.Vd = self.dscr("Vd", [SM, 768], BF16, dbg)
        self.QTm = self.dscr("QTm", [6, 96, SM], BF16, dbg)
        self.KTm = self.dscr("KTm", [6, 96, SM], BF16, dbg)
        self.Vm = self.dscr("Vm", [SM, 384], BF16, dbg)
        self.OT = self.dscr("OT", [D, SM], BF16, dbg)
        self.H3T = self.dscr("H3T", [D, SM + 2], BF16, dbg)
        self.ACTT = self.dscr("ACTT", [DFF, SM], BF16, dbg)
        self.dum0 = self.dscr("dum0", [1, 16], F32)
        self.dum1 = self.dscr("dum1", [1, 16], F32)

        with self.es:
            self.T = Tracker(nc, self.es)
            T = self.T
            self.consts()
            for (pname, S) in parts:
                xin = I["xp"] if pname == "p" else I["xs"]
                mem = I["memp"] if pname == "p" else I["mems"]
                yout = self.yp if pname == "p" else self.ys
                nl = cfg.get("layers", L)
                for l in range(nl):
                    last = (l == nl - 1)
                    xsrc = xin if l == 0 else self.X
                    ph = cfg.get("phases", "1nmdxfg")
                    if "1" in ph:
                        self.phase1(l, S, xsrc)
                    if "n" in ph:
                        self.attn_na(l, S)
                    if "d" in ph:
                        self.attn_dil(l, S)
                    if "m" in ph:
                        self.attn_mla(l, S)
                    if "x" in ph:
                        self.phase_x(l, S, xsrc, mem)
                    if "f" in ph:
                        self.ffn_up(l, S)
                    if "g" in ph:
                        self.ffn_down(l, S, yout if (last and not cfg.get("debug")) else self.X)
            T.barrier(self.inp["ident"][0:1, 0:16], self.dum1)
        return nc

    def bar(self):
        self.T.barrier(self.inp["ident"][0:1, 0:16], self.dum1)

    def consts(self):
        T = self.T
        nc = self.nc
        I = self.inp
        es = self.es
        self.ident = T.sb(es, "ident", [128, 128], BF16)
        self.ones = T.sb(es, "ones", [128, 128], BF16)
        self.ea = T.sb(es, "ea", [128, 128], BF16)
        self.rsmall = T.sb(es, "rsmall", [128, 192], BF16)
        T.op(T.pool, lambda: nc.gpsimd.memset(self.ea.t[:], 0.0), writes=[self.ea])
        T.op(T.pool, lambda: nc.gpsimd.memset(self.rsmall.t[:], 0.0), writes=[self.rsmall])
        ds = T.get_ds("pool")
        T.dma(T.pool, self.ident.t[:], I["ident"][:, :], ds, writes=[self.ident])
        ds = T.get_ds("pool")
        T.dma(T.pool, self.ea.t[0:2, :], I["ea"][:, :], ds, writes=[self.ea])
        ds = T.get_ds("pool")
        T.dma(T.pool, self.rsmall.t[0:2, :], I["rsmall"][:, :], ds, writes=[self.rsmall])
        T.op(T.pool, lambda: nc.gpsimd.memset(self.ones.t[:], 1.0), writes=[self.ones])

    def load_w(self, ctx, name, src, K, N, nsplit=1):
        T = self.T
        kc = K // 128
        w = T.sb(ctx, name, [128, kc, N], BF16)
        bufs = []
        step = (N + nsplit - 1) // nsplit
        for c in range(kc):
            b = Buf("%s_c%d" % (name, c), w.t)
            ds = self.newds("pool")
            for n0 in range(0, N, step):
                n1 = min(N, n0 + step)
                T.dma(T.pool, w.t[:, c, n0:n1], src[c * 128:(c + 1) * 128, n0:n1], ds, writes=[b])
            bufs.append(b)
        return w.t, bufs

    def load_rep(self, ctx, name, src1d, n):
        T = self.T
        g = T.sb(ctx, name, [128, n], F32)
        ds = self.newds()
        T.dma(T.sp, g.t[:], src1d.partition_broadcast(128), ds, writes=[g])
        return g

    def begin_phase(self):
        self.phase_ds = []

    def end_phase(self):
        self.bar()
        for ds in self.phase_ds:
            self.T.free_ds.setdefault(ds.kind, []).append(ds)
        self.phase_ds = []

    def newds(self, kind="sp"):
        ds = self.T.get_ds(kind)
        self.phase_ds.append(ds)
        return ds

    def rms_rstd(self, ss, rstd, n, sc, bias):
        T = self.T
        nc = self.nc
        T.op(T.act, lambda: nc.scalar.activation(out=rstd.t[:, 0:n], in_=ss.t[:, 0:n], func=AF.Sqrt,
                                                 bias=self.cbias(bias), scale=sc), reads=[ss, self._cb[float(bias)]], writes=[rstd])
        T.op(T.dve, lambda: nc.vector.reciprocal(out=rstd.t[:, 0:n], in_=rstd.t[:, 0:n]), reads=[rstd], writes=[rstd])

    def cbias(self, v):
        key = float(v)
        if key not in self._cb:
            raise KeyError(key)
        return self._cb[key].t[:, 0:1]

    def make_cbias(self, ctx, vals):
        T = self.T
        nc = self.nc
        self._cb = {}
        for v in vals:
            b = T.sb(ctx, "cb", [128, 1], F32)
            T.op(T.pool, lambda: nc.gpsimd.memset(b.t[:], float(v)), writes=[b])
            self._cb[float(v)] = b
        self._cb_bufs = list(self._cb.values())

    def phase1(self, l, S, xsrc):
        T = self.T
        nc = self.nc
        I = self.inp
        NT = S // 128
        E2 = T.pool
        H2 = nc.gpsimd
        self.begin_phase()
        with ExitStack() as ctx:
            self.make_cbias(ctx, [EPS, 64 * EPS, 96 * EPS])
            win, winb = self.load_w(ctx, "win", I["w_in"][l], D, D_IN, nsplit=2)
            wuq, wuqb = self.load_w(ctx, "wuq", I["w_uq"][l], 256, 576)
            wuk, wukb = self.load_w(ctx, "wuk", I["w_uk"][l], 128, 384)
            wuv, wuvb = self.load_w(ctx, "wuv", I["w_uv"][l], 128, 384)
            gv = self.load_rep(ctx, "gv1", I["gvec"][l, 0:G_CROSS], G_CROSS)
            ropes = [T.sb(ctx, "rope", [128, 96], F32) for _ in range(3)]
            rope_ds = [self.newds() for _ in range(3)]
            xts = [T.sb(ctx, "xt", [128, D], F32) for _ in range(2)]
            xds = [self.newds(), self.newds()]
            junk = T.sb(ctx, "junk", [128, D], BF16)
            ss1 = Rot([T.sb(ctx, "ss1", [128, 8], F32) for _ in range(2)])
            rs1 = Rot([T.sb(ctx, "rs1", [128, 8], F32) for _ in range(2)])
            hb = Rot([T.sb(ctx, "hb", [128, D], BF16) for _ in range(2)])
            hT = Rot([T.sb(ctx, "hT", [128, 8, 128], BF16) for _ in range(2)])
            pT = Rot([T.ps(ctx, "pT", [128, 1024], BF16) for _ in range(1)])
            pz = Rot([T.ps(ctx, "pz", [128, 512], F32) for _ in range(5)])
            pX = Rot([T.ps(ctx, "pX", [128, 1024], BF16) for _ in range(2)])
            sq = Rot([T.sb(ctx, "sq", [128, 576], F32) for _ in range(6)])
            yy = Rot([T.sb(ctx, "yy", [128, 576], F32) for _ in range(6)])
            y2 = Rot([T.sb(ctx, "y2", [128, 576], F32) for _ in range(6)])
            tt = Rot([T.sb(ctx, "tt", [128, 6, 32], F32) for _ in range(16)])
            yb = Rot([T.sb(ctx, "yb", [128, 576], BF16) for _ in range(10)])
            ssh = Rot([T.sb(ctx, "ssh", [128, 8], F32) for _ in range(12)])
            rsh = Rot([T.sb(ctx, "rsh", [128, 8], F32) for _ in range(12)])
            cqn = Rot([T.sb(ctx, "cqn", [128, 384], BF16) for _ in range(2)])
            cT = Rot([T.sb(ctx, "cT", [128, 3, 128], BF16) for _ in range(2)])
            krb = Rot([T.sb(ctx, "krb", [128, 32], F32) for _ in range(2)])
            sskr_r = Rot([T.sb(ctx, "sskr", [128, 1], F32) for _ in range(2)])
            stq = [T.sb(ctx, "stq", [128, 18, 128], BF16) for _ in range(2)]
            stm = [T.sb(ctx, "stm", [96, 12, 128], BF16) for _ in range(2)]
            stq_ds = [self.newds() for _ in range(2)]
            stm_ds = [self.newds() for _ in range(2)]
            vst = [T.sb(ctx, "vst", [128, 1536], BF16) for _ in range(2)]
            vds = [self.newds(), self.newds()]

            dq = []
            DEPTH = 2

            def defer(fn):
                dq.append(fn)
                while len(dq) > DEPTH:
                    dq.pop(0)()

            def load_x(t):
                b = xts[t % 2]
                T.dma(T.sp, b.t[:], xsrc[t * 128:(t + 1) * 128, :], xds[t % 2], writes=[b])
                rp = ropes[t % 3]
                T.dma(T.sp, rp.t[:, 0:32], I["cosd"][:, t, :], rope_ds[t % 3], writes=[rp])
                T.dma(T.sp, rp.t[:, 32:64], I["sind"][:, t, :], rope_ds[t % 3], writes=[rp])
                T.dma(T.sp, rp.t[:, 64:80], I["cosm"][:, t, :], rope_ds[t % 3], writes=[rp])
                T.dma(T.sp, rp.t[:, 80:96], I["sinm"][:, t, :], rope_ds[t % 3], writes=[rp])

            def headnorm_g(pz_b, ncols, H, d, sc, bias, res, extra_ss=None):
                s_ = sq.next()
                T.op(T.act, lambda: nc.scalar.activation(out=s_.t[:, 0:ncols], in_=pz_b.t[:, 0:ncols], func=AF.Square),
                     reads=[pz_b], writes=[s_])
                ssb = ssh.next()
                T.op(T.dve, lambda: nc.vector.tensor_reduce(out=ssb.t[:, 0:H],
                                                            in_=s_.t[:, 0:ncols].rearrange("p (h d) -> p h d", d=d),
                                                            axis=AX.X, op=ALU.add), reads=[s_], writes=[ssb])
                if extra_ss is not None:
                    T.op(T.dve, lambda: nc.vector.tensor_scalar(out=ssb.t[:, 0:H], in0=ssb.t[:, 0:H],
                                                                scalar1=extra_ss.t[:, 0:1], scalar2=None, op0=ALU.add),
                         reads=[ssb, extra_ss], writes=[ssb])
                yield
                rb = rsh.next()
                self.rms_rstd(ssb, rb, H, sc, bias)
                res.append(rb)
                yield

            def tile_jobs(t):
                xt = xts[t % 2]
                rp = ropes[t % 3]
                par = t % 2
                sQ = stq[par]
                sM = stm[par]
                vb = vst[par]
                st = {}

                def head():
                    ssb = ss1.next()
                    T.op(T.act, lambda: nc.scalar.activation(out=junk.t[:], in_=xt.t[:], func=AF.Square,
                                                             accum_out=ssb.t[:, 0:1]), reads=[xt], writes=[junk, ssb])
                    rb = rs1.next()
                    self.rms_rstd(ssb, rb, 1, 1.0 / D, EPS)
                    yield
                    h = hb.next()
                    T.op(T.dve, lambda: nc.vector.scalar_tensor_tensor(out=h.t[:], in0=xt.t[:], scalar=rb.t[:, 0:1],
                                                                       in1=gv.t[:, G_MIX:G_MIX + D], op0=ALU.mult, op1=ALU.mult),
                         reads=[xt, rb, gv], writes=[h])
                    yield
                    hTb = hT.next()
                    self.tr8(h, pT.items[0], hTb)
                    st["hT"] = hTb

                def proj(col0, ncols):
                    z = pz.next()
                    hTb = st["hT"]

                    def mm():
                        ins = None
                        for c in range(8):
                            ins = nc.tensor.matmul(z.t[:, 0:ncols], lhsT=hTb.t[:, c, :], rhs=win[:, c, col0:col0 + ncols],
                                                   start=(c == 0), stop=(c == 7))
                        return ins
                    T.op(T.pe, mm, reads=[hTb] + winb, writes=[z])
                    return z

                def transpose_out(src_b, slabs, width, dst_b, dst_idx0, rows):
                    def now():
                        px = pX.next()

                        def trs():
                            ins = None
                            for s_ in range(slabs):
                                ins = nc.tensor.transpose(out=px.t[0:width, s_ * 128:(s_ + 1) * 128],
                                                          in_=src_b.t[:, s_ * width:(s_ + 1) * width], identity=self.ident.t[:])
                            return ins
                        T.op(T.pe, trs, reads=[src_b, self.ident], writes=[px])
                        T.op(T.act, lambda: nc.scalar.copy(
                            out=dst_b.t[0:rows, dst_idx0:dst_idx0 + slabs, 0:128],
                            in_=px.t[0:rows, 0:slabs * 128].rearrange("p (s t) -> p s t", t=128)),
                            reads=[px], writes=[dst_b])
                    defer(now)

                def rope_g(y_b, H, d, hd, c0, s0, lo0, out_b):
                    yv = y_b.t[:, 0:H * d].rearrange("p (h d) -> p h d", d=d)
                    ov = out_b.t[:, 0:H * d].rearrange("p (h d) -> p h d", d=d)
                    lo = yv[:, :, lo0:lo0 + hd]
                    hi = yv[:, :, lo0 + hd:lo0 + 2 * hd]
                    cs = rp.t[:, c0:c0 + hd].unsqueeze(1).to_broadcast([128, H, hd])
                    sn = rp.t[:, s0:s0 + hd].unsqueeze(1).to_broadcast([128, H, hd])
                    t1, t2, t3, t4 = tt.next(), tt.next(), tt.next(), tt.next()
                    T.op(T.dve, lambda: nc.vector.tensor_tensor(out=t1.t[:, 0:H, 0:hd], in0=lo, in1=cs, op=ALU.mult),
                         reads=[y_b, rp], writes=[t1])
                    T.op(E2, lambda: H2.tensor_tensor(out=t2.t[:, 0:H, 0:hd], in0=hi, in1=sn, op=ALU.mult),
                         reads=[y_b, rp], writes=[t2])
                    T.op(T.dve, lambda: nc.vector.tensor_tensor(out=t3.t[:, 0:H, 0:hd], in0=hi, in1=cs, op=ALU.mult),
                         reads=[y_b, rp], writes=[t3])
                    T.op(E2, lambda: H2.tensor_tensor(out=t4.t[:, 0:H, 0:hd], in0=lo, in1=sn, op=ALU.mult),
                         reads=[y_b, rp], writes=[t4])
                    yield
                    T.op(T.dve, lambda: nc.vector.tensor_tensor(out=ov[:, :, lo0:lo0 + hd], in0=t1.t[:, 0:H, 0:hd],
                                                                in1=t2.t[:, 0:H, 0:hd], op=ALU.subtract),
                         reads=[t1, t2], writes=[out_b])
                    T.op(E2, lambda: H2.tensor_tensor(out=ov[:, :, lo0 + hd:lo0 + 2 * hd], in0=t3.t[:, 0:H, 0:hd],
                                                      in1=t4.t[:, 0:H, 0:hd], op=ALU.add),
                         reads=[t3, t4], writes=[out_b])
                    if lo0 > 0:
                        T.op(E2, lambda: H2.tensor_copy(out=ov[:, :, 0:lo0], in_=yv[:, :, 0:lo0]),
                             reads=[y_b], writes=[out_b])
                    yield

                def qk_block(col0, H, goff, qscale, rope, dst_idx0):
                    d = 64
                    ncols = H * d
                    z = proj(col0, ncols)
                    yield
                    res = []
                    yield from headnorm_g(z, ncols, H, d, 1.0 if qscale else 1.0 / d, d * EPS if qscale else EPS, res)
                    rb_ = res[0]
                    y = yy.next()
                    T.op(T.dve, lambda: nc.vector.tensor_tensor(
                        out=y.t[:, 0:ncols].rearrange("p (h d) -> p h d", d=d),
                        in0=z.t[:, 0:ncols].rearrange("p (h d) -> p h d", d=d),
                        in1=rb_.t[:, 0:H].unsqueeze(2).to_broadcast([128, H, d]), op=ALU.mult),
                        reads=[z, rb_], writes=[y])
                    yield
                    ob = yb.next()
                    gb = gv.t[:, goff:goff + d].unsqueeze(1).to_broadcast([128, H, d])
                    if rope:
                        yg = y2.next()
                        T.op(E2, lambda: H2.tensor_tensor(
                            out=yg.t[:, 0:ncols].rearrange("p (h d) -> p h d", d=d),
                            in0=y.t[:, 0:ncols].rearrange("p (h d) -> p h d", d=d), in1=gb, op=ALU.mult),
                            reads=[y, gv], writes=[yg])
                        yield
                        yield from rope_g(yg, H, d, 32, 0, 32, 0, ob)
                    else:
                        T.op(E2, lambda: H2.tensor_tensor(
                            out=ob.t[:, 0:ncols].rearrange("p (h d) -> p h d", d=d),
                            in0=y.t[:, 0:ncols].rearrange("p (h d) -> p h d", d=d), in1=gb, op=ALU.mult),
                            reads=[y, gv], writes=[ob])
                        yield
                    transpose_out(ob, H // 2, 128, sQ, dst_idx0, 128)

                def v_block(col0, ncols, voff):
                    z = proj(col0, ncols)
                    yield
                    T.op(T.act, lambda: nc.scalar.copy(out=vb.t[:, voff:voff + ncols], in_=z.t[:, 0:ncols]),
                         reads=[z], writes=[vb])

                def mla():
                    z0 = proj(0, 416)
                    yield
                    cq = cqn.next()
                    res = []
                    yield from headnorm_g(z0, 256, 1, 256, 1.0 / 256, EPS, res)
                    rq = res[0]
                    T.op(T.dve, lambda: nc.vector.scalar_tensor_tensor(out=cq.t[:, 0:256], in0=z0.t[:, 0:256], scalar=rq.t[:, 0:1],
                                                                       in1=gv.t[:, G_CQ:G_CQ + 256], op0=ALU.mult, op1=ALU.mult),
                         reads=[z0, rq, gv], writes=[cq])
                    s_ = sq.next()
                    ssk = ssh.next()
                    T.op(T.act, lambda: nc.scalar.activation(out=s_.t[:, 0:128], in_=z0.t[:, 256:384], func=AF.Square,
                                                             accum_out=ssk.t[:, 0:1]), reads=[z0], writes=[s_, ssk])
                    yield
                    rk = rsh.next()
                    self.rms_rstd(ssk, rk, 1, 1.0 / 128, EPS)
                    yield
                    T.op(T.dve, lambda: nc.vector.scalar_tensor_tensor(out=cq.t[:, 256:384], in0=z0.t[:, 256:384], scalar=rk.t[:, 0:1],
                                                                       in1=gv.t[:, G_CKV:G_CKV + 128], op0=ALU.mult, op1=ALU.mult),
                         reads=[z0, rk, gv], writes=[cq])
                    kr = krb.next()
                    s2_ = sq.next()
                    sskr = sskr_r.next()
                    T.op(T.dve, lambda: nc.vector.tensor_copy(out=kr.t[:], in_=z0.t[:, 384:416]), reads=[z0], writes=[kr])
                    T.op(T.dve, lambda: nc.vector.tensor_tensor(out=s2_.t[:, 0:32], in0=kr.t[:], in1=kr.t[:], op=ALU.mult),
                         reads=[kr], writes=[s2_])
                    T.op(T.dve, lambda: nc.vector.tensor_reduce(out=sskr.t[:, 0:1], in_=s2_.t[:, 0:32], axis=AX.X, op=ALU.add),
                         reads=[s2_], writes=[sskr])
                    yield
                    p2 = pT.items[0]

                    def tr3():
                        ins = None
                        for c in range(3):
                            ins = nc.tensor.transpose(out=p2.t[:, c * 128:(c + 1) * 128], in_=cq.t[:, c * 128:(c + 1) * 128],
                                                      identity=self.ident.t[:])
                        return ins
                    T.op(T.pe, tr3, reads=[cq, self.ident], writes=[p2])
                    cTb = cT.next()
                    T.op(T.act, lambda: nc.scalar.copy(out=cTb.t[:].rearrange("p c t -> p (c t)"), in_=p2.t[:, 0:384]),
                         reads=[p2], writes=[cTb])
                    yield
                    for hq in range(2):
                        zq = pz.next()

                        def mmq():
                            ins = None
                            for c in range(2):
                                ins = nc.tensor.matmul(zq.t[:, 0:288], lhsT=cTb.t[:, c, :], rhs=wuq[:, c, hq * 288:(hq + 1) * 288],
                                                       start=(c == 0), stop=(c == 1))
                            return ins
                        T.op(T.pe, mmq, reads=[cTb] + wuqb, writes=[zq])
                        yield
                        res = []
                        yield from headnorm_g(zq, 288, 3, 96, 1.0, 96 * EPS, res)
                        rb_ = res[0]
                        y = yy.next()
                        T.op(T.dve, lambda: nc.vector.tensor_tensor(
                            out=y.t[:, 0:288].rearrange("p (h d) -> p h d", d=96),
                            in0=zq.t[:, 0:288].rearrange("p (h d) -> p h d", d=96),
                            in1=rb_.t[:, 0:3].unsqueeze(2).to_broadcast([128, 3, 96]), op=ALU.mult),
                            reads=[zq, rb_], writes=[y])
                        yield
                        yg = y2.next()
                        T.op(E2, lambda: H2.tensor_tensor(
                            out=yg.t[:, 0:288].rearrange("p (h d) -> p h d", d=96),
                            in0=y.t[:, 0:288].rearrange("p (h d) -> p h d", d=96),
                            in1=gv.t[:, G_MQN:G_MQN + 96].unsqueeze(1).to_broadcast([128, 3, 96]), op=ALU.mult),
                            reads=[y, gv], writes=[yg])
                        yield
                        ob = yb.next()
                        yield from rope_g(yg, 3, 96, 16, 64, 80, 64, ob)
                        transpose_out(ob, 3, 96, sM, hq * 3, 96)
                    zk = pz.next()
                    T.op(T.pe, lambda: nc.tensor.matmul(zk.t[:, 0:384], lhsT=cTb.t[:, 2, :], rhs=wuk[:, 0, :], start=True, stop=True),
                         reads=[cTb] + wukb, writes=[zk])
                    zv = pz.next()
                    T.op(T.pe, lambda: nc.tensor.matmul(zv.t[:, 0:384], lhsT=cTb.t[:, 2, :], rhs=wuv[:, 0, :], start=True, stop=True),
                         reads=[cTb] + wuvb, writes=[zv])
                    yield
                    T.op(T.act, lambda: nc.scalar.copy(out=vb.t[:, 0:384], in_=zv.t[:, 0:384]), reads=[zv], writes=[vb])
                    res = []
                    yield from headnorm_g(zk, 384, 6, 64, 1.0 / 96, EPS, res, extra_ss=sskr)
                    rbk = res[0]
                    yk = yy.next()
                    ykv = yk.t[:, 0:576].rearrange("p (h d) -> p h d", d=96)
                    T.op(T.dve, lambda: nc.vector.tensor_tensor(
                        out=ykv[:, :, 0:64], in0=zk.t[:, 0:384].rearrange("p (h d) -> p h d", d=64),
                        in1=rbk.t[:, 0:6].unsqueeze(2).to_broadcast([128, 6, 64]), op=ALU.mult),
                        reads=[zk, rbk], writes=[yk])
                    T.op(T.dve, lambda: nc.vector.tensor_tensor(
                        out=ykv[:, :, 64:96], in0=kr.t[:, :].unsqueeze(1).to_broadcast([128, 6, 32]),
                        in1=rbk.t[:, 0:6].unsqueeze(2).to_broadcast([128, 6, 32]), op=ALU.mult),
                        reads=[kr, rbk], writes=[yk])
                    yield
                    ykg = y2.next()
                    T.op(E2, lambda: H2.tensor_tensor(
                        out=ykg.t[:, 0:576].rearrange("p (h d) -> p h d", d=96), in0=ykv,
                        in1=gv.t[:, G_MKN:G_MKN + 96].unsqueeze(1).to_broadcast([128, 6, 96]), op=ALU.mult),
                        reads=[yk, gv], writes=[ykg])
                    yield
                    obk = yb.next()
                    yield from rope_g(ykg, 6, 96, 16, 64, 80, 64, obk)
                    transpose_out(obk, 6, 96, sM, 6, 96)

                def stores():
                    r0 = t * 128
                    T.dma(T.sp, self.Vm[r0:r0 + 128, :], vb.t[:, 0:384], vds[par], reads=[vb])
                    T.dma(T.sp, self.Vn[r0:r0 + 128, :], vb.t[:, 384:768], vds[par], reads=[vb])
                    T.dma(T.sp, self.Vd[r0:r0 + 128, :], vb.t[:, 768:1536], vds[par], reads=[vb])
                    for (dst, i0, ns) in ((self.QTn, 0, 3), (self.KTn, 3, 3), (self.QTd, 6, 6), (self.KTd, 12, 6)):
                        T.dma(T.sp, dst.rearrange("(s p) t -> p s t", p=128)[:, :, r0:r0 + 128], sQ.t[:, i0:i0 + ns, 0:128],
                              stq_ds[par], reads=[sQ])
                    T.dma(T.sp, self.QTm[:, :, r0:r0 + 128].rearrange("h d t -> d h t"), sM.t[:, 0:6, 0:128], stm_ds[par], reads=[sM])
                    T.dma(T.sp, self.KTm[:, :, r0:r0 + 128].rearrange("h d t -> d h t"), sM.t[:, 6:12, 0:128], stm_ds[par], reads=[sM])

                blocks = [
                    mla,
                    lambda: qk_block(416, 6, G_NQ, True, False, 0),
                    lambda: qk_block(800, 6, G_NK, False, False, 3),
                    lambda: qk_block(1568, 6, G_DQ, True, True, 6),
                    lambda: v_block(1184, 384, 384),
                    lambda: qk_block(1952, 6, G_DQ, True, True, 9),
                    lambda: qk_block(2336, 6, G_DK, False, True, 12),
                    lambda: v_block(3104, 384, 768),
                    lambda: qk_block(2720, 6, G_DK, False, True, 15),
                    lambda: v_block(3488, 384, 1152),
                ]
                return head, blocks, stores

            W = 4
            active = []

            def pump():
                for g in list(active):
                    try:
                        next(g)
                    except StopIteration:
                        active.remove(g)

            load_x(0)
            head0, blocks0, stores0 = tile_jobs(0)
            for _ in head0():
                pass
            cur = (blocks0, stores0)
            for t in range(NT):
                if t + 1 < NT:
                    load_x(t + 1)
                blocks, stores = cur
                for bf in blocks:
                    active.append(bf())
                    while len(active) >= W:
                        pump()
                if t + 1 < NT:
                    hd, nb, ns = tile_jobs(t + 1)
                    active.append(hd())
                    cur = (nb, ns)
                while active:
                    pump()
                defer(stores)
            while dq:
                dq.pop(0)()
            self.end_phase()

    def attention(self, S, slots, l, G=2, SKEW=2):
        T = self.T
        nc = self.nc
        I = self.inp
        NT = S // 128
        NQB = S // 512
        nstream = len(slots[0]["streams"])
        nset = 2 if nstream == 1 else 1
        use_dil = any(m[0] == "dil" for sl in slots for (_, _, ms) in sl["chunks"](0) for m in ms)
        dpad = slots[0]["streams"][0]["d"]
        self.begin_phase()
        with ExitStack() as ctx:
            sets = []
            for si in range(nset):
                st = []
                for k in range(nstream):
                    kt = T.sb(ctx, "kt", [128, S], BF16)
                    va = T.sb(ctx, "va", [128, NT, 128], BF16)
                    T.op(T.pool, lambda: nc.gpsimd.memset(va.t[:, :, 64:128], 1.0), writes=[va])
                    T.op(T.pool, lambda: nc.gpsimd.memset(kt.t[dpad:128, :], 0.0), writes=[kt])
                    st.append((kt, va, self.newds(), self.newds()))
                nat = T.sb(ctx, "nat", [128, NA_MM * 64], BF16)
                sets.append((st, nat, self.newds("pool")))
            dstrip = None
            if use_dil:
                dstrip = T.sb(ctx, "dstrip", [128, DIL_W], BF16)
                dsd = self.newds("pool")
                for c0 in range(0, DIL_W, 2048):
                    c1 = min(DIL_W, c0 + 2048)
                    T.dma(T.pool, dstrip.t[:, c0:c1], I["dstrip"][:, c0:c1], dsd, writes=[dstrip])
            qtb = [Rot([T.sb(ctx, "qtb", [128, 512], BF16) for _ in range(3)]) for _ in range(nstream)]
            for r_ in qtb:
                for b_ in r_.items:
                    T.op(T.pool, lambda: nc.gpsimd.memset(b_.t[dpad:128, :], 0.0), writes=[b_])
            qds = [[self.newds(), self.newds(), self.newds()] for _ in range(nstream)]
            ptr = Rot([T.sb(ctx, "pt", [128, 2, 512], BF16) for _ in range(5 if G == 2 else 8)])
            pS_t = [T.ps(ctx, "pS", [128, 1024], F32) for _ in range(3)]
            if G == 2:
                pS = Rot([(b_, 0) for b_ in pS_t])
            else:
                pS = Rot([(Buf("pSh", b_.t), off_) for b_ in pS_t for off_ in (0, 512)])
            pO = Rot([T.ps(ctx, "pO", [128, 512], F32) for _ in range(2)])
            rzr = Rot([T.sb(ctx, "rz", [64, 512], F32) for _ in range(2)])
            ost = [T.sb(ctx, "ost", [64, 512], BF16) for _ in range(3)]
            ods = [self.newds() for _ in range(3)]
            oi = 0

            def load_slot(i):
                sl = slots[i]
                st, nat, nds = sets[i % nset]
                for k, sm in enumerate(sl["streams"]):
                    kt, va, kds, vds_ = st[k]
                    d = sm["d"]
                    T.dma(T.sp, kt.t[0:d, :], sm["kt"], kds, writes=[kt])
                    for c0 in range(0, NT, 16):
                        c1 = min(NT, c0 + 16)
                        T.dma(T.sp, va.t[:, c0:c1, 0:64],
                              sm["v"][c0 * 128:c1 * 128, :].rearrange("(c p) d -> p c d", p=128), vds_, writes=[va])
                if sl.get("natab") is not None:
                    T.dma(T.pool, nat.t[:], sl["natab"], nds, writes=[nat])

            pending = []

            def emit_pv(G):
                grp, st_, pt, po, g0, total, fin = G

                def pv():
                    ins = None
                    for gi, (k, c, masks) in enumerate(grp):
                        va = st_[k][1]
                        idx = g0 + gi
                        ins = nc.tensor.matmul(po.t[:, :], lhsT=va.t[:, c, :], rhs=pt.t[:, gi, :],
                                               start=(idx == 0), stop=(idx == total - 1))
                    return ins
                T.op(T.pe, pv, reads=[pt] + [st_[k][1] for (k, _, _) in grp], writes=[po])
                if fin is not None:
                    row0, jq = fin
                    rz = rzr.next()
                    T.op(T.dve, lambda: nc.vector.reciprocal(out=rz.t[:, :], in_=po.t[64:128, :]), reads=[po], writes=[rz])
                    ob = ost[self._oi % 3]
                    T.op(T.dve, lambda: nc.vector.tensor_tensor(out=ob.t[:, :], in0=po.t[0:64, :], in1=rz.t[:, :], op=ALU.mult),
                         reads=[po, rz], writes=[ob])
                    T.dma(T.sp, self.OT[row0:row0 + 64, jq * 512:(jq + 1) * 512], ob.t[:, :], ods[self._oi % 3], reads=[ob])
                    self._oi += 1

            self._oi = 0
            load_slot(0)
            for i, sl in enumerate(slots):
                if nset == 2 and i + 1 < len(slots):
                    while pending:
                        emit_pv(pending.pop(0))
                    load_slot(i + 1)
                st, nat, _ = sets[i % nset]
                def load_q(jq_):
                    out_ = []
                    for k, sm in enumerate(sl["streams"]):
                        qb = qtb[k].next()
                        d = sm["d"]
                        T.dma(T.sp, qb.t[0:d, :], sm["qt"][:, jq_ * 512:(jq_ + 1) * 512], qds[k][(qtb[k].i - 1) % 3], writes=[qb])
                        out_.append(qb)
                    return out_
                qs_next = load_q(0)
                for jq in range(NQB):
                    qs = qs_next
                    if jq + 1 < NQB:
                        qs_next = load_q(jq + 1)
                    items = sl["chunks"](jq)
                    total = len(items)
                    po = pO.next()
                    for g0 in range(0, total, G):
                        grp = items[g0:g0 + G]
                        ps, poff = pS.next()

                        def qk():
                            ins = None
                            for gi, (k, c, masks) in enumerate(grp):
                                kt = st[k][0]
                                d = sl["streams"][k]["d"]
                                o_ = ps.t[:, poff + gi * 512:poff + (gi + 1) * 512]
                                mm_masks = [m for m in masks if m[0] != "dil"]
                                ins = nc.tensor.matmul(o_, lhsT=kt.t[:, c * 128:(c + 1) * 128], rhs=qs[k].t[:, :],
                                                       start=True, stop=(len(mm_masks) == 0))
                                masks = mm_masks
                                for mi, m in enumerate(masks):
                                    last = (mi == len(masks) - 1)
                                    if m[0] == "nat":
                                        ins = nc.tensor.matmul(o_, lhsT=self.ident.t[:], rhs=nat.t[:, m[1]:m[1] + 512],
                                                               start=False, stop=last)
                                    elif m[0] == "dil":
                                        pass
                                    else:
                                        ins = nc.tensor.matmul(o_.rearrange("p (j c) -> p j c", c=64), lhsT=self.ea.t[:, :],
                                                               rhs=self.rsmall.t[:, m[1] * 8:(m[1] + 1) * 8].unsqueeze(2).to_broadcast([128, 8, 64]),
                                                               start=False, stop=last)
                            return ins
                        rd = [st[k][0] for (k, _, _) in grp] + [qs[k] for (k, _, _) in grp] + [self.ident, self.ea, self.rsmall, nat]
                        if dstrip is not None:
                            rd.append(dstrip)
                        T.op(T.pe, qk, reads=rd, writes=[ps])
                        pt = ptr.next()
                        n = len(grp)
                        T.op(T.act, lambda: nc.scalar.activation(out=pt.t[:, 0:n, :].rearrange("p g q -> p (g q)"),
                                                                 in_=ps.t[:, poff:poff + n * 512], func=AF.Exp), reads=[ps], writes=[pt])
                        for gi, (k_, c_, masks_) in enumerate(grp):
                            for m in masks_:
                                if m[0] == "dil":
                                    self._mi = getattr(self, "_mi", 0) + 1
                                    if self._mi % 3 == 0:
                                        T.op(T.pool, lambda: nc.gpsimd.tensor_tensor(out=pt.t[:, gi, :], in0=pt.t[:, gi, :],
                                                                                     in1=dstrip.t[:, m[1]:m[1] + 512], op=ALU.mult),
                                             reads=[pt, dstrip], writes=[pt])
                                    else:
                                        T.op(T.dve, lambda: nc.vector.tensor_tensor(out=pt.t[:, gi, :], in0=pt.t[:, gi, :],
                                                                                    in1=dstrip.t[:, m[1]:m[1] + 512], op=ALU.mult),
                                             reads=[pt, dstrip], writes=[pt])
                        fin = (sl["row0"], jq) if g0 + G >= total else None
                        pending.append((grp, st, pt, po, g0, total, fin))
                        if len(pending) > SKEW:
                            emit_pv(pending.pop(0))
                if nset == 1:
                    while pending:
                        emit_pv(pending.pop(0))
                    if i + 1 < len(slots):
                        load_slot(i + 1)
            while pending:
                emit_pv(pending.pop(0))
            self.end_phase()

    def attn_mla(self, l, S):
        NT = S // 128
        slots = []
        for h in range(6):
            slots.append(dict(row0=h * 64, natab=None,
                              streams=[dict(qt=self.QTm[h, :, 0:S], kt=self.KTm[h, :, 0:S], v=self.Vm[0:S, h * 64:(h + 1) * 64], d=96)],
                              chunks=(lambda jq: [(0, c, []) for c in range(NT)])))
        self.attention(S, slots, l)

    def attn_na(self, l, S):
        NT = S // 128
        NQB = S // 512
        I = self.inp

        def chunks(jq):
            ty = 0 if jq == 0 else (2 if jq == NQB - 1 else 1)
            out = []
            for ci in range(8):
                c = 4 * jq - 2 + ci
                if 0 <= c < NT:
                    m0 = 11 - 2 * ci
                    out.append((0, c, [("nat", (m0 + 3) * 64), ("row", ty * 8 + ci)]))
            return out
        slots = []
        for h in range(6):
            slots.append(dict(row0=384 + h * 64, natab=I["natab"][l, h],
                              streams=[dict(qt=self.QTn[h * 64:(h + 1) * 64, 0:S], kt=self.KTn[h * 64:(h + 1) * 64, 0:S],
                                            v=self.Vn[0:S, h * 64:(h + 1) * 64], d=64)], chunks=chunks))
        self.attention(S, slots, l, G=1, SKEW=4)

    def attn_dil(self, l, S):
        NT = S // 128

        def chunks(jq):
            out = []
            for g in range(3):
                for Dd in range(DIL_DMIN[g], DIL_DMAX[g] + 1, 128):
                    c = (jq * 512 + Dd) // 128
                    if 0 <= c < NT:
                        out.append((g, c, [("dil", DIL_OFF[g] + DIL_DMAX[g] - Dd)]))
            return out
        slots = []
        for h in range(4):
            sts = []
            for g in range(3):
                hh = g * 4 + h
                sts.append(dict(qt=self.QTd[hh * 64:(hh + 1) * 64, 0:S], kt=self.KTd[hh * 64:(hh + 1) * 64, 0:S],
                                v=self.Vd[0:S, hh * 64:(hh + 1) * 64], d=64))
            slots.append(dict(row0=768 + h * 64, natab=None, streams=sts, chunks=chunks))
        self.attention(S, slots, l, G=1, SKEW=4)

    def phase_x(self, l, S, xsrc, mem):
        T = self.T
        nc = self.nc
        I = self.inp
        NT = S // 128
        GO = G_CROSS
        self.begin_phase()
        with ExitStack() as ctx:
            self.make_cbias(ctx, [EPS, 256 * EPS])
            wo, wob = self.load_w(ctx, "wo", I["w_o"][l], D, D)
            wcq, wcqb = self.load_w(ctx, "wcq", I["w_cq"][l], D, D)
            wco, wcob = self.load_w(ctx, "wco", I["w_co"][l], D, D)
            wkv, wkvb = self.load_w(ctx, "wkv", I["w_ckv"][l], D, 2 * D)
            gv = self.load_rep(ctx, "gv2", I["gvec"][l, GO:NG], NG - GO)
            g_cross, g_xq, g_ffn, g_mem, g_xk = 0, G_XQ - GO, G_FFN - GO, G_MEM - GO, G_XK - GO
            pA = T.ps(ctx, "pA", [128, 1024], F32)
            pC = T.ps(ctx, "pC", [128, 1024], F32)
            pD = T.ps(ctx, "pD", [128, 1024], F32)
            pT = T.ps(ctx, "pT", [128, 1024], BF16)
            pZ = T.ps(ctx, "pZ", [128, 512], F32)
            xts = [T.sb(ctx, "xt", [128, D], F32) for _ in range(3)]
            xds = [self.newds() for _ in range(3)]
            ots = [T.sb(ctx, "oT", [128, 8, 128], BF16) for _ in range(2)]
            otds = [self.newds() for _ in range(2)]
            junk = T.sb(ctx, "junk", [128, D], BF16)
            ss = Rot([T.sb(ctx, "ss", [128, 8], F32) for _ in range(6)])
            rs = Rot([T.sb(ctx, "rs", [128, 8], F32) for _ in range(6)])
            hb = Rot([T.sb(ctx, "hb", [128, D], BF16) for _ in range(4)])
            hT = Rot([T.sb(ctx, "hT", [128, 8, 128], BF16) for _ in range(3)])
            sqb = T.sb(ctx, "sqb", [128, D], F32)
            yf = T.sb(ctx, "yf", [128, D], F32)
            kmT = T.sb(ctx, "kmT", [128, 8, 256], BF16)
            vms = T.sb(ctx, "vms", [128, 2, D], BF16)
            ptb = Rot([T.sb(ctx, "ptb", [128, 8, 128], BF16) for _ in range(2)])
            rzb = T.sb(ctx, "rzb", [128, 512], F32)
            ocT = Rot([T.sb(ctx, "ocT", [128, 8, 128], BF16) for _ in range(2)])
            h3s = [T.sb(ctx, "h3s", [128, 8, 128], BF16) for _ in range(2)]
            h3ds = [self.newds() for _ in range(2)]
            zt = T.sb(ctx, "zt", [128, 8, 1], BF16)
            zds = self.newds()
            H3v = self.H3T.rearrange("(c p) t -> p c t", p=128)
            OTv = self.OT.rearrange("(c p) t -> p c t", p=128)

            def rmsnorm_T(xb, goff, hT_dst):
                ssb = ss.next()
                T.op(T.act, lambda: nc.scalar.activation(out=junk.t[:], in_=xb.t[:], func=AF.Square, accum_out=ssb.t[:, 0:1]),
                     reads=[xb], writes=[junk, ssb])
                rb = rs.next()
                self.rms_rstd(ssb, rb, 1, 1.0 / D, EPS)
                h = hb.next()
                T.op(T.dve, lambda: nc.vector.scalar_tensor_tensor(out=h.t[:], in0=xb.t[:], scalar=rb.t[:, 0:1],
                                                                   in1=gv.t[:, goff:goff + D], op0=ALU.mult, op1=ALU.mult),
                     reads=[xb, rb, gv], writes=[h])
                self.tr8(h, pT, hT_dst)

            def proj2(dst_ps, lhs_b, w, wb, coff=0):
                def mm():
                    ins = None
                    for nh in range(2):
                        for c in range(8):
                            ins = nc.tensor.matmul(dst_ps.t[:, nh * 512:(nh + 1) * 512], lhsT=lhs_b.t[:, c, :],
                                                   rhs=w[:, c, coff + nh * 512:coff + (nh + 1) * 512], start=(c == 0), stop=(c == 7))
                    return ins
                T.op(T.pe, mm, reads=[lhs_b] + wb, writes=[dst_ps])

            def headnorm4(src_ps, goff, sc, bias, out_b):
                T.op(T.act, lambda: nc.scalar.activation(out=sqb.t[:], in_=src_ps.t[:], func=AF.Square), reads=[src_ps], writes=[sqb])
                ssb = ss.next()
                T.op(T.dve, lambda: nc.vector.tensor_reduce(out=ssb.t[:, 0:4], in_=sqb.t[:].rearrange("p (h d) -> p h d", d=256),
                                                            axis=AX.X, op=ALU.add), reads=[sqb], writes=[ssb])
                rb = rs.next()
                self.rms_rstd(ssb, rb, 4, sc, bias)
                T.op(T.dve, lambda: nc.vector.tensor_tensor(out=yf.t[:].rearrange("p (h d) -> p h d", d=256),
                                                            in0=src_ps.t[:].rearrange("p (h d) -> p h d", d=256),
                                                            in1=rb.t[:, 0:4].unsqueeze(2).to_broadcast([128, 4, 256]), op=ALU.mult),
                     reads=[src_ps, rb], writes=[yf])
                T.op(T.dve, lambda: nc.vector.tensor_tensor(out=out_b.t[:].rearrange("p (h d) -> p h d", d=256),
                                                            in0=yf.t[:].rearrange("p (h d) -> p h d", d=256),
                                                            in1=gv.t[:, goff:goff + 256].unsqueeze(1).to_broadcast([128, 4, 256]), op=ALU.mult),
                     reads=[yf, gv], writes=[out_b])

            for mt in range(2):
                xb = xts[mt]
                T.dma(T.sp, xb.t[:], mem[mt * 128:(mt + 1) * 128, :], xds[mt], writes=[xb])
                mT = hT.next()
                rmsnorm_T(xb, g_mem, mT)
                proj2(pA, mT, wkv, wkvb, 0)
                kn = hb.next()
                headnorm4(pA, g_xk, 1.0 / 256, EPS, kn)

                def trk():
                    ins = None
                    for c in range(8):
                        ins = nc.tensor.transpose(out=pT.t[:, c * 128:(c + 1) * 128], in_=kn.t[:, c * 128:(c + 1) * 128],
                                                  identity=self.ident.t[:])
                    return ins
                T.op(T.pe, trk, reads=[kn, self.ident], writes=[pT])
                T.op(T.act, lambda: nc.scalar.copy(out=kmT.t[:, :, mt * 128:(mt + 1) * 128],
                                                   in_=pT.t[:].rearrange("p (c t) -> p c t", t=128)), reads=[pT], writes=[kmT])
                proj2(pC, mT, wkv, wkvb, D)
                T.op(T.act, lambda: nc.scalar.copy(out=vms.t[:, mt, :], in_=pC.t[:]), reads=[pC], writes=[vms])

            def loads(t):
                xb = xts[t % 3]
                T.dma(T.sp, xb.t[:], xsrc[t * 128:(t + 1) * 128, :], xds[t % 3], writes=[xb])
                ob = ots[t % 2]
                T.dma(T.sp, ob.t[:], OTv[:, :, t * 128:(t + 1) * 128], otds[t % 2], writes=[ob])

            qcTs = [T.sb(ctx, "qcT", [128, 8, 128], BF16) for _ in range(2)]

            def genA(t):
                xb = xts[t % 3]
                ob = ots[t % 2]
                proj2(pA, ob, wo, wob)
                T.op(T.dve, lambda: nc.vector.tensor_tensor(out=xb.t[:], in0=pA.t[:], in1=xb.t[:], op=ALU.add), reads=[pA, xb], writes=[xb])
                yield
                h2T = hT.next()
                rmsnorm_T(xb, g_cross, h2T)
                yield
                proj2(pA, h2T, wcq, wcqb)
                qn = hb.next()
                headnorm4(pA, g_xq, 1.0, 256 * EPS, qn)
                yield
                self.tr8(qn, pT, qcTs[t % 2])

            def genB(t):
                xb = xts[t % 3]
                qcT = qcTs[t % 2]

                def sc_mm():
                    ins = None
                    for hh in range(4):
                        for kc in range(2):
                            o_ = pC.t[:, (hh * 2 + kc) * 128:(hh * 2 + kc + 1) * 128]
                            for dc in range(2):
                                ins = nc.tensor.matmul(o_, lhsT=kmT.t[:, hh * 2 + dc, kc * 128:(kc + 1) * 128], rhs=qcT.t[:, hh * 2 + dc, :],
                                                       start=(dc == 0), stop=(dc == 1))
                    return ins
                T.op(T.pe, sc_mm, reads=[kmT, qcT], writes=[pC])
                pt = ptb.next()
                T.op(T.act, lambda: nc.scalar.activation(out=pt.t[:].rearrange("p c q -> p (c q)"), in_=pC.t[:], func=AF.Exp),
                     reads=[pC], writes=[pt])
                yield

                def pv_mm():
                    ins = None
                    for hh in range(4):
                        for kc in range(2):
                            ins = nc.tensor.matmul(pZ.t[:, hh * 128:(hh + 1) * 128], lhsT=self.ones.t[:], rhs=pt.t[:, hh * 2 + kc, :],
                                                   start=(kc == 0), stop=(kc == 1))
                        for dvc in range(2):
                            for kc in range(2):
                                ins = nc.tensor.matmul(pD.t[:, (hh * 2 + dvc) * 128:(hh * 2 + dvc + 1) * 128],
                                                       lhsT=vms.t[:, kc, hh * 256 + dvc * 128:hh * 256 + (dvc + 1) * 128],
                                                       rhs=pt.t[:, hh * 2 + kc, :], start=(kc == 0), stop=(kc == 1))
                    return ins
                T.op(T.pe, pv_mm, reads=[pt, vms, self.ones], writes=[pZ, pD])
                T.op(T.dve, lambda: nc.vector.reciprocal(out=rzb.t[:], in_=pZ.t[:]), reads=[pZ], writes=[rzb])
                oc = ocT.next()
                T.op(T.dve, lambda: nc.vector.tensor_tensor(
                    out=oc.t[:].rearrange("p (h e) q -> p h e q", e=2),
                    in0=pD.t[:].rearrange("p (h e q) -> p h e q", e=2, q=128),
                    in1=rzb.t[:].rearrange("p (h q) -> p h q", q=128).unsqueeze(2).to_broadcast([128, 4, 2, 128]), op=ALU.mult),
                    reads=[pD, rzb], writes=[oc])
                yield
                proj2(pC, oc, wco, wcob)
                T.op(T.dve, lambda: nc.vector.tensor_tensor(out=xb.t[:], in0=pC.t[:], in1=xb.t[:], op=ALU.add), reads=[pC, xb], writes=[xb])
                T.dma(T.sp, self.X[t * 128:(t + 1) * 128, :], xb.t[:], xds[t % 3], reads=[xb])
                yield
                h3 = h3s[t % 2]
                rmsnorm_T(xb, g_ffn, h3)
                T.dma(T.sp, H3v[:, :, 1 + t * 128:1 + (t + 1) * 128], h3.t[:], h3ds[t % 2], reads=[h3])

            loads(0)
            if NT > 1:
                loads(1)
            for _ in genA(0):
                pass
            for t in range(NT):
                if t + 2 < NT:
                    loads(t + 2)
                gens = [genB(t)]
                if t + 1 < NT:
                    gens.insert(0, genA(t + 1))
                while gens:
                    for g in list(gens):
                        try:
                            next(g)
                        except StopIteration:
                            gens.remove(g)
            self.end_phase()

    def tr8(self, src_b, pT, dst_b):
        T = self.T
        nc = self.nc

        def tr():
            ins = None
            for c in range(8):
                ins = nc.tensor.transpose(out=pT.t[:, c * 128:(c + 1) * 128], in_=src_b.t[:, c * 128:(c + 1) * 128],
                                          identity=self.ident.t[:])
            return ins
        T.op(T.pe, tr, reads=[src_b, self.ident], writes=[pT])
        T.op(T.act, lambda: nc.scalar.copy(out=dst_b.t[:].rearrange("p c t -> p (c t)"), in_=pT.t[:]), reads=[pT], writes=[dst_b])

    def ffn_up(self, l, S):
        T = self.T
        nc = self.nc
        I = self.inp
        nblk = (S + 509) // 510
        blocks = []
        for b in range(nblk):
            c0 = 510 * b
            w = min(512, S + 2 - c0)
            blocks.append((c0, w))
        passes = [blocks[i:i + 9] for i in range(0, nblk, 9)]
        H3v = self.H3T.rearrange("(c p) t -> p c t", p=128)
        WUv = I["w_up"][l].rearrange("(c p) n -> p c n", p=128)
        self.begin_phase()
        with ExitStack() as ctx:
            cp = T.sb(ctx, "convp", [128, 4, 44], F32)
            T.dma(T.sp, cp.t[:], I["convp"][l], self.newds(), writes=[cp])
            wab = [T.sb(ctx, "wab", [128, 8, 256], BF16) for _ in range(2)]
            wds = [self.newds("pool") for _ in range(2)]
            maxc = max(pb[-1][0] + pb[-1][1] - pb[0][0] for pb in passes)
            hres = T.sb(ctx, "hres", [128, 8, maxc], BF16)
            hds = self.newds()
            pU = Rot([T.ps(ctx, "pU", [128, 512], F32) for _ in range(6)])
            ca = Rot([T.sb(ctx, "ca", [128, 512], F32) for _ in range(4)])
            cg = Rot([T.sb(ctx, "cg", [128, 512], F32) for _ in range(4)])
            sg = Rot([T.sb(ctx, "sg", [128, 512], F32) for _ in range(3)])
            ast = [T.sb(ctx, "ast", [128, 512], BF16) for _ in range(4)]
            ads = [self.newds() for _ in range(4)]
            ai = 0

            def load_wab(i):
                b = wab[i % 2]
                T.dma(T.pool, b.t[:, :, 0:128], WUv[:, :, i * 128:(i + 1) * 128], wds[i % 2], writes=[b])
                T.dma(T.pool, b.t[:, :, 128:256], WUv[:, :, DFF + i * 128:DFF + (i + 1) * 128], wds[i % 2], writes=[b])

            for pb in passes:
                col_lo = pb[0][0]
                col_hi = pb[-1][0] + pb[-1][1]
                v_lo = max(col_lo, 1)
                v_hi = min(col_hi, S + 1)
                T.dma(T.sp, hres.t[:, :, v_lo - col_lo:v_hi - col_lo], H3v[:, :, v_lo:v_hi], hds, writes=[hres])
                if col_lo == 0:
                    T.op(T.pool, lambda: nc.gpsimd.memset(hres.t[:, :, 0:1], 0.0), writes=[hres])
                if col_hi == S + 2:
                    T.op(T.pool, lambda: nc.gpsimd.memset(hres.t[:, :, col_hi - col_lo - 1:col_hi - col_lo], 0.0), writes=[hres])
                load_wab(0)
                for i in range(22):
                    if i + 1 < 22:
                        load_wab(i + 1)
                    wb = wab[i % 2]
                    for (c0, w) in pb:
                        off = c0 - col_lo
                        ua = pU.next()
                        ug = pU.next()

                        def mm():
                            ins = None
                            for (dst, wo_) in ((ua, 0), (ug, 128)):
                                for c in range(8):
                                    ins = nc.tensor.matmul(dst.t[:, 0:w], lhsT=wb.t[:, c, wo_:wo_ + 128], rhs=hres.t[:, c, off:off + w],
                                                           start=(c == 0), stop=(c == 7))
                            return ins
                        T.op(T.pe, mm, reads=[wb, hres], writes=[ua, ug])
                        n = w - 2
                        outs = []
                        for (u, col, dstr) in ((ua, i, ca), (ug, 22 + i, cg)):
                            cb_ = dstr.next()
                            if n >= 256:
                                T.op(T.act, lambda: nc.scalar.activation(out=cb_.t[:, 0:n], in_=u.t[:, 0:n], func=AF.Identity,
                                                                         scale=cp.t[:, 0, col:col + 1], bias=cp.t[:, 3, col:col + 1]),
                                     reads=[u, cp], writes=[cb_])
                            else:
                                T.op(T.dve, lambda: nc.vector.tensor_scalar(out=cb_.t[:, 0:n], in0=u.t[:, 0:n], scalar1=cp.t[:, 0, col:col + 1],
                                                                            scalar2=cp.t[:, 3, col:col + 1], op0=ALU.mult, op1=ALU.add),
                                     reads=[u, cp], writes=[cb_])
                            T.op(T.dve, lambda: nc.vector.scalar_tensor_tensor(out=cb_.t[:, 0:n], in0=u.t[:, 1:n + 1], scalar=cp.t[:, 1, col:col + 1],
                                                                               in1=cb_.t[:, 0:n], op0=ALU.mult, op1=ALU.add),
                                 reads=[u, cp, cb_], writes=[cb_])
                            T.op(T.dve, lambda: nc.vector.scalar_tensor_tensor(out=cb_.t[:, 0:n], in0=u.t[:, 2:n + 2], scalar=cp.t[:, 2, col:col + 1],
                                                                               in1=cb_.t[:, 0:n], op0=ALU.mult, op1=ALU.add),
                                 reads=[u, cp, cb_], writes=[cb_])
                            outs.append(cb_)
                        sgb = sg.next()
                        T.op(T.act, lambda: nc.scalar.activation(out=sgb.t[:, 0:n], in_=outs[1].t[:, 0:n], func=AF.Silu), reads=[outs[1]], writes=[sgb])
                        ab = ast[ai % 4]
                        T.op(T.pool, lambda: nc.gpsimd.tensor_tensor(out=ab.t[:, 0:n], in0=sgb.t[:, 0:n], in1=outs[0].t[:, 0:n], op=ALU.mult),
                             reads=[sgb, outs[0]], writes=[ab])
                        T.dma(T.sp, self.ACTT[i * 128:(i + 1) * 128, c0:c0 + n], ab.t[:, 0:n], ads[ai % 4], reads=[ab])
                        ai += 1
            self.end_phase()

    def ffn_down(self, l, S, dst):
        T = self.T
        nc = self.nc
        I = self.inp
        NT = S // 128
        AV = self.ACTT.rearrange("(c p) t -> p c t", p=128)
        self.begin_phase()
        with ExitStack() as ctx:
            wd, wdb = self.load_w(ctx, "wd", I["w_down"][l], DFF, D)
            pY = Rot([T.ps(ctx, "pY", [128, 1024], F32) for _ in range(2)])
            xts = [T.sb(ctx, "xt", [128, D], F32) for _ in range(3)]
            xds = [self.newds() for _ in range(3)]
            ats = [T.sb(ctx, "aT", [128, 22, 128], BF16) for _ in range(2)]
            atds = [self.newds() for _ in range(2)]

            def loads(t):
                T.dma(T.sp, xts[t % 3].t[:], self.X[t * 128:(t + 1) * 128, :], xds[t % 3], writes=[xts[t % 3]])
                T.dma(T.sp, ats[t % 2].t[:], AV[:, :, t * 128:(t + 1) * 128], atds[t % 2], writes=[ats[t % 2]])
            loads(0)
            for t in range(NT):
                if t + 1 < NT:
                    loads(t + 1)
                xb = xts[t % 3]
                ab = ats[t % 2]
                py = pY.next()

                def mm():
                    ins = None
                    for nh in range(2):
                        for c in range(22):
                            ins = nc.tensor.matmul(py.t[:, nh * 512:(nh + 1) * 512], lhsT=ab.t[:, c, :], rhs=wd[:, c, nh * 512:(nh + 1) * 512],
                                                   start=(c == 0), stop=(c == 21))
                    return ins
                T.op(T.pe, mm, reads=[ab] + wdb, writes=[py])
                T.op(T.dve, lambda: nc.vector.tensor_tensor(out=xb.t[:], in0=py.t[:], in1=xb.t[:], op=ALU.add), reads=[py, xb], writes=[xb])
                T.dma(T.sp, dst[t * 128:(t + 1) * 128, :], xb.t[:], xds[t % 3], reads=[xb])
            self.end_phase()


def Buf_view(b):
    return b


def _rope_tab(half):
    pos = np.arange(SS_, dtype=np.float32)
    inv = (np.float32(10000.0) ** (-np.arange(half, dtype=np.float32) / np.float32(half))).astype(np.float32)
    ang = (pos[:, None] * inv[None, :]).astype(np.float32)
    c = np.cos(ang).astype(np.float32).reshape(64, 128, half).transpose(1, 0, 2)
    s = np.sin(ang).astype(np.float32).reshape(64, 128, half).transpose(1, 0, 2)
    return np.ascontiguousarray(c), np.ascontiguousarray(s)


def _dil_strips():
    out = np.zeros((128, DIL_W), np.float32)
    p = np.arange(128)[:, None]
    for g in range(3):
        r = DIL_R[g]
        w = DIL_DMAX[g] - DIL_DMIN[g] + 512
        x = np.arange(w)[None, :]
        delta = p - x + DIL_DMAX[g]
        ok = (delta % r == 0) & (np.abs(delta) <= 64 * r)
        out[:, DIL_OFF[g]:DIL_OFF[g] + w] = np.where(ok, 1.0, 0.0)
    return out


def _na_rowmask():
    R = 1000
    out = np.full((2, 3 * 8 * 8), NEG, np.float32)
    for ty, r0 in ((0, 0), (1, 496), (2, R - 8)):
        for ci in range(8):
            kr0 = r0 - 4 + 2 * ci
            for a in range(2):
                kr = kr0 + a
                for j in range(8):
                    r = r0 + j
                    rs = min(max(r - 4, 0), R - 8)
                    if rs <= kr < rs + 8:
                        out[a, (ty * 8 + ci) * 8 + j] = 0.0
    return out


def _na_table(rpb):
    Lh = rpb.shape[0]
    kc = np.arange(64)[:, None]
    c = np.arange(64)[None, :]
    cs = np.clip(c - 8, 0, 48)
    mcol = (kc >= cs) & (kc < cs + 16)
    dcidx = np.clip(kc - c + 15, 0, 30)
    out = np.full((Lh, 6, 128, NA_MM, 64), NEG, np.float32)
    for a in range(2):
        for mm in range(-3, NA_MM - 3):
            m = mm - a
            if 0 <= m <= 14:
                vals = rpb[:, :, 14 - m, :][:, :, dcidx]
                out[:, :, a * 64:(a + 1) * 64, mm + 3, :] = np.where(mcol[None, None], vals, np.float32(NEG))
    return np.ascontiguousarray(out.reshape(Lh, 6, 128, NA_MM * 64))


def prep_inputs(inputs, n_cores=8):
    f = lambda a: np.ascontiguousarray(np.asarray(a, dtype=np.float32))
    gv = np.concatenate([f(inputs[k]) for k in ("norm_mix", "mla_q_norm", "mla_kv_norm", "mla_qn", "mla_kn", "na_qn", "na_kn",
                                                "dil_qn", "dil_kn", "norm_cross", "x_qn", "norm_ffn", "norm_mem", "x_kn")], axis=1)
    assert gv.shape == (L, NG)
    cw = f(inputs["conv_w"])
    cbv = f(inputs["conv_b"])
    convp = np.concatenate([cw, cbv[:, None, :]], axis=1).reshape(L, 4, 44, 128).transpose(0, 3, 1, 2)
    cosd, sind = _rope_tab(32)
    cosm, sinm = _rope_tab(16)
    ea = np.zeros((2, 128), np.float32)
    ea[0, :64] = 1.0
    ea[1, 64:] = 1.0
    shared = {
        "gvec": np.ascontiguousarray(gv), "convp": np.ascontiguousarray(convp), "natab": _na_table(f(inputs["na_rpb"])),
        "ident": np.eye(128, dtype=np.float32), "cosd": cosd, "sind": sind, "cosm": cosm, "sinm": sinm,
        "dstrip": _dil_strips(), "rsmall": _na_rowmask(), "ea": ea,
    }
    for k in ("w_in", "w_uq", "w_uk", "w_uv", "w_o", "w_cq", "w_ckv", "w_co", "w_up", "w_down"):
        shared[k] = f(inputs[k])
    xp = f(inputs["x_prompt"])
    xs = f(inputs["x_sample"])
    mp = f(inputs["mem_prompt"])
    ms = f(inputs["mem_sample"])
    zs = np.zeros((SS_, D), np.float32)
    zm = np.zeros((MEM, D), np.float32)
    maps = []
    for c in range(n_cores):
        m = dict(shared)
        m["xp"] = xp[c]
        m["memp"] = mp[c]
        if c == 0:
            m["xs"], m["mems"] = xs[0], ms[0]
        elif c == 4:
            m["xs"], m["mems"] = xs[1], ms[1]
        else:
            m["xs"], m["mems"] = zs, zm
        maps.append(m)
    return maps


def kernel(**inputs):
    cfg = {"parts": [("p", SP_), ("s", SS_)]}
    nc = Prog(cfg).build()
    maps = prep_inputs(inputs)
    res = run_bass_kernel_spmd(nc, maps, core_ids=list(range(8)))
    yp = np.stack([np.asarray(res.results[c]["yp"], dtype=np.float32) for c in range(8)], axis=0)
    ys = np.stack([np.asarray(res.results[c]["ys"], dtype=np.float32) for c in (0, 4)], axis=0)
    return (yp, ys)
```

```python
import numpy as np
import concourse.bass as bass
import concourse.mybir as mybir
from concourse.bass_utils import run_bass_kernel_spmd
from contextlib import ExitStack

F32 = mybir.dt.float32
BF16 = mybir.dt.bfloat16
AF = mybir.ActivationFunctionType
ALU = mybir.AluOpType
AX = mybir.AxisListType

D = 1024
L = 4
EPS = 1e-6
NEG = -30000.0
D_IN = 3872
DFF = 2816
SP_ = 2048
SS_ = 8192
MEM = 256
G_MIX, G_CQ, G_CKV, G_MQN, G_MKN, G_NQ, G_NK, G_DQ, G_DK, G_CROSS, G_XQ, G_FFN, G_MEM, G_XK = (
    0, 1024, 1280, 1408, 1504, 1600, 1664, 1728, 1792, 1856, 2880, 3136, 4160, 5184)
NG = 5440
DIL_R = (1, 4, 16)
DIL_DMIN = (-128, -256, -1024)
DIL_DMAX = (512, 640, 1408)
DIL_OFF = (0, 1152, 2560)
DIL_W = 5504
NA_MM = 22


class Ev:
    __slots__ = ("sem", "val", "eng", "ds")

    def __init__(self, sem, val, eng, ds=None):
        self.sem = sem
        self.val = val
        self.eng = eng
        self.ds = ds


class Buf:
    def __init__(self, name, t=None):
        self.name = name
        self.t = t
        self.w = []
        self.r = {}
        self.rd = []


class DS:
    def __init__(self, sem):
        self.sem = sem
        self.cnt = 0


class Eng:
    def __init__(self, name, h, pe=False):
        self.name = name
        self.h = h
        self.pe = pe
        self.sem = None
        self.count = 0
        self.waited = {}
        self.nsem = 0


class Tracker:
    EPOCH = 1 << 20

    def __init__(self, nc, es):
        self.nc = nc
        self.es = es
        self.uid = 0
        self.pe = Eng("pe", nc.tensor, True)
        self.act = Eng("act", nc.scalar)
        self.dve = Eng("dve", nc.vector)
        self.pool = Eng("pool", nc.gpsimd)
        self.sp = Eng("sp", nc.sync)
        self.engs = [self.pe, self.act, self.dve, self.pool, self.sp]
        for e in self.engs:
            e.sem = self.newsem(e.name + "_s0")
        self.free_ds = {}
        self.all_ds = []
        self.bar = DS(self.newsem("bar"))
        self.bar.kind = "sp"
        self.ninstr = 0

    def newsem(self, name):
        return self.es.enter_context(self.nc.semaphore(name))

    def get_ds(self, kind="sp"):
        fl = self.free_ds.setdefault(kind, [])
        if fl:
            return fl.pop()
        ds = DS(self.newsem("ds%s%d" % (kind, len(self.all_ds))))
        ds.kind = kind
        self.all_ds.append(ds)
        return ds

    def sb(self, ctx, name, shape, dt):
        self.uid += 1
        t = ctx.enter_context(self.nc.sbuf_tensor("%s_%d" % (name, self.uid), list(shape), dt))
        return Buf(name, t)

    def ps(self, ctx, name, shape, dt):
        self.uid += 1
        t = ctx.enter_context(self.nc.psum_tensor("%s_%d" % (name, self.uid), list(shape), dt))
        return Buf(name, t)

    def wait(self, eng, ev):
        if ev.eng is eng and eng.pe:
            return
        k = id(ev.sem)
        if eng.waited.get(k, 0) >= ev.val:
            return
        val = ev.ds.cnt if ev.ds is not None else ev.val
        eng.h.wait_ge(ev.sem, val)
        eng.waited[k] = val

    def deps(self, eng, reads, writes):
        for b in reads:
            for ev in b.w:
                self.wait(eng, ev)
        for b in writes:
            for ev in b.w:
                self.wait(eng, ev)
            for ev in b.r.values():
                self.wait(eng, ev)
            for ev in b.rd:
                self.wait(eng, ev)

    def done(self, ev, reads, writes):
        for b in reads:
            if ev.eng is None:
                b.rd.append(ev)
            else:
                b.r[ev.eng.name] = ev
        for b in writes:
            b.w = [ev]
            b.r = {}
            b.rd = []

    def op(self, eng, fn, reads=(), writes=()):
        self.deps(eng, reads, writes)
        ins = fn()
        eng.count += 1
        ins.then_inc(eng.sem, 1)
        self.ninstr += 1
        ev = Ev(eng.sem, eng.count, eng)
        self.done(ev, reads, writes)
        if eng.count >= self.EPOCH:
            eng.nsem += 1
            eng.sem = self.newsem("%s_s%d" % (eng.name, eng.nsem))
            eng.count = 0
        return ev

    def dma(self, q, out, in_, ds, reads=(), writes=()):
        assert ds.kind == ("pool" if q is self.pool else "sp"), (ds.kind, q.name)
        self.deps(q, reads, writes)
        ins = q.h.dma_start(out=out, in_=in_)
        ds.cnt += 16
        ins.then_inc(ds.sem, 16)
        self.ninstr += 1
        ev = Ev(ds.sem, ds.cnt, None, ds)
        self.done(ev, reads, writes)
        return ev

    def barrier(self, dummy_src, dummy_dst):
        sp = self.sp
        for e in self.engs:
            if e is not sp and e.count > 0:
                self.wait(sp, Ev(e.sem, e.count, e))
        for ds in self.all_ds:
            if ds.cnt > 0:
                self.wait(sp, Ev(ds.sem, ds.cnt, None, ds))
        ins = sp.h.dma_start(out=dummy_dst, in_=dummy_src)
        self.bar.cnt += 16
        ins.then_inc(self.bar.sem, 16)
        for e in self.engs:
            e.h.wait_ge(self.bar.sem, self.bar.cnt)


class Rot:
    def __init__(self, items):
        self.items = items
        self.i = 0

    def next(self):
        b = self.items[self.i % len(self.items)]
        self.i += 1
        return b


class Prog:
    def __init__(self, cfg):
        self.cfg = cfg
        self.nc = bass.Bass("TRN2", target_bir_lowering=False)
        self.es = ExitStack()

    def din(self, name, shape, dt=F32):
        return self.nc.dram_tensor(name, list(shape), dt, kind="ExternalInput").ap()

    def dscr(self, name, shape, dt, dbg=False):
        kind = "ExternalOutput" if (dbg and self.cfg.get("debug")) else "Internal"
        return self.nc.dram_tensor(name, list(shape), dt, kind=kind).ap()

    def build(self):
        nc = self.nc
        cfg = self.cfg
        parts = cfg["parts"]
        self.inp = {}
        I = self.inp
        I["xp"] = self.din("xp", [SP_, D])
        I["xs"] = self.din("xs", [SS_, D])
        I["memp"] = self.din("memp", [MEM, D])
        I["mems"] = self.din("mems", [MEM, D])
        I["w_in"] = self.din("w_in", [L, D, D_IN])
        I["w_uq"] = self.din("w_uq", [L, 256, 576])
        I["w_uk"] = self.din("w_uk", [L, 128, 384])
        I["w_uv"] = self.din("w_uv", [L, 128, 384])
        I["w_o"] = self.din("w_o", [L, D, D])
        I["w_cq"] = self.din("w_cq", [L, D, D])
        I["w_ckv"] = self.din("w_ckv", [L, D, 2 * D])
        I["w_co"] = self.din("w_co", [L, D, D])
        I["w_up"] = self.din("w_up", [L, D, 2 * DFF])
        I["w_down"] = self.din("w_down", [L, DFF, D])
        I["gvec"] = self.din("gvec", [L, NG])
        I["convp"] = self.din("convp", [L, 128, 4, 44])
        I["natab"] = self.din("natab", [L, 6, 128, NA_MM * 64])
        I["ident"] = self.din("ident", [128, 128])
        I["cosd"] = self.din("cosd", [128, 64, 32])
        I["sind"] = self.din("sind", [128, 64, 32])
        I["cosm"] = self.din("cosm", [128, 64, 16])
        I["sinm"] = self.din("sinm", [128, 64, 16])
        I["dstrip"] = self.din("dstrip", [128, DIL_W])
        I["rsmall"] = self.din("rsmall", [2, 3 * 8 * 8])
        I["ea"] = self.din("ea", [2, 128])
        self.yp = nc.dram_tensor("yp", [SP_, D], F32, kind="ExternalOutput").ap()
        self.ys = nc.dram_tensor("ys", [SS_, D], F32, kind="ExternalOutput").ap()
        SM = max(S for _, S in parts)
        self.SM = SM
        dbg = True
        self.X = self.dscr("X", [SM, D], F32, dbg)
        self.QTn = self.dscr("QTn", [384, SM], BF16, dbg)
        self.KTn = self.dscr("KTn", [384, SM], BF16, dbg)
        self.Vn = self.dscr("Vn", [SM, 384], BF16, dbg)
        self.QTd = self.dscr("QTd", [768, SM], BF16, dbg)
        self.KTd = self.dscr("KTd", [768, SM], BF16, dbg)
        self.Vd = self.dscr("Vd", [SM, 768], BF16, dbg)
        self.QTm = self.dscr("QTm", [6, 96, SM], BF16, dbg)
        self.KTm = self.dscr("KTm", [6, 96, SM], BF16, dbg)
        self.Vm = self.dscr("Vm", [SM, 384], BF16, dbg)
        self.OT = self.dscr("OT", [D, SM], BF16, dbg)
        self.H3T = self.dscr("H3T", [D, SM + 2], BF16, dbg)
        self.ACTT = self.dscr("ACTT", [DFF, SM], BF16, dbg)
        self.dum0 = self.dscr("dum0", [1, 16], F32)
        self.dum1 = self.dscr("dum1", [1, 16], F32)

        with self.es:
            self.T = Tracker(nc, self.es)
            T = self.T
            self.consts()
            for (pname, S) in parts:
                xin = I["xp"] if pname == "p" else I["xs"]
                mem = I["memp"] if pname == "p" else I["mems"]
                yout = self.yp if pname == "p" else self.ys
                nl = cfg.get("layers", L)
                for l in range(nl):
                    last = (l == nl - 1)
                    xsrc = xin if l == 0 else self.X
                    ph = cfg.get("phases", "1nmdxfg")
                    if "1" in ph:
                        self.phase1(l, S, xsrc)
                    if "n" in ph:
                        self.attn_na(l, S)
                    if "d" in ph:
                        self.attn_dil(l, S)
                    if "m" in ph:
                        self.attn_mla(l, S)
                    if "x" in ph:
                        self.phase_x(l, S, xsrc, mem)
                    if "f" in ph:
                        self.ffn_up(l, S)
                    if "g" in ph:
                        self.ffn_down(l, S, yout if (last and not cfg.get("debug")) else self.X)
            T.barrier(self.inp["ident"][0:1, 0:16], self.dum1)
        return nc

    def bar(self):
        self.T.barrier(self.inp["ident"][0:1, 0:16], self.dum1)

    def consts(self):
        T = self.T
        nc = self.nc
        I = self.inp
        es = self.es
        self.ident = T.sb(es, "ident", [128, 128], BF16)
        self.ones = T.sb(es, "ones", [128, 128], BF16)
        self.ea = T.sb(es, "ea", [128, 128], BF16)
        self.rsmall = T.sb(es, "rsmall", [128, 192], BF16)
        T.op(T.pool, lambda: nc.gpsimd.memset(self.ea.t[:], 0.0), writes=[self.ea])
        T.op(T.pool, lambda: nc.gpsimd.memset(self.rsmall.t[:], 0.0), writes=[self.rsmall])
        ds = T.get_ds("pool")
        T.dma(T.pool, self.ident.t[:], I["ident"][:, :], ds, writes=[self.ident])
        ds = T.get_ds("pool")
        T.dma(T.pool, self.ea.t[0:2, :], I["ea"][:, :], ds, writes=[self.ea])
        ds = T.get_ds("pool")
        T.dma(T.pool, self.rsmall.t[0:2, :], I["rsmall"][:, :], ds, writes=[self.rsmall])
        T.op(T.pool, lambda: nc.gpsimd.memset(self.ones.t[:], 1.0), writes=[self.ones])

    def load_w(self, ctx, name, src, K, N, nsplit=1):
        T = self.T
        kc = K // 128
        w = T.sb(ctx, name, [128, kc, N], BF16)
        bufs = []
        step = (N + nsplit - 1) // nsplit
        for c in range(kc):
            b = Buf("%s_c%d" % (name, c), w.t)
            ds = self.newds("pool")
            for n0 in range(0, N, step):
                n1 = min(N, n0 + step)
                T.dma(T.pool, w.t[:, c, n0:n1], src[c * 128:(c + 1) * 128, n0:n1], ds, writes=[b])
            bufs.append(b)
        return w.t, bufs

    def load_rep(self, ctx, name, src1d, n):
        T = self.T
        g = T.sb(ctx, name, [128, n], F32)
        ds = self.newds()
        T.dma(T.sp, g.t[:], src1d.partition_broadcast(128), ds, writes=[g])
        return g

    def begin_phase(self):
        self.phase_ds = []

    def end_phase(self):
        self.bar()
        for ds in self.phase_ds:
            self.T.free_ds.setdefault(ds.kind, []).append(ds)
        self.phase_ds = []

    def newds(self, kind="sp"):
        ds = self.T.get_ds(kind)
        self.phase_ds.append(ds)
        return ds

    def rms_rstd(self, ss, rstd, n, sc, bias):
        T = self.T
        nc = self.nc
        T.op(T.act, lambda: nc.scalar.activation(out=rstd.t[:, 0:n], in_=ss.t[:, 0:n], func=AF.Sqrt,
                                                 bias=self.cbias(bias), scale=sc), reads=[ss, self._cb[float(bias)]], writes=[rstd])
        T.op(T.dve, lambda: nc.vector.reciprocal(out=rstd.t[:, 0:n], in_=rstd.t[:, 0:n]), reads=[rstd], writes=[rstd])

    def cbias(self, v):
        key = float(v)
        if key not in self._cb:
            raise KeyError(key)
        return self._cb[key].t[:, 0:1]

    def make_cbias(self, ctx, vals):
        T = self.T
        nc = self.nc
        self._cb = {}
        for v in vals:
            b = T.sb(ctx, "cb", [128, 1], F32)
            T.op(T.pool, lambda: nc.gpsimd.memset(b.t[:], float(v)), writes=[b])
            self._cb[float(v)] = b
        self._cb_bufs = list(self._cb.values())

    def phase1(self, l, S, xsrc):
        T = self.T
        nc = self.nc
        I = self.inp
        NT = S // 128
        E2 = T.pool
        H2 = nc.gpsimd
        self.begin_phase()
        with ExitStack() as ctx:
            self.make_cbias(ctx, [EPS, 64 * EPS, 96 * EPS])
            win, winb = self.load_w(ctx, "win", I["w_in"][l], D, D_IN, nsplit=2)
            wuq, wuqb = self.load_w(ctx, "wuq", I["w_uq"][l], 256, 576)
            wuk, wukb = self.load_w(ctx, "wuk", I["w_uk"][l], 128, 384)
            wuv, wuvb = self.load_w(ctx, "wuv", I["w_uv"][l], 128, 384)
            gv = self.load_rep(ctx, "gv1", I["gvec"][l, 0:G_CROSS], G_CROSS)
            ropes = [T.sb(ctx, "rope", [128, 96], F32) for _ in range(3)]
            rope_ds = [self.newds() for _ in range(3)]
            xts = [T.sb(ctx, "xt", [128, D], F32) for _ in range(2)]
            xds = [self.newds(), self.newds()]
            junk = T.sb(ctx, "junk", [128, D], BF16)
            ss1 = Rot([T.sb(ctx, "ss1", [128, 8], F32) for _ in range(2)])
            rs1 = Rot([T.sb(ctx, "rs1", [128, 8], F32) for _ in range(2)])
            hb = Rot([T.sb(ctx, "hb", [128, D], BF16) for _ in range(2)])
            hT = Rot([T.sb(ctx, "hT", [128, 8, 128], BF16) for _ in range(2)])
            pT = Rot([T.ps(ctx, "pT", [128, 1024], BF16) for _ in range(1)])
            pz = Rot([T.ps(ctx, "pz", [128, 512], F32) for _ in range(5)])
            pX = Rot([T.ps(ctx, "pX", [128, 1024], BF16) for _ in range(2)])
            sq = Rot([T.sb(ctx, "sq", [128, 576], F32) for _ in range(6)])
            yy = Rot([T.sb(ctx, "yy", [128, 576], F32) for _ in range(6)])
            y2 = Rot([T.sb(ctx, "y2", [128, 576], F32) for _ in range(6)])
            tt = Rot([T.sb(ctx, "tt", [128, 6, 32], F32) for _ in range(16)])
            yb = Rot([T.sb(ctx, "yb", [128, 576], BF16) for _ in range(10)])
            ssh = Rot([T.sb(ctx, "ssh", [128, 8], F32) for _ in range(12)])
            rsh = Rot([T.sb(ctx, "rsh", [128, 8], F32) for _ in range(12)])
            cqn = Rot([T.sb(ctx, "cqn", [128, 384], BF16) for _ in range(2)])
            cT = Rot([T.sb(ctx, "cT", [128, 3, 128], BF16) for _ in range(2)])
            krb = Rot([T.sb(ctx, "krb", [128, 32], F32) for _ in range(2)])
            sskr_r = Rot([T.sb(ctx, "sskr", [128, 1], F32) for _ in range(2)])
            stq = [T.sb(ctx, "stq", [128, 18, 128], BF16) for _ in range(2)]
            stm = [T.sb(ctx, "stm", [96, 12, 128], BF16) for _ in range(2)]
            stq_ds = [self.newds() for _ in range(2)]
            stm_ds = [self.newds() for _ in range(2)]
            vst = [T.sb(ctx, "vst", [128, 1536], BF16) for _ in range(2)]
            vds = [self.newds(), self.newds()]

            dq = []
            DEPTH = 2

            def defer(fn):
                dq.append(fn)
                while len(dq) > DEPTH:
                    dq.pop(0)()

            def load_x(t):
                b = xts[t % 2]
                T.dma(T.sp, b.t[:], xsrc[t * 128:(t + 1) * 128, :], xds[t % 2], writes=[b])
                rp = ropes[t % 3]
                T.dma(T.sp, rp.t[:, 0:32], I["cosd"][:, t, :], rope_ds[t % 3], writes=[rp])
                T.dma(T.sp, rp.t[:, 32:64], I["sind"][:, t, :], rope_ds[t % 3], writes=[rp])
                T.dma(T.sp, rp.t[:, 64:80], I["cosm"][:, t, :], rope_ds[t % 3], writes=[rp])
                T.dma(T.sp, rp.t[:, 80:96], I["sinm"][:, t, :], rope_ds[t % 3], writes=[rp])

            def headnorm_g(pz_b, ncols, H, d, sc, bias, res, extra_ss=None):
                s_ = sq.next()
                T.op(T.act, lambda: nc.scalar.activation(out=s_.t[:, 0:ncols], in_=pz_b.t[:, 0:ncols], func=AF.Square),
                     reads=[pz_b], writes=[s_])
                ssb = ssh.next()
                T.op(T.dve, lambda: nc.vector.tensor_reduce(out=ssb.t[:, 0:H],
                                                            in_=s_.t[:, 0:ncols].rearrange("p (h d) -> p h d", d=d),
                                                            axis=AX.X, op=ALU.add), reads=[s_], writes=[ssb])
                if extra_ss is not None:
                    T.op(T.dve, lambda: nc.vector.tensor_scalar(out=ssb.t[:, 0:H], in0=ssb.t[:, 0:H],
                                                                scalar1=extra_ss.t[:, 0:1], scalar2=None, op0=ALU.add),
                         reads=[ssb, extra_ss], writes=[ssb])
                yield
                rb = rsh.next()
                self.rms_rstd(ssb, rb, H, sc, bias)
                res.append(rb)
                yield

            def tile_jobs(t):
                xt = xts[t % 2]
                rp = ropes[t % 3]
                par = t % 2
                sQ = stq[par]
                sM = stm[par]
                vb = vst[par]
                st = {}

                def head():
                    ssb = ss1.next()
                    T.op(T.act, lambda: nc.scalar.activation(out=junk.t[:], in_=xt.t[:], func=AF.Square,
                                                             accum_out=ssb.t[:, 0:1]), reads=[xt], writes=[junk, ssb])
                    rb = rs1.next()
                    self.rms_rstd(ssb, rb, 1, 1.0 / D, EPS)
                    yield
                    h = hb.next()
                    T.op(T.dve, lambda: nc.vector.scalar_tensor_tensor(out=h.t[:], in0=xt.t[:], scalar=rb.t[:, 0:1],
                                                                       in1=gv.t[:, G_MIX:G_MIX + D], op0=ALU.mult, op1=ALU.mult),
                         reads=[xt, rb, gv], writes=[h])
                    yield
                    hTb = hT.next()
                    self.tr8(h, pT.items[0], hTb)
                    st["hT"] = hTb

                def proj(col0, ncols):
                    z = pz.next()
                    hTb = st["hT"]

                    def mm():
                        ins = None
                        for c in range(8):
                            ins = nc.tensor.matmul(z.t[:, 0:ncols], lhsT=hTb.t[:, c, :], rhs=win[:, c, col0:col0 + ncols],
                                                   start=(c == 0), stop=(c == 7))
                        return ins
                    T.op(T.pe, mm, reads=[hTb] + winb, writes=[z])
                    return z

                def transpose_out(src_b, slabs, width, dst_b, dst_idx0, rows):
                    def now():
                        px = pX.next()

                        def trs():
                            ins = None
                            for s_ in range(slabs):
                                ins = nc.tensor.transpose(out=px.t[0:width, s_ * 128:(s_ + 1) * 128],
                                                          in_=src_b.t[:, s_ * width:(s_ + 1) * width], identity=self.ident.t[:])
                            return ins
                        T.op(T.pe, trs, reads=[src_b, self.ident], writes=[px])
                        T.op(T.act, lambda: nc.scalar.copy(
                            out=dst_b.t[0:rows, dst_idx0:dst_idx0 + slabs, 0:128],
                            in_=px.t[0:rows, 0:slabs * 128].rearrange("p (s t) -> p s t", t=128)),
                            reads=[px], writes=[dst_b])
                    defer(now)

                def rope_g(y_b, H, d, hd, c0, s0, lo0, out_b):
                    yv = y_b.t[:, 0:H * d].rearrange("p (h d) -> p h d", d=d)
                    ov = out_b.t[:, 0:H * d].rearrange("p (h d) -> p h d", d=d)
                    lo = yv[:, :, lo0:lo0 + hd]
                    hi = yv[:, :, lo0 + hd:lo0 + 2 * hd]
                    cs = rp.t[:, c0:c0 + hd].unsqueeze(1).to_broadcast([128, H, hd])
                    sn = rp.t[:, s0:s0 + hd].unsqueeze(1).to_broadcast([128, H, hd])
                    t1, t2, t3, t4 = tt.next(), tt.next(), tt.next(), tt.next()
                    T.op(T.dve, lambda: nc.vector.tensor_tensor(out=t1.t[:, 0:H, 0:hd], in0=lo, in1=cs, op=ALU.mult),
                         reads=[y_b, rp], writes=[t1])
                    T.op(E2, lambda: H2.tensor_tensor(out=t2.t[:, 0:H, 0:hd], in0=hi, in1=sn, op=ALU.mult),
                         reads=[y_b, rp], writes=[t2])
                    T.op(T.dve, lambda: nc.vector.tensor_tensor(out=t3.t[:, 0:H, 0:hd], in0=hi, in1=cs, op=ALU.mult),
                         reads=[y_b, rp], writes=[t3])
                    T.op(E2, lambda: H2.tensor_tensor(out=t4.t[:, 0:H, 0:hd], in0=lo, in1=sn, op=ALU.mult),
                         reads=[y_b, rp], writes=[t4])
                    yield
                    T.op(T.dve, lambda: nc.vector.tensor_tensor(out=ov[:, :, lo0:lo0 + hd], in0=t1.t[:, 0:H, 0:hd],
                                                                in1=t2.t[:, 0:H, 0:hd], op=ALU.subtract),
                         reads=[t1, t2], writes=[out_b])
                    T.op(E2, lambda: H2.tensor_tensor(out=ov[:, :, lo0 + hd:lo0 + 2 * hd], in0=t3.t[:, 0:H, 0:hd],
                                                      in1=t4.t[:, 0:H, 0:hd], op=ALU.add),
                         reads=[t3, t4], writes=[out_b])
                    if lo0 > 0:
                        T.op(E2, lambda: H2.tensor_copy(out=ov[:, :, 0:lo0], in_=yv[:, :, 0:lo0]),
                             reads=[y_b], writes=[out_b])
                    yield

                def qk_block(col0, H, goff, qscale, rope, dst_idx0):
                    d = 64
                    ncols = H * d
                    z = proj(col0, ncols)
                    yield
                    res = []
                    yield from headnorm_g(z, ncols, H, d, 1.0 if qscale else 1.0 / d, d * EPS if qscale else EPS, res)
                    rb_ = res[0]
                    y = yy.next()
                    T.op(T.dve, lambda: nc.vector.tensor_tensor(
                        out=y.t[:, 0:ncols].rearrange("p (h d) -> p h d", d=d),
                        in0=z.t[:, 0:ncols].rearrange("p (h d) -> p h d", d=d),
                        in1=rb_.t[:, 0:H].unsqueeze(2).to_broadcast([128, H, d]), op=ALU.mult),
                        reads=[z, rb_], writes=[y])
                    yield
                    ob = yb.next()
                    gb = gv.t[:, goff:goff + d].unsqueeze(1).to_broadcast([128, H, d])
                    if rope:
                        yg = y2.next()
                        T.op(E2, lambda: H2.tensor_tensor(
                            out=yg.t[:, 0:ncols].rearrange("p (h d) -> p h d", d=d),
                            in0=y.t[:, 0:ncols].rearrange("p (h d) -> p h d", d=d), in1=gb, op=ALU.mult),
                            reads=[y, gv], writes=[yg])
                        yield
                        yield from rope_g(yg, H, d, 32, 0, 32, 0, ob)
                    else:
                        T.op(E2, lambda: H2.tensor_tensor(
                            out=ob.t[:, 0:ncols].rearrange("p (h d) -> p h d", d=d),
                            in0=y.t[:, 0:ncols].rearrange("p (h d) -> p h d", d=d), in1=gb, op=ALU.mult),
                            reads=[y, gv], writes=[ob])
                        yield
                    transpose_out(ob, H // 2, 128, sQ, dst_idx0, 128)

                def v_block(col0, ncols, voff):
                    z = proj(col0, ncols)
                    yield
                    T.op(T.act, lambda: nc.scalar.copy(out=vb.t[:, voff:voff + ncols], in_=z.t[:, 0:ncols]),
                         reads=[z], writes=[vb])

                def mla():
                    z0 = proj(0, 416)
                    yield
                    cq = cqn.next()
                    res = []
                    yield from headnorm_g(z0, 256, 1, 256, 1.0 / 256, EPS, res)
                    rq = res[0]
                    T.op(T.dve, lambda: nc.vector.scalar_tensor_tensor(out=cq.t[:, 0:256], in0=z0.t[:, 0:256], scalar=rq.t[:, 0:1],
                                                                       in1=gv.t[:, G_CQ:G_CQ + 256], op0=ALU.mult, op1=ALU.mult),
                         reads=[z0, rq, gv], writes=[cq])
                    s_ = sq.next()
                    ssk = ssh.next()
                    T.op(T.act, lambda: nc.scalar.activation(out=s_.t[:, 0:128], in_=z0.t[:, 256:384], func=AF.Square,
                                                             accum_out=ssk.t[:, 0:1]), reads=[z0], writes=[s_, ssk])
                    yield
                    rk = rsh.next()
                    self.rms_rstd(ssk, rk, 1, 1.0 / 128, EPS)
                    yield
                    T.op(T.dve, lambda: nc.vector.scalar_tensor_tensor(out=cq.t[:, 256:384], in0=z0.t[:, 256:384], scalar=rk.t[:, 0:1],
                                                                       in1=gv.t[:, G_CKV:G_CKV + 128], op0=ALU.mult, op1=ALU.mult),
                         reads=[z0, rk, gv], writes=[cq])
                    kr = krb.next()
                    s2_ = sq.next()
                    sskr = sskr_r.next()
                    T.op(T.dve, lambda: nc.vector.tensor_copy(out=kr.t[:], in_=z0.t[:, 384:416]), reads=[z0], writes=[kr])
                    T.op(T.dve, lambda: nc.vector.tensor_tensor(out=s2_.t[:, 0:32], in0=kr.t[:], in1=kr.t[:], op=ALU.mult),
                         reads=[kr], writes=[s2_])
                    T.op(T.dve, lambda: nc.vector.tensor_reduce(out=sskr.t[:, 0:1], in_=s2_.t[:, 0:32], axis=AX.X, op=ALU.add),
                         reads=[s2_], writes=[sskr])
                    yield
                    p2 = pT.items[0]

                    def tr3():
                        ins = None
                        for c in range(3):
                            ins = nc.tensor.transpose(out=p2.t[:, c * 128:(c + 1) * 128], in_=cq.t[:, c * 128:(c + 1) * 128],
                                                      identity=self.ident.t[:])
                        return ins
                    T.op(T.pe, tr3, reads=[cq, self.ident], writes=[p2])
                    cTb = cT.next()
                    T.op(T.act, lambda: nc.scalar.copy(out=cTb.t[:].rearrange("p c t -> p (c t)"), in_=p2.t[:, 0:384]),
                         reads=[p2], writes=[cTb])
                    yield
                    for hq in range(2):
                        zq = pz.next()

                        def mmq():
                            ins = None
                            for c in range(2):
                                ins = nc.tensor.matmul(zq.t[:, 0:288], lhsT=cTb.t[:, c, :], rhs=wuq[:, c, hq * 288:(hq + 1) * 288],
                                                       start=(c == 0), stop=(c == 1))
                            return ins
                        T.op(T.pe, mmq, reads=[cTb] + wuqb, writes=[zq])
                        yield
                        res = []
                        yield from headnorm_g(zq, 288, 3, 96, 1.0, 96 * EPS, res)
                        rb_ = res[0]
                        y = yy.next()
                        T.op(T.dve, lambda: nc.vector.tensor_tensor(
                            out=y.t[:, 0:288].rearrange("p (h d) -> p h d", d=96),
                            in0=zq.t[:, 0:288].rearrange("p (h d) -> p h d", d=96),
                            in1=rb_.t[:, 0:3].unsqueeze(2).to_broadcast([128, 3, 96]), op=ALU.mult),
                            reads=[zq, rb_], writes=[y])
                        yield
                        yg = y2.next()
                        T.op(E2, lambda: H2.tensor_tensor(
                            out=yg.t[:, 0:288].rearrange("p (h d) -> p h d", d=96),
                            in0=y.t[:, 0:288].rearrange("p (h d) -> p h d", d=96),
                            in1=gv.t[:, G_MQN:G_MQN + 96].unsqueeze(1).to_broadcast([128, 3, 96]), op=ALU.mult),
                            reads=[y, gv], writes=[yg])
                        yield
                        ob = yb.next()
                        yield from rope_g(yg, 3, 96, 16, 64, 80, 64, ob)
                        transpose_out(ob, 3, 96, sM, hq * 3, 96)
                    zk = pz.next()
                    T.op(T.pe, lambda: nc.tensor.matmul(zk.t[:, 0:384], lhsT=cTb.t[:, 2, :], rhs=wuk[:, 0, :], start=True, stop=True),
                         reads=[cTb] + wukb, writes=[zk])
                    zv = pz.next()
                    T.op(T.pe, lambda: nc.tensor.matmul(zv.t[:, 0:384], lhsT=cTb.t[:, 2, :], rhs=wuv[:, 0, :], start=True, stop=True),
                         reads=[cTb] + wuvb, writes=[zv])
                    yield
                    T.op(T.act, lambda: nc.scalar.copy(out=vb.t[:, 0:384], in_=zv.t[:, 0:384]), reads=[zv], writes=[vb])
                    res = []
                    yield from headnorm_g(zk, 384, 6, 64, 1.0 / 96, EPS, res, extra_ss=sskr)
                    rbk = res[0]
                    yk = yy.next()
                    ykv = yk.t[:, 0:576].rearrange("p (h d) -> p h d", d=96)
                    T.op(T.dve, lambda: nc.vector.tensor_tensor(
                        out=ykv[:, :, 0:64], in0=zk.t[:, 0:384].rearrange("p (h d) -> p h d", d=64),
                        in1=rbk.t[:, 0:6].unsqueeze(2).to_broadcast([128, 6, 64]), op=ALU.mult),
                        reads=[zk, rbk], writes=[yk])
                    T.op(T.dve, lambda: nc.vector.tensor_tensor(
                        out=ykv[:, :, 64:96], in0=kr.t[:, :].unsqueeze(1).to_broadcast([128, 6, 32]),
                        in1=rbk.t[:, 0:6].unsqueeze(2).to_broadcast([128, 6, 32]), op=ALU.mult),
                        reads=[kr, rbk], writes=[yk])
                    yield
                    ykg = y2.next()
                    T.op(E2, lambda: H2.tensor_tensor(
                        out=ykg.t[:, 0:576].rearrange("p (h d) -> p h d", d=96), in0=ykv,
                        in1=gv.t[:, G_MKN:G_MKN + 96].unsqueeze(1).to_broadcast([128, 6, 96]), op=ALU.mult),
                        reads=[yk, gv], writes=[ykg])
                    yield
                    obk = yb.next()
                    yield from rope_g(ykg, 6, 96, 16, 64, 80, 64, obk)
                    transpose_out(obk, 6, 96, sM, 6, 96)

                def stores():
                    r0 = t * 128
                    T.dma(T.sp, self.Vm[r0:r0 + 128, :], vb.t[:, 0:384], vds[par], reads=[vb])
                    T.dma(T.sp, self.Vn[r0:r0 + 128, :], vb.t[:, 384:768], vds[par], reads=[vb])
                    T.dma(T.sp, self.Vd[r0:r0 + 128, :], vb.t[:, 768:1536], vds[par], reads=[vb])
                    for (dst, i0, ns) in ((self.QTn, 0, 3), (self.KTn, 3, 3), (self.QTd, 6, 6), (self.KTd, 12, 6)):
                        T.dma(T.sp, dst.rearrange("(s p) t -> p s t", p=128)[:, :, r0:r0 + 128], sQ.t[:, i0:i0 + ns, 0:128],
                              stq_ds[par], reads=[sQ])
                    T.dma(T.sp, self.QTm[:, :, r0:r0 + 128].rearrange("h d t -> d h t"), sM.t[:, 0:6, 0:128], stm_ds[par], reads=[sM])
                    T.dma(T.sp, self.KTm[:, :, r0:r0 + 128].rearrange("h d t -> d h t"), sM.t[:, 6:12, 0:128], stm_ds[par], reads=[sM])

                blocks = [
                    mla,
                    lambda: qk_block(416, 6, G_NQ, True, False, 0),
                    lambda: qk_block(800, 6, G_NK, False, False, 3),
                    lambda: qk_block(1568, 6, G_DQ, True, True, 6),
                    lambda: v_block(1184, 384, 384),
                    lambda: qk_block(1952, 6, G_DQ, True, True, 9),
                    lambda: qk_block(2336, 6, G_DK, False, True, 12),
                    lambda: v_block(3104, 384, 768),
                    lambda: qk_block(2720, 6, G_DK, False, True, 15),
                    lambda: v_block(3488, 384, 1152),
                ]
                return head, blocks, stores

            W = 4
            active = []

            def pump():
                for g in list(active):
                    try:
                        next(g)
                    except StopIteration:
                        active.remove(g)

            load_x(0)
            head0, blocks0, stores0 = tile_jobs(0)
            for _ in head0():
                pass
            cur = (blocks0, stores0)
            for t in range(NT):
                if t + 1 < NT:
                    load_x(t + 1)
                blocks, stores = cur
                for bf in blocks:
                    active.append(bf())
                    while len(active) >= W:
                        pump()
                if t + 1 < NT:
                    hd, nb, ns = tile_jobs(t + 1)
                    active.append(hd())
                    cur = (nb, ns)
                while active:
                    pump()
                defer(stores)
            while dq:
                dq.pop(0)()
            self.end_phase()

    def attention(self, S, slots, l, G=2, SKEW=2):
        T = self.T
        nc = self.nc
        I = self.inp
        NT = S // 128
        NQB = S // 512
        nstream = len(slots[0]["streams"])
        nset = 2 if nstream == 1 else 1
        use_dil = any(m[0] == "dil" for sl in slots for (_, _, ms) in sl["chunks"](0) for m in ms)
        dpad = slots[0]["streams"][0]["d"]
        self.begin_phase()
        with ExitStack() as ctx:
            sets = []
            for si in range(nset):
                st = []
                for k in range(nstream):
                    kt = T.sb(ctx, "kt", [128, S], BF16)
                    va = T.sb(ctx, "va", [128, NT, 128], BF16)
                    T.op(T.pool, lambda: nc.gpsimd.memset(va.t[:, :, 64:128], 1.0), writes=[va])
                    T.op(T.pool, lambda: nc.gpsimd.memset(kt.t[dpad:128, :], 0.0), writes=[kt])
                    st.append((kt, va, self.newds(), self.newds()))
                nat = T.sb(ctx, "nat", [128, NA_MM * 64], BF16)
                sets.append((st, nat, self.newds("pool")))
            dstrip = None
            if use_dil:
                dstrip = T.sb(ctx, "dstrip", [128, DIL_W], BF16)
                dsd = self.newds("pool")
                for c0 in range(0, DIL_W, 2048):
                    c1 = min(DIL_W, c0 + 2048)
                    T.dma(T.pool, dstrip.t[:, c0:c1], I["dstrip"][:, c0:c1], dsd, writes=[dstrip])
            qtb = [Rot([T.sb(ctx, "qtb", [128, 512], BF16) for _ in range(3)]) for _ in range(nstream)]
            for r_ in qtb:
                for b_ in r_.items:
                    T.op(T.pool, lambda: nc.gpsimd.memset(b_.t[dpad:128, :], 0.0), writes=[b_])
            qds = [[self.newds(), self.newds(), self.newds()] for _ in range(nstream)]
            ptr = Rot([T.sb(ctx, "pt", [128, 2, 512], BF16) for _ in range(5 if G == 2 else 10)])
            pS_t = [T.ps(ctx, "pS", [128, 1024], F32) for _ in range(3)]
            if G == 2:
                pS = Rot([(b_, 0) for b_ in pS_t])
            else:
                pS = Rot([(Buf("pSh", b_.t), off_) for b_ in pS_t for off_ in (0, 512)])
            pO = Rot([T.ps(ctx, "pO", [128, 512], F32) for _ in range(2)])
            rzr = Rot([T.sb(ctx, "rz", [64, 512], F32) for _ in range(2)])
            ost = [T.sb(ctx, "ost", [64, 512], BF16) for _ in range(3)]
            ods = [self.newds() for _ in range(3)]
            oi = 0

            def load_slot(i):
                sl = slots[i]
                st, nat, nds = sets[i % nset]
                for k, sm in enumerate(sl["streams"]):
                    kt, va, kds, vds_ = st[k]
                    d = sm["d"]
                    T.dma(T.sp, kt.t[0:d, :], sm["kt"], kds, writes=[kt])
                    for c0 in range(0, NT, 16):
                        c1 = min(NT, c0 + 16)
                        T.dma(T.sp, va.t[:, c0:c1, 0:64],
                              sm["v"][c0 * 128:c1 * 128, :].rearrange("(c p) d -> p c d", p=128), vds_, writes=[va])
                if sl.get("natab") is not None:
                    T.dma(T.pool, nat.t[:], sl["natab"], nds, writes=[nat])

            pending = []

            def emit_pv(G):
                grp, st_, pt, po, g0, total, fin = G

                def pv():
                    ins = None
                    for gi, (k, c, masks) in enumerate(grp):
                        va = st_[k][1]
                        idx = g0 + gi
                        ins = nc.tensor.matmul(po.t[:, :], lhsT=va.t[:, c, :], rhs=pt.t[:, gi, :],
                                               start=(idx == 0), stop=(idx == total - 1))
                    return ins
                T.op(T.pe, pv, reads=[pt] + [st_[k][1] for (k, _, _) in grp], writes=[po])
                if fin is not None:
                    row0, jq = fin
                    rz = rzr.next()
                    T.op(T.dve, lambda: nc.vector.reciprocal(out=rz.t[:, :], in_=po.t[64:128, :]), reads=[po], writes=[rz])
                    ob = ost[self._oi % 3]
                    T.op(T.dve, lambda: nc.vector.tensor_tensor(out=ob.t[:, :], in0=po.t[0:64, :], in1=rz.t[:, :], op=ALU.mult),
                         reads=[po, rz], writes=[ob])
                    T.dma(T.sp, self.OT[row0:row0 + 64, jq * 512:(jq + 1) * 512], ob.t[:, :], ods[self._oi % 3], reads=[ob])
                    self._oi += 1

            self._oi = 0
            load_slot(0)
            for i, sl in enumerate(slots):
                if nset == 2 and i + 1 < len(slots):
                    while pending:
                        emit_pv(pending.pop(0))
                    load_slot(i + 1)
                st, nat, _ = sets[i % nset]
                def load_q(jq_):
                    out_ = []
                    for k, sm in enumerate(sl["streams"]):
                        qb = qtb[k].next()
                        d = sm["d"]
                        T.dma(T.sp, qb.t[0:d, :], sm["qt"][:, jq_ * 512:(jq_ + 1) * 512], qds[k][(qtb[k].i - 1) % 3], writes=[qb])
                        out_.append(qb)
                    return out_
                qs_next = load_q(0)
                for jq in range(NQB):
                    qs = qs_next
                    if jq + 1 < NQB:
                        qs_next = load_q(jq + 1)
                    items = sl["chunks"](jq)
                    total = len(items)
                    po = pO.next()
                    for g0 in range(0, total, G):
                        grp = items[g0:g0 + G]
                        ps, poff = pS.next()

                        def qk():
                            ins = None
                            for gi, (k, c, masks) in enumerate(grp):
                                kt = st[k][0]
                                d = sl["streams"][k]["d"]
                                o_ = ps.t[:, poff + gi * 512:poff + (gi + 1) * 512]
                                mm_masks = [m for m in masks if m[0] != "dil"]
                                ins = nc.tensor.matmul(o_, lhsT=kt.t[:, c * 128:(c + 1) * 128], rhs=qs[k].t[:, :],
                                                       start=True, stop=(len(mm_masks) == 0))
                                masks = mm_masks
                                for mi, m in enumerate(masks):
                                    last = (mi == len(masks) - 1)
                                    if m[0] == "nat":
                                        ins = nc.tensor.matmul(o_, lhsT=self.ident.t[:], rhs=nat.t[:, m[1]:m[1] + 512],
                                                               start=False, stop=last)
                                    elif m[0] == "dil":
                                        pass
                                    else:
                                        ins = nc.tensor.matmul(o_.rearrange("p (j c) -> p j c", c=64), lhsT=self.ea.t[:, :],
                                                               rhs=self.rsmall.t[:, m[1] * 8:(m[1] + 1) * 8].unsqueeze(2).to_broadcast([128, 8, 64]),
                                                               start=False, stop=last)
                            return ins
                        rd = [st[k][0] for (k, _, _) in grp] + [qs[k] for (k, _, _) in grp] + [self.ident, self.ea, self.rsmall, nat]
                        if dstrip is not None:
                            rd.append(dstrip)
                        T.op(T.pe, qk, reads=rd, writes=[ps])
                        pt = ptr.next()
                        n = len(grp)
                        T.op(T.act, lambda: nc.scalar.activation(out=pt.t[:, 0:n, :].rearrange("p g q -> p (g q)"),
                                                                 in_=ps.t[:, poff:poff + n * 512], func=AF.Exp), reads=[ps], writes=[pt])
                        for gi, (k_, c_, masks_) in enumerate(grp):
                            for m in masks_:
                                if m[0] == "dil":
                                    self._mi = getattr(self, "_mi", 0) + 1
                                    if self._mi % 3 == 0:
                                        T.op(T.pool, lambda: nc.gpsimd.tensor_tensor(out=pt.t[:, gi, :], in0=pt.t[:, gi, :],
                                                                                     in1=dstrip.t[:, m[1]:m[1] + 512], op=ALU.mult),
                                             reads=[pt, dstrip], writes=[pt])
                                    else:
                                        T.op(T.dve, lambda: nc.vector.tensor_tensor(out=pt.t[:, gi, :], in0=pt.t[:, gi, :],
                                                                                    in1=dstrip.t[:, m[1]:m[1] + 512], op=ALU.mult),
                                             reads=[pt, dstrip], writes=[pt])
                        fin = (sl["row0"], jq) if g0 + G >= total else None
                        pending.append((grp, st, pt, po, g0, total, fin))
                        if len(pending) > SKEW:
                            emit_pv(pending.pop(0))
                if nset == 1:
                    while pending:
                        emit_pv(pending.pop(0))
                    if i + 1 < len(slots):
                        load_slot(i + 1)
            while pending:
                emit_pv(pending.pop(0))
            self.end_phase()

    def attn_mla(self, l, S):
        NT = S // 128
        slots = []
        for h in range(6):
            slots.append(dict(row0=h * 64, natab=None,
                              streams=[dict(qt=self.QTm[h, :, 0:S], kt=self.KTm[h, :, 0:S], v=self.Vm[0:S, h * 64:(h + 1) * 64], d=96)],
                              chunks=(lambda jq: [(0, c, []) for c in range(NT)])))
        self.attention(S, slots, l)

    def attn_na(self, l, S):
        NT = S // 128
        NQB = S // 512
        I = self.inp

        def chunks(jq):
            ty = 0 if jq == 0 else (2 if jq == NQB - 1 else 1)
            out = []
            for ci in range(8):
                c = 4 * jq - 2 + ci
                if 0 <= c < NT:
                    m0 = 11 - 2 * ci
                    out.append((0, c, [("nat", (m0 + 3) * 64), ("row", ty * 8 + ci)]))
            return out
        slots = []
        for h in range(6):
            slots.append(dict(row0=384 + h * 64, natab=I["natab"][l, h],
                              streams=[dict(qt=self.QTn[h * 64:(h + 1) * 64, 0:S], kt=self.KTn[h * 64:(h + 1) * 64, 0:S],
                                            v=self.Vn[0:S, h * 64:(h + 1) * 64], d=64)], chunks=chunks))
        self.attention(S, slots, l, G=1, SKEW=5)

    def attn_dil(self, l, S):
        NT = S // 128

        def chunks(jq):
            out = []
            for g in range(3):
                for Dd in range(DIL_DMIN[g], DIL_DMAX[g] + 1, 128):
                    c = (jq * 512 + Dd) // 128
                    if 0 <= c < NT:
                        out.append((g, c, [("dil", DIL_OFF[g] + DIL_DMAX[g] - Dd)]))
            return out
        slots = []
        for h in range(4):
            sts = []
            for g in range(3):
                hh = g * 4 + h
                sts.append(dict(qt=self.QTd[hh * 64:(hh + 1) * 64, 0:S], kt=self.KTd[hh * 64:(hh + 1) * 64, 0:S],
                                v=self.Vd[0:S, hh * 64:(hh + 1) * 64], d=64))
            slots.append(dict(row0=768 + h * 64, natab=None, streams=sts, chunks=chunks))
        self.attention(S, slots, l, G=1, SKEW=5)

    def phase_x(self, l, S, xsrc, mem):
        T = self.T
        nc = self.nc
        I = self.inp
        NT = S // 128
        GO = G_CROSS
        self.begin_phase()
        with ExitStack() as ctx:
            self.make_cbias(ctx, [EPS, 256 * EPS])
            wo, wob = self.load_w(ctx, "wo", I["w_o"][l], D, D)
            wcq, wcqb = self.load_w(ctx, "wcq", I["w_cq"][l], D, D)
            wco, wcob = self.load_w(ctx, "wco", I["w_co"][l], D, D)
            wkv, wkvb = self.load_w(ctx, "wkv", I["w_ckv"][l], D, 2 * D)
            gv = self.load_rep(ctx, "gv2", I["gvec"][l, GO:NG], NG - GO)
            g_cross, g_xq, g_ffn, g_mem, g_xk = 0, G_XQ - GO, G_FFN - GO, G_MEM - GO, G_XK - GO
            pA = T.ps(ctx, "pA", [128, 1024], F32)
            pC = T.ps(ctx, "pC", [128, 1024], F32)
            pD = T.ps(ctx, "pD", [128, 1024], F32)
            pT = T.ps(ctx, "pT", [128, 1024], BF16)
            pZ = T.ps(ctx, "pZ", [128, 512], F32)
            xts = [T.sb(ctx, "xt", [128, D], F32) for _ in range(3)]
            xds = [self.newds() for _ in range(3)]
            ots = [T.sb(ctx, "oT", [128, 8, 128], BF16) for _ in range(2)]
            otds = [self.newds() for _ in range(2)]
            junk = T.sb(ctx, "junk", [128, D], BF16)
            ss = Rot([T.sb(ctx, "ss", [128, 8], F32) for _ in range(6)])
            rs = Rot([T.sb(ctx, "rs", [128, 8], F32) for _ in range(6)])
            hb = Rot([T.sb(ctx, "hb", [128, D], BF16) for _ in range(4)])
            hT = Rot([T.sb(ctx, "hT", [128, 8, 128], BF16) for _ in range(3)])
            sqb = T.sb(ctx, "sqb", [128, D], F32)
            yf = T.sb(ctx, "yf", [128, D], F32)
            kmT = T.sb(ctx, "kmT", [128, 8, 256], BF16)
            vms = T.sb(ctx, "vms", [128, 2, D], BF16)
            ptb = Rot([T.sb(ctx, "ptb", [128, 8, 128], BF16) for _ in range(2)])
            rzb = T.sb(ctx, "rzb", [128, 512], F32)
            ocT = Rot([T.sb(ctx, "ocT", [128, 8, 128], BF16) for _ in range(2)])
            h3s = [T.sb(ctx, "h3s", [128, 8, 128], BF16) for _ in range(2)]
            h3ds = [self.newds() for _ in range(2)]
            zt = T.sb(ctx, "zt", [128, 8, 1], BF16)
            zds = self.newds()
            H3v = self.H3T.rearrange("(c p) t -> p c t", p=128)
            OTv = self.OT.rearrange("(c p) t -> p c t", p=128)

            def rmsnorm_T(xb, goff, hT_dst):
                ssb = ss.next()
                T.op(T.act, lambda: nc.scalar.activation(out=junk.t[:], in_=xb.t[:], func=AF.Square, accum_out=ssb.t[:, 0:1]),
                     reads=[xb], writes=[junk, ssb])
                rb = rs.next()
                self.rms_rstd(ssb, rb, 1, 1.0 / D, EPS)
                h = hb.next()
                T.op(T.dve, lambda: nc.vector.scalar_tensor_tensor(out=h.t[:], in0=xb.t[:], scalar=rb.t[:, 0:1],
                                                                   in1=gv.t[:, goff:goff + D], op0=ALU.mult, op1=ALU.mult),
                     reads=[xb, rb, gv], writes=[h])
                self.tr8(h, pT, hT_dst)

            def proj2(dst_ps, lhs_b, w, wb, coff=0):
                def mm():
                    ins = None
                    for nh in range(2):
                        for c in range(8):
                            ins = nc.tensor.matmul(dst_ps.t[:, nh * 512:(nh + 1) * 512], lhsT=lhs_b.t[:, c, :],
                                                   rhs=w[:, c, coff + nh * 512:coff + (nh + 1) * 512], start=(c == 0), stop=(c == 7))
                    return ins
                T.op(T.pe, mm, reads=[lhs_b] + wb, writes=[dst_ps])

            def headnorm4(src_ps, goff, sc, bias, out_b):
                T.op(T.act, lambda: nc.scalar.activation(out=sqb.t[:], in_=src_ps.t[:], func=AF.Square), reads=[src_ps], writes=[sqb])
                ssb = ss.next()
                T.op(T.dve, lambda: nc.vector.tensor_reduce(out=ssb.t[:, 0:4], in_=sqb.t[:].rearrange("p (h d) -> p h d", d=256),
                                                            axis=AX.X, op=ALU.add), reads=[sqb], writes=[ssb])
                rb = rs.next()
                self.rms_rstd(ssb, rb, 4, sc, bias)
                T.op(T.dve, lambda: nc.vector.tensor_tensor(out=yf.t[:].rearrange("p (h d) -> p h d", d=256),
                                                            in0=src_ps.t[:].rearrange("p (h d) -> p h d", d=256),
                                                            in1=rb.t[:, 0:4].unsqueeze(2).to_broadcast([128, 4, 256]), op=ALU.mult),
                     reads=[src_ps, rb], writes=[yf])
                T.op(T.dve, lambda: nc.vector.tensor_tensor(out=out_b.t[:].rearrange("p (h d) -> p h d", d=256),
                                                            in0=yf.t[:].rearrange("p (h d) -> p h d", d=256),
                                                            in1=gv.t[:, goff:goff + 256].unsqueeze(1).to_broadcast([128, 4, 256]), op=ALU.mult),
                     reads=[yf, gv], writes=[out_b])

            for mt in range(2):
                xb = xts[mt]
                T.dma(T.sp, xb.t[:], mem[mt * 128:(mt + 1) * 128, :], xds[mt], writes=[xb])
                mT = hT.next()
                rmsnorm_T(xb, g_mem, mT)
                proj2(pA, mT, wkv, wkvb, 0)
                kn = hb.next()
                headnorm4(pA, g_xk, 1.0 / 256, EPS, kn)

                def trk():
                    ins = None
                    for c in range(8):
                        ins = nc.tensor.transpose(out=pT.t[:, c * 128:(c + 1) * 128], in_=kn.t[:, c * 128:(c + 1) * 128],
                                                  identity=self.ident.t[:])
                    return ins
                T.op(T.pe, trk, reads=[kn, self.ident], writes=[pT])
                T.op(T.act, lambda: nc.scalar.copy(out=kmT.t[:, :, mt * 128:(mt + 1) * 128],
                                                   in_=pT.t[:].rearrange("p (c t) -> p c t", t=128)), reads=[pT], writes=[kmT])
                proj2(pC, mT, wkv, wkvb, D)
                T.op(T.act, lambda: nc.scalar.copy(out=vms.t[:, mt, :], in_=pC.t[:]), reads=[pC], writes=[vms])

            def loads(t):
                xb = xts[t % 3]
                T.dma(T.sp, xb.t[:], xsrc[t * 128:(t + 1) * 128, :], xds[t % 3], writes=[xb])
                ob = ots[t % 2]
                T.dma(T.sp, ob.t[:], OTv[:, :, t * 128:(t + 1) * 128], otds[t % 2], writes=[ob])

            qcTs = [T.sb(ctx, "qcT", [128, 8, 128], BF16) for _ in range(2)]

            def genA(t):
                xb = xts[t % 3]
                ob = ots[t % 2]
                proj2(pA, ob, wo, wob)
                T.op(T.dve, lambda: nc.vector.tensor_tensor(out=xb.t[:], in0=pA.t[:], in1=xb.t[:], op=ALU.add), reads=[pA, xb], writes=[xb])
                yield
                h2T = hT.next()
                rmsnorm_T(xb, g_cross, h2T)
                yield
                proj2(pA, h2T, wcq, wcqb)
                qn = hb.next()
                headnorm4(pA, g_xq, 1.0, 256 * EPS, qn)
                yield
                self.tr8(qn, pT, qcTs[t % 2])

            def genB(t):
                xb = xts[t % 3]
                qcT = qcTs[t % 2]

                def sc_mm():
                    ins = None
                    for hh in range(4):
                        for kc in range(2):
                            o_ = pC.t[:, (hh * 2 + kc) * 128:(hh * 2 + kc + 1) * 128]
                            for dc in range(2):
                                ins = nc.tensor.matmul(o_, lhsT=kmT.t[:, hh * 2 + dc, kc * 128:(kc + 1) * 128], rhs=qcT.t[:, hh * 2 + dc, :],
                                                       start=(dc == 0), stop=(dc == 1))
                    return ins
                T.op(T.pe, sc_mm, reads=[kmT, qcT], writes=[pC])
                pt = ptb.next()
                T.op(T.act, lambda: nc.scalar.activation(out=pt.t[:].rearrange("p c q -> p (c q)"), in_=pC.t[:], func=AF.Exp),
                     reads=[pC], writes=[pt])
                yield

                def pv_mm():
                    ins = None
                    for hh in range(4):
                        for kc in range(2):
                            ins = nc.tensor.matmul(pZ.t[:, hh * 128:(hh + 1) * 128], lhsT=self.ones.t[:], rhs=pt.t[:, hh * 2 + kc, :],
                                                   start=(kc == 0), stop=(kc == 1))
                        for dvc in range(2):
                            for kc in range(2):
                                ins = nc.tensor.matmul(pD.t[:, (hh * 2 + dvc) * 128:(hh * 2 + dvc + 1) * 128],
                                                       lhsT=vms.t[:, kc, hh * 256 + dvc * 128:hh * 256 + (dvc + 1) * 128],
                                                       rhs=pt.t[:, hh * 2 + kc, :], start=(kc == 0), stop=(kc == 1))
                    return ins
                T.op(T.pe, pv_mm, reads=[pt, vms, self.ones], writes=[pZ, pD])
                T.op(T.dve, lambda: nc.vector.reciprocal(out=rzb.t[:], in_=pZ.t[:]), reads=[pZ], writes=[rzb])
                oc = ocT.next()
                T.op(T.dve, lambda: nc.vector.tensor_tensor(
                    out=oc.t[:].rearrange("p (h e) q -> p h e q", e=2),
                    in0=pD.t[:].rearrange("p (h e q) -> p h e q", e=2, q=128),
                    in1=rzb.t[:].rearrange("p (h q) -> p h q", q=128).unsqueeze(2).to_broadcast([128, 4, 2, 128]), op=ALU.mult),
                    reads=[pD, rzb], writes=[oc])
                yield
                proj2(pC, oc, wco, wcob)
                T.op(T.dve, lambda: nc.vector.tensor_tensor(out=xb.t[:], in0=pC.t[:], in1=xb.t[:], op=ALU.add), reads=[pC, xb], writes=[xb])
                T.dma(T.sp, self.X[t * 128:(t + 1) * 128, :], xb.t[:], xds[t % 3], reads=[xb])
                yield
                h3 = h3s[t % 2]
                rmsnorm_T(xb, g_ffn, h3)
                T.dma(T.sp, H3v[:, :, 1 + t * 128:1 + (t + 1) * 128], h3.t[:], h3ds[t % 2], reads=[h3])

            loads(0)
            if NT > 1:
                loads(1)
            for _ in genA(0):
                pass
            for t in range(NT):
                if t + 2 < NT:
                    loads(t + 2)
                gens = [genB(t)]
                if t + 1 < NT:
                    gens.insert(0, genA(t + 1))
                while gens:
                    for g in list(gens):
                        try:
                            next(g)
                        except StopIteration:
                            gens.remove(g)
            self.end_phase()

    def tr8(self, src_b, pT, dst_b):
        T = self.T
        nc = self.nc

        def tr():
            ins = None
            for c in range(8):
                ins = nc.tensor.transpose(out=pT.t[:, c * 128:(c + 1) * 128], in_=src_b.t[:, c * 128:(c + 1) * 128],
                                          identity=self.ident.t[:])
            return ins
        T.op(T.pe, tr, reads=[src_b, self.ident], writes=[pT])
        T.op(T.act, lambda: nc.scalar.copy(out=dst_b.t[:].rearrange("p c t -> p (c t)"), in_=pT.t[:]), reads=[pT], writes=[dst_b])

    def ffn_up(self, l, S):
        T = self.T
        nc = self.nc
        I = self.inp
        nblk = (S + 509) // 510
        blocks = []
        for b in range(nblk):
            c0 = 510 * b
            w = min(512, S + 2 - c0)
            blocks.append((c0, w))
        passes = [blocks[i:i + 9] for i in range(0, nblk, 9)]
        H3v = self.H3T.rearrange("(c p) t -> p c t", p=128)
        WUv = I["w_up"][l].rearrange("(c p) n -> p c n", p=128)
        self.begin_phase()
        with ExitStack() as ctx:
            cp = T.sb(ctx, "convp", [128, 4, 44], F32)
            T.dma(T.sp, cp.t[:], I["convp"][l], self.newds(), writes=[cp])
            wab = [T.sb(ctx, "wab", [128, 8, 256], BF16) for _ in range(2)]
            wds = [self.newds("pool") for _ in range(2)]
            maxc = max(pb[-1][0] + pb[-1][1] - pb[0][0] for pb in passes)
            hres = T.sb(ctx, "hres", [128, 8, maxc], BF16)
            hds = self.newds()
            pU = Rot([T.ps(ctx, "pU", [128, 512], F32) for _ in range(6)])
            ca = Rot([T.sb(ctx, "ca", [128, 512], F32) for _ in range(4)])
            cg = Rot([T.sb(ctx, "cg", [128, 512], F32) for _ in range(4)])
            sg = Rot([T.sb(ctx, "sg", [128, 512], F32) for _ in range(3)])
            ast = [T.sb(ctx, "ast", [128, 512], BF16) for _ in range(4)]
            ads = [self.newds() for _ in range(4)]
            ai = 0

            def load_wab(i):
                b = wab[i % 2]
                T.dma(T.pool, b.t[:, :, 0:128], WUv[:, :, i * 128:(i + 1) * 128], wds[i % 2], writes=[b])
                T.dma(T.pool, b.t[:, :, 128:256], WUv[:, :, DFF + i * 128:DFF + (i + 1) * 128], wds[i % 2], writes=[b])

            for pb in passes:
                col_lo = pb[0][0]
                col_hi = pb[-1][0] + pb[-1][1]
                v_lo = max(col_lo, 1)
                v_hi = min(col_hi, S + 1)
                T.dma(T.sp, hres.t[:, :, v_lo - col_lo:v_hi - col_lo], H3v[:, :, v_lo:v_hi], hds, writes=[hres])
                if col_lo == 0:
                    T.op(T.pool, lambda: nc.gpsimd.memset(hres.t[:, :, 0:1], 0.0), writes=[hres])
                if col_hi == S + 2:
                    T.op(T.pool, lambda: nc.gpsimd.memset(hres.t[:, :, col_hi - col_lo - 1:col_hi - col_lo], 0.0), writes=[hres])
                load_wab(0)
                for i in range(22):
                    if i + 1 < 22:
                        load_wab(i + 1)
                    wb = wab[i % 2]
                    for (c0, w) in pb:
                        off = c0 - col_lo
                        ua = pU.next()
                        ug = pU.next()

                        def mm():
                            ins = None
                            for (dst, wo_) in ((ua, 0), (ug, 128)):
                                for c in range(8):
                                    ins = nc.tensor.matmul(dst.t[:, 0:w], lhsT=wb.t[:, c, wo_:wo_ + 128], rhs=hres.t[:, c, off:off + w],
                                                           start=(c == 0), stop=(c == 7))
                            return ins
                        T.op(T.pe, mm, reads=[wb, hres], writes=[ua, ug])
                        n = w - 2
                        outs = []
                        for (u, col, dstr) in ((ua, i, ca), (ug, 22 + i, cg)):
                            cb_ = dstr.next()
                            if n >= 256:
                                T.op(T.act, lambda: nc.scalar.activation(out=cb_.t[:, 0:n], in_=u.t[:, 0:n], func=AF.Identity,
                                                                         scale=cp.t[:, 0, col:col + 1], bias=cp.t[:, 3, col:col + 1]),
                                     reads=[u, cp], writes=[cb_])
                            else:
                                T.op(T.dve, lambda: nc.vector.tensor_scalar(out=cb_.t[:, 0:n], in0=u.t[:, 0:n], scalar1=cp.t[:, 0, col:col + 1],
                                                                            scalar2=cp.t[:, 3, col:col + 1], op0=ALU.mult, op1=ALU.add),
                                     reads=[u, cp], writes=[cb_])
                            T.op(T.dve, lambda: nc.vector.scalar_tensor_tensor(out=cb_.t[:, 0:n], in0=u.t[:, 1:n + 1], scalar=cp.t[:, 1, col:col + 1],
                                                                               in1=cb_.t[:, 0:n], op0=ALU.mult, op1=ALU.add),
                                 reads=[u, cp, cb_], writes=[cb_])
                            T.op(T.dve, lambda: nc.vector.scalar_tensor_tensor(out=cb_.t[:, 0:n], in0=u.t[:, 2:n + 2], scalar=cp.t[:, 2, col:col + 1],
                                                                               in1=cb_.t[:, 0:n], op0=ALU.mult, op1=ALU.add),
                                 reads=[u, cp, cb_], writes=[cb_])
                            outs.append(cb_)
                        sgb = sg.next()
                        T.op(T.act, lambda: nc.scalar.activation(out=sgb.t[:, 0:n], in_=outs[1].t[:, 0:n], func=AF.Silu), reads=[outs[1]], writes=[sgb])
                        ab = ast[ai % 4]
                        T.op(T.pool, lambda: nc.gpsimd.tensor_tensor(out=ab.t[:, 0:n], in0=sgb.t[:, 0:n], in1=outs[0].t[:, 0:n], op=ALU.mult),
                             reads=[sgb, outs[0]], writes=[ab])
                        T.dma(T.sp, self.ACTT[i * 128:(i + 1) * 128, c0:c0 + n], ab.t[:, 0:n], ads[ai % 4], reads=[ab])
                        ai += 1
            self.end_phase()

    def ffn_down(self, l, S, dst):
        T = self.T
        nc = self.nc
        I = self.inp
        NT = S // 128
        AV = self.ACTT.rearrange("(c p) t -> p c t", p=128)
        self.begin_phase()
        with ExitStack() as ctx:
            wd, wdb = self.load_w(ctx, "wd", I["w_down"][l], DFF, D)
            pY = Rot([T.ps(ctx, "pY", [128, 1024], F32) for _ in range(2)])
            xts = [T.sb(ctx, "xt", [128, D], F32) for _ in range(3)]
            xds = [self.newds() for _ in range(3)]
            ats = [T.sb(ctx, "aT", [128, 22, 128], BF16) for _ in range(2)]
            atds = [self.newds() for _ in range(2)]

            def loads(t):
                T.dma(T.sp, xts[t % 3].t[:], self.X[t * 128:(t + 1) * 128, :], xds[t % 3], writes=[xts[t % 3]])
                T.dma(T.sp, ats[t % 2].t[:], AV[:, :, t * 128:(t + 1) * 128], atds[t % 2], writes=[ats[t % 2]])
            loads(0)
            for t in range(NT):
                if t + 1 < NT:
                    loads(t + 1)
                xb = xts[t % 3]
                ab = ats[t % 2]
                py = pY.next()

                def mm():
                    ins = None
                    for nh in range(2):
                        for c in range(22):
                            ins = nc.tensor.matmul(py.t[:, nh * 512:(nh + 1) * 512], lhsT=ab.t[:, c, :], rhs=wd[:, c, nh * 512:(nh + 1) * 512],
                                                   start=(c == 0), stop=(c == 21))
                    return ins
                T.op(T.pe, mm, reads=[ab] + wdb, writes=[py])
                T.op(T.dve, lambda: nc.vector.tensor_tensor(out=xb.t[:], in0=py.t[:], in1=xb.t[:], op=ALU.add), reads=[py, xb], writes=[xb])
                T.dma(T.sp, dst[t * 128:(t + 1) * 128, :], xb.t[:], xds[t % 3], reads=[xb])
            self.end_phase()


def Buf_view(b):
    return b


def _rope_tab(half):
    pos = np.arange(SS_, dtype=np.float32)
    inv = (np.float32(10000.0) ** (-np.arange(half, dtype=np.float32) / np.float32(half))).astype(np.float32)
    ang = (pos[:, None] * inv[None, :]).astype(np.float32)
    c = np.cos(ang).astype(np.float32).reshape(64, 128, half).transpose(1, 0, 2)
    s = np.sin(ang).astype(np.float32).reshape(64, 128, half).transpose(1, 0, 2)
    return np.ascontiguousarray(c), np.ascontiguousarray(s)


def _dil_strips():
    out = np.zeros((128, DIL_W), np.float32)
    p = np.arange(128)[:, None]
    for g in range(3):
        r = DIL_R[g]
        w = DIL_DMAX[g] - DIL_DMIN[g] + 512
        x = np.arange(w)[None, :]
        delta = p - x + DIL_DMAX[g]
        ok = (delta % r == 0) & (np.abs(delta) <= 64 * r)
        out[:, DIL_OFF[g]:DIL_OFF[g] + w] = np.where(ok, 1.0, 0.0)
    return out


def _na_rowmask():
    R = 1000
    out = np.full((2, 3 * 8 * 8), NEG, np.float32)
    for ty, r0 in ((0, 0), (1, 496), (2, R - 8)):
        for ci in range(8):
            kr0 = r0 - 4 + 2 * ci
            for a in range(2):
                kr = kr0 + a
                for j in range(8):
                    r = r0 + j
                    rs = min(max(r - 4, 0), R - 8)
                    if rs <= kr < rs + 8:
                        out[a, (ty * 8 + ci) * 8 + j] = 0.0
    return out


def _na_table(rpb):
    Lh = rpb.shape[0]
    kc = np.arange(64)[:, None]
    c = np.arange(64)[None, :]
    cs = np.clip(c - 8, 0, 48)
    mcol = (kc >= cs) & (kc < cs + 16)
    dcidx = np.clip(kc - c + 15, 0, 30)
    out = np.full((Lh, 6, 128, NA_MM, 64), NEG, np.float32)
    for a in range(2):
        for mm in range(-3, NA_MM - 3):
            m = mm - a
            if 0 <= m <= 14:
                vals = rpb[:, :, 14 - m, :][:, :, dcidx]
                out[:, :, a * 64:(a + 1) * 64, mm + 3, :] = np.where(mcol[None, None], vals, np.float32(NEG))
    return np.ascontiguousarray(out.reshape(Lh, 6, 128, NA_MM * 64))


def prep_inputs(inputs, n_cores=8):
    f = lambda a: np.ascontiguousarray(np.asarray(a, dtype=np.float32))
    gv = np.concatenate([f(inputs[k]) for k in ("norm_mix", "mla_q_norm", "mla_kv_norm", "mla_qn", "mla_kn", "na_qn", "na_kn",
                                                "dil_qn", "dil_kn", "norm_cross", "x_qn", "norm_ffn", "norm_mem", "x_kn")], axis=1)
    assert gv.shape == (L, NG)
    cw = f(inputs["conv_w"])
    cbv = f(inputs["conv_b"])
    convp = np.concatenate([cw, cbv[:, None, :]], axis=1).reshape(L, 4, 44, 128).transpose(0, 3, 1, 2)
    cosd, sind = _rope_tab(32)
    cosm, sinm = _rope_tab(16)
    ea = np.zeros((2, 128), np.float32)
    ea[0, :64] = 1.0
    ea[1, 64:] = 1.0
    shared = {
        "gvec": np.ascontiguousarray(gv), "convp": np.ascontiguousarray(convp), "natab": _na_table(f(inputs["na_rpb"])),
        "ident": np.eye(128, dtype=np.float32), "cosd": cosd, "sind": sind, "cosm": cosm, "sinm": sinm,
        "dstrip": _dil_strips(), "rsmall": _na_rowmask(), "ea": ea,
    }
    for k in ("w_in", "w_uq", "w_uk", "w_uv", "w_o", "w_cq", "w_ckv", "w_co", "w_up", "w_down"):
        shared[k] = f(inputs[k])
    xp = f(inputs["x_prompt"])
    xs = f(inputs["x_sample"])
    mp = f(inputs["mem_prompt"])
    ms = f(inputs["mem_sample"])
    zs = np.zeros((SS_, D), np.float32)
    zm = np.zeros((MEM, D), np.float32)
    maps = []
    for c in range(n_cores):
        m = dict(shared)
        m["xp"] = xp[c]
        m["memp"] = mp[c]
        if c == 0:
            m["xs"], m["mems"] = xs[0], ms[0]
        elif c == 4:
            m["xs"], m["mems"] = xs[1], ms[1]
        else:
            m["xs"], m["mems"] = zs, zm
        maps.append(m)
    return maps


def kernel(**inputs):
    cfg = {"parts": [("p", SP_), ("s", SS_)]}
    nc = Prog(cfg).build()
    maps = prep_inputs(inputs)
    res = run_bass_kernel_spmd(nc, maps, core_ids=list(range(8)))
    yp = np.stack([np.asarray(res.results[c]["yp"], dtype=np.float32) for c in range(8)], axis=0)
    ys = np.stack([np.asarray(res.results[c]["ys"], dtype=np.float32) for c in (0, 4)], axis=0)
    return (yp, ys)
```
